# Optimizing a Trainium2 kernel written in Bass

```python
import math
import jax, jax.numpy as jnp
from jax import lax
import numpy as np

D_MODEL = 2048
BATCH = 4
SEQ = 2048
DEPTH = 2
DEC_BATCH = 128
DEC_SEQ = 4
PAST_LEN = 2048
PAGE_SIZE = 128

DH = 128
H_A = D_MODEL // (2 * DH)
HKV = 2
GRP = H_A // HKV
L_CMP = 32
D_CMP = 16
CMP_R = L_CMP // D_CMP
CMP_HID = 2 * DH
L_SEL = 64
N_TOP = 16
WINDOW = 512
Q_BLOCK = 64
N_KV_SLOTS = 4
H_B = 4
DK_B = D_MODEL // (4 * H_B)
DV_B = D_MODEL // (2 * H_B)
GLA_LR = 16
GLA_GATE_NORM = 16.0
DK_C = 128
H_C = D_MODEL // DK_C
DV_C = D_MODEL // H_C
REC_CHUNK = 16
N_A_LAYERS = (DEPTH + 1) // 2
N_C_LAYERS = DEPTH // 2
EPS = 1e-6
FORCE_SCORE = 1e9
A_SIZES = (H_A * DH, 6 * HKV * DH, 3 * H_A, H_A * DH, H_B * DK_B, H_B * DK_B, H_B * DV_B, GLA_LR, H_B * DV_B)
C_SIZES = (H_C * DK_C, H_C * DK_C, H_C * DV_C, H_C * DV_C)

kernel_name = 'nsa_gla_hgrn2_hybrid_step'


def split_cols(a, sizes):
    return jnp.split(a, np.cumsum(sizes)[:-1].tolist(), axis=-1)


def rmsnorm(x, w):
    xf = x.astype(jnp.float32)
    y = xf * lax.rsqrt(jnp.mean(xf * xf, axis=-1, keepdims=True) + EPS)
    return (y * w.astype(jnp.float32)).astype(x.dtype)


def masked_softmax(s, mask):
    s = jnp.where(mask, s.astype(jnp.float32), -jnp.inf)
    m = jnp.max(s, axis=-1, keepdims=True)
    m = jnp.where(jnp.isfinite(m), m, 0.0)
    e = jnp.where(mask, jnp.exp(s - m), 0.0)
    return e / jnp.maximum(jnp.sum(e, axis=-1, keepdims=True), 1e-30)


def gated_recurrence(q, k, v, log_a, s0):
    B_, T, H, _ = q.shape
    V = v.shape[-1]
    c = math.gcd(T, REC_CHUNK)
    n = T // c
    f32 = jnp.float32

    def blocks(a):
        return a.astype(f32).reshape(B_, n, c, H, a.shape[-1]).transpose(1, 0, 3, 2, 4)

    causal = jnp.tril(jnp.ones((c, c), dtype=bool))[None, None, :, :, None]

    def step(s, xs):
        qc, kc, vc, gc = xs
        b = jnp.cumsum(gc, axis=2)
        decay = jnp.exp(jnp.where(causal, b[:, :, :, None, :] - b[:, :, None, :, :], -jnp.inf))
        att = jnp.einsum('bhik,bhijk,bhjk->bhij', qc, decay, kc)
        o = jnp.einsum('bhij,bhjv->bhiv', att, vc) + jnp.einsum('bhik,bhkv->bhiv', qc * jnp.exp(b), s)
        b_last = b[:, :, -1:, :]
        s = jnp.exp(b_last)[:, :, 0, :, None] * s + jnp.einsum('bhjk,bhjv->bhkv', kc * jnp.exp(b_last - b), vc)
        return s, o

    s_fin, o = lax.scan(step, s0.astype(f32), (blocks(q), blocks(k), blocks(v), blocks(log_a)))
    o = o.transpose(1, 0, 3, 2, 4).reshape(B_, T, H, V)
    return o.astype(v.dtype), s_fin.astype(s0.dtype)


def compress_blocks(x, pe, w1, b1, w2):
    B_, L = x.shape[:2]
    n_str = L // D_CMP
    nc = n_str - CMP_R + 1
    s = x[:, :n_str * D_CMP].reshape(B_, n_str, D_CMP, HKV, DH)
    w1r = w1.reshape(CMP_R, D_CMP, DH, CMP_HID)
    u = jnp.einsum('bnphd,rpde->rbnhe', s, w1r)
    h = b1 + jnp.einsum('rpd,rpde->e', pe.reshape(CMP_R, D_CMP, DH), w1r)
    for r in range(CMP_R):
        h = h + u[r, :, r:r + nc]
    return jax.nn.gelu(h) @ w2


def nsa_attend(q, gates, kv, win_kv, q_start, win_base, cmp_pe, cmp_w1, cmp_b1, cmp_w2):
    B_, Tq = q.shape[:2]
    L = kv.shape[1]
    scale = DH ** -0.5
    kc = compress_blocks(kv[:, :, 0], cmp_pe[0], cmp_w1[0], cmp_b1[0], cmp_w2[0])
    vc = compress_blocks(kv[:, :, 1], cmp_pe[1], cmp_w1[1], cmp_b1[1], cmp_w2[1])
    nc = kc.shape[1]
    ns = -(-L // L_SEL)

    def sel_blocks(a):
        a = jnp.pad(a, ((0, 0), (0, ns * L_SEL - L), (0, 0), (0, 0)))
        return a.reshape(B_, ns, L_SEL, HKV, DH).transpose(0, 3, 1, 2, 4)

    ks_all = sel_blocks(kv[:, :, 2])
    vs_all = sel_blocks(kv[:, :, 3])
    c_start = np.arange(nc)[:, None] * D_CMP
    s_start = np.arange(ns)[None, :] * L_SEL
    overlap = np.clip(np.minimum(c_start + L_CMP, s_start + L_SEL) - np.maximum(c_start, s_start), 0, None)
    cmp_to_sel = jnp.asarray(overlap / D_CMP, dtype=jnp.float32)
    cmp_end = jnp.asarray(np.arange(nc) * D_CMP + L_CMP - 1, dtype=jnp.int32)
    blk = jnp.arange(ns, dtype=jnp.int32)
    n_top = min(N_TOP, ns)
    qb = math.gcd(Tq, Q_BLOCK)
    nq = Tq // qb
    sel_off = jnp.arange(L_SEL, dtype=jnp.int32)
    win_off = jnp.arange(WINDOW + qb, dtype=jnp.int32) - WINDOW
    g_idx = jnp.arange(HKV, dtype=jnp.int32)[None, :, None]

    def chunk(args):
        qx, gx, bi, t0 = args
        tpos = t0 + jnp.arange(qb, dtype=jnp.int32)
        qg = qx.reshape(qb, HKV, GRP, DH)
        s = jnp.einsum('qgrd,cgd->qgrc', qg, kc[bi]) * scale
        p_c = masked_softmax(s, (cmp_end[None, :] <= tpos[:, None])[:, None, None, :])
        o_c = jnp.einsum('qgrc,cgd->qgrd', p_c, vc[bi])
        imp = jnp.einsum('qgrc,cs->qgs', p_c, cmp_to_sel)
        cur = (tpos // L_SEL)[:, None]
        valid = blk[None, :] <= cur
        forced = (blk[None, :] == 0) | (blk[None, :] == cur) | (blk[None, :] == cur - 1)
        score = jnp.where(valid[:, None, :], jnp.where(forced[:, None, :], FORCE_SCORE, imp), -jnp.inf)
        _, top = lax.top_k(score, n_top)
        ks = ks_all[bi][g_idx, top]
        vs = vs_all[bi][g_idx, top]
        kpos = (top[..., None] * L_SEL + sel_off).reshape(qb, HKV, 1, n_top * L_SEL)
        s = jnp.einsum('qgrd,qgnkd->qgrnk', qg, ks).reshape(qb, HKV, GRP, n_top * L_SEL) * scale
        p_s = masked_softmax(s, kpos <= tpos[:, None, None, None])
        o_s = jnp.einsum('qgrm,qgmd->qgrd', p_s, vs.reshape(qb, HKV, n_top * L_SEL, DH))
        wk = lax.dynamic_slice_in_dim(win_kv[bi], t0 - WINDOW - win_base, WINDOW + qb, axis=0)
        wpos = t0 + win_off
        wmask = (wpos[None, :] >= 0) & (wpos[None, :] <= tpos[:, None]) & (wpos[None, :] > tpos[:, None] - WINDOW)
        s = jnp.einsum('qgrd,kgd->qgrk', qg, wk[:, 0]) * scale
        p_w = masked_softmax(s, wmask[:, None, None, :])
        o_w = jnp.einsum('qgrk,kgd->qgrd', p_w, wk[:, 1])
        gx = gx.reshape(qb, HKV, GRP, 3)
        o = gx[..., 0:1] * o_c + gx[..., 1:2] * o_s + gx[..., 2:3] * o_w
        return o.reshape(qb, H_A, DH).astype(qx.dtype)

    bi = jnp.repeat(jnp.arange(B_, dtype=jnp.int32), nq)
    t0 = q_start + jnp.tile(jnp.arange(nq, dtype=jnp.int32) * qb, B_)
    out = lax.map(chunk, (q.reshape(B_ * nq, qb, H_A, DH), gates.reshape(B_ * nq, qb, H_A, 3), bi, t0))
    return out.reshape(B_, Tq, H_A * DH)


def layer_a(x, norm_w, w_in, gla_w2, gla_b, gla_gn, cmp_pe, cmp_w1, cmp_b1, cmp_w2, w_out,
            past_kv, win_buf, gla_s0, q_start):
    B_, T, _ = x.shape
    h = rmsnorm(x, norm_w)
    q, kv, g_br, z_a, q_b, k_b, v_b, lr_b, z_b = split_cols(h @ w_in, A_SIZES)
    q = q.reshape(B_, T, H_A, DH)
    kv = kv.reshape(B_, T, 6, HKV, DH)
    gates = jax.nn.sigmoid(g_br.astype(jnp.float32)).reshape(B_, T, H_A, 3)
    new_rows = kv[:, :, :N_KV_SLOTS]
    new_win = kv[:, :, N_KV_SLOTS:]
    if past_kv is None:
        full = new_rows
        keep = min(WINDOW, T)
        win_state = new_win[:, T - keep:]
        win_kv = jnp.pad(new_win, ((0, 0), (WINDOW, 0), (0, 0), (0, 0), (0, 0)))
        win_base = -WINDOW
    else:
        full = jnp.concatenate([past_kv, new_rows], axis=1)
        buf = jnp.concatenate([win_buf, new_win], axis=1)
        wb = win_buf.shape[1]
        win_state = buf[:, buf.shape[1] - wb:]
        win_kv = jnp.pad(buf, ((0, 0), (WINDOW - wb, 0), (0, 0), (0, 0), (0, 0)))
        win_base = q_start - WINDOW
    o_a = nsa_attend(q, gates, full, win_kv, q_start, win_base, cmp_pe, cmp_w1, cmp_b1, cmp_w2) * jax.nn.silu(z_a)
    log_a = jax.nn.log_sigmoid((lr_b @ gla_w2 + gla_b).astype(jnp.float32)) / GLA_GATE_NORM
    o_b, s_b = gated_recurrence(q_b.reshape(B_, T, H_B, DK_B) * (DK_B ** -0.5),
                                k_b.reshape(B_, T, H_B, DK_B),
                                v_b.reshape(B_, T, H_B, DV_B),
                                log_a.reshape(B_, T, H_B, DK_B), gla_s0)
    o_b = rmsnorm(o_b, gla_gn).reshape(B_, T, H_B * DV_B) * jax.nn.silu(z_b)
    y = x + jnp.concatenate([o_a, o_b], axis=-1) @ w_out
    return y, new_rows, win_state, s_b


def layer_c(x, layer_idx, norm_w, w_in, lb_logits, gn, w_out, s0):
    B_, T, _ = x.shape
    h = rmsnorm(x, norm_w)
    q, f, i, z = split_cols(h @ w_in, C_SIZES)
    prob = jax.nn.softmax(lb_logits.astype(jnp.float32), axis=0)
    lb = jnp.cumsum(prob, axis=0)[layer_idx] - prob[0]
    log_f = jnp.logaddexp(jnp.log(lb), jnp.log1p(-lb) + jax.nn.log_sigmoid(f.astype(jnp.float32)))
    k = -jnp.expm1(log_f)
    o, s = gated_recurrence(jax.nn.silu(q).reshape(B_, T, H_C, DK_C), k.reshape(B_, T, H_C, DK_C),
                            i.reshape(B_, T, H_C, DV_C), log_f.reshape(B_, T, H_C, DK_C), s0)
    o = rmsnorm(o, gn).reshape(B_, T, H_C * DV_C) * jax.nn.silu(z)
    return x + o @ w_out, s


def setup_inputs(seed: int = 0) -> dict:
    key = jax.random.key(seed)
    ks = jax.random.split(key, 24)
    n_pages = PAST_LEN // PAGE_SIZE
    n_used = DEC_BATCH * n_pages
    n_pool = n_used + n_used // 4
    wb = min(WINDOW, PAST_LEN)

    def nrm(k, shape, s):
        return s * jax.random.normal(k, shape, jnp.float32)

    a_out_in = H_A * DH + H_B * DV_B
    return {
        'x_prompt': nrm(ks[0], (BATCH, SEQ, D_MODEL), 1.0),
        'x_sample': nrm(ks[1], (DEC_BATCH, DEC_SEQ, D_MODEL), 1.0),
        'cache_kv': nrm(ks[2], (N_A_LAYERS, n_pool, PAGE_SIZE, N_KV_SLOTS, HKV, DH), 1.0),
        'cache_win': nrm(ks[3], (N_A_LAYERS, DEC_BATCH, wb, 2, HKV, DH), 1.0),
        'state_gla': nrm(ks[4], (N_A_LAYERS, DEC_BATCH, H_B, DK_B, DV_B), 1.0),
        'state_hgrn': nrm(ks[5], (N_C_LAYERS, DEC_BATCH, H_C, DK_C, DV_C), 0.5),
        'page_table': jax.random.permutation(ks[6], n_pool)[:n_used].reshape(DEC_BATCH, n_pages).astype(jnp.int32),
        'a_norm': 1.0 + nrm(ks[7], (N_A_LAYERS, D_MODEL), 0.02),
        'a_w_in': nrm(ks[8], (N_A_LAYERS, D_MODEL, sum(A_SIZES)), D_MODEL ** -0.5),
        'a_gla_w2': nrm(ks[9], (N_A_LAYERS, GLA_LR, H_B * DK_B), GLA_LR ** -0.5),
        'a_gla_b': nrm(ks[10], (N_A_LAYERS, H_B * DK_B), 0.1),
        'a_gla_gn': 1.0 + nrm(ks[11], (N_A_LAYERS, DV_B), 0.02),
        'a_cmp_pe': nrm(ks[12], (N_A_LAYERS, 2, L_CMP, DH), 0.1),
        'a_cmp_w1': nrm(ks[13], (N_A_LAYERS, 2, L_CMP * DH, CMP_HID), (L_CMP * DH) ** -0.5),
        'a_cmp_b1': nrm(ks[14], (N_A_LAYERS, 2, CMP_HID), 0.02),
        'a_cmp_w2': nrm(ks[15], (N_A_LAYERS, 2, CMP_HID, DH), CMP_HID ** -0.5),
        'a_w_out': nrm(ks[16], (N_A_LAYERS, a_out_in, D_MODEL), a_out_in ** -0.5),
        'c_norm': 1.0 + nrm(ks[17], (N_C_LAYERS, D_MODEL), 0.02),
        'c_w_in': nrm(ks[18], (N_C_LAYERS, D_MODEL, sum(C_SIZES)), D_MODEL ** -0.5),
        'c_lb_logits': nrm(ks[19], (DEPTH, H_C * DK_C), 0.5),
        'c_gn': 1.0 + nrm(ks[20], (N_C_LAYERS, DV_C), 0.02),
        'c_w_out': nrm(ks[21], (N_C_LAYERS, H_C * DV_C, D_MODEL), (H_C * DV_C) ** -0.5),
        'final_norm': 1.0 + nrm(ks[22], (D_MODEL,), 0.02),
    }


def reference(x_prompt, x_sample, cache_kv, cache_win, state_gla, state_hgrn, page_table,
              a_norm, a_w_in, a_gla_w2, a_gla_b, a_gla_gn, a_cmp_pe, a_cmp_w1, a_cmp_b1, a_cmp_w2, a_w_out,
              c_norm, c_w_in, c_lb_logits, c_gn, c_w_out, final_norm):
    n_seq, n_pages = page_table.shape
    past_len = n_pages * cache_kv.shape[2]

    def run(x, sample):
        rows_l, win_l, gla_l, hg_l = [], [], [], []
        for l in range(DEPTH):
            if l % 2 == 0:
                a = l // 2
                if sample:
                    past = cache_kv[a][page_table].reshape(n_seq, past_len, N_KV_SLOTS, HKV, DH)
                    buf, s0, q_start = cache_win[a], state_gla[a], past_len
                else:
                    past, buf, q_start = None, None, 0
                    s0 = jnp.zeros((x.shape[0], H_B, DK_B, DV_B), x.dtype)
                x, rows, win, sg = layer_a(x, a_norm[a], a_w_in[a], a_gla_w2[a], a_gla_b[a], a_gla_gn[a],
                                           a_cmp_pe[a], a_cmp_w1[a], a_cmp_b1[a], a_cmp_w2[a], a_w_out[a],
                                           past, buf, s0, q_start)
                rows_l.append(rows)
                win_l.append(win)
                gla_l.append(sg)
            else:
                c = l // 2
                s0 = state_hgrn[c] if sample else jnp.zeros((x.shape[0], H_C, DK_C, DV_C), x.dtype)
                x, sh = layer_c(x, l, c_norm[c], c_w_in[c], c_lb_logits, c_gn[c], c_w_out[c], s0)
                hg_l.append(sh)
        return rmsnorm(x, final_norm), jnp.stack(rows_l), jnp.stack(win_l), jnp.stack(gla_l), jnp.stack(hg_l)

    y_prompt, kv_p, win_p, gla_p, hg_p = run(x_prompt, False)
    y_sample, kv_s, win_s, gla_s, hg_s = run(x_sample, True)
    return (y_prompt, y_sample, kv_p, kv_s, win_p, win_s, gla_p, gla_s, hg_p, hg_s)
```

```python
import numpy as np
from contextlib import ExitStack
import concourse.bass as bass
import concourse.mybir as mybir
from concourse.bass_utils import run_bass_kernel_spmd

F32 = mybir.dt.float32
BF16 = mybir.dt.bfloat16
I32 = mybir.dt.int32
AF = mybir.ActivationFunctionType
ALU = mybir.AluOpType
AX = mybir.AxisListType

NCORES = 8
D = 2048
T = 2048
NS = 16
TS = 64
TA = T + TS
NT = 17
EPS = 1e-6
STAGE = 4
DEBUG = False
STOPAT = 99


class Buf:
    __slots__ = ("name", "w", "r")

    def __init__(self, name=""):
        self.name = name
        self.w = None
        self.r = []


class TT:
    def __init__(self, t, name):
        self.t = t
        self.b = Buf(name)

    def __getitem__(self, k):
        return self.t[k]


class Sched:
    ENG = ("pe", "act", "dve", "pool", "sp")
    NDMA = 12

    def __init__(self, nc, es):
        self.nc = nc
        self.es = es
        self.eng = {"pe": nc.tensor, "act": nc.scalar, "dve": nc.vector, "pool": nc.gpsimd, "sp": nc.sync}
        self.sems = {}
        self.cnt = {}
        for e in ("pe", "act", "dve", "pool"):
            self.sems[e] = es.enter_context(nc.semaphore("s_" + e))
            self.cnt[e] = 0
        for q in ("sp", "act", "pool"):
            for i in range(self.NDMA):
                k = f"d_{q}{i}"
                self.sems[k] = es.enter_context(nc.semaphore(k))
                self.cnt[k] = 0
        self.dma_rr = {"sp": 0, "act": 0, "pool": 0}
        self.waited = {e: {} for e in self.ENG}
        self.n_inst = 0
        self.n_wait = 0
        self.uid = 0
        self.freed = {}
        self._scopes = []

    def scope(self):
        from contextlib import contextmanager

        @contextmanager
        def cm():
            rec = []
            self._scopes.append(rec)
            try:
                with ExitStack() as es:
                    yield es
            finally:
                self._scopes.pop()
                for tt in rec:
                    toks = list(tt.b.r) + ([tt.b.w] if tt.b.w else [])
                    for k, v in toks:
                        if self.freed.get(k, 0) < v:
                            self.freed[k] = v
        return cm()

    def tile(self, name, shape, dtype, es=None):
        self.uid += 1
        t = (es or self.es).enter_context(self.nc.sbuf_tensor(f"{name}_{self.uid}", list(shape), dtype))
        tt = TT(t, name)
        tt.b.r = list(self.freed.items())
        if es is not None and self._scopes:
            self._scopes[-1].append(tt)
        return tt

    def ptile(self, name, shape, dtype=F32, es=None):
        self.uid += 1
        t = (es or self.es).enter_context(self.nc.psum_tensor(f"{name}_{self.uid}", list(shape), dtype))
        return TT(t, name)

    def _deps(self, reads, writes):
        deps = {}

        def add(t):
            if t is None:
                return
            k, v = t
            if deps.get(k, 0) < v:
                deps[k] = v
        for b in reads:
            add(b.w)
        for b in writes:
            add(b.w)
            for t in b.r:
                add(t)
        return deps

    def _wait(self, e, deps):
        eng = self.eng[e]
        wd = self.waited[e]
        for k, v in deps.items():
            if wd.get(k, 0) < v:
                eng.wait_ge(self.sems[k], v)
                wd[k] = v
                self.n_wait += 1

    def _commit(self, tok, reads, writes):
        for b in reads:
            b.r.append(tok)
            if len(b.r) > 64:
                m = {}
                for k, v in b.r:
                    if m.get(k, 0) < v:
                        m[k] = v
                b.r = list(m.items())
        for b in writes:
            b.w = tok
            b.r = []

    @staticmethod
    def _bl(xs):
        out = []
        for x in xs:
            b = x.b if isinstance(x, (TT, View)) else x
            if isinstance(b, (list, tuple)):
                out.extend(b)
            else:
                out.append(b)
        return out

    def op(self, e, fn, reads=(), writes=()):
        reads = self._bl(reads)
        writes = self._bl(writes)
        deps = self._deps(reads, writes)
        if e == "pe":
            deps.pop("pe", None)
        self._wait(e, deps)
        ins = fn(self.eng[e])
        self.cnt[e] += 1
        ins.then_inc(self.sems[e], 1)
        self._commit((e, self.cnt[e]), reads, writes)
        self.n_inst += 1
        return ins

    def dma(self, q, out, in_, reads=(), writes=(), fn=None, **kw):
        reads = self._bl(reads)
        writes = self._bl(writes)
        deps = self._deps(reads, writes)
        self._wait(q, deps)
        i = self.dma_rr[q]
        self.dma_rr[q] = (i + 1) % self.NDMA
        k = f"d_{q}{i}"
        if fn is None:
            ins = self.eng[q].dma_start(out=out, in_=in_, **kw)
        else:
            ins = fn(self.eng[q])
        self.cnt[k] += 16
        ins.then_inc(self.sems[k], 16)
        self._commit((k, self.cnt[k]), reads, writes)
        self.n_inst += 1

    def finish(self, bufs):
        deps = self._deps([], self._bl(bufs))
        self._wait("sp", deps)


class View:
    def __init__(self, ap, b):
        self.t = ap
        self.b = b

    def __getitem__(self, k):
        return self.t[k]


class Rot:
    def __init__(self, items):
        self.items = items
        self.i = 0

    def next(self):
        x = self.items[self.i]
        self.i = (self.i + 1) % len(self.items)
        return x


def make_consts():
    c = {}
    c["ident"] = np.eye(128, dtype=np.float32)
    j = np.arange(128)[:, None]
    i = np.arange(128)[None, :]
    c["mle"] = (j <= i).astype(np.float32)
    c["mgt"] = (j > i).astype(np.float32)
    j6 = np.arange(64)[:, None]
    i6 = np.arange(64)[None, :]
    c["bd"] = ((j6 // 4 == i6 // 4) & (j6 <= i6)).astype(np.float32)
    cm = (np.arange(64)[None, :] // 4 == np.arange(16)[:, None]).astype(np.float32)
    c["colmask"] = np.broadcast_to(cm[None], (128, 16, 64)).copy()
    c["rowmask"] = (np.arange(64)[:, None] // 4 == np.arange(16)[None, :]).astype(np.float32)
    def c2s(nc_, ns_):
        cst = np.arange(nc_)[:, None] * 16
        sst = np.arange(ns_)[None, :] * 64
        ov = np.clip(np.minimum(cst + 32, sst + 64) - np.maximum(cst, sst), 0, None)
        return (ov / 16).astype(np.float32)
    c["c2sp"] = c2s(127, 32)
    c["c2ss"] = c2s(127, 33)
    def eexp(ns_, nk_):
        key = np.arange(nk_ * 128)
        return (np.arange(ns_)[:, None] == (key[None, :] // 64)).astype(np.float32)
    c["eexp"] = eexp(32, 16)
    c["eexs"] = eexp(33, 17)
    x = np.arange(63)[None, :] - 31 - (np.arange(128)[:, None] >= 64)
    c["wsel"] = np.where(x > 0, -1e9, np.where(x >= -1, 1e9, 0.0)).astype(np.float32)
    c["rsum"] = (np.arange(16)[:, None] % 4 == np.arange(4)[None, :]).astype(np.float32)
    return c


CONST_SHAPES = {k: v.shape for k, v in make_consts().items()}


def build_program(stage=STAGE):
    nc = bass.Bass("TRN2", target_bir_lowering=False)

    def din(name, shape, dt=F32):
        return nc.dram_tensor(name, list(shape), dt, kind="ExternalInput").ap()

    def dout(name, shape, dt=F32):
        return nc.dram_tensor(name, list(shape), dt, kind="ExternalOutput").ap()

    def dscr(name, shape, dt=F32):
        return nc.dram_tensor(name, list(shape), dt, kind="ExternalOutput" if DEBUG else "Internal").ap()

    x_p = din("x_p", [T, D])
    x_s = din("x_s", [TS, D])
    cache_win = din("cache_win", [NS, 512, 512])
    cache_kv = din("cache_kv", [2560 * 128 * 2, 512])
    page_table = din("page_table", [NS, 16], I32)
    state_gla = din("state_gla", [NS, 4, 128, 256])
    state_hgrn = din("state_hgrn", [NS, 16, 128, 128])
    a_norm = din("a_norm", [128, 16])
    c_norm = din("c_norm", [128, 16])
    f_norm = din("f_norm", [D])
    wA = din("wA", [128, 16, 6696])
    wG = din("wG", [128, 16, 3088])
    wC = din("wC", [128, 16, 8192])
    wOA = din("wOA", [128, 16, 2048])
    wOC = din("wOC", [128, 16, 2048])
    gla_w2a = din("gla_w2a", [17, 512])
    gla_gn = din("gla_gn", [256])
    c_gn = din("c_gn", [128])
    lb_log = din("lb_log", [128, 2, 16])
    cmp_w1 = din("cmp_w1", [2, 128, 32, 256])
    cmp_w2 = din("cmp_w2", [128, 2, 2, 128])
    cmp_pe = din("cmp_pe", [128, 2, 32])
    cmp_b1 = din("cmp_b1", [128, 2, 2])
    cst = {k: din("c_" + k, list(s)) for k, s in CONST_SHAPES.items()}

    y_p = dout("y_p", [T, D])
    y_s = dout("y_s", [TS, D])
    kv_p = dout("kv_p", [T, 1024])
    kv_s = dout("kv_s", [TS, 1024])
    win_p = dout("win_p", [512, 512])
    win_s = dout("win_s", [NS, 512, 512])
    gla_p = dout("gla_p", [4, 128, 256])
    gla_s = dout("gla_s", [NS, 4, 128, 256])
    hg_p = dout("hg_p", [16, 128, 128])
    hg_s = dout("hg_s", [NS, 16, 128, 128])

    oT_scr = dscr("oT_scr", [NT, 128, 16, 128], BF16)
    g_scr = dscr("g_scr", [TS, 24])
    z_scr = dscr("z_scr", [TS, 1024], BF16)
    x1_scr = dscr("x1_scr", [TA, D])
    y_scr = dscr("y_scr", [TA, D])
    oT_b = [[Buf(f"oT{t}_{f}") for f in range(16)] for t in range(NT)]
    x1_b = [[Buf(f"x1_{t}_{k}") for k in range(4)] for t in range(NT)]
    ys_b = [[Buf(f"ys_{t}_{k}") for k in range(4)] for t in range(NT)]

    outs = []
    dbg_n = [0]

    with ExitStack() as es:
        S = Sched(nc, es)

        def dbg(name, tt, ap, shape, dt=F32):
            if not DEBUG:
                return
            dbg_n[0] += 1
            d = nc.dram_tensor(f"dbg_{name}", list(shape), dt, kind="ExternalOutput").ap()
            b = Buf(name)
            outs.append(b)
            S.dma("sp", d, ap, reads=[tt], writes=[b])

        def obuf(name):
            b = Buf(name)
            outs.append(b)
            return b

        hT_ref = [None]
        wbuf = Rot([S.tile(f"wbuf{i}", [128, 16, 512], BF16) for i in range(2)])
        ident_f = S.tile("ident_f", [128, 128], F32)
        ident_b = S.tile("ident_b", [128, 128], BF16)
        mle_b = S.tile("mle_b", [128, 128], BF16)
        normA = S.tile("normA", [128, 16], F32)
        normC = S.tile("normC", [128, 16], F32)
        PP = [S.ptile(f"pp{i}", [128, 1024], F32) for i in range(4)]
        bankb = [Buf(f"bank{i}") for i in range(8)]
        psb = [View(PP[i // 2][:, (i % 2) * 512:(i % 2) * 512 + 512], bankb[i]) for i in range(8)]
        psA = View(PP[2][:, :], [bankb[4], bankb[5]])
        psB = View(PP[3][:, :], [bankb[6], bankb[7]])
        psbf = Rot([View(PP[3][:, k * 512:(k + 1) * 512].bitcast(BF16), bankb[6 + k]) for k in range(2)])

        S.dma("sp", ident_f[:], cst["ident"][:, :], writes=[ident_f])
        S.dma("pool", ident_b[:], cst["ident"][:, :], writes=[ident_b])
        S.dma("pool", mle_b[:], cst["mle"][:, :], writes=[mle_b])
        S.dma("sp", normA[:], a_norm[:, :], writes=[normA])
        S.dma("sp", normC[:], c_norm[:, :], writes=[normC])

        for s in range(NS):
            S.dma("sp", win_s[s, 0:508, :], cache_win[s, 4:512, :], writes=[obuf("win_s")])

        def norm_pass(src_fn, normw, src_deps):
            with S.scope() as es1:
                xrot = Rot([S.tile(f"xt{i}", [128, D], F32, es1) for i in range(2)])
                ssr = Rot([S.tile(f"ss{i}", [128, 1], F32, es1) for i in range(2)])
                rsr = Rot([S.tile(f"rs{i}", [128, 1], F32, es1) for i in range(2)])
                rstdr = Rot([S.tile(f"rstd{i}", [128, 1], F32, es1) for i in range(2)])
                junk = S.tile("junk", [128, D], BF16, es1)
                prot = Rot(psb[0:4])
                for t in range(NT):
                    rows = 128 if t < 16 else TS
                    tok0 = t * 128
                    xt, ss, rs, rstd = xrot.next(), ssr.next(), rsr.next(), rstdr.next()
                    S.dma("sp", xt[:rows, :], src_fn(t, rows), reads=src_deps(t), writes=[xt])
                    S.op("act", lambda e: e.activation(out=junk[:rows, :], in_=xt[:rows, :], func=AF.Square,
                                                       accum_out=ss[:rows, 0:1]), reads=[xt], writes=[junk, ss])
                    S.op("act", lambda e: e.activation(out=rs[:rows, :], in_=ss[:rows, :], func=AF.Sqrt,
                                                       scale=1.0 / D, bias=EPS), reads=[ss], writes=[rs])
                    S.op("dve", lambda e: e.reciprocal(out=rstd[:rows, :], in_=rs[:rows, :]), reads=[rs], writes=[rstd])
                    S.op("act", lambda e: e.activation(out=xt[:rows, :], in_=xt[:rows, :], func=AF.Copy,
                                                       scale=rstd[:rows, 0:1]), reads=[xt, rstd], writes=[xt])
                    for cq in range(4):
                        pb = prot.next()
                        for k in range(4):
                            c = 4 * cq + k
                            S.op("pe", lambda e: e.transpose(out=pb[:, k * 128:k * 128 + rows],
                                                             in_=xt[:rows, c * 128:(c + 1) * 128],
                                                             identity=ident_f[:rows, :rows]),
                                 reads=[xt, ident_f], writes=[pb])
                        S.op("dve", lambda e: e.tensor_tensor(
                            out=hT_ref[0][:, 4 * cq:4 * cq + 4, tok0:tok0 + rows],
                            in0=pb[:, :].rearrange("p (k t) -> p k t", k=4)[:, :, :rows],
                            in1=normw[:, 4 * cq:4 * cq + 4].unsqueeze(2).to_broadcast([128, 4, rows]),
                            op=ALU.mult), reads=[pb, normw], writes=[hT_ref[0]])


        def load_w(col0, n, src):
            wb = wbuf.next()
            S.dma("pool", wb[:, :, 0:n], src[:, :, col0:col0 + n], writes=[wb])
            return wb

        prj = Rot(psb[0:4])
        evq = Rot(["act", "dve"])

        def proj_tok(wb, n, consume, tiles=range(NT), wc0=0):
            for t in tiles:
                rows = 128 if t < 16 else TS
                pb = prj.next()
                for c in range(16):
                    S.op("pe", lambda e: e.matmul(pb[:rows, 0:n], lhsT=hT_ref[0][:, c, t * 128:t * 128 + rows],
                                                  rhs=wb[:, c, wc0:wc0 + n], start=(c == 0), stop=(c == 15)),
                         reads=[hT_ref[0], wb], writes=[pb])
                consume(t, rows, pb)

        def proj_feat(wb, wc0, m, consume, chunks=range(5)):
            for j in chunks:
                n = 512 if j < 4 else TS
                pb = prj.next()
                for c in range(16):
                    S.op("pe", lambda e: e.matmul(pb[:m, 0:n], lhsT=wb[:, c, wc0:wc0 + m],
                                                  rhs=hT_ref[0][:, c, j * 512:j * 512 + n], start=(c == 0), stop=(c == 15)),
                         reads=[hT_ref[0], wb], writes=[pb])
                consume(j, n, pb)

        def evac(out_ap, in_ap, reads, writes, func=None):
            q = "act" if func is not None else evq.next()
            if q == "act":
                S.op("act", lambda e: e.activation(out=out_ap, in_=in_ap, func=func or AF.Copy), reads=reads, writes=writes)
            else:
                S.op("dve", lambda e: e.tensor_copy(out=out_ap, in_=in_ap), reads=reads, writes=writes)

        OQ, OKV, OG, OZA = 0, 1024, 2560, 2584

        SC = 128 ** -0.5
        with S.scope() as esN:
            gts = S.tile("gts", [128, NT, 24], F32, esN)
            qTs = S.tile("qTs", [128, 8, TS], BF16, esN)
            zs = S.tile("zs", [128, 1024], BF16, esN)
            vnew_src = S.tile("vnew_src", [TS, 2, 2, 130], BF16, esN)
            S.op("pool", lambda e: e.memset(vnew_src[:, :, :, 128:130], 1.0), writes=[vnew_src])
            mgt_b = S.tile("mgt_b", [128, 128], BF16, esN)
            S.dma("pool", mgt_b[:, :], cst["mgt"][:, :], writes=[mgt_b])
            kselTs = S.tile("kselTs", [128, 2, TS], BF16, esN)
            kwinTs = S.tile("kwinTs", [128, 2, TS], BF16, esN)
            with S.scope() as esH:
                hT_ref[0] = S.tile("hT", [128, 16, TA], BF16, esH)
                norm_pass(lambda t, rows: (x_p[t * 128:(t + 1) * 128, :] if t < 16 else x_s[:, :]), normA, lambda t: [])
                R = dict(mle_b=mle_b, ident_b=ident_b, ident_f=ident_f, psb=psb, psbf=psbf, cst=cst, obuf=obuf,
                         oT_scr=oT_scr, oT_b=oT_b, dbg=dbg)

                with S.scope() as esG:
                    lrT = S.tile("lrT", [17, TA], F32, esG)
                    w2a = S.tile("w2a", [17, 512], F32, esG)
                    S.dma("sp", w2a[:, :], gla_w2a[:, :], writes=[w2a])
                    for j in range(0, TA, 128):
                        n = min(128, TA - j)
                        S.dma("sp", lrT[16:17, j:j + n], cst["mle"][0:1, 0:n], writes=[lrT])
                    wb = load_w(3072, 16, wG)
                    proj_feat(wb, 0, 16, lambda j, n, pb: evac(lrT[0:16, j * 512:j * 512 + n], pb[0:16, 0:n], [pb], [lrT]))
                    zb_shared = S.tile("zb", [128, NT, 256], BF16, esG)
                    hsets = Rot([dict(q=S.tile(f"qbT{i}", [128, TA], BF16, esG), k=S.tile(f"kbT{i}", [128, TA], BF16, esG),
                                      v=S.tile(f"vb{i}", [128, NT, 256], BF16, esG), z=zb_shared)
                                 for i in range(2)])
                    G = rec_setup(S, esG, R, V=256)
                    gnb = S.tile("gnb", [128, 256], F32, esG)
                    S.dma("sp", gnb[:, :], gla_gn.partition_broadcast(128), writes=[gnb])
                    lg = Rot([S.tile(f"lg{i}", [128, 128], F32, esG) for i in range(2)])

                    for h in range(4):
                        hs = hsets.next()
                        wb = load_w(h * 768, 512, wG)
                        proj_feat(wb, 0, 128, lambda j, n, pb: evac(hs["q"][:, j * 512:j * 512 + n], pb[:, 0:n], [pb], [hs["q"]]))
                        proj_feat(wb, 128, 128, lambda j, n, pb: evac(hs["k"][:, j * 512:j * 512 + n], pb[:, 0:n], [pb], [hs["k"]]))
                        proj_tok(wb, 256, lambda t, rows, pb: evac(hs["v"][:rows, t, :], pb[:rows, 0:256], [pb], [hs["v"]]), wc0=256)
                        wb = load_w(h * 768 + 512, 256, wG)
                        proj_tok(wb, 256, lambda t, rows, pb: evac(hs["z"][:rows, t, :], pb[:rows, 0:256], [pb], [hs["z"]], func=AF.Silu))

                        def gate(tok, n, h=h):
                            l_ = lg.next()
                            p_g = psb[3]
                            S.op("pe", lambda e: e.matmul(p_g[:, 0:n], lhsT=w2a[0:17, h * 128:(h + 1) * 128], rhs=lrT[0:17, tok],
                                                          start=True, stop=True), reads=[w2a, lrT], writes=[p_g])
                            S.op("act", lambda e: e.activation(out=l_[:, 0:n], in_=p_g[:, 0:n], func=AF.Exp, scale=-1.0), reads=[p_g], writes=[l_])
                            S.op("act", lambda e: e.activation(out=l_[:, 0:n], in_=l_[:, 0:n], func=AF.Ln, bias=1.0, scale=1.0), reads=[l_], writes=[l_])
                            return l_[:, 0:n], l_
                        oh = rec_head(S, G, hs, V=256, COEF=-1.0 / 16.0, QS=128 ** -0.5, gate=gate,
                                      state_in=lambda s: state_gla[s, h, :, :], out_p=gla_p[h, :, :], out_s=lambda s: gla_s[s, h, :, :])
                        if h == 0:
                            dbg("gnb", gnb, gnb[:, :], [128, 256])
                            dbg("oh", oh, oh[:, 0, :], [128, 256], BF16)
                            dbg("zb", hs["z"], hs["z"][:, 0, :], [128, 256], BF16)
                        post_head(S, G, oh, hs["z"], gnb, V=256, fc0=8 + 2 * h, dbgh=(h == 0))
                        if STOPAT <= 1:
                            break

                with S.scope() as esNP:
                    vsel = S.tile("vsel", [128, NT, 2, 130], BF16, esNP)
                    vwin = S.tile("vwin", [128, NT, 2, 130], BF16, esNP)
                    S.op("pool", lambda e: e.memset(vsel[:, :, :, 128:130], 1.0), writes=[vsel])
                    S.op("pool", lambda e: e.memset(vwin[:, :, :, 128:130], 1.0), writes=[vwin])
                    kselT = S.tile("kselT", [128, 2, T], BF16, esNP)
                    kwinT = S.tile("kwinT", [128, 2, T], BF16, esNP)
                    kcT = S.tile("kcT", [128, 2, 128], BF16, esNP)
                    vca = S.tile("vca", [128, 2, 162], BF16, esNP)
                    eexp = S.tile("eexp", [32, 16, 128], BF16, esNP)
                    S.dma("pool", eexp[:, :, :], cst["eexp"].rearrange("s (k j) -> s k j", k=16), writes=[eexp])
                    wsel = S.tile("wsel", [128, 63], F32, esNP)
                    S.dma("sp", wsel[:, :], cst["wsel"][:, :], writes=[wsel])
                    S.op("pool", lambda e: e.memset(vca[:, :, 128:129], 1.0), writes=[vca])
                    for g in range(2):
                        S.dma("pool", vca[0:127, g, 129:161], cst["c2sp"][:, :], writes=[vca])

                    with S.scope() as esKV:
                        stg = Rot([S.tile(f"stg{i}", [128, 512], F32, esKV) for i in range(3)])
                        for blk in range(3):
                            wb = load_w(OKV + blk * 512, 512, wA)

                            def cons(t, rows, pb, blk=blk):
                                st = stg.next()
                                evac(st[:rows, :], pb[:rows, :], [pb], [st])
                                if blk < 2:
                                    dst = (kv_p[t * 128:t * 128 + rows, blk * 512:(blk + 1) * 512] if t < 16
                                           else kv_s[:, blk * 512:(blk + 1) * 512])
                                    S.dma("sp", dst, st[:rows, :], reads=[st], writes=[obuf("kv")])
                                else:
                                    if t >= 12 and t < 16:
                                        S.dma("sp", win_p[(t - 12) * 128:(t - 11) * 128, :], st[:rows, :], reads=[st], writes=[obuf("win")])
                                    elif t == 16:
                                        for s in range(NS):
                                            S.dma("sp", win_s[s, 508:512, :], st[4 * s:4 * s + 4, :], reads=[st], writes=[obuf("wins")])
                                if blk >= 1:
                                    vt = vsel if blk == 1 else vwin
                                    S.op("pool", lambda e: e.tensor_copy(out=vt[:rows, t, :, 0:128],
                                                                         in_=st[:rows, 256:512].rearrange("p (g d) -> p g d", g=2)),
                                         reads=[st], writes=[vt])
                                    if t == 16:
                                        S.op("pool", lambda e: e.tensor_copy(out=vnew_src[:, blk - 1, :, 0:128],
                                                                             in_=st[:rows, 256:512].rearrange("p (g d) -> p g d", g=2)),
                                             reads=[st], writes=[vnew_src])
                            proj_tok(wb, 512, cons)

                    if stage >= 3:
                        with S.scope() as esCmp:
                            kcmpT = S.tile("kcmpT", [128, 2, T], BF16, esCmp)
                            vcmpT = S.tile("vcmpT", [128, 2, T], BF16, esCmp)
                            wb = load_w(OKV, 512, wA)
                            for i, dstt in enumerate((kcmpT, kcmpT, vcmpT, vcmpT)):
                                proj_feat(wb, i * 128, 128, lambda j, n, pb, dstt=dstt, i=i: evac(dstt[:, i % 2, j * 512:j * 512 + n], pb[:, 0:n], [pb], [dstt]),
                                          chunks=range(4))
                            wb = load_w(OKV + 512, 256, wA)
                            for g in range(2):
                                proj_feat(wb, g * 128, 128, lambda j, n, pb, g=g: (evac(kselT[:, g, j * 512:j * 512 + n], pb[:, 0:n], [pb], [kselT]) if j < 4
                                                                                  else evac(kselTs[:, g, :], pb[:, 0:n], [pb], [kselTs])))
                            wb = load_w(OKV + 1024, 256, wA)
                            for g in range(2):
                                proj_feat(wb, g * 128, 128, lambda j, n, pb, g=g: (evac(kwinT[:, g, j * 512:j * 512 + n], pb[:, 0:n], [pb], [kwinT]) if j < 4
                                                                                  else evac(kwinTs[:, g, :], pb[:, 0:n], [pb], [kwinTs])))
                            w1b = S.tile("w1b", [128, 32, 256], BF16, esCmp)
                            w2b = S.tile("w2b", [128, 2, 2, 128], BF16, esCmp)
                            peb = S.tile("peb", [128, 2, 32], BF16, esCmp)
                            b1t = S.tile("b1t", [128, 2, 2], F32, esCmp)
                            hb = S.tile("hb", [128, 2, 2], F32, esCmp)
                            gT = Rot([S.tile(f"gT{i}", [128, 2, 128], BF16, esCmp) for i in range(2)])
                            S.dma("pool", w2b[:, :, :, :], cmp_w2[:, :, :, :], writes=[w2b])
                            S.dma("pool", peb[:, :, :], cmp_pe[:, :, :], writes=[peb])
                            S.dma("sp", b1t[:, :, :], cmp_b1[:, :, :], writes=[b1t])
                            for kv in range(2):
                                S.dma("pool", w1b[:, :, :], cmp_w1[kv, :, :, :], writes=[w1b])
                                xT = kcmpT if kv == 0 else vcmpT
                                pp = psb[3]
                                for half in range(2):
                                    for rp in range(32):
                                        S.op("pe", lambda e: e.matmul(pp[:, half:half + 1], lhsT=w1b[:, rp, half * 128:(half + 1) * 128],
                                                                      rhs=peb[:, kv, rp:rp + 1], start=(rp == 0), stop=(rp == 31)),
                                             reads=[w1b, peb], writes=[pp])
                                S.op("dve", lambda e: e.tensor_tensor(out=hb[:, kv, :], in0=pp[:, 0:2], in1=b1t[:, kv, :], op=ALU.add),
                                     reads=[pp, b1t], writes=[hb])
                                for g in range(2):
                                    gt = gT.next()
                                    for half in range(2):
                                        ph = prj.next()
                                        for rp in range(32):
                                            r_, p_ = rp // 16, rp % 16
                                            st0 = 16 * r_ + p_
                                            S.op("pe", lambda e: e.matmul(ph[:, 0:127], lhsT=w1b[:, rp, half * 128:(half + 1) * 128],
                                                                          rhs=xT[:, g, st0:st0 + 16 * 126 + 1:16], start=(rp == 0), stop=(rp == 31)),
                                                 reads=[w1b, xT], writes=[ph])
                                        S.op("act", lambda e: e.activation(out=gt[:, half, 0:127], in_=ph[:, 0:127], func=AF.Gelu_apprx_tanh,
                                                                           bias=hb[:, kv, half:half + 1]), reads=[ph, hb], writes=[gt])
                                    po = prj.next()
                                    if kv == 0:
                                        for half in range(2):
                                            S.op("pe", lambda e: e.matmul(po[:, 0:127], lhsT=w2b[:, 0, half, :], rhs=gt[:, half, 0:127],
                                                                          start=(half == 0), stop=(half == 1)), reads=[w2b, gt], writes=[po])
                                        evac(kcT[:, g, 0:127], po[:, 0:127], [po], [kcT])
                                    else:
                                        for half in range(2):
                                            S.op("pe", lambda e: e.matmul(po[0:127, 0:128], lhsT=gt[:, half, 0:127], rhs=w2b[:, 1, half, :],
                                                                          start=(half == 0), stop=(half == 1)), reads=[w2b, gt], writes=[po])
                                        evac(vca[0:127, g, 0:128], po[0:127, 0:128], [po], [vca])

                        wb = load_w(OG, 24, wA)
                        proj_tok(wb, 24, lambda t, rows, pb: evac(gts[:rows, t, :], pb[:rows, 0:24], [pb], [gts], func=AF.Sigmoid))

                        for g in range(2):
                            with S.scope() as esQ:
                                qTg = S.tile("qTg", [128, 4, T], BF16, esQ)
                                zg = S.tile("zg", [128, 16, 512], BF16, esQ)
                                wb = load_w(OQ + g * 512, 512, wA)
                                for r in range(4):
                                    def qcons(j, n, pb, r=r):
                                        if j < 4:
                                            evac(qTg[:, r, j * 512:j * 512 + n], pb[:, 0:n], [pb], [qTg])
                                        else:
                                            evac(qTs[:, 4 * g + r, :], pb[:, 0:n], [pb], [qTs])
                                    proj_feat(wb, r * 128, 128, qcons)
                                wb = load_w(OZA + g * 512, 512, wA)

                                def zcons(t, rows, pb):
                                    if t < 16:
                                        evac(zg[:, t, :], pb[:, :], [pb], [zg], func=AF.Silu)
                                    else:
                                        evac(zs[:rows, g * 512:(g + 1) * 512], pb[:rows, :], [pb], [zs], func=AF.Silu)
                                proj_tok(wb, 512, zcons)
                                nsa_prompt(S, esQ, g, dict(qTg=qTg, zg=zg, kcT=kcT, vca=vca, kselT=kselT, kwinT=kwinT, vsel=vsel, vwin=vwin,
                                                           gts=gts, wsel=wsel, eexp=eexp, mle_b=mle_b, mgt_b=mgt_b, ident_b=ident_b,
                                                           psb=psb, psA=psA, psB=psB, oT_scr=oT_scr, oT_b=oT_b, SC=SC))
            if stage >= 4:
                with S.scope() as esS:
                    nsa_sample(S, esS, nc, dict(qTs=qTs, zs=zs, gts=gts, kselT=kselTs, kwinT=kwinTs, vnew_src=vnew_src, mle_b=mle_b,
                                                mgt_b=mgt_b, ident_b=ident_b, ident_f=ident_f, psb=psb, wbuf=wbuf, cst=cst, SC=SC,
                                                cache_kv=cache_kv, cache_win=cache_win, page_table=page_table, cmp_w1=cmp_w1,
                                                cmp_w2=cmp_w2, cmp_pe=cmp_pe, cmp_b1=cmp_b1, g_scr=g_scr, z_scr=z_scr,
                                                oT_scr=oT_scr, oT_b=oT_b, evac=evac))

            if stage < 3:
                with S.scope() as esZ:
                    zt = S.tile("zt", [128, 8, 128], BF16, esZ)
                    S.op("pool", lambda e: e.memset(zt[:, :, :], 0.0), writes=[zt])
                    for t in range(NT):
                        S.dma("sp", oT_scr[t, :, 0:8, :], zt[:, :, :], reads=[zt], writes=oT_b[t][0:8])
            elif stage < 4:
                with S.scope() as esZ:
                    zt = S.tile("zt", [128, 8, 128], BF16, esZ)
                    S.op("pool", lambda e: e.memset(zt[:, :, :], 0.0), writes=[zt])
                    S.dma("sp", oT_scr[16, :, 0:8, :], zt[:, :, :], reads=[zt], writes=oT_b[16][0:8])

        def wout_phase(wsrc, res_fn, res_deps, dst, dst_b):
            with S.scope() as e3:
                otr = Rot([S.tile(f"ot{i}", [128, 16, 128], BF16, e3) for i in range(3)])
                xr = Rot([S.tile(f"xr{i}", [128, 512], F32, e3) for i in range(3)])
                for blk in range(4):
                    wb = load_w(blk * 512, 512, wsrc)
                    for t in range(NT):
                        rows = 128 if t < 16 else TS
                        ot, xt = otr.next(), xr.next()
                        S.dma("sp", ot[:, :, :], oT_scr[t, :, :, :], reads=oT_b[t], writes=[ot])
                        S.dma("sp", xt[:rows, :], res_fn(t, rows, blk), reads=res_deps(t, blk), writes=[xt])
                        pb = prj.next()
                        for c in range(16):
                            S.op("pe", lambda e: e.matmul(pb[:rows, 0:512], lhsT=ot[:, c, 0:rows], rhs=wb[:, c, 0:512],
                                                          start=(c == 0), stop=(c == 15)), reads=[ot, wb], writes=[pb])
                        S.op("dve", lambda e: e.tensor_tensor(out=xt[:rows, :], in0=pb[:rows, 0:512], in1=xt[:rows, :], op=ALU.add),
                             reads=[pb, xt], writes=[xt])
                        S.dma("sp", dst[t * 128:t * 128 + rows, blk * 512:(blk + 1) * 512], xt[:rows, :], reads=[xt], writes=[dst_b[t][blk]])

        def xsrc(t, rows, blk):
            return (x_p[t * 128:(t + 1) * 128, blk * 512:(blk + 1) * 512] if t < 16 else x_s[:, blk * 512:(blk + 1) * 512])

        esH2 = es.enter_context(S.scope())
        hT_ref[0] = S.tile("hT2", [128, 16, TA], BF16, esH2)
        if STOPAT > 1:
            wout_phase(wOA, xsrc, lambda t, blk: [], x1_scr, x1_b)
        run_c = STOPAT > 2

        if run_c:
          norm_pass(lambda t, rows: x1_scr[t * 128:t * 128 + rows, :], normC, lambda t: x1_b[t])
        with S.scope() as esC:
          if run_c:
            lbt = S.tile("lbt", [128, 2, 16], F32, esC)
            lb = S.tile("lb", [128, 16], F32, esC)
            oml = S.tile("oml", [128, 16], F32, esC)
            S.dma("sp", lbt[:, :, :], lb_log[:, :, :], writes=[lbt])
            S.op("dve", lambda e: e.tensor_tensor(out=lb[:, :], in0=lbt[:, 1, :], in1=lbt[:, 0, :], op=ALU.subtract), reads=[lbt], writes=[lb])
            S.op("act", lambda e: e.activation(out=lb[:, :], in_=lb[:, :], func=AF.Sigmoid), reads=[lb], writes=[lb])
            S.op("dve", lambda e: e.tensor_scalar(out=oml[:, :], in0=lb[:, :], scalar1=-1.0, scalar2=1.0, op0=ALU.mult, op1=ALU.add),
                 reads=[lb], writes=[oml])
            hsets = Rot([dict(q=S.tile(f"qcT{i}", [128, TA], BF16, esC), k=S.tile(f"kcT{i}", [128, TA], BF16, esC),
                              g=S.tile(f"gcT{i}", [128, TA], F32, esC),
                              v=S.tile(f"vc{i}", [128, NT, 128], BF16, esC), z=S.tile(f"zc{i}", [128, NT, 128], BF16, esC))
                         for i in range(2)])
            G = rec_setup(S, esC, R, V=128)
            gnc = S.tile("gnc", [128, 128], F32, esC)
            S.dma("sp", gnc[:, :], c_gn.partition_broadcast(128), writes=[gnc])
            sgr = Rot([S.tile(f"sg{i}", [128, 512], F32, esC) for i in range(2)])
            for h in range(16):
                hs = hsets.next()
                wb = load_w(h * 512, 512, wC)
                proj_feat(wb, 0, 128, lambda j, n, pb: evac(hs["q"][:, j * 512:j * 512 + n], pb[:, 0:n], [pb], [hs["q"]], func=AF.Silu))

                def fgate(j, n, pb, h=h, hs=hs):
                    sg = sgr.next()
                    sl = slice(j * 512, j * 512 + n)
                    S.op("act", lambda e: e.activation(out=sg[:, 0:n], in_=pb[:, 0:n], func=AF.Sigmoid), reads=[pb], writes=[sg])
                    S.op("dve", lambda e: e.tensor_scalar(out=sg[:, 0:n], in0=sg[:, 0:n], scalar1=oml[:, h:h + 1], scalar2=lb[:, h:h + 1],
                                                          op0=ALU.mult, op1=ALU.add), reads=[sg, oml, lb], writes=[sg])
                    S.op("act", lambda e: e.activation(out=hs["g"][:, sl], in_=sg[:, 0:n], func=AF.Ln), reads=[sg], writes=[hs["g"]])
                    S.op("dve", lambda e: e.tensor_scalar(out=hs["k"][:, sl], in0=sg[:, 0:n], scalar1=-1.0, scalar2=1.0,
                                                          op0=ALU.mult, op1=ALU.add), reads=[sg], writes=[hs["k"]])
                proj_feat(wb, 128, 128, fgate)
                proj_tok(wb, 128, lambda t, rows, pb: evac(hs["v"][:rows, t, :], pb[:rows, 0:128], [pb], [hs["v"]]), wc0=256)
                proj_tok(wb, 128, lambda t, rows, pb: evac(hs["z"][:rows, t, :], pb[:rows, 0:128], [pb], [hs["z"]], func=AF.Silu), wc0=384)
                oh = rec_head(S, G, hs, V=128, COEF=1.0, QS=1.0, gate=lambda tok, n, hs=hs: (hs["g"][:, tok], hs["g"]),
                              state_in=lambda s, h=h: state_hgrn[s, h, :, :], out_p=hg_p[h, :, :], out_s=lambda s, h=h: hg_s[s, h, :, :])
                post_head(S, G, oh, hs["z"], gnc, V=128, fc0=h)

        if run_c:
          wout_phase(wOC, lambda t, rows, blk: x1_scr[t * 128:t * 128 + rows, blk * 512:(blk + 1) * 512],
                     lambda t, blk: [x1_b[t][blk]], y_scr, ys_b)

        with S.scope() as e4:
          if run_c:
            xrot = Rot([S.tile(f"yt{i}", [128, D], F32, e4) for i in range(3)])
            ssr = Rot([S.tile(f"yss{i}", [128, 1], F32, e4) for i in range(2)])
            rsr = Rot([S.tile(f"yrs{i}", [128, 1], F32, e4) for i in range(2)])
            rstdr = Rot([S.tile(f"yrstd{i}", [128, 1], F32, e4) for i in range(2)])
            junk = S.tile("yjunk", [128, D], BF16, e4)
            fnb = S.tile("fnb", [128, D], F32, e4)
            S.dma("sp", fnb[:, :], f_norm.partition_broadcast(128), writes=[fnb])
            for t in range(NT):
                rows = 128 if t < 16 else TS
                xt, ss, rs, rstd = xrot.next(), ssr.next(), rsr.next(), rstdr.next()
                S.dma("sp", xt[:rows, :], y_scr[t * 128:t * 128 + rows, :], reads=ys_b[t], writes=[xt])
                S.op("act", lambda e: e.activation(out=junk[:rows, :], in_=xt[:rows, :], func=AF.Square,
                                                   accum_out=ss[:rows, 0:1]), reads=[xt], writes=[junk, ss])
                S.op("act", lambda e: e.activation(out=rs[:rows, :], in_=ss[:rows, :], func=AF.Sqrt,
                                                   scale=1.0 / D, bias=EPS), reads=[ss], writes=[rs])
                S.op("dve", lambda e: e.reciprocal(out=rstd[:rows, :], in_=rs[:rows, :]), reads=[rs], writes=[rstd])
                S.op("dve", lambda e: e.scalar_tensor_tensor(out=xt[:rows, :], in0=xt[:rows, :], scalar=rstd[:rows, 0:1], in1=fnb[:rows, :],
                                                             op0=ALU.mult, op1=ALU.mult), reads=[xt, rstd, fnb], writes=[xt])
                dst = y_p[t * 128:(t + 1) * 128, :] if t < 16 else y_s[:, :]
                S.dma("sp", dst, xt[:rows, :], reads=[xt], writes=[obuf("y")])

        S.finish(outs)
        print("instructions", S.n_inst, "waits", S.n_wait)
    return nc


def nsa_prompt(S, es, g, N):
    qTg, zg, kcT, vca, kselT, kwinT, vsel, vwin, gts = (N[k] for k in ("qTg", "zg", "kcT", "vca", "kselT", "kwinT", "vsel", "vwin", "gts"))
    wsel, eexp, mle_b, mgt_b, ident_b, psb, psA, psB, SC = (N[k] for k in ("wsel", "eexp", "mle_b", "mgt_b", "ident_b", "psb", "psA", "psB", "SC"))
    tl = lambda n, s, d=F32: S.tile(n, s, d, es)
    et = Rot([tl(f"et{i}", [128, 512], BF16) for i in range(3)])
    den = Rot([tl(f"den{i}", [128, 4]) for i in range(3)])
    rden = Rot([tl(f"rden{i}", [128, 4]) for i in range(3)])
    cf = Rot([tl(f"cf{i}", [128, 4]) for i in range(3)])
    acc = Rot([tl(f"acc{i}", [128, 4, 128]) for i in range(2)])
    tmpo = Rot([tl(f"tmpo{i}", [128, 4, 128]) for i in range(2)])
    impn = tl("impn", [128, 4, 32])
    sc = Rot([tl(f"sc{i}", [128, 32]) for i in range(2)])
    sc2 = tl("sc2", [128, 32])
    m8 = tl("m8", [128, 16])
    selm = tl("selm", [128, 32], BF16)
    selT = Rot([tl(f"selT{i}", [32, 128], BF16) for i in range(2)])
    m2 = Rot([tl(f"m2{i}", [128, 128], BF16) for i in range(2)])
    ob = Rot([tl(f"nob{i}", [128, 512], BF16) for i in range(2)])
    st = Rot([tl(f"nst{i}", [128, 4, 128], BF16) for i in range(2)])
    scb = Rot([psb[0], psb[1]])
    pmk, pmisc = psb[2], psb[3]
    pmisc_bf = View(pmisc[:, :].bitcast(BF16), pmisc.b)
    A3 = View(psA[:, :].rearrange("p (r c) -> p r c", r=4), psA.b)
    B3 = View(psB[:, :].rearrange("p (r c) -> p r c", r=4), psB.b)

    def v4(x):
        return x[:, :].rearrange("p (r q) -> p r q", r=4)

    def finish_branch(P3, br, t, a_, first):
        d_, r_, c_ = den.next(), rden.next(), cf.next()
        S.op("dve", lambda e: e.tensor_scalar(out=d_[:, :], in0=P3[:, :, 128], scalar1=1e-30, scalar2=None, op0=ALU.max), reads=[P3], writes=[d_])
        S.op("dve", lambda e: e.reciprocal(out=r_[:, :], in_=d_[:, :]), reads=[d_], writes=[r_])
        S.op("dve", lambda e: e.tensor_tensor(out=c_[:, :], in0=r_[:, :], in1=gts[:, t, 12 * g + br:12 * g + 12:3], op=ALU.mult),
             reads=[r_, gts], writes=[c_])
        cb = c_[:, :].unsqueeze(2).to_broadcast([128, 4, 128])
        if first:
            S.op("dve", lambda e: e.tensor_tensor(out=a_[:, :, :], in0=P3[:, :, 0:128], in1=cb, op=ALU.mult), reads=[P3, c_], writes=[a_])
        else:
            tm = tmpo.next()
            S.op("dve", lambda e: e.tensor_tensor(out=tm[:, :, :], in0=P3[:, :, 0:128], in1=cb, op=ALU.mult), reads=[P3, c_], writes=[tm])
            S.op("pool", lambda e: e.tensor_tensor(out=a_[:, :, :], in0=a_[:, :, :], in1=tm[:, :, :], op=ALU.add), reads=[a_, tm], writes=[a_])
        return r_

    for t in range(16):
        t0 = 128 * t
        qrhs = qTg[:, :, t0:t0 + 128]
        a_ = acc.next()
        ps = scb.next()
        S.op("pe", lambda e: e.matmul(v4(ps)[0:127], lhsT=kcT[:, g, 0:127], rhs=qrhs, start=True, stop=True), reads=[kcT, qTg], writes=[ps])
        ec = et.next()
        S.op("act", lambda e: e.activation(out=ec[0:127, :], in_=ps[0:127, :], func=AF.Exp, scale=SC), reads=[ps], writes=[ec])
        S.op("pool", lambda e: e.affine_select(out=v4(ec)[0:127], in_=v4(ec)[0:127], pattern=[[0, 4], [1, 128]], compare_op=ALU.is_ge,
                                               fill=0.0, base=t0 - 31, channel_multiplier=-16), reads=[ec], writes=[ec])
        for r in range(4):
            S.op("pe", lambda e: e.matmul(A3[:, r, 0:161], lhsT=ec[0:127, r * 128:(r + 1) * 128], rhs=vca[0:127, g, 0:161],
                                          start=True, stop=True), reads=[ec, vca], writes=[A3])
        rd = finish_branch(A3, 0, t, a_, True)
        sel = t >= 8
        if sel:
            s_ = sc.next()
            S.op("dve", lambda e: e.tensor_tensor(out=impn[:, :, :], in0=A3[:, :, 129:161], in1=rd[:, :].unsqueeze(2).to_broadcast([128, 4, 32]),
                                                  op=ALU.mult), reads=[A3, rd], writes=[impn])
            S.op("dve", lambda e: e.tensor_reduce(out=s_[:, :], in_=impn[:, :, :].rearrange("p r s -> p s r"), axis=AX.X, op=ALU.add),
                 reads=[impn], writes=[s_])
            S.op("dve", lambda e: e.tensor_tensor(out=s_[:, :], in0=s_[:, :], in1=wsel[:, 31 - 2 * t:63 - 2 * t], op=ALU.add),
                 reads=[s_, wsel], writes=[s_])
            S.op("dve", lambda e: e.memset(s_[:, 0:1], 1e9), reads=[], writes=[s_])
            S.op("dve", lambda e: e.max(out=m8[:, 0:8], in_=s_[:, :]), reads=[s_], writes=[m8])
            S.op("dve", lambda e: e.match_replace(out=sc2[:, :], in_to_replace=m8[:, 0:8], in_values=s_[:, :], imm_value=-3e38),
                 reads=[s_, m8], writes=[sc2])
            S.op("dve", lambda e: e.max(out=m8[:, 8:16], in_=sc2[:, :]), reads=[sc2], writes=[m8])
            S.op("dve", lambda e: e.tensor_scalar(out=selm[:, :], in0=s_[:, :], scalar1=m8[:, 15:16], scalar2=None, op0=ALU.is_ge),
                 reads=[s_, m8], writes=[selm])
            S.op("pe", lambda e: e.transpose(out=pmisc_bf[0:32, 0:128], in_=selm[:, 0:32], identity=ident_b[:, :]),
                 reads=[selm, ident_b], writes=[pmisc_bf])
            sT = selT.next()
            S.op("act", lambda e: e.activation(out=sT[:, :], in_=pmisc_bf[0:32, 0:128], func=AF.Copy), reads=[pmisc_bf], writes=[sT])
        S.op("dve", lambda e: e.memset(B3[:, :, 0:129], 0.0), reads=[], writes=[B3])
        for kc in range(t + 1):
            ps = scb.next()
            S.op("pe", lambda e: e.matmul(v4(ps), lhsT=kselT[:, g, kc * 128:(kc + 1) * 128], rhs=qrhs, start=True, stop=True),
                 reads=[kselT, qTg], writes=[ps])
            e_ = et.next()
            S.op("act", lambda e: e.activation(out=e_[:, :], in_=ps[:, :], func=AF.Exp, scale=SC), reads=[ps], writes=[e_])
            if sel:
                S.op("pe", lambda e: e.matmul(pmk[:, 0:128], lhsT=eexp[0:32, kc, :], rhs=sT[0:32, :], start=True, stop=True),
                     reads=[eexp, sT], writes=[pmk])
                if kc == t:
                    mm = m2.next()
                    S.op("dve", lambda e: e.tensor_tensor(out=mm[:, :], in0=pmk[:, 0:128], in1=mle_b[:, :], op=ALU.mult),
                         reads=[pmk, mle_b], writes=[mm])
                    msrc = mm
                else:
                    msrc = pmk
                S.op("dve", lambda e: e.tensor_tensor(out=v4(e_), in0=v4(e_), in1=msrc[:, 0:128].unsqueeze(1).to_broadcast([128, 4, 128]),
                                                      op=ALU.mult), reads=[e_, msrc], writes=[e_])
            elif kc == t:
                S.op("pool", lambda e: e.tensor_tensor(out=v4(e_), in0=v4(e_), in1=mle_b[:, :].unsqueeze(1).to_broadcast([128, 4, 128]),
                                                       op=ALU.mult), reads=[e_, mle_b], writes=[e_])
            for r in range(4):
                S.op("pe", lambda e: e.matmul(B3[:, r, 0:129], lhsT=e_[:, r * 128:(r + 1) * 128], rhs=vsel[:, kc, g, 0:129],
                                              start=False, stop=(kc == t), skip_group_check=True), reads=[e_, vsel], writes=[B3])
        finish_branch(B3, 1, t, a_, False)
        S.op("dve", lambda e: e.memset(A3[:, :, 0:129], 0.0), reads=[], writes=[A3])
        k0 = max(0, t - 4)
        for kc in range(k0, t + 1):
            ps = scb.next()
            S.op("pe", lambda e: e.matmul(v4(ps), lhsT=kwinT[:, g, kc * 128:(kc + 1) * 128], rhs=qrhs, start=True, stop=True),
                 reads=[kwinT, qTg], writes=[ps])
            e_ = et.next()
            S.op("act", lambda e: e.activation(out=e_[:, :], in_=ps[:, :], func=AF.Exp, scale=SC), reads=[ps], writes=[e_])
            mk = mle_b if kc == t else (mgt_b if kc == t - 4 else None)
            if mk is not None:
                S.op("pool", lambda e: e.tensor_tensor(out=v4(e_), in0=v4(e_), in1=mk[:, :].unsqueeze(1).to_broadcast([128, 4, 128]),
                                                       op=ALU.mult), reads=[e_, mk], writes=[e_])
            for r in range(4):
                S.op("pe", lambda e: e.matmul(A3[:, r, 0:129], lhsT=e_[:, r * 128:(r + 1) * 128], rhs=vwin[:, kc, g, 0:129],
                                              start=False, stop=(kc == t), skip_group_check=True), reads=[e_, vwin], writes=[A3])
        finish_branch(A3, 2, t, a_, False)
        o_ = ob.next()
        S.op("dve", lambda e: e.tensor_tensor(out=o_[:, :], in0=a_[:, :, :].rearrange("p r d -> p (r d)"), in1=zg[:, t, :], op=ALU.mult),
             reads=[a_, zg], writes=[o_])
        for r in range(4):
            S.op("pe", lambda e: e.transpose(out=pmisc_bf[:, 128 + r * 128:256 + r * 128], in_=o_[:, r * 128:(r + 1) * 128], identity=ident_b[:, :]),
                 reads=[o_, ident_b], writes=[pmisc_bf])
        s_t = st.next()
        S.op("act", lambda e: e.activation(out=s_t[:, :, :], in_=pmisc_bf[:, 128:640].rearrange("p (r q) -> p r q", r=4), func=AF.Copy),
             reads=[pmisc_bf], writes=[s_t])
        S.dma("sp", N["oT_scr"][t, :, 4 * g:4 * g + 4, :], s_t[:, :, :], reads=[s_t], writes=N["oT_b"][t][4 * g:4 * g + 4])


def nsa_sample(S, es, nc, N):
    qTs, zs, gts, kselT, kwinT, vnew_src, mle_b, mgt_b, ident_b, ident_f, psb, wbuf, cst, SC, evac = (
        N[k] for k in ("qTs", "zs", "gts", "kselT", "kwinT", "vnew_src", "mle_b", "mgt_b", "ident_b", "ident_f", "psb", "wbuf", "cst", "SC", "evac"))
    ckv, cwin = N["cache_kv"], N["cache_win"]
    tl = lambda n, s, d=F32: S.tile(n, s, d, es)
    big = Rot(psb[0:4])
    small = Rot(psb[4:8])
    ptb = tl("ptb", [128, 256], I32)
    S.dma("sp", ptb[:, :], N["page_table"].rearrange("s p -> (s p)").partition_broadcast(128), writes=[ptb])
    pci = tl("pci", [128, 1], I32)
    S.op("pool", lambda e: e.iota(pci[:, :], pattern=[[0, 1]], base=0, channel_multiplier=2), writes=[pci])
    pcf = tl("pcf", [128, 1])
    S.op("dve", lambda e: e.tensor_copy(out=pcf[:, :], in_=pci[:, :]), reads=[pci], writes=[pcf])
    idxA = tl("idxA", [128, 256], I32)
    idxB = tl("idxB", [128, 256], I32)
    S.op("dve", lambda e: e.tensor_scalar(out=idxA[:, :], in0=ptb[:, :], scalar1=256.0, scalar2=pcf[:, 0:1], op0=ALU.mult, op1=ALU.add),
         reads=[ptb, pcf], writes=[idxA])
    S.op("dve", lambda e: e.tensor_scalar(out=idxB[:, :], in0=idxA[:, :], scalar1=1.0, scalar2=None, op0=ALU.add), reads=[idxA], writes=[idxB])
    bg, bz = Buf("gscr"), Buf("zscr")
    S.dma("sp", N["g_scr"][:, :], gts[:TS, 16, :], reads=[gts], writes=[bg])
    S.dma("sp", N["z_scr"][:, :], zs[:TS, :], reads=[zs], writes=[bz])
    gsm = tl("gsm", [16, NS, 2, 3])
    zr = tl("zr", [16, NS, 2, 128], BF16)
    gv = N["g_scr"].rearrange("(s t) (g r b) -> r t s g b", t=4, g=2, r=4)
    zv = N["z_scr"].rearrange("(s t) (g r d) -> r t s g d", t=4, g=2, r=4)
    for r in range(4):
        for g in range(2):
            S.dma("sp", gsm[4 * r:4 * r + 4, :, g, :], gv[r][:, :, g, :], reads=[bg], writes=[gsm])
            S.dma("sp", zr[4 * r:4 * r + 4, :, g, :], zv[r][:, :, g, :], reads=[bz], writes=[zr])
    vnew = tl("vnew", [4, NS, 2, 2, 130], BF16)
    for s in range(NS):
        S.dma("sp", vnew[0:4, s, :, :, :], vnew_src[4 * s:4 * s + 4, :, :, :], reads=[vnew_src], writes=[vnew])
    w1 = [tl(f"w1_{kv}", [128, 32, 256], BF16) for kv in range(2)]
    w2b = tl("w2b", [128, 2, 2, 128], BF16)
    peb = tl("peb", [128, 2, 32], BF16)
    b1t = tl("b1t", [128, 2, 2])
    hb = tl("hb", [128, 2, 2])
    for kv in range(2):
        S.dma("pool", w1[kv][:, :, :], N["cmp_w1"][kv, :, :, :], writes=[w1[kv]])
    S.dma("pool", w2b[:, :, :, :], N["cmp_w2"][:, :, :, :], writes=[w2b])
    S.dma("pool", peb[:, :, :], N["cmp_pe"][:, :, :], writes=[peb])
    S.dma("sp", b1t[:, :, :], N["cmp_b1"][:, :, :], writes=[b1t])
    for kv in range(2):
        pp = small.next()
        for half in range(2):
            for rp in range(32):
                S.op("pe", lambda e: e.matmul(pp[:, half:half + 1], lhsT=w1[kv][:, rp, half * 128:(half + 1) * 128],
                                              rhs=peb[:, kv, rp:rp + 1], start=(rp == 0), stop=(rp == 31)), reads=[w1[kv], peb], writes=[pp])
        S.op("dve", lambda e: e.tensor_tensor(out=hb[:, kv, :], in0=pp[:, 0:2], in1=b1t[:, kv, :], op=ALU.add), reads=[pp, b1t], writes=[hb])
    rsum = tl("rsum", [16, 4])
    S.dma("sp", rsum[:, :], cst["rsum"][:, :], writes=[rsum])
    xTk = tl("xTk", [128, 2, 2048], BF16)
    xTv = tl("xTv", [128, 2, 2048], BF16)
    gT = Rot([tl(f"sgT{i}", [128, 2, 128], BF16) for i in range(2)])
    kcTs = tl("kcTs", [128, 2, 128], BF16)
    vcas = tl("vcas", [128, 2, 162], BF16)
    S.op("pool", lambda e: e.memset(vcas[:, :, 128:129], 1.0), writes=[vcas])
    for g in range(2):
        S.dma("pool", vcas[0:127, g, 129:162], cst["c2ss"][:, :], writes=[vcas])
    ec = tl("sec", [128, 2, 16], BF16)
    den = Rot([tl(f"sden{i}", [16, 2]) for i in range(3)])
    rden = Rot([tl(f"srden{i}", [16, 2]) for i in range(3)])
    cf = Rot([tl(f"scf{i}", [16, 2]) for i in range(3)])
    acc = tl("sacc", [16, 2, 128])
    tmpo = tl("stmpo", [16, 2, 128])
    impn = tl("simpn", [16, 2, 33])
    scs = tl("sscs", [4, 2, 33])
    sc2 = tl("ssc2", [4, 33])
    m8 = tl("sm8", [4, 16])
    selm = tl("sselm", [4, 2, 33], BF16)
    selx = tl("sselx", [4, 2, 33, 64], BF16)
    maskT = tl("smaskT", [128, 2, 16, 4], BF16)
    vsa = tl("vsa", [128, 16, 2, 130], BF16)
    S.op("pool", lambda e: e.memset(vsa[:, :, :, 128:130], 1.0), writes=[vsa])
    esel = tl("esel", [128, 2, 272], BF16)
    wl = tl("wl", [128, 4, 512])
    kwT = tl("kwT", [128, 2, 512], BF16)
    vwa = tl("vwa", [128, 4, 2, 130], BF16)
    S.op("pool", lambda e: e.memset(vwa[:, :, :, 128:130], 1.0), writes=[vwa])
    ewin = tl("ewin", [128, 2, 80], BF16)
    ob = tl("sob", [16, 2, 128], BF16)
    oTs = tl("oTs", [128, 8, TS], BF16)

    def pbview(wb):
        return View(wb[:, :, :].bitcast(F32).rearrange("p c (a f) -> p (c a) f", a=1).rearrange("p (k two) f -> p k (two f)", two=2), wb.b)

    def gather(idx, s, hs):
        wb = wbuf.next()
        pv = pbview(wb)
        for pgl in range(8):
            col = s * 16 + hs * 8 + pgl
            S.dma("pool", None, None, reads=[idx], writes=[pv],
                  fn=lambda e: e.indirect_dma_start(out=pv[:, pgl, :], out_offset=None, in_=ckv[:, :],
                                                    in_offset=bass.IndirectOffsetOnAxis(ap=idx[:, col:col + 1], axis=0)))
        return pv

    def transpose4(src_fn, dst_ap, reads, dstt):
        pt = big.next()
        for k in range(4):
            S.op("pe", lambda e: e.transpose(out=pt[:, k * 128:(k + 1) * 128], in_=src_fn(k), identity=ident_f[:, :]),
                 reads=reads + [ident_f], writes=[pt])
        evac(dst_ap, pt[:, 0:512], [pt], [dstt])

    def finish_branch(P, br, s, first):
        d_, r_, c_ = den.next(), rden.next(), cf.next()
        S.op("dve", lambda e: e.tensor_scalar(out=d_[:, :], in0=P[0:16, :, 128], scalar1=1e-30, scalar2=None, op0=ALU.max), reads=[P], writes=[d_])
        S.op("dve", lambda e: e.reciprocal(out=r_[:, :], in_=d_[:, :]), reads=[d_], writes=[r_])
        S.op("dve", lambda e: e.tensor_tensor(out=c_[:, :], in0=r_[:, :], in1=gsm[:, s, :, br], op=ALU.mult), reads=[r_, gsm], writes=[c_])
        cb = c_[:, :].unsqueeze(2).to_broadcast([16, 2, 128])
        if first:
            S.op("dve", lambda e: e.tensor_tensor(out=acc[:, :, :], in0=P[0:16, :, 0:128], in1=cb, op=ALU.mult), reads=[P, c_], writes=[acc])
        else:
            S.op("dve", lambda e: e.tensor_tensor(out=tmpo[:, :, :], in0=P[0:16, :, 0:128], in1=cb, op=ALU.mult), reads=[P, c_], writes=[tmpo])
            S.op("dve", lambda e: e.tensor_tensor(out=acc[:, :, :], in0=acc[:, :, :], in1=tmpo[:, :, :], op=ALU.add), reads=[acc, tmpo], writes=[acc])
        return r_

    def v3(bank):
        return View(bank[:, :].rearrange("p (g c) -> p g c", g=2), bank.b)

    for s in range(NS):
        q16 = [qTs[:, 4 * g:4 * g + 4, 4 * s:4 * s + 4] for g in range(2)]
        for hs in range(2):
            pv = gather(idxA, s, hs)
            for slot in range(2):
                dstt = xTk if slot == 0 else xTv
                for g in range(2):
                    for q4 in range(2):
                        c0 = (slot * 2 + g) * 128
                        transpose4(lambda k: pv[:, 4 * q4 + k, c0:c0 + 128],
                                   dstt[:, g, (8 * hs + 4 * q4) * 128:(8 * hs + 4 * q4 + 4) * 128], [pv], dstt)
        for kv in range(2):
            xT = xTk if kv == 0 else xTv
            for g in range(2):
                gt = gT.next()
                for half in range(2):
                    ph = big.next()
                    for rp in range(32):
                        st0 = rp
                        S.op("pe", lambda e: e.matmul(ph[:, 0:127], lhsT=w1[kv][:, rp, half * 128:(half + 1) * 128],
                                                      rhs=xT[:, g, st0:st0 + 16 * 126 + 1:16], start=(rp == 0), stop=(rp == 31)),
                             reads=[w1[kv], xT], writes=[ph])
                    S.op("act", lambda e: e.activation(out=gt[:, half, 0:127], in_=ph[:, 0:127], func=AF.Gelu_apprx_tanh,
                                                       bias=hb[:, kv, half:half + 1]), reads=[ph, hb], writes=[gt])
                po = big.next()
                if kv == 0:
                    for half in range(2):
                        S.op("pe", lambda e: e.matmul(po[:, 0:127], lhsT=w2b[:, 0, half, :], rhs=gt[:, half, 0:127],
                                                      start=(half == 0), stop=(half == 1)), reads=[w2b, gt], writes=[po])
                    evac(kcTs[:, g, 0:127], po[:, 0:127], [po], [kcTs])
                else:
                    for half in range(2):
                        S.op("pe", lambda e: e.matmul(po[0:127, 0:128], lhsT=gt[:, half, 0:127], rhs=w2b[:, 1, half, :],
                                                      start=(half == 0), stop=(half == 1)), reads=[w2b, gt], writes=[po])
                    evac(vcas[0:127, g, 0:128], po[0:127, 0:128], [po], [vcas])
        ps = small.next()
        for g in range(2):
            S.op("pe", lambda e: e.matmul(ps[0:127, g * 16:(g + 1) * 16].rearrange("p (r q) -> p r q", r=4), lhsT=kcTs[:, g, 0:127], rhs=q16[g],
                                          start=True, stop=True), reads=[kcTs, qTs], writes=[ps])
        S.op("act", lambda e: e.activation(out=ec[0:127, :, :], in_=ps[0:127, 0:32].rearrange("p (g c) -> p g c", g=2), func=AF.Exp, scale=SC),
             reads=[ps], writes=[ec])
        pc = v3(small.next())
        for g in range(2):
            S.op("pe", lambda e: e.matmul(pc[0:16, g, 0:162], lhsT=ec[0:127, g, :], rhs=vcas[0:127, g, 0:162], start=True, stop=True),
                 reads=[ec, vcas], writes=[pc])
        rd = finish_branch(pc, 0, s, True)
        S.op("dve", lambda e: e.tensor_tensor(out=impn[:, :, :], in0=pc[0:16, :, 129:162], in1=rd[:, :].unsqueeze(2).to_broadcast([16, 2, 33]),
                                              op=ALU.mult), reads=[pc, rd], writes=[impn])
        pi = small.next()
        S.op("pe", lambda e: e.matmul(pi[0:4, 0:66], lhsT=rsum[:, :], rhs=impn[:, :, :].rearrange("p g s -> p (g s)"), start=True, stop=True),
             reads=[rsum, impn], writes=[pi])
        S.op("dve", lambda e: e.tensor_copy(out=scs[:, :, :], in_=pi[0:4, 0:66].rearrange("p (g s) -> p g s", g=2)), reads=[pi], writes=[scs])
        S.op("dve", lambda e: e.memset(scs[:, :, 0:1], 1e9), reads=[], writes=[scs])
        S.op("dve", lambda e: e.memset(scs[:, :, 31:33], 1e9), reads=[], writes=[scs])
        for g in range(2):
            S.op("dve", lambda e: e.max(out=m8[:, 0:8], in_=scs[:, g, :]), reads=[scs], writes=[m8])
            S.op("dve", lambda e: e.match_replace(out=sc2[:, :], in_to_replace=m8[:, 0:8], in_values=scs[:, g, :], imm_value=-3e38),
                 reads=[scs, m8], writes=[sc2])
            S.op("dve", lambda e: e.max(out=m8[:, 8:16], in_=sc2[:, :]), reads=[sc2], writes=[m8])
            S.op("dve", lambda e: e.tensor_scalar(out=selm[:, g, :], in0=scs[:, g, :], scalar1=m8[:, 15:16], scalar2=None, op0=ALU.is_ge),
                 reads=[scs, m8], writes=[selm])
        S.op("dve", lambda e: e.tensor_copy(out=selx[:, :, :, :].rearrange("p g s k -> p (g s) k"),
                                            in_=selm[:, :, :].rearrange("p g s -> p (g s)").unsqueeze(2).to_broadcast([4, 66, 64])),
             reads=[selm], writes=[selx])
        pm = small.next()
        for g in range(2):
            for kc in range(16):
                S.op("pe", lambda e: e.matmul(pm[:, (g * 16 + kc) * 4:(g * 16 + kc) * 4 + 4],
                                              lhsT=selx[0:4, g, 2 * kc:2 * kc + 2, :].rearrange("p a b -> p (a b)"), rhs=ident_b[0:4, 0:4],
                                              start=True, stop=True), reads=[selx, ident_b], writes=[pm])
        S.op("act", lambda e: e.activation(out=maskT[:, :, :, :].rearrange("p g k t -> p (g k t)"), in_=pm[:, 0:128], func=AF.Copy),
             reads=[pm], writes=[maskT])
        for hs in range(2):
            pv = gather(idxB, s, hs)
            for g in range(2):
                for q4 in range(2):
                    transpose4(lambda k: pv[:, 4 * q4 + k, g * 128:(g + 1) * 128],
                               xTk[:, g, (8 * hs + 4 * q4) * 128:(8 * hs + 4 * q4 + 4) * 128], [pv], xTk)
            S.op("pool", lambda e: e.tensor_copy(out=vsa[:, 8 * hs:8 * hs + 8, :, 0:128],
                                                 in_=pv[:, :, 256:512].rearrange("p k (g d) -> p k g d", g=2)), reads=[pv], writes=[vsa])
        pss = v3(small.next())
        psn = v3(small.next())
        for g in range(2):
            for kc in range(16):
                S.op("pe", lambda e: e.matmul(pss[:, g, kc * 16:(kc + 1) * 16].rearrange("p (r q) -> p r q", r=4),
                                              lhsT=xTk[:, g, kc * 128:(kc + 1) * 128], rhs=q16[g], start=True, stop=True),
                     reads=[xTk, qTs], writes=[pss])
            S.op("pe", lambda e: e.matmul(psn[0:4, g, 0:16].rearrange("p (r q) -> p r q", r=4),
                                          lhsT=kselT[:, g, 4 * s:4 * s + 4], rhs=q16[g], start=True, stop=True),
                 reads=[kselT, qTs], writes=[psn])
        S.op("act", lambda e: e.activation(out=esel[:, :, 0:256], in_=pss[:, :, 0:256], func=AF.Exp, scale=SC), reads=[pss], writes=[esel])
        S.op("act", lambda e: e.activation(out=esel[0:4, :, 256:272], in_=psn[0:4, :, 0:16], func=AF.Exp, scale=SC), reads=[psn], writes=[esel])
        for g in range(2):
            S.op("dve", lambda e: e.tensor_tensor(out=esel[:, g, 0:256].rearrange("p (k r q) -> p k r q", k=16, r=4),
                                                  in0=esel[:, g, 0:256].rearrange("p (k r q) -> p k r q", k=16, r=4),
                                                  in1=maskT[:, g, :, :].unsqueeze(2).to_broadcast([128, 16, 4, 4]), op=ALU.mult),
                 reads=[esel, maskT], writes=[esel])
        S.op("dve", lambda e: e.tensor_tensor(out=esel[0:4, :, 256:272].rearrange("p g (r q) -> p g r q", r=4),
                                              in0=esel[0:4, :, 256:272].rearrange("p g (r q) -> p g r q", r=4),
                                              in1=mle_b[0:4, 0:4].unsqueeze(1).unsqueeze(1).to_broadcast([4, 2, 4, 4]), op=ALU.mult),
             reads=[esel, mle_b], writes=[esel])
        po_ = v3(small.next())
        for g in range(2):
            for kc in range(16):
                S.op("pe", lambda e: e.matmul(po_[0:16, g, 0:129], lhsT=esel[:, g, kc * 16:(kc + 1) * 16], rhs=vsa[:, kc, g, 0:129],
                                              start=(kc == 0), stop=False), reads=[esel, vsa], writes=[po_])
            S.op("pe", lambda e: e.matmul(po_[0:16, g, 0:129], lhsT=esel[0:4, g, 256:272], rhs=vnew[0:4, s, 0, g, 0:129],
                                          start=False, stop=True), reads=[esel, vnew], writes=[po_])
        finish_branch(po_, 1, s, False)
        S.dma("sp", wl[:, :, :], cwin[s, :, :].rearrange("(c p) f -> p c f", p=128), writes=[wl])
        for g in range(2):
            transpose4(lambda k: wl[:, k, g * 128:(g + 1) * 128], kwT[:, g, :], [wl], kwT)
        S.op("pool", lambda e: e.tensor_copy(out=vwa[:, :, :, 0:128], in_=wl[:, :, 256:512].rearrange("p k (g d) -> p k g d", g=2)),
             reads=[wl], writes=[vwa])
        psw = v3(small.next())
        pwn = v3(small.next())
        for g in range(2):
            for kc in range(4):
                S.op("pe", lambda e: e.matmul(psw[:, g, kc * 16:(kc + 1) * 16].rearrange("p (r q) -> p r q", r=4),
                                              lhsT=kwT[:, g, kc * 128:(kc + 1) * 128], rhs=q16[g], start=True, stop=True),
                     reads=[kwT, qTs], writes=[psw])
            S.op("pe", lambda e: e.matmul(pwn[0:4, g, 0:16].rearrange("p (r q) -> p r q", r=4),
                                          lhsT=kwinT[:, g, 4 * s:4 * s + 4], rhs=q16[g], start=True, stop=True),
                 reads=[kwinT, qTs], writes=[pwn])
        S.op("act", lambda e: e.activation(out=ewin[:, :, 0:64], in_=psw[:, :, 0:64], func=AF.Exp, scale=SC), reads=[psw], writes=[ewin])
        S.op("act", lambda e: e.activation(out=ewin[0:4, :, 64:80], in_=pwn[0:4, :, 0:16], func=AF.Exp, scale=SC), reads=[pwn], writes=[ewin])
        S.op("dve", lambda e: e.tensor_tensor(out=ewin[:, :, 0:16].rearrange("p g (r q) -> p g r q", r=4),
                                              in0=ewin[:, :, 0:16].rearrange("p g (r q) -> p g r q", r=4),
                                              in1=mgt_b[:, 0:4].unsqueeze(1).unsqueeze(1).to_broadcast([128, 2, 4, 4]), op=ALU.mult),
             reads=[ewin, mgt_b], writes=[ewin])
        S.op("dve", lambda e: e.tensor_tensor(out=ewin[0:4, :, 64:80].rearrange("p g (r q) -> p g r q", r=4),
                                              in0=ewin[0:4, :, 64:80].rearrange("p g (r q) -> p g r q", r=4),
                                              in1=mle_b[0:4, 0:4].unsqueeze(1).unsqueeze(1).to_broadcast([4, 2, 4, 4]), op=ALU.mult),
             reads=[ewin, mle_b], writes=[ewin])
        pw_ = v3(small.next())
        for g in range(2):
            for kc in range(4):
                S.op("pe", lambda e: e.matmul(pw_[0:16, g, 0:129], lhsT=ewin[:, g, kc * 16:(kc + 1) * 16], rhs=vwa[:, kc, g, 0:129],
                                              start=(kc == 0), stop=False), reads=[ewin, vwa], writes=[pw_])
            S.op("pe", lambda e: e.matmul(pw_[0:16, g, 0:129], lhsT=ewin[0:4, g, 64:80], rhs=vnew[0:4, s, 1, g, 0:129],
                                          start=False, stop=True), reads=[ewin, vnew], writes=[pw_])
        finish_branch(pw_, 2, s, False)
        S.op("dve", lambda e: e.tensor_tensor(out=ob[:, :, :], in0=acc[:, :, :], in1=zr[:, s, :, :], op=ALU.mult), reads=[acc, zr], writes=[ob])
        pt = small.next()
        ptb_ = View(pt[:, :].bitcast(BF16), pt.b)
        for g in range(2):
            S.op("pe", lambda e: e.transpose(out=ptb_[:, g * 16:(g + 1) * 16], in_=ob[0:16, g, :], identity=ident_b[0:16, 0:16]),
                 reads=[ob, ident_b], writes=[ptb_])
        S.op("act", lambda e: e.activation(out=oTs[:, :, 4 * s:4 * s + 4], in_=ptb_[:, 0:32].rearrange("p (f q) -> p f q", f=8), func=AF.Copy),
             reads=[ptb_], writes=[oTs])
    S.dma("sp", N["oT_scr"][16, :, 0:8, 0:TS], oTs[:, :, :], reads=[oTs], writes=N["oT_b"][16][0:8])


def rec_setup(S, e2, R, V):
    G = dict(R)
    tl = lambda n, s, d=F32: S.tile(n, s, d, e2)
    G["cs"] = Rot([tl(f"cs{i}", [128, 128]) for i in range(2)])
    G["E1"] = Rot([tl(f"E1{i}", [128, 128]) for i in range(2)])
    G["E2"] = Rot([tl(f"E2{i}", [128, 128]) for i in range(2)])
    G["sm"] = Rot([tl(f"sm{i}", [128, 16]) for i in range(3)])
    G["E3"] = Rot([tl(f"E3{i}", [128, 128]) for i in range(2)])
    G["E4"] = Rot([tl(f"E4{i}", [128, 128]) for i in range(2)])
    G["qx"] = Rot([tl(f"qx{i}", [128, 64], BF16) for i in range(2)])
    G["qS"] = Rot([tl(f"qS{i}", [128, 128], BF16) for i in range(2)])
    G["kS"] = Rot([tl(f"kS{i}", [128, 128], BF16) for i in range(2)])
    G["KA"] = Rot([tl(f"KA{i}", [128, 128], BF16) for i in range(2)])
    G["KB"] = Rot([tl(f"KB{i}", [128, 128], BF16) for i in range(2)])
    for kk in ("KA", "KB"):
        for tt in G[kk].items:
            S.op("pool", lambda e: e.memset(tt[:, :], 0.0), writes=[tt])
    G["qp"] = Rot([tl(f"qp{i}", [128, 128], BF16) for i in range(2)])
    G["kp"] = Rot([tl(f"kp{i}", [128, 128], BF16) for i in range(2)])
    G["ktok"] = Rot([tl(f"ktok{i}", [128, 128], BF16) for i in range(2)])
    G["attm"] = Rot([tl(f"attm{i}", [128, 128], BF16) for i in range(2)])
    G["Sst"] = tl("Sst", [128, V])
    G["Sbf"] = Rot([tl(f"Sbf{i}", [128, V], BF16) for i in range(2)])
    G["ones"] = tl("ones", [128, 128])
    S.op("pool", lambda e: e.memset(G["ones"][:, :], 1.0), writes=[G["ones"]])
    G["bdm"] = tl("bdm", [64, 64], BF16)
    S.dma("pool", G["bdm"][:, :], R["cst"]["bd"][:, :], writes=[G["bdm"]])
    G["colm"] = tl("colm", [128, 16, 64], BF16)
    S.dma("pool", G["colm"][:, :, :], R["cst"]["colmask"][:, :, :], writes=[G["colm"]])
    G["rowm"] = tl("rowm", [64, 16], BF16)
    S.dma("pool", G["rowm"][:, :], R["cst"]["rowmask"][:, :], writes=[G["rowm"]])
    G["s0"] = Rot([tl(f"s0{i}", [128, V]) for i in range(3)])
    G["s0b"] = Rot([tl(f"s0b{i}", [128, V], BF16) for i in range(3)])
    G["sn"] = Rot([tl(f"sn{i}", [128, V]) for i in range(3)])
    G["qm"] = tl("qm", [128, 16, 64], BF16)
    G["km"] = tl("km", [64, 16, 128], BF16)
    G["oh"] = Rot([tl(f"oh{i}", [128, NT, V], BF16) for i in range(1 if V == 256 else 2)])
    G["pss"] = tl("pss", [128, NT])
    G["prs"] = tl("prs", [128, NT])
    G["pjunk"] = tl("pjunk", [128, V], BF16)
    G["ptmp"] = Rot([tl(f"ptmp{i}", [128, V]) for i in range(2)])
    G["pob"] = Rot([tl(f"pob{i}", [128, V], BF16) for i in range(2)])
    G["post"] = Rot([tl(f"post{i}", [128, V // 128, 128], BF16) for i in range(3)])
    return G


def rec_head(S, G, hs, V, COEF, QS, gate, state_in, out_p, out_s):
    qT, kT, v = hs["q"], hs["k"], hs["v"]
    mle_b, ident_b, psb, obuf = (G[k] for k in ("mle_b", "ident_b", "psb", "obuf"))
    p_att, p_o, p_ds = psb[4], psb[5], Rot([psb[0], psb[1]])
    Sst = G["Sst"]
    oh = G["oh"].next()
    for t in range(16):
        tok = slice(t * 128, (t + 1) * 128)
        c_, e1, e2_, e3, e4, s_ = (G[k].next() for k in ("cs", "E1", "E2", "E3", "E4", "sm"))
        q_, qx, qS, KA, KB, kS, kt_, at_ = (G[k].next() for k in ("qp", "qx", "qS", "KA", "KB", "kS", "ktok", "attm"))
        gap, gtt = gate(tok, 128)
        S.op("dve", lambda e: e.tensor_tensor_scan(out=c_[:, :], data0=G["ones"][:, :], data1=gap, initial=0.0,
                                                   op0=ALU.mult, op1=ALU.add), reads=[G["ones"], gtt], writes=[c_])
        S.op("dve", lambda e: e.tensor_copy(out=s_[:, 0:4], in_=c_[:, 31:128:32]), reads=[c_], writes=[s_])
        S.op("dve", lambda e: e.tensor_scalar(out=s_[:, 4:8], in0=s_[:, 0:4], scalar1=-COEF, scalar2=None, op0=ALU.mult), reads=[s_], writes=[s_])
        S.op("dve", lambda e: e.tensor_scalar(out=s_[:, 8:12], in0=s_[:, 0:4], scalar1=COEF, scalar2=None, op0=ALU.mult), reads=[s_], writes=[s_])
        S.op("dve", lambda e: e.tensor_tensor(out=s_[:, 12:13], in0=s_[:, 2:3], in1=s_[:, 0:1], op=ALU.subtract), reads=[s_], writes=[s_])
        S.op("dve", lambda e: e.tensor_copy(out=s_[:, 13:14], in_=s_[:, 3:4]), reads=[s_], writes=[s_])
        S.op("act", lambda e: e.activation(out=s_[:, 14:16], in_=s_[:, 12:14], func=AF.Exp, scale=COEF), reads=[s_], writes=[s_])
        lo, hi = slice(0, 64), slice(64, 128)
        S.op("act", lambda e: e.activation(out=e1[:, lo], in_=c_[:, lo], func=AF.Exp, scale=COEF, bias=s_[:, 4:5]), reads=[c_, s_], writes=[e1])
        S.op("act", lambda e: e.activation(out=e1[:, hi], in_=c_[:, hi], func=AF.Exp, scale=COEF, bias=s_[:, 6:7]), reads=[c_, s_], writes=[e1])
        S.op("act", lambda e: e.activation(out=e2_[:, lo], in_=c_[:, lo], func=AF.Exp, scale=-COEF, bias=s_[:, 8:9]), reads=[c_, s_], writes=[e2_])
        S.op("act", lambda e: e.activation(out=e2_[:, hi], in_=c_[:, hi], func=AF.Exp, scale=-COEF, bias=s_[:, 10:11]), reads=[c_, s_], writes=[e2_])
        S.op("act", lambda e: e.activation(out=e3[:, :], in_=c_[:, :], func=AF.Exp, scale=-COEF, bias=s_[:, 11:12]), reads=[c_, s_], writes=[e3])
        S.op("act", lambda e: e.activation(out=e4[:, :], in_=c_[:, :], func=AF.Exp, scale=COEF), reads=[c_], writes=[e4])
        S.op("dve", lambda e: e.scalar_tensor_tensor(out=q_[:, :], in0=qT[:, tok], scalar=QS, in1=e1[:, :],
                                                     op0=ALU.mult, op1=ALU.mult), reads=[qT, e1], writes=[q_])
        S.op("dve", lambda e: e.tensor_scalar(out=qx[:, :], in0=q_[:, hi], scalar1=s_[:, 14:15], scalar2=None, op0=ALU.mult),
             reads=[q_, s_], writes=[qx])
        S.op("dve", lambda e: e.tensor_tensor(out=KA[:, lo], in0=kT[:, t * 128:t * 128 + 64], in1=e2_[:, lo], op=ALU.mult),
             reads=[kT, e2_], writes=[KA])
        S.op("dve", lambda e: e.tensor_tensor(out=KB[:, hi], in0=kT[:, t * 128 + 64:(t + 1) * 128], in1=e2_[:, hi], op=ALU.mult),
             reads=[kT, e2_], writes=[KB])
        S.op("dve", lambda e: e.tensor_tensor(out=kS[:, :], in0=kT[:, tok], in1=e3[:, :], op=ALU.mult), reads=[kT, e3], writes=[kS])
        S.op("pe", lambda e: e.matmul(p_att[:, 0:64], lhsT=KA[:, :], rhs=q_[:, lo], start=True, stop=True),
             reads=[KA, q_], writes=[p_att])
        S.op("pe", lambda e: e.matmul(p_att[:, 64:128], lhsT=KA[:, :], rhs=qx[:, :], start=True, stop=False),
             reads=[KA, qx], writes=[p_att])
        S.op("pe", lambda e: e.matmul(p_att[:, 64:128], lhsT=KB[:, :], rhs=q_[:, hi], start=False, stop=True),
             reads=[KB, q_], writes=[p_att])
        S.op("dve", lambda e: e.tensor_tensor(out=at_[:, :], in0=p_att[:, 0:128], in1=mle_b[:, :], op=ALU.mult),
             reads=[p_att, mle_b], writes=[at_])
        pf = G["psbf"].next()
        S.op("pe", lambda e: e.transpose(out=pf[:, 0:128], in_=kS[:, :], identity=ident_b[:, :]),
             reads=[kS, ident_b], writes=[pf])
        S.op("act", lambda e: e.activation(out=kt_[:, :], in_=pf[:, 0:128], func=AF.Copy), reads=[pf], writes=[kt_])
        vv = v[:, t, :]
        if t > 0:
            sb_ = G["Sbf"].next()
            S.op("pool", lambda e: e.tensor_copy(out=sb_[:, :], in_=Sst[:, :]), reads=[Sst], writes=[sb_])
            S.op("dve", lambda e: e.scalar_tensor_tensor(out=qS[:, :], in0=qT[:, tok], scalar=QS, in1=e4[:, :],
                                                         op0=ALU.mult, op1=ALU.mult), reads=[qT, e4], writes=[qS])
        S.op("pe", lambda e: e.matmul(p_o[:, 0:V], lhsT=at_[:, :], rhs=vv, start=True, stop=(t == 0)),
             reads=[at_, v], writes=[p_o])
        if t > 0:
            S.op("pe", lambda e: e.matmul(p_o[:, 0:V], lhsT=qS[:, :], rhs=sb_[:, :], start=False, stop=True),
                 reads=[qS, sb_], writes=[p_o])
        S.op("act", lambda e: e.activation(out=oh[:, t, :], in_=p_o[:, 0:V], func=AF.Copy), reads=[p_o], writes=[oh])
        pd = p_ds.next()
        S.op("pe", lambda e: e.matmul(pd[:, 0:V], lhsT=kt_[:, :], rhs=vv, start=True, stop=True), reads=[kt_, v], writes=[pd])
        if t == 0:
            S.op("dve", lambda e: e.tensor_copy(out=Sst[:, :], in_=pd[:, 0:V]), reads=[pd], writes=[Sst])
        else:
            S.op("dve", lambda e: e.scalar_tensor_tensor(out=Sst[:, :], in0=Sst[:, :], scalar=s_[:, 15:16], in1=pd[:, 0:V],
                                                         op0=ALU.mult, op1=ALU.add), reads=[pd, s_, Sst], writes=[Sst])
    S.dma("sp", out_p, Sst[:, :], reads=[Sst], writes=[obuf("st_p")])

    tok = slice(T, TA)
    c_, e1, e2_ = G["cs"].next(), G["E1"].next(), G["E2"].next()
    q_, k_, kt_, at_ = G["qp"].next(), G["kp"].next(), G["ktok"].next(), G["attm"].next()
    qm, km, bdm, colm, rowm = G["qm"], G["km"], G["bdm"], G["colm"], G["rowm"]
    gap, gtt = gate(tok, TS)
    l3 = gap.rearrange("p (s t) -> p s t", t=4)
    c3 = c_[:, 0:TS].rearrange("p (s t) -> p s t", t=4)
    S.op("dve", lambda e: e.tensor_copy(out=c3[:, :, 0], in_=l3[:, :, 0]), reads=[gtt], writes=[c_])
    for i in range(1, 4):
        S.op("dve", lambda e: e.tensor_tensor(out=c3[:, :, i], in0=c3[:, :, i - 1], in1=l3[:, :, i], op=ALU.add),
             reads=[gtt, c_], writes=[c_])
    S.op("act", lambda e: e.activation(out=e1[:, 0:TS], in_=c_[:, 0:TS], func=AF.Exp, scale=COEF), reads=[c_], writes=[e1])
    S.op("act", lambda e: e.activation(out=e2_[:, 0:TS], in_=c_[:, 0:TS], func=AF.Exp, scale=-COEF), reads=[c_], writes=[e2_])
    S.op("dve", lambda e: e.scalar_tensor_tensor(out=q_[:, 0:TS], in0=qT[:, tok], scalar=QS, in1=e1[:, 0:TS],
                                                 op0=ALU.mult, op1=ALU.mult), reads=[qT, e1], writes=[q_])
    S.op("dve", lambda e: e.tensor_tensor(out=k_[:, 0:TS], in0=kT[:, tok], in1=e2_[:, 0:TS], op=ALU.mult),
         reads=[kT, e2_], writes=[k_])
    S.op("pe", lambda e: e.matmul(p_att[0:TS, 0:TS], lhsT=k_[:, 0:TS], rhs=q_[:, 0:TS], start=True, stop=True),
         reads=[k_, q_], writes=[p_att])
    S.op("dve", lambda e: e.tensor_tensor(out=at_[0:TS, 0:TS], in0=p_att[0:TS, 0:TS], in1=bdm[:, :], op=ALU.mult),
         reads=[p_att, bdm], writes=[at_])
    pf = G["psbf"].next()
    S.op("pe", lambda e: e.transpose(out=pf[0:TS, 0:128], in_=k_[:, 0:TS], identity=ident_b[:, :]),
         reads=[k_, ident_b], writes=[pf])
    S.op("act", lambda e: e.activation(out=kt_[0:TS, :], in_=pf[0:TS, 0:128], func=AF.Copy), reads=[pf], writes=[kt_])
    S.op("dve", lambda e: e.tensor_tensor(out=qm[:, :, :], in0=q_[:, 0:TS].unsqueeze(1).to_broadcast([128, 16, TS]),
                                          in1=colm[:, :, :], op=ALU.mult), reads=[q_, colm], writes=[qm])
    S.op("dve", lambda e: e.tensor_tensor(out=km[:, :, :], in0=kt_[0:TS, :].unsqueeze(1).to_broadcast([TS, 16, 128]),
                                          in1=rowm[:, :].unsqueeze(2).to_broadcast([TS, 16, 128]), op=ALU.mult),
         reads=[kt_, rowm], writes=[km])
    vv = v[0:TS, 16, :]
    S.op("pe", lambda e: e.matmul(p_o[0:TS, 0:V], lhsT=at_[0:TS, 0:TS], rhs=vv, start=True, stop=False),
         reads=[at_, v], writes=[p_o])
    for s in range(NS):
        a0, a0b, an = G["s0"].next(), G["s0b"].next(), G["sn"].next()
        S.dma("sp", a0[:, :], state_in(s), writes=[a0])
        S.op("pool", lambda e: e.tensor_copy(out=a0b[:, :], in_=a0[:, :]), reads=[a0], writes=[a0b])
        S.op("pe", lambda e: e.matmul(p_o[0:TS, 0:V], lhsT=qm[:, s, :], rhs=a0b[:, :], start=False, stop=(s == NS - 1)),
             reads=[qm, a0b], writes=[p_o])
        pd = p_ds.next()
        S.op("pe", lambda e: e.matmul(pd[:, 0:V], lhsT=km[:, s, :], rhs=vv, start=True, stop=True), reads=[km, v], writes=[pd])
        etot = e1[:, 4 * s + 3:4 * s + 4]
        S.op("act", lambda e: e.activation(out=a0[:, :], in_=a0[:, :], func=AF.Copy, scale=etot), reads=[a0, e1], writes=[a0])
        S.op("dve", lambda e: e.scalar_tensor_tensor(out=an[:, :], in0=pd[:, 0:V], scalar=etot, in1=a0[:, :],
                                                     op0=ALU.mult, op1=ALU.add), reads=[pd, e1, a0], writes=[an])
        S.dma("sp", out_s(s), an[:, :], reads=[an], writes=[obuf("st_s")])
    S.op("act", lambda e: e.activation(out=oh[0:TS, 16, :], in_=p_o[0:TS, 0:V], func=AF.Copy), reads=[p_o], writes=[oh])
    return oh


def post_head(S, G, oh, z, gnb, V, fc0, dbgh=False):
    pss, prs, pjunk = G["pss"], G["prs"], G["pjunk"]
    ident_b = G["ident_b"]
    nf = V // 128
    for t in range(NT):
        rows = 128 if t < 16 else TS
        S.op("act", lambda e: e.activation(out=pjunk[:rows, :], in_=oh[:rows, t, :], func=AF.Square, accum_out=pss[:rows, t:t + 1]),
             reads=[oh], writes=[pjunk, pss])
    S.op("act", lambda e: e.activation(out=prs[:TS, :], in_=pss[:TS, :], func=AF.Sqrt, scale=1.0 / V, bias=EPS), reads=[pss], writes=[prs])
    S.op("act", lambda e: e.activation(out=prs[TS:, 0:16], in_=pss[TS:, 0:16], func=AF.Sqrt, scale=1.0 / V, bias=EPS), reads=[pss], writes=[prs])
    S.op("dve", lambda e: e.reciprocal(out=prs[:TS, :], in_=prs[:TS, :]), reads=[prs], writes=[prs])
    S.op("dve", lambda e: e.reciprocal(out=prs[TS:, 0:16], in_=prs[TS:, 0:16]), reads=[prs], writes=[prs])
    for t in range(NT):
        rows = 128 if t < 16 else TS
        tmp, ob, st = G["ptmp"].next(), G["pob"].next(), G["post"].next()
        S.op("dve", lambda e: e.scalar_tensor_tensor(out=tmp[:rows, :], in0=oh[:rows, t, :], scalar=prs[:rows, t:t + 1], in1=gnb[:rows, :],
                                                     op0=ALU.mult, op1=ALU.mult), reads=[oh, prs, gnb], writes=[tmp])
        S.op("dve", lambda e: e.tensor_tensor(out=ob[:rows, :], in0=tmp[:rows, :], in1=z[:rows, t, :], op=ALU.mult),
             reads=[tmp, z], writes=[ob])
        pf = G["psbf"].next()
        for k in range(nf):
            S.op("pe", lambda e: e.transpose(out=pf[:, k * 128:k * 128 + rows], in_=ob[:rows, k * 128:(k + 1) * 128],
                                             identity=ident_b[:rows, :rows]), reads=[ob, ident_b], writes=[pf])
        S.op("act", lambda e: e.activation(out=st[:, :, 0:rows], in_=pf[:, 0:nf * 128].rearrange("p (k t) -> p k t", k=nf)[:, :, 0:rows],
                                           func=AF.Copy), reads=[pf], writes=[st])
        S.dma("sp", G["oT_scr"][t, :, fc0:fc0 + nf, 0:rows], st[:, :, 0:rows], reads=[st], writes=G["oT_b"][t][fc0:fc0 + nf])
        if dbgh and t == 0:
            G["dbg"]("pss", pss, pss[:, :], [128, NT])
            G["dbg"]("prs", prs, prs[:, :], [128, NT])
            G["dbg"]("tmp", tmp, tmp[:, :], [128, V])
            G["dbg"]("ob", ob, ob[:, :], [128, V], BF16)
            G["dbg"]("st", st, st[:, :, :], [128, nf, 128], BF16)


_PROG = {}


def _arr(w, ncols):
    return np.ascontiguousarray(w.reshape(16, 128, ncols).transpose(1, 0, 2))


def kernel(x_prompt, x_sample, cache_kv, cache_win, state_gla, state_hgrn, page_table,
           a_norm, a_w_in, a_gla_w2, a_gla_b, a_gla_gn, a_cmp_pe, a_cmp_w1, a_cmp_b1, a_cmp_w2, a_w_out,
           c_norm, c_w_in, c_lb_logits, c_gn, c_w_out, final_norm):
    f32 = np.float32
    asc = lambda a: np.ascontiguousarray(np.asarray(a, dtype=f32))
    x_prompt, x_sample = asc(x_prompt), asc(x_sample)
    if "nc" not in _PROG:
        _PROG["nc"] = build_program()
    nc = _PROG["nc"]
    consts = make_consts()
    w0 = np.asarray(a_w_in, f32)[0]
    wA = _arr(w0, 6696)
    cols = []
    for h in range(4):
        cols += [np.arange(3608 + h * 128, 3608 + (h + 1) * 128), np.arange(4120 + h * 128, 4120 + (h + 1) * 128),
                 np.arange(4632 + h * 256, 4632 + (h + 1) * 256), np.arange(5672 + h * 256, 5672 + (h + 1) * 256)]
    cols.append(np.arange(5656, 5672))
    wG = _arr(np.ascontiguousarray(w0[:, np.concatenate(cols)]), 3088)
    wc = np.asarray(c_w_in, f32)[0]
    cols = []
    for h in range(16):
        cols += [np.arange(k * 2048 + h * 128, k * 2048 + (h + 1) * 128) for k in range(4)]
    wC = _arr(np.ascontiguousarray(wc[:, np.concatenate(cols)]), 8192)
    wOA = _arr(np.asarray(a_w_out, f32)[0], 2048)
    wOC = _arr(np.asarray(c_w_out, f32)[0], 2048)
    a_norm_r = asc(np.asarray(a_norm, f32)[0].reshape(16, 128).T)
    c_norm_r = asc(np.asarray(c_norm, f32)[0].reshape(16, 128).T)
    lb_log = asc(np.asarray(c_lb_logits, f32).reshape(2, 16, 128).transpose(2, 0, 1))
    w2a = asc(np.concatenate([np.asarray(a_gla_w2, f32)[0], np.asarray(a_gla_b, f32)[0][None, :]], axis=0))
    cw1 = asc(np.asarray(a_cmp_w1, f32)[0].reshape(2, 32, 128, 256).transpose(0, 2, 1, 3))
    cw2 = asc(np.asarray(a_cmp_w2, f32)[0].reshape(2, 2, 128, 128).transpose(2, 0, 1, 3))
    cpe = asc(np.asarray(a_cmp_pe, f32)[0].transpose(2, 0, 1))
    cb1 = asc(np.asarray(a_cmp_b1, f32)[0].reshape(2, 2, 128).transpose(2, 0, 1))
    state_gla = np.asarray(state_gla, f32)
    state_hgrn = np.asarray(state_hgrn, f32)
    cache_win = np.asarray(cache_win, f32)
    ckv2 = asc(cache_kv).reshape(2560 * 128 * 2, 512)
    in_maps = []
    for c in range(NCORES):
        sl = slice(c * NS, (c + 1) * NS)
        m = {
            "x_p": x_prompt[c % 4],
            "x_s": asc(x_sample[sl].reshape(TS, D)),
            "cache_win": asc(cache_win[0, sl].reshape(NS, 512, 512)),
            "cache_kv": ckv2, "page_table": np.ascontiguousarray(np.asarray(page_table)[sl].astype(np.int32)),
            "state_gla": asc(state_gla[0, sl]),
            "state_hgrn": asc(state_hgrn[0, sl]),
            "a_norm": a_norm_r, "c_norm": c_norm_r, "f_norm": asc(final_norm),
            "wA": wA, "wG": wG, "wC": wC, "wOA": wOA, "wOC": wOC,
            "gla_w2a": w2a, "gla_gn": asc(np.asarray(a_gla_gn, f32)[0]), "c_gn": asc(np.asarray(c_gn, f32)[0]),
            "lb_log": lb_log, "cmp_w1": cw1, "cmp_w2": cw2, "cmp_pe": cpe, "cmp_b1": cb1,
        }
        for k, v in consts.items():
            m["c_" + k] = v
        in_maps.append(m)
    res = run_bass_kernel_spmd(nc, in_maps, core_ids=list(range(NCORES)))
    R = res.results
    B, SEQ, DB = 4, 2048, 128
    cat = lambda name, shp: np.concatenate([np.asarray(R[c][name], f32).reshape(shp) for c in range(NCORES)])
    stk = lambda name, shp: np.stack([np.asarray(R[c][name], f32).reshape(shp) for c in range(4)])
    y_p = stk("y_p", (SEQ, D))
    y_s = cat("y_s", (NS, 4, D))
    kv_p = stk("kv_p", (SEQ, 4, 2, 128))[None]
    kv_s = cat("kv_s", (NS, 4, 4, 2, 128))[None]
    win_p = stk("win_p", (512, 2, 2, 128))[None]
    win_s = cat("win_s", (NS, 512, 2, 2, 128))[None]
    gla_p = stk("gla_p", (4, 128, 256))[None]
    gla_s = cat("gla_s", (NS, 4, 128, 256))[None]
    hg_p = stk("hg_p", (16, 128, 128))[None]
    hg_s = cat("hg_s", (NS, 16, 128, 128))[None]
    return (y_p, y_s, kv_p, kv_s, win_p, win_s, gla_p, gla_s, hg_p, hg_s)
```

```python
import numpy as np
from contextlib import ExitStack
import concourse.bass as bass
import concourse.mybir as mybir
from concourse.bass_utils import run_bass_kernel_spmd

F32 = mybir.dt.float32
BF16 = mybir.dt.bfloat16
I32 = mybir.dt.int32
AF = mybir.ActivationFunctionType
ALU = mybir.AluOpType
AX = mybir.AxisListType

NCORES = 8
D = 2048
T = 2048
NS = 16
TS = 64
TA = T + TS
NT = 17
EPS = 1e-6
STAGE = 4
DEBUG = False
STOPAT = 99


class Buf:
    __slots__ = ("name", "w", "r")

    def __init__(self, name=""):
        self.name = name
        self.w = None
        self.r = []


class TT:
    def __init__(self, t, name):
        self.t = t
        self.b = Buf(name)

    def __getitem__(self, k):
        return self.t[k]


class Sched:
    ENG = ("pe", "act", "dve", "pool", "sp")
    NDMA = 12

    def __init__(self, nc, es):
        self.nc = nc
        self.es = es
        self.eng = {"pe": nc.tensor, "act": nc.scalar, "dve": nc.vector, "pool": nc.gpsimd, "sp": nc.sync}
        self.sems = {}
        self.cnt = {}
        for e in ("pe", "act", "dve", "pool"):
            self.sems[e] = es.enter_context(nc.semaphore("s_" + e))
            self.cnt[e] = 0
        for q in ("sp", "act", "pool"):
            for i in range(self.NDMA):
                k = f"d_{q}{i}"
                self.sems[k] = es.enter_context(nc.semaphore(k))
                self.cnt[k] = 0
        self.dma_rr = {"sp": 0, "act": 0, "pool": 0}
        self.waited = {e: {} for e in self.ENG}
        self.n_inst = 0
        self.n_wait = 0
        self.uid = 0
        self.freed = {}
        self._scopes = []

    def scope(self):
        from contextlib import contextmanager

        @contextmanager
        def cm():
            rec = []
            self._scopes.append(rec)
            try:
                with ExitStack() as es:
                    yield es
            finally:
                self._scopes.pop()
                for tt in rec:
                    toks = list(tt.b.r) + ([tt.b.w] if tt.b.w else [])
                    for k, v in toks:
                        if self.freed.get(k, 0) < v:
                            self.freed[k] = v
        return cm()

    def tile(self, name, shape, dtype, es=None):
        self.uid += 1
        t = (es or self.es).enter_context(self.nc.sbuf_tensor(f"{name}_{self.uid}", list(shape), dtype))
        tt = TT(t, name)
        tt.b.r = list(self.freed.items())
        if es is not None and self._scopes:
            self._scopes[-1].append(tt)
        return tt

    def ptile(self, name, shape, dtype=F32, es=None):
        self.uid += 1
        t = (es or self.es).enter_context(self.nc.psum_tensor(f"{name}_{self.uid}", list(shape), dtype))
        return TT(t, name)

    def _deps(self, reads, writes):
        deps = {}

        def add(t):
            if t is None:
                return
            k, v = t
            if deps.get(k, 0) < v:
                deps[k] = v
        for b in reads:
            add(b.w)
        for b in writes:
            add(b.w)
            for t in b.r:
                add(t)
        return deps

    def _wait(self, e, deps):
        eng = self.eng[e]
        wd = self.waited[e]
        for k, v in deps.items():
            if wd.get(k, 0) < v:
                eng.wait_ge(self.sems[k], v)
                wd[k] = v
                self.n_wait += 1

    def _commit(self, tok, reads, writes):
        for b in reads:
            b.r.append(tok)
            if len(b.r) > 64:
                m = {}
                for k, v in b.r:
                    if m.get(k, 0) < v:
                        m[k] = v
                b.r = list(m.items())
        for b in writes:
            b.w = tok
            b.r = []

    @staticmethod
    def _bl(xs):
        out = []
        for x in xs:
            b = x.b if isinstance(x, (TT, View)) else x
            if isinstance(b, (list, tuple)):
                out.extend(b)
            else:
                out.append(b)
        return out

    def op(self, e, fn, reads=(), writes=()):
        reads = self._bl(reads)
        writes = self._bl(writes)
        deps = self._deps(reads, writes)
        if e == "pe":
            deps.pop("pe", None)
        self._wait(e, deps)
        ins = fn(self.eng[e])
        self.cnt[e] += 1
        ins.then_inc(self.sems[e], 1)
        self._commit((e, self.cnt[e]), reads, writes)
        self.n_inst += 1
        return ins

    def dma(self, q, out, in_, reads=(), writes=(), fn=None, **kw):
        reads = self._bl(reads)
        writes = self._bl(writes)
        deps = self._deps(reads, writes)
        self._wait(q, deps)
        i = self.dma_rr[q]
        self.dma_rr[q] = (i + 1) % self.NDMA
        k = f"d_{q}{i}"
        if fn is None:
            ins = self.eng[q].dma_start(out=out, in_=in_, **kw)
        else:
            ins = fn(self.eng[q])
        self.cnt[k] += 16
        ins.then_inc(self.sems[k], 16)
        self._commit((k, self.cnt[k]), reads, writes)
        self.n_inst += 1

    def finish(self, bufs):
        deps = self._deps([], self._bl(bufs))
        self._wait("sp", deps)


class View:
    def __init__(self, ap, b):
        self.t = ap
        self.b = b

    def __getitem__(self, k):
        return self.t[k]


class Rot:
    def __init__(self, items):
        self.items = items
        self.i = 0

    def next(self):
        x = self.items[self.i]
        self.i = (self.i + 1) % len(self.items)
        return x


def make_consts():
    c = {}
    c["ident"] = np.eye(128, dtype=np.float32)
    j = np.arange(128)[:, None]
    i = np.arange(128)[None, :]
    c["mle"] = (j <= i).astype(np.float32)
    c["mgt"] = (j > i).astype(np.float32)
    j6 = np.arange(64)[:, None]
    i6 = np.arange(64)[None, :]
    c["bd"] = ((j6 // 4 == i6 // 4) & (j6 <= i6)).astype(np.float32)
    cm = (np.arange(64)[None, :] // 4 == np.arange(16)[:, None]).astype(np.float32)
    c["colmask"] = np.broadcast_to(cm[None], (128, 16, 64)).copy()
    c["rowmask"] = (np.arange(64)[:, None] // 4 == np.arange(16)[None, :]).astype(np.float32)
    def c2s(nc_, ns_):
        cst = np.arange(nc_)[:, None] * 16
        sst = np.arange(ns_)[None, :] * 64
        ov = np.clip(np.minimum(cst + 32, sst + 64) - np.maximum(cst, sst), 0, None)
        return (ov / 16).astype(np.float32)
    c["c2sp"] = c2s(127, 32)
    c["c2ss"] = c2s(127, 33)
    def eexp(ns_, nk_):
        key = np.arange(nk_ * 128)
        return (np.arange(ns_)[:, None] == (key[None, :] // 64)).astype(np.float32)
    c["eexp"] = eexp(32, 16)
    c["eexs"] = eexp(33, 17)
    x = np.arange(63)[None, :] - 31 - (np.arange(128)[:, None] >= 64)
    c["wsel"] = np.where(x > 0, -1e9, np.where(x >= -1, 1e9, 0.0)).astype(np.float32)
    c["rsum"] = (np.arange(16)[:, None] % 4 == np.arange(4)[None, :]).astype(np.float32)
    return c


CONST_SHAPES = {k: v.shape for k, v in make_consts().items()}


def build_program(stage=STAGE):
    nc = bass.Bass("TRN2", target_bir_lowering=False)

    def din(name, shape, dt=F32):
        return nc.dram_tensor(name, list(shape), dt, kind="ExternalInput").ap()

    def dout(name, shape, dt=F32):
        return nc.dram_tensor(name, list(shape), dt, kind="ExternalOutput").ap()

    def dscr(name, shape, dt=F32):
        return nc.dram_tensor(name, list(shape), dt, kind="ExternalOutput" if DEBUG else "Internal").ap()

    x_p = din("x_p", [T, D])
    x_s = din("x_s", [TS, D])
    cache_win = din("cache_win", [NS, 512, 512])
    cache_kv = din("cache_kv", [2560 * 128 * 2, 512])
    page_table = din("page_table", [NS, 16], I32)
    state_gla = din("state_gla", [NS, 4, 128, 256])
    state_hgrn = din("state_hgrn", [NS, 16, 128, 128])
    a_norm = din("a_norm", [128, 16])
    c_norm = din("c_norm", [128, 16])
    f_norm = din("f_norm", [D])
    wA = din("wA", [128, 16, 6696])
    wG = din("wG", [128, 16, 3088])
    wC = din("wC", [128, 16, 8192])
    wOA = din("wOA", [128, 16, 2048])
    wOC = din("wOC", [128, 16, 2048])
    gla_w2a = din("gla_w2a", [17, 512])
    gla_gn = din("gla_gn", [256])
    c_gn = din("c_gn", [128])
    lb_log = din("lb_log", [128, 2, 16])
    cmp_w1 = din("cmp_w1", [2, 128, 32, 256])
    cmp_w2 = din("cmp_w2", [128, 2, 2, 128])
    cmp_pe = din("cmp_pe", [128, 2, 32])
    cmp_b1 = din("cmp_b1", [128, 2, 2])
    cst = {k: din("c_" + k, list(s)) for k, s in CONST_SHAPES.items()}

    y_p = dout("y_p", [T, D])
    y_s = dout("y_s", [TS, D])
    kv_p = dout("kv_p", [T, 1024])
    kv_s = dout("kv_s", [TS, 1024])
    win_p = dout("win_p", [512, 512])
    win_s = dout("win_s", [NS, 512, 512])
    gla_p = dout("gla_p", [4, 128, 256])
    gla_s = dout("gla_s", [NS, 4, 128, 256])
    hg_p = dout("hg_p", [16, 128, 128])
    hg_s = dout("hg_s", [NS, 16, 128, 128])

    oT_scr = dscr("oT_scr", [NT, 128, 16, 128], BF16)
    g_scr = dscr("g_scr", [TS, 24])
    z_scr = dscr("z_scr", [TS, 1024], BF16)
    x1_scr = dscr("x1_scr", [TA, D])
    y_scr = dscr("y_scr", [TA, D])
    oT_b = [[Buf(f"oT{t}_{f}") for f in range(16)] for t in range(NT)]
    x1_b = [[Buf(f"x1_{t}_{k}") for k in range(4)] for t in range(NT)]
    ys_b = [[Buf(f"ys_{t}_{k}") for k in range(4)] for t in range(NT)]

    outs = []
    dbg_n = [0]

    with ExitStack() as es:
        S = Sched(nc, es)

        def dbg(name, tt, ap, shape, dt=F32):
            if not DEBUG:
                return
            dbg_n[0] += 1
            d = nc.dram_tensor(f"dbg_{name}", list(shape), dt, kind="ExternalOutput").ap()
            b = Buf(name)
            outs.append(b)
            S.dma("sp", d, ap, reads=[tt], writes=[b])

        def obuf(name):
            b = Buf(name)
            outs.append(b)
            return b

        hT_ref = [None]
        wbuf = Rot([S.tile(f"wbuf{i}", [128, 16, 512], BF16) for i in range(2)])
        ident_f = S.tile("ident_f", [128, 128], F32)
        ident_b = S.tile("ident_b", [128, 128], BF16)
        mle_b = S.tile("mle_b", [128, 128], BF16)
        normA = S.tile("normA", [128, 16], F32)
        normC = S.tile("normC", [128, 16], F32)
        PP = [S.ptile(f"pp{i}", [128, 1024], F32) for i in range(4)]
        bankb = [Buf(f"bank{i}") for i in range(8)]
        psb = [View(PP[i // 2][:, (i % 2) * 512:(i % 2) * 512 + 512], bankb[i]) for i in range(8)]
        psA = View(PP[2][:, :], [bankb[4], bankb[5]])
        psB = View(PP[3][:, :], [bankb[6], bankb[7]])
        psbf = Rot([View(PP[3][:, k * 512:(k + 1) * 512].bitcast(BF16), bankb[6 + k]) for k in range(2)])

        S.dma("sp", ident_f[:], cst["ident"][:, :], writes=[ident_f])
        S.dma("pool", ident_b[:], cst["ident"][:, :], writes=[ident_b])
        S.dma("pool", mle_b[:], cst["mle"][:, :], writes=[mle_b])
        S.dma("sp", normA[:], a_norm[:, :], writes=[normA])
        S.dma("sp", normC[:], c_norm[:, :], writes=[normC])

        for s in range(NS):
            S.dma("sp", win_s[s, 0:508, :], cache_win[s, 4:512, :], writes=[obuf("win_s")])

        def norm_pass(src_fn, normw, src_deps):
            with S.scope() as es1:
                xrot = Rot([S.tile(f"xt{i}", [128, D], F32, es1) for i in range(2)])
                ssr = Rot([S.tile(f"ss{i}", [128, 1], F32, es1) for i in range(2)])
                rsr = Rot([S.tile(f"rs{i}", [128, 1], F32, es1) for i in range(2)])
                rstdr = Rot([S.tile(f"rstd{i}", [128, 1], F32, es1) for i in range(2)])
                junk = S.tile("junk", [128, D], BF16, es1)
                prot = Rot(psb[0:4])
                for t in range(NT):
                    rows = 128 if t < 16 else TS
                    tok0 = t * 128
                    xt, ss, rs, rstd = xrot.next(), ssr.next(), rsr.next(), rstdr.next()
                    S.dma("sp", xt[:rows, :], src_fn(t, rows), reads=src_deps(t), writes=[xt])
                    S.op("act", lambda e: e.activation(out=junk[:rows, :], in_=xt[:rows, :], func=AF.Square,
                                                       accum_out=ss[:rows, 0:1]), reads=[xt], writes=[junk, ss])
                    S.op("act", lambda e: e.activation(out=rs[:rows, :], in_=ss[:rows, :], func=AF.Sqrt,
                                                       scale=1.0 / D, bias=EPS), reads=[ss], writes=[rs])
                    S.op("dve", lambda e: e.reciprocal(out=rstd[:rows, :], in_=rs[:rows, :]), reads=[rs], writes=[rstd])
                    S.op("act", lambda e: e.activation(out=xt[:rows, :], in_=xt[:rows, :], func=AF.Copy,
                                                       scale=rstd[:rows, 0:1]), reads=[xt, rstd], writes=[xt])
                    for cq in range(4):
                        pb = prot.next()
                        for k in range(4):
                            c = 4 * cq + k
                            S.op("pe", lambda e: e.transpose(out=pb[:, k * 128:k * 128 + rows],
                                                             in_=xt[:rows, c * 128:(c + 1) * 128],
                                                             identity=ident_f[:rows, :rows]),
                                 reads=[xt, ident_f], writes=[pb])
                        S.op("dve", lambda e: e.tensor_tensor(
                            out=hT_ref[0][:, 4 * cq:4 * cq + 4, tok0:tok0 + rows],
                            in0=pb[:, :].rearrange("p (k t) -> p k t", k=4)[:, :, :rows],
                            in1=normw[:, 4 * cq:4 * cq + 4].unsqueeze(2).to_broadcast([128, 4, rows]),
                            op=ALU.mult), reads=[pb, normw], writes=[hT_ref[0]])


        def load_w(col0, n, src):
            wb = wbuf.next()
            S.dma("pool", wb[:, :, 0:n], src[:, :, col0:col0 + n], writes=[wb])
            return wb

        prj = Rot(psb[0:4])
        evq = Rot(["act", "dve"])

        def proj_tok(wb, n, consume, tiles=range(NT), wc0=0):
            for t in tiles:
                rows = 128 if t < 16 else TS
                pb = prj.next()
                for c in range(16):
                    S.op("pe", lambda e: e.matmul(pb[:rows, 0:n], lhsT=hT_ref[0][:, c, t * 128:t * 128 + rows],
                                                  rhs=wb[:, c, wc0:wc0 + n], start=(c == 0), stop=(c == 15)),
                         reads=[hT_ref[0], wb], writes=[pb])
                consume(t, rows, pb)

        def proj_feat(wb, wc0, m, consume, chunks=range(5)):
            for j in chunks:
                n = 512 if j < 4 else TS
                pb = prj.next()
                for c in range(16):
                    S.op("pe", lambda e: e.matmul(pb[:m, 0:n], lhsT=wb[:, c, wc0:wc0 + m],
                                                  rhs=hT_ref[0][:, c, j * 512:j * 512 + n], start=(c == 0), stop=(c == 15)),
                         reads=[hT_ref[0], wb], writes=[pb])
                consume(j, n, pb)

        def evac(out_ap, in_ap, reads, writes, func=None):
            q = "act" if func is not None else evq.next()
            if q == "act":
                S.op("act", lambda e: e.activation(out=out_ap, in_=in_ap, func=func or AF.Copy), reads=reads, writes=writes)
            else:
                S.op("dve", lambda e: e.tensor_copy(out=out_ap, in_=in_ap), reads=reads, writes=writes)

        OQ, OKV, OG, OZA = 0, 1024, 2560, 2584

        SC = 128 ** -0.5
        with S.scope() as esN:
            gts = S.tile("gts", [128, NT, 24], F32, esN)
            qTs = S.tile("qTs", [128, 8, TS], BF16, esN)
            zs = S.tile("zs", [128, 1024], BF16, esN)
            vnew_src = S.tile("vnew_src", [TS, 2, 2, 130], BF16, esN)
            S.op("pool", lambda e: e.memset(vnew_src[:, :, :, 128:130], 1.0), writes=[vnew_src])
            mgt_b = S.tile("mgt_b", [128, 128], BF16, esN)
            S.dma("pool", mgt_b[:, :], cst["mgt"][:, :], writes=[mgt_b])
            kselTs = S.tile("kselTs", [128, 2, TS], BF16, esN)
            kwinTs = S.tile("kwinTs", [128, 2, TS], BF16, esN)
            with S.scope() as esH:
                hT_ref[0] = S.tile("hT", [128, 16, TA], BF16, esH)
                norm_pass(lambda t, rows: (x_p[t * 128:(t + 1) * 128, :] if t < 16 else x_s[:, :]), normA, lambda t: [])
                R = dict(mle_b=mle_b, ident_b=ident_b, ident_f=ident_f, psb=psb, psbf=psbf, cst=cst, obuf=obuf,
                         oT_scr=oT_scr, oT_b=oT_b, dbg=dbg)

                with S.scope() as esG:
                    lrT = S.tile("lrT", [17, TA], F32, esG)
                    w2a = S.tile("w2a", [17, 512], F32, esG)
                    S.dma("sp", w2a[:, :], gla_w2a[:, :], writes=[w2a])
                    for j in range(0, TA, 128):
                        n = min(128, TA - j)
                        S.dma("sp", lrT[16:17, j:j + n], cst["mle"][0:1, 0:n], writes=[lrT])
                    wb = load_w(3072, 16, wG)
                    proj_feat(wb, 0, 16, lambda j, n, pb: evac(lrT[0:16, j * 512:j * 512 + n], pb[0:16, 0:n], [pb], [lrT]))
                    zb_shared = S.tile("zb", [128, NT, 256], BF16, esG)
                    hsets = Rot([dict(q=S.tile(f"qbT{i}", [128, TA], BF16, esG), k=S.tile(f"kbT{i}", [128, TA], BF16, esG),
                                      v=S.tile(f"vb{i}", [128, NT, 256], BF16, esG), z=zb_shared)
                                 for i in range(2)])
                    G = rec_setup(S, esG, R, V=256)
                    gnb = S.tile("gnb", [128, 256], F32, esG)
                    S.dma("sp", gnb[:, :], gla_gn.partition_broadcast(128), writes=[gnb])
                    lg = Rot([S.tile(f"lg{i}", [128, 128], F32, esG) for i in range(2)])

                    for h in range(4):
                        hs = hsets.next()
                        wb = load_w(h * 768, 512, wG)
                        proj_feat(wb, 0, 128, lambda j, n, pb: evac(hs["q"][:, j * 512:j * 512 + n], pb[:, 0:n], [pb], [hs["q"]]))
                        proj_feat(wb, 128, 128, lambda j, n, pb: evac(hs["k"][:, j * 512:j * 512 + n], pb[:, 0:n], [pb], [hs["k"]]))
                        proj_tok(wb, 256, lambda t, rows, pb: evac(hs["v"][:rows, t, :], pb[:rows, 0:256], [pb], [hs["v"]]), wc0=256)
                        wb = load_w(h * 768 + 512, 256, wG)
                        proj_tok(wb, 256, lambda t, rows, pb: evac(hs["z"][:rows, t, :], pb[:rows, 0:256], [pb], [hs["z"]], func=AF.Silu))

                        def gate(tok, n, h=h):
                            l_ = lg.next()
                            p_g = psb[3]
                            S.op("pe", lambda e: e.matmul(p_g[:, 0:n], lhsT=w2a[0:17, h * 128:(h + 1) * 128], rhs=lrT[0:17, tok],
                                                          start=True, stop=True), reads=[w2a, lrT], writes=[p_g])
                            S.op("act", lambda e: e.activation(out=l_[:, 0:n], in_=p_g[:, 0:n], func=AF.Exp, scale=-1.0), reads=[p_g], writes=[l_])
                            S.op("act", lambda e: e.activation(out=l_[:, 0:n], in_=l_[:, 0:n], func=AF.Ln, bias=1.0, scale=1.0), reads=[l_], writes=[l_])
                            return l_[:, 0:n], l_
                        oh = rec_head(S, G, hs, V=256, COEF=-1.0 / 16.0, QS=128 ** -0.5, gate=gate,
                                      state_in=lambda s: state_gla[s, h, :, :], out_p=gla_p[h, :, :], out_s=lambda s: gla_s[s, h, :, :])
                        if h == 0:
                            dbg("gnb", gnb, gnb[:, :], [128, 256])
                            dbg("oh", oh, oh[:, 0, :], [128, 256], BF16)
                            dbg("zb", hs["z"], hs["z"][:, 0, :], [128, 256], BF16)
                        post_head(S, G, oh, hs["z"], gnb, V=256, fc0=8 + 2 * h, dbgh=(h == 0))
                        if STOPAT <= 1:
                            break

                with S.scope() as esNP:
                    vsel = S.tile("vsel", [128, NT, 2, 130], BF16, esNP)
                    vwin = S.tile("vwin", [128, NT, 2, 130], BF16, esNP)
                    S.op("pool", lambda e: e.memset(vsel[:, :, :, 128:130], 1.0), writes=[vsel])
                    S.op("pool", lambda e: e.memset(vwin[:, :, :, 128:130], 1.0), writes=[vwin])
                    kselT = S.tile("kselT", [128, 2, T], BF16, esNP)
                    kwinT = S.tile("kwinT", [128, 2, T], BF16, esNP)
                    kcT = S.tile("kcT", [128, 2, 128], BF16, esNP)
                    vca = S.tile("vca", [128, 2, 162], BF16, esNP)
                    eexp = S.tile("eexp", [32, 16, 128], BF16, esNP)
                    S.dma("pool", eexp[:, :, :], cst["eexp"].rearrange("s (k j) -> s k j", k=16), writes=[eexp])
                    wsel = S.tile("wsel", [128, 63], F32, esNP)
                    S.dma("sp", wsel[:, :], cst["wsel"][:, :], writes=[wsel])
                    S.op("pool", lambda e: e.memset(vca[:, :, 128:129], 1.0), writes=[vca])
                    for g in range(2):
                        S.dma("pool", vca[0:127, g, 129:161], cst["c2sp"][:, :], writes=[vca])

                    with S.scope() as esKV:
                        stg = Rot([S.tile(f"stg{i}", [128, 512], F32, esKV) for i in range(3)])
                        for blk in range(3):
                            wb = load_w(OKV + blk * 512, 512, wA)

                            def cons(t, rows, pb, blk=blk):
                                st = stg.next()
                                evac(st[:rows, :], pb[:rows, :], [pb], [st])
                                if blk < 2:
                                    dst = (kv_p[t * 128:t * 128 + rows, blk * 512:(blk + 1) * 512] if t < 16
                                           else kv_s[:, blk * 512:(blk + 1) * 512])
                                    S.dma("sp", dst, st[:rows, :], reads=[st], writes=[obuf("kv")])
                                else:
                                    if t >= 12 and t < 16:
                                        S.dma("sp", win_p[(t - 12) * 128:(t - 11) * 128, :], st[:rows, :], reads=[st], writes=[obuf("win")])
                                    elif t == 16:
                                        for s in range(NS):
                                            S.dma("sp", win_s[s, 508:512, :], st[4 * s:4 * s + 4, :], reads=[st], writes=[obuf("wins")])
                                if blk >= 1:
                                    vt = vsel if blk == 1 else vwin
                                    S.op("pool", lambda e: e.tensor_copy(out=vt[:rows, t, :, 0:128],
                                                                         in_=st[:rows, 256:512].rearrange("p (g d) -> p g d", g=2)),
                                         reads=[st], writes=[vt])
                                    if t == 16:
                                        S.op("pool", lambda e: e.tensor_copy(out=vnew_src[:, blk - 1, :, 0:128],
                                                                             in_=st[:rows, 256:512].rearrange("p (g d) -> p g d", g=2)),
                                             reads=[st], writes=[vnew_src])
                            proj_tok(wb, 512, cons)

                    if stage >= 3:
                        with S.scope() as esCmp:
                            kcmpT = S.tile("kcmpT", [128, 2, T], BF16, esCmp)
                            vcmpT = S.tile("vcmpT", [128, 2, T], BF16, esCmp)
                            wb = load_w(OKV, 512, wA)
                            for i, dstt in enumerate((kcmpT, kcmpT, vcmpT, vcmpT)):
                                proj_feat(wb, i * 128, 128, lambda j, n, pb, dstt=dstt, i=i: evac(dstt[:, i % 2, j * 512:j * 512 + n], pb[:, 0:n], [pb], [dstt]),
                                          chunks=range(4))
                            wb = load_w(OKV + 512, 256, wA)
                            for g in range(2):
                                proj_feat(wb, g * 128, 128, lambda j, n, pb, g=g: (evac(kselT[:, g, j * 512:j * 512 + n], pb[:, 0:n], [pb], [kselT]) if j < 4
                                                                                  else evac(kselTs[:, g, :], pb[:, 0:n], [pb], [kselTs])))
                            wb = load_w(OKV + 1024, 256, wA)
                            for g in range(2):
                                proj_feat(wb, g * 128, 128, lambda j, n, pb, g=g: (evac(kwinT[:, g, j * 512:j * 512 + n], pb[:, 0:n], [pb], [kwinT]) if j < 4
                                                                                  else evac(kwinTs[:, g, :], pb[:, 0:n], [pb], [kwinTs])))
                            w1b = S.tile("w1b", [128, 32, 256], BF16, esCmp)
                            w2b = S.tile("w2b", [128, 2, 2, 128], BF16, esCmp)
                            peb = S.tile("peb", [128, 2, 32], BF16, esCmp)
                            b1t = S.tile("b1t", [128, 2, 2], F32, esCmp)
                            hb = S.tile("hb", [128, 2, 2], F32, esCmp)
                            gT = Rot([S.tile(f"gT{i}", [128, 2, 128], BF16, esCmp) for i in range(2)])
                            S.dma("pool", w2b[:, :, :, :], cmp_w2[:, :, :, :], writes=[w2b])
                            S.dma("pool", peb[:, :, :], cmp_pe[:, :, :], writes=[peb])
                            S.dma("sp", b1t[:, :, :], cmp_b1[:, :, :], writes=[b1t])
                            for kv in range(2):
                                S.dma("pool", w1b[:, :, :], cmp_w1[kv, :, :, :], writes=[w1b])
                                xT = kcmpT if kv == 0 else vcmpT
                                pp = psb[3]
                                for half in range(2):
                                    for rp in range(32):
                                        S.op("pe", lambda e: e.matmul(pp[:, half:half + 1], lhsT=w1b[:, rp, half * 128:(half + 1) * 128],
                                                                      rhs=peb[:, kv, rp:rp + 1], start=(rp == 0), stop=(rp == 31)),
                                             reads=[w1b, peb], writes=[pp])
                                S.op("dve", lambda e: e.tensor_tensor(out=hb[:, kv, :], in0=pp[:, 0:2], in1=b1t[:, kv, :], op=ALU.add),
                                     reads=[pp, b1t], writes=[hb])
                                for g in range(2):
                                    gt = gT.next()
                                    for half in range(2):
                                        ph = prj.next()
                                        for rp in range(32):
                                            r_, p_ = rp // 16, rp % 16
                                            st0 = 16 * r_ + p_
                                            S.op("pe", lambda e: e.matmul(ph[:, 0:127], lhsT=w1b[:, rp, half * 128:(half + 1) * 128],
                                                                          rhs=xT[:, g, st0:st0 + 16 * 126 + 1:16], start=(rp == 0), stop=(rp == 31)),
                                                 reads=[w1b, xT], writes=[ph])
                                        S.op("act", lambda e: e.activation(out=gt[:, half, 0:127], in_=ph[:, 0:127], func=AF.Gelu_apprx_tanh,
                                                                           bias=hb[:, kv, half:half + 1]), reads=[ph, hb], writes=[gt])
                                    po = prj.next()
                                    if kv == 0:
                                        for half in range(2):
                                            S.op("pe", lambda e: e.matmul(po[:, 0:127], lhsT=w2b[:, 0, half, :], rhs=gt[:, half, 0:127],
                                                                          start=(half == 0), stop=(half == 1)), reads=[w2b, gt], writes=[po])
                                        evac(kcT[:, g, 0:127], po[:, 0:127], [po], [kcT])
                                    else:
                                        for half in range(2):
                                            S.op("pe", lambda e: e.matmul(po[0:127, 0:128], lhsT=gt[:, half, 0:127], rhs=w2b[:, 1, half, :],
                                                                          start=(half == 0), stop=(half == 1)), reads=[w2b, gt], writes=[po])
                                        evac(vca[0:127, g, 0:128], po[0:127, 0:128], [po], [vca])

                        wb = load_w(OG, 24, wA)
                        proj_tok(wb, 24, lambda t, rows, pb: evac(gts[:rows, t, :], pb[:rows, 0:24], [pb], [gts], func=AF.Sigmoid))

                        for g in range(2):
                            with S.scope() as esQ:
                                qTg = S.tile("qTg", [128, 4, T], BF16, esQ)
                                zg = S.tile("zg", [128, 16, 512], BF16, esQ)
                                wb = load_w(OQ + g * 512, 512, wA)
                                for r in range(4):
                                    def qcons(j, n, pb, r=r):
                                        if j < 4:
                                            evac(qTg[:, r, j * 512:j * 512 + n], pb[:, 0:n], [pb], [qTg])
                                        else:
                                            evac(qTs[:, 4 * g + r, :], pb[:, 0:n], [pb], [qTs])
                                    proj_feat(wb, r * 128, 128, qcons)
                                wb = load_w(OZA + g * 512, 512, wA)

                                def zcons(t, rows, pb):
                                    if t < 16:
                                        evac(zg[:, t, :], pb[:, :], [pb], [zg], func=AF.Silu)
                                    else:
                                        evac(zs[:rows, g * 512:(g + 1) * 512], pb[:rows, :], [pb], [zs], func=AF.Silu)
                                proj_tok(wb, 512, zcons)
                                nsa_prompt(S, esQ, g, dict(qTg=qTg, zg=zg, kcT=kcT, vca=vca, kselT=kselT, kwinT=kwinT, vsel=vsel, vwin=vwin,
                                                           gts=gts, wsel=wsel, eexp=eexp, mle_b=mle_b, mgt_b=mgt_b, ident_b=ident_b,
                                                           psb=psb, psA=psA, psB=psB, oT_scr=oT_scr, oT_b=oT_b, SC=SC))
            if stage >= 4:
                with S.scope() as esS:
                    nsa_sample(S, esS, nc, dict(qTs=qTs, zs=zs, gts=gts, kselT=kselTs, kwinT=kwinTs, vnew_src=vnew_src, mle_b=mle_b,
                                                mgt_b=mgt_b, ident_b=ident_b, ident_f=ident_f, psb=psb, wbuf=wbuf, cst=cst, SC=SC,
                                                cache_kv=cache_kv, cache_win=cache_win, page_table=page_table, cmp_w1=cmp_w1,
                                                cmp_w2=cmp_w2, cmp_pe=cmp_pe, cmp_b1=cmp_b1, g_scr=g_scr, z_scr=z_scr,
                                                oT_scr=oT_scr, oT_b=oT_b, evac=evac))

            if stage < 3:
                with S.scope() as esZ:
                    zt = S.tile("zt", [128, 8, 128], BF16, esZ)
                    S.op("pool", lambda e: e.memset(zt[:, :, :], 0.0), writes=[zt])
                    for t in range(NT):
                        S.dma("sp", oT_scr[t, :, 0:8, :], zt[:, :, :], reads=[zt], writes=oT_b[t][0:8])
            elif stage < 4:
                with S.scope() as esZ:
                    zt = S.tile("zt", [128, 8, 128], BF16, esZ)
                    S.op("pool", lambda e: e.memset(zt[:, :, :], 0.0), writes=[zt])
                    S.dma("sp", oT_scr[16, :, 0:8, :], zt[:, :, :], reads=[zt], writes=oT_b[16][0:8])

        def wout_phase(wsrc, res_fn, res_deps, dst, dst_b):
            with S.scope() as e3:
                otr = Rot([S.tile(f"ot{i}", [128, 16, 128], BF16, e3) for i in range(4)])
                xr = Rot([S.tile(f"xr{i}", [128, 512], F32, e3) for i in range(5)])
                for blk in range(4):
                    wb = load_w(blk * 512, 512, wsrc)
                    for t in range(NT):
                        rows = 128 if t < 16 else TS
                        ot, xt = otr.next(), xr.next()
                        S.dma("sp", ot[:, :, :], oT_scr[t, :, :, :], reads=oT_b[t], writes=[ot])
                        S.dma("sp", xt[:rows, :], res_fn(t, rows, blk), reads=res_deps(t, blk), writes=[xt])
                        pb = prj.next()
                        for c in range(16):
                            S.op("pe", lambda e: e.matmul(pb[:rows, 0:512], lhsT=ot[:, c, 0:rows], rhs=wb[:, c, 0:512],
                                                          start=(c == 0), stop=(c == 15)), reads=[ot, wb], writes=[pb])
                        S.op("dve", lambda e: e.tensor_tensor(out=xt[:rows, :], in0=pb[:rows, 0:512], in1=xt[:rows, :], op=ALU.add),
                             reads=[pb, xt], writes=[xt])
                        S.dma("pool", dst[t * 128:t * 128 + rows, blk * 512:(blk + 1) * 512], xt[:rows, :], reads=[xt], writes=[dst_b[t][blk]])

        def xsrc(t, rows, blk):
            return (x_p[t * 128:(t + 1) * 128, blk * 512:(blk + 1) * 512] if t < 16 else x_s[:, blk * 512:(blk + 1) * 512])

        esH2 = es.enter_context(S.scope())
        hT_ref[0] = S.tile("hT2", [128, 16, TA], BF16, esH2)
        if STOPAT > 1:
            wout_phase(wOA, xsrc, lambda t, blk: [], x1_scr, x1_b)
        run_c = STOPAT > 2

        if run_c:
          norm_pass(lambda t, rows: x1_scr[t * 128:t * 128 + rows, :], normC, lambda t: x1_b[t])
        with S.scope() as esC:
          if run_c:
            lbt = S.tile("lbt", [128, 2, 16], F32, esC)
            lb = S.tile("lb", [128, 16], F32, esC)
            oml = S.tile("oml", [128, 16], F32, esC)
            S.dma("sp", lbt[:, :, :], lb_log[:, :, :], writes=[lbt])
            S.op("dve", lambda e: e.tensor_tensor(out=lb[:, :], in0=lbt[:, 1, :], in1=lbt[:, 0, :], op=ALU.subtract), reads=[lbt], writes=[lb])
            S.op("act", lambda e: e.activation(out=lb[:, :], in_=lb[:, :], func=AF.Sigmoid), reads=[lb], writes=[lb])
            S.op("dve", lambda e: e.tensor_scalar(out=oml[:, :], in0=lb[:, :], scalar1=-1.0, scalar2=1.0, op0=ALU.mult, op1=ALU.add),
                 reads=[lb], writes=[oml])
            hsets = Rot([dict(q=S.tile(f"qcT{i}", [128, TA], BF16, esC), k=S.tile(f"kcT{i}", [128, TA], BF16, esC),
                              g=S.tile(f"gcT{i}", [128, TA], F32, esC),
                              v=S.tile(f"vc{i}", [128, NT, 128], BF16, esC), z=S.tile(f"zc{i}", [128, NT, 128], BF16, esC))
                         for i in range(2)])
            G = rec_setup(S, esC, R, V=128)
            gnc = S.tile("gnc", [128, 128], F32, esC)
            S.dma("sp", gnc[:, :], c_gn.partition_broadcast(128), writes=[gnc])
            sgr = Rot([S.tile(f"sg{i}", [128, 512], F32, esC) for i in range(2)])
            for h in range(16):
                hs = hsets.next()
                wb = load_w(h * 512, 512, wC)
                proj_feat(wb, 0, 128, lambda j, n, pb: evac(hs["q"][:, j * 512:j * 512 + n], pb[:, 0:n], [pb], [hs["q"]], func=AF.Silu))

                def fgate(j, n, pb, h=h, hs=hs):
                    sg = sgr.next()
                    sl = slice(j * 512, j * 512 + n)
                    S.op("act", lambda e: e.activation(out=sg[:, 0:n], in_=pb[:, 0:n], func=AF.Sigmoid), reads=[pb], writes=[sg])
                    S.op("dve", lambda e: e.tensor_scalar(out=sg[:, 0:n], in0=sg[:, 0:n], scalar1=oml[:, h:h + 1], scalar2=lb[:, h:h + 1],
                                                          op0=ALU.mult, op1=ALU.add), reads=[sg, oml, lb], writes=[sg])
                    S.op("act", lambda e: e.activation(out=hs["g"][:, sl], in_=sg[:, 0:n], func=AF.Ln), reads=[sg], writes=[hs["g"]])
                    S.op("dve", lambda e: e.tensor_scalar(out=hs["k"][:, sl], in0=sg[:, 0:n], scalar1=-1.0, scalar2=1.0,
                                                          op0=ALU.mult, op1=ALU.add), reads=[sg], writes=[hs["k"]])
                proj_feat(wb, 128, 128, fgate)
                proj_tok(wb, 128, lambda t, rows, pb: evac(hs["v"][:rows, t, :], pb[:rows, 0:128], [pb], [hs["v"]]), wc0=256)
                proj_tok(wb, 128, lambda t, rows, pb: evac(hs["z"][:rows, t, :], pb[:rows, 0:128], [pb], [hs["z"]], func=AF.Silu), wc0=384)
                oh = rec_head(S, G, hs, V=128, COEF=1.0, QS=1.0, gate=lambda tok, n, hs=hs: (hs["g"][:, tok], hs["g"]),
                              state_in=lambda s, h=h: state_hgrn[s, h, :, :], out_p=hg_p[h, :, :], out_s=lambda s, h=h: hg_s[s, h, :, :])
                post_head(S, G, oh, hs["z"], gnc, V=128, fc0=h)

        if run_c:
          wout_phase(wOC, lambda t, rows, blk: x1_scr[t * 128:t * 128 + rows, blk * 512:(blk + 1) * 512],
                     lambda t, blk: [x1_b[t][blk]], y_scr, ys_b)

        with S.scope() as e4:
          if run_c:
            xrot = Rot([S.tile(f"yt{i}", [128, D], F32, e4) for i in range(3)])
            ssr = Rot([S.tile(f"yss{i}", [128, 1], F32, e4) for i in range(2)])
            rsr = Rot([S.tile(f"yrs{i}", [128, 1], F32, e4) for i in range(2)])
            rstdr = Rot([S.tile(f"yrstd{i}", [128, 1], F32, e4) for i in range(2)])
            junk = S.tile("yjunk", [128, D], BF16, e4)
            fnb = S.tile("fnb", [128, D], F32, e4)
            S.dma("sp", fnb[:, :], f_norm.partition_broadcast(128), writes=[fnb])
            for t in range(NT):
                rows = 128 if t < 16 else TS
                xt, ss, rs, rstd = xrot.next(), ssr.next(), rsr.next(), rstdr.next()
                S.dma("sp", xt[:rows, :], y_scr[t * 128:t * 128 + rows, :], reads=ys_b[t], writes=[xt])
                S.op("act", lambda e: e.activation(out=junk[:rows, :], in_=xt[:rows, :], func=AF.Square,
                                                   accum_out=ss[:rows, 0:1]), reads=[xt], writes=[junk, ss])
                S.op("act", lambda e: e.activation(out=rs[:rows, :], in_=ss[:rows, :], func=AF.Sqrt,
                                                   scale=1.0 / D, bias=EPS), reads=[ss], writes=[rs])
                S.op("dve", lambda e: e.reciprocal(out=rstd[:rows, :], in_=rs[:rows, :]), reads=[rs], writes=[rstd])
                S.op("dve", lambda e: e.scalar_tensor_tensor(out=xt[:rows, :], in0=xt[:rows, :], scalar=rstd[:rows, 0:1], in1=fnb[:rows, :],
                                                             op0=ALU.mult, op1=ALU.mult), reads=[xt, rstd, fnb], writes=[xt])
                dst = y_p[t * 128:(t + 1) * 128, :] if t < 16 else y_s[:, :]
                S.dma("pool", dst, xt[:rows, :], reads=[xt], writes=[obuf("y")])

        S.finish(outs)
        print("instructions", S.n_inst, "waits", S.n_wait)
    return nc


def nsa_prompt(S, es, g, N):
    qTg, zg, kcT, vca, kselT, kwinT, vsel, vwin, gts = (N[k] for k in ("qTg", "zg", "kcT", "vca", "kselT", "kwinT", "vsel", "vwin", "gts"))
    wsel, eexp, mle_b, mgt_b, ident_b, psb, psA, psB, SC = (N[k] for k in ("wsel", "eexp", "mle_b", "mgt_b", "ident_b", "psb", "psA", "psB", "SC"))
    tl = lambda n, s, d=F32: S.tile(n, s, d, es)
    et = Rot([tl(f"et{i}", [128, 512], BF16) for i in range(3)])
    den = Rot([tl(f"den{i}", [128, 4]) for i in range(3)])
    rden = Rot([tl(f"rden{i}", [128, 4]) for i in range(3)])
    cf = Rot([tl(f"cf{i}", [128, 4]) for i in range(3)])
    acc = Rot([tl(f"acc{i}", [128, 4, 128]) for i in range(2)])
    tmpo = Rot([tl(f"tmpo{i}", [128, 4, 128]) for i in range(2)])
    impn = tl("impn", [128, 4, 32])
    sc = Rot([tl(f"sc{i}", [128, 32]) for i in range(2)])
    sc2 = tl("sc2", [128, 32])
    m8 = tl("m8", [128, 16])
    selm = tl("selm", [128, 32], BF16)
    selT = Rot([tl(f"selT{i}", [32, 128], BF16) for i in range(2)])
    m2 = Rot([tl(f"m2{i}", [128, 128], BF16) for i in range(2)])
    ob = Rot([tl(f"nob{i}", [128, 512], BF16) for i in range(2)])
    st = Rot([tl(f"nst{i}", [128, 4, 128], BF16) for i in range(2)])
    scb = Rot([psb[0], psb[1]])
    pmk, pmisc = psb[2], psb[3]
    pmisc_bf = View(pmisc[:, :].bitcast(BF16), pmisc.b)
    A3 = View(psA[:, :].rearrange("p (r c) -> p r c", r=4), psA.b)
    B3 = View(psB[:, :].rearrange("p (r c) -> p r c", r=4), psB.b)

    def v4(x):
        return x[:, :].rearrange("p (r q) -> p r q", r=4)

    def finish_branch(P3, br, t, a_, first):
        d_, r_, c_ = den.next(), rden.next(), cf.next()
        S.op("dve", lambda e: e.tensor_scalar(out=d_[:, :], in0=P3[:, :, 128], scalar1=1e-30, scalar2=None, op0=ALU.max), reads=[P3], writes=[d_])
        S.op("dve", lambda e: e.reciprocal(out=r_[:, :], in_=d_[:, :]), reads=[d_], writes=[r_])
        S.op("dve", lambda e: e.tensor_tensor(out=c_[:, :], in0=r_[:, :], in1=gts[:, t, 12 * g + br:12 * g + 12:3], op=ALU.mult),
             reads=[r_, gts], writes=[c_])
        cb = c_[:, :].unsqueeze(2).to_broadcast([128, 4, 128])
        if first:
            S.op("dve", lambda e: e.tensor_tensor(out=a_[:, :, :], in0=P3[:, :, 0:128], in1=cb, op=ALU.mult), reads=[P3, c_], writes=[a_])
        else:
            tm = tmpo.next()
            S.op("dve", lambda e: e.tensor_tensor(out=tm[:, :, :], in0=P3[:, :, 0:128], in1=cb, op=ALU.mult), reads=[P3, c_], writes=[tm])
            S.op("pool", lambda e: e.tensor_tensor(out=a_[:, :, :], in0=a_[:, :, :], in1=tm[:, :, :], op=ALU.add), reads=[a_, tm], writes=[a_])
        return r_

    for t in range(16):
        t0 = 128 * t
        qrhs = qTg[:, :, t0:t0 + 128]
        a_ = acc.next()
        ps = scb.next()
        S.op("pe", lambda e: e.matmul(v4(ps)[0:127], lhsT=kcT[:, g, 0:127], rhs=qrhs, start=True, stop=True), reads=[kcT, qTg], writes=[ps])
        ec = et.next()
        S.op("act", lambda e: e.activation(out=ec[0:127, :], in_=ps[0:127, :], func=AF.Exp, scale=SC), reads=[ps], writes=[ec])
        S.op("pool", lambda e: e.affine_select(out=v4(ec)[0:127], in_=v4(ec)[0:127], pattern=[[0, 4], [1, 128]], compare_op=ALU.is_ge,
                                               fill=0.0, base=t0 - 31, channel_multiplier=-16), reads=[ec], writes=[ec])
        for r in range(4):
            S.op("pe", lambda e: e.matmul(A3[:, r, 0:161], lhsT=ec[0:127, r * 128:(r + 1) * 128], rhs=vca[0:127, g, 0:161],
                                          start=True, stop=True), reads=[ec, vca], writes=[A3])
        rd = finish_branch(A3, 0, t, a_, True)
        sel = t >= 8
        if sel:
            s_ = sc.next()
            S.op("dve", lambda e: e.tensor_tensor(out=impn[:, :, :], in0=A3[:, :, 129:161], in1=rd[:, :].unsqueeze(2).to_broadcast([128, 4, 32]),
                                                  op=ALU.mult), reads=[A3, rd], writes=[impn])
            S.op("dve", lambda e: e.tensor_reduce(out=s_[:, :], in_=impn[:, :, :].rearrange("p r s -> p s r"), axis=AX.X, op=ALU.add),
                 reads=[impn], writes=[s_])
            S.op("dve", lambda e: e.tensor_tensor(out=s_[:, :], in0=s_[:, :], in1=wsel[:, 31 - 2 * t:63 - 2 * t], op=ALU.add),
                 reads=[s_, wsel], writes=[s_])
            S.op("dve", lambda e: e.memset(s_[:, 0:1], 1e9), reads=[], writes=[s_])
            S.op("dve", lambda e: e.max(out=m8[:, 0:8], in_=s_[:, :]), reads=[s_], writes=[m8])
            S.op("dve", lambda e: e.match_replace(out=sc2[:, :], in_to_replace=m8[:, 0:8], in_values=s_[:, :], imm_value=-3e38),
                 reads=[s_, m8], writes=[sc2])
            S.op("dve", lambda e: e.max(out=m8[:, 8:16], in_=sc2[:, :]), reads=[sc2], writes=[m8])
            S.op("dve", lambda e: e.tensor_scalar(out=selm[:, :], in0=s_[:, :], scalar1=m8[:, 15:16], scalar2=None, op0=ALU.is_ge),
                 reads=[s_, m8], writes=[selm])
            S.op("pe", lambda e: e.transpose(out=pmisc_bf[0:32, 0:128], in_=selm[:, 0:32], identity=ident_b[:, :]),
                 reads=[selm, ident_b], writes=[pmisc_bf])
            sT = selT.next()
            S.op("act", lambda e: e.activation(out=sT[:, :], in_=pmisc_bf[0:32, 0:128], func=AF.Copy), reads=[pmisc_bf], writes=[sT])
        S.op("dve", lambda e: e.memset(B3[:, :, 0:129], 0.0), reads=[], writes=[B3])
        for kc in range(t + 1):
            ps = scb.next()
            S.op("pe", lambda e: e.matmul(v4(ps), lhsT=kselT[:, g, kc * 128:(kc + 1) * 128], rhs=qrhs, start=True, stop=True),
                 reads=[kselT, qTg], writes=[ps])
            e_ = et.next()
            S.op("act", lambda e: e.activation(out=e_[:, :], in_=ps[:, :], func=AF.Exp, scale=SC), reads=[ps], writes=[e_])
            if sel:
                S.op("pe", lambda e: e.matmul(pmk[:, 0:128], lhsT=eexp[0:32, kc, :], rhs=sT[0:32, :], start=True, stop=True),
                     reads=[eexp, sT], writes=[pmk])
                if kc == t:
                    mm = m2.next()
                    S.op("dve", lambda e: e.tensor_tensor(out=mm[:, :], in0=pmk[:, 0:128], in1=mle_b[:, :], op=ALU.mult),
                         reads=[pmk, mle_b], writes=[mm])
                    msrc = mm
                else:
                    msrc = pmk
                S.op("dve", lambda e: e.tensor_tensor(out=v4(e_), in0=v4(e_), in1=msrc[:, 0:128].unsqueeze(1).to_broadcast([128, 4, 128]),
                                                      op=ALU.mult), reads=[e_, msrc], writes=[e_])
            elif kc == t:
                S.op("pool", lambda e: e.tensor_tensor(out=v4(e_), in0=v4(e_), in1=mle_b[:, :].unsqueeze(1).to_broadcast([128, 4, 128]),
                                                       op=ALU.mult), reads=[e_, mle_b], writes=[e_])
            for r in range(4):
                S.op("pe", lambda e: e.matmul(B3[:, r, 0:129], lhsT=e_[:, r * 128:(r + 1) * 128], rhs=vsel[:, kc, g, 0:129],
                                              start=False, stop=(kc == t), skip_group_check=True), reads=[e_, vsel], writes=[B3])
        finish_branch(B3, 1, t, a_, False)
        S.op("dve", lambda e: e.memset(A3[:, :, 0:129], 0.0), reads=[], writes=[A3])
        k0 = max(0, t - 4)
        for kc in range(k0, t + 1):
            ps = scb.next()
            S.op("pe", lambda e: e.matmul(v4(ps), lhsT=kwinT[:, g, kc * 128:(kc + 1) * 128], rhs=qrhs, start=True, stop=True),
                 reads=[kwinT, qTg], writes=[ps])
            e_ = et.next()
            S.op("act", lambda e: e.activation(out=e_[:, :], in_=ps[:, :], func=AF.Exp, scale=SC), reads=[ps], writes=[e_])
            mk = mle_b if kc == t else (mgt_b if kc == t - 4 else None)
            if mk is not None:
                S.op("pool", lambda e: e.tensor_tensor(out=v4(e_), in0=v4(e_), in1=mk[:, :].unsqueeze(1).to_broadcast([128, 4, 128]),
                                                       op=ALU.mult), reads=[e_, mk], writes=[e_])
            for r in range(4):
                S.op("pe", lambda e: e.matmul(A3[:, r, 0:129], lhsT=e_[:, r * 128:(r + 1) * 128], rhs=vwin[:, kc, g, 0:129],
                                              start=False, stop=(kc == t), skip_group_check=True), reads=[e_, vwin], writes=[A3])
        finish_branch(A3, 2, t, a_, False)
        o_ = ob.next()
        S.op("dve", lambda e: e.tensor_tensor(out=o_[:, :], in0=a_[:, :, :].rearrange("p r d -> p (r d)"), in1=zg[:, t, :], op=ALU.mult),
             reads=[a_, zg], writes=[o_])
        for r in range(4):
            S.op("pe", lambda e: e.transpose(out=pmisc_bf[:, 128 + r * 128:256 + r * 128], in_=o_[:, r * 128:(r + 1) * 128], identity=ident_b[:, :]),
                 reads=[o_, ident_b], writes=[pmisc_bf])
        s_t = st.next()
        S.op("act", lambda e: e.activation(out=s_t[:, :, :], in_=pmisc_bf[:, 128:640].rearrange("p (r q) -> p r q", r=4), func=AF.Copy),
             reads=[pmisc_bf], writes=[s_t])
        S.dma("sp", N["oT_scr"][t, :, 4 * g:4 * g + 4, :], s_t[:, :, :], reads=[s_t], writes=N["oT_b"][t][4 * g:4 * g + 4])


def nsa_sample(S, es, nc, N):
    qTs, zs, gts, kselT, kwinT, vnew_src, mle_b, mgt_b, ident_b, ident_f, psb, wbuf, cst, SC, evac = (
        N[k] for k in ("qTs", "zs", "gts", "kselT", "kwinT", "vnew_src", "mle_b", "mgt_b", "ident_b", "ident_f", "psb", "wbuf", "cst", "SC", "evac"))
    ckv, cwin = N["cache_kv"], N["cache_win"]
    tl = lambda n, s, d=F32: S.tile(n, s, d, es)
    big = Rot(psb[0:4])
    small = Rot(psb[4:8])
    ptb = tl("ptb", [128, 256], I32)
    S.dma("sp", ptb[:, :], N["page_table"].rearrange("s p -> (s p)").partition_broadcast(128), writes=[ptb])
    pci = tl("pci", [128, 1], I32)
    S.op("pool", lambda e: e.iota(pci[:, :], pattern=[[0, 1]], base=0, channel_multiplier=2), writes=[pci])
    pcf = tl("pcf", [128, 1])
    S.op("dve", lambda e: e.tensor_copy(out=pcf[:, :], in_=pci[:, :]), reads=[pci], writes=[pcf])
    idxA = tl("idxA", [128, 256], I32)
    idxB = tl("idxB", [128, 256], I32)
    S.op("dve", lambda e: e.tensor_scalar(out=idxA[:, :], in0=ptb[:, :], scalar1=256.0, scalar2=pcf[:, 0:1], op0=ALU.mult, op1=ALU.add),
         reads=[ptb, pcf], writes=[idxA])
    S.op("dve", lambda e: e.tensor_scalar(out=idxB[:, :], in0=idxA[:, :], scalar1=1.0, scalar2=None, op0=ALU.add), reads=[idxA], writes=[idxB])
    bg, bz = Buf("gscr"), Buf("zscr")
    S.dma("sp", N["g_scr"][:, :], gts[:TS, 16, :], reads=[gts], writes=[bg])
    S.dma("sp", N["z_scr"][:, :], zs[:TS, :], reads=[zs], writes=[bz])
    gsm = tl("gsm", [16, NS, 2, 3])
    zr = tl("zr", [16, NS, 2, 128], BF16)
    gv = N["g_scr"].rearrange("(s t) (g r b) -> r t s g b", t=4, g=2, r=4)
    zv = N["z_scr"].rearrange("(s t) (g r d) -> r t s g d", t=4, g=2, r=4)
    for r in range(4):
        for g in range(2):
            S.dma("sp", gsm[4 * r:4 * r + 4, :, g, :], gv[r][:, :, g, :], reads=[bg], writes=[gsm])
            S.dma("sp", zr[4 * r:4 * r + 4, :, g, :], zv[r][:, :, g, :], reads=[bz], writes=[zr])
    vnew = tl("vnew", [4, NS, 2, 2, 130], BF16)
    for s in range(NS):
        S.dma("sp", vnew[0:4, s, :, :, :], vnew_src[4 * s:4 * s + 4, :, :, :], reads=[vnew_src], writes=[vnew])
    w1 = [tl(f"w1_{kv}", [128, 32, 256], BF16) for kv in range(2)]
    w2b = tl("w2b", [128, 2, 2, 128], BF16)
    peb = tl("peb", [128, 2, 32], BF16)
    b1t = tl("b1t", [128, 2, 2])
    hb = tl("hb", [128, 2, 2])
    for kv in range(2):
        S.dma("pool", w1[kv][:, :, :], N["cmp_w1"][kv, :, :, :], writes=[w1[kv]])
    S.dma("pool", w2b[:, :, :, :], N["cmp_w2"][:, :, :, :], writes=[w2b])
    S.dma("pool", peb[:, :, :], N["cmp_pe"][:, :, :], writes=[peb])
    S.dma("sp", b1t[:, :, :], N["cmp_b1"][:, :, :], writes=[b1t])
    for kv in range(2):
        pp = small.next()
        for half in range(2):
            for rp in range(32):
                S.op("pe", lambda e: e.matmul(pp[:, half:half + 1], lhsT=w1[kv][:, rp, half * 128:(half + 1) * 128],
                                              rhs=peb[:, kv, rp:rp + 1], start=(rp == 0), stop=(rp == 31)), reads=[w1[kv], peb], writes=[pp])
        S.op("dve", lambda e: e.tensor_tensor(out=hb[:, kv, :], in0=pp[:, 0:2], in1=b1t[:, kv, :], op=ALU.add), reads=[pp, b1t], writes=[hb])
    rsum = tl("rsum", [16, 4])
    S.dma("sp", rsum[:, :], cst["rsum"][:, :], writes=[rsum])
    xTk = tl("xTk", [128, 2, 2048], BF16)
    xTv = tl("xTv", [128, 2, 2048], BF16)
    gT = Rot([tl(f"sgT{i}", [128, 2, 128], BF16) for i in range(2)])
    kcTs = tl("kcTs", [128, 2, 128], BF16)
    vcas = tl("vcas", [128, 2, 162], BF16)
    S.op("pool", lambda e: e.memset(vcas[:, :, 128:129], 1.0), writes=[vcas])
    for g in range(2):
        S.dma("pool", vcas[0:127, g, 129:162], cst["c2ss"][:, :], writes=[vcas])
    ec = tl("sec", [128, 2, 16], BF16)
    den = Rot([tl(f"sden{i}", [16, 2]) for i in range(3)])
    rden = Rot([tl(f"srden{i}", [16, 2]) for i in range(3)])
    cf = Rot([tl(f"scf{i}", [16, 2]) for i in range(3)])
    acc = tl("sacc", [16, 2, 128])
    tmpo = tl("stmpo", [16, 2, 128])
    impn = tl("simpn", [16, 2, 33])
    scs = tl("sscs", [4, 2, 33])
    sc2 = tl("ssc2", [4, 33])
    m8 = tl("sm8", [4, 16])
    selm = tl("sselm", [4, 2, 33], BF16)
    selx = tl("sselx", [4, 2, 33, 64], BF16)
    maskT = tl("smaskT", [128, 2, 16, 4], BF16)
    vsa = tl("vsa", [128, 16, 2, 130], BF16)
    S.op("pool", lambda e: e.memset(vsa[:, :, :, 128:130], 1.0), writes=[vsa])
    esel = tl("esel", [128, 2, 272], BF16)
    wl = tl("wl", [128, 4, 512])
    kwT = tl("kwT", [128, 2, 512], BF16)
    vwa = tl("vwa", [128, 4, 2, 130], BF16)
    S.op("pool", lambda e: e.memset(vwa[:, :, :, 128:130], 1.0), writes=[vwa])
    ewin = tl("ewin", [128, 2, 80], BF16)
    ob = tl("sob", [16, 2, 128], BF16)
    oTs = tl("oTs", [128, 8, TS], BF16)

    def pbview(wb):
        return View(wb[:, :, :].bitcast(F32).rearrange("p c (a f) -> p (c a) f", a=1).rearrange("p (k two) f -> p k (two f)", two=2), wb.b)

    def gather(idx, s, hs):
        wb = wbuf.next()
        pv = pbview(wb)
        for pgl in range(8):
            col = s * 16 + hs * 8 + pgl
            S.dma("pool", None, None, reads=[idx], writes=[pv],
                  fn=lambda e: e.indirect_dma_start(out=pv[:, pgl, :], out_offset=None, in_=ckv[:, :],
                                                    in_offset=bass.IndirectOffsetOnAxis(ap=idx[:, col:col + 1], axis=0)))
        return pv

    def transpose4(src_fn, dst_ap, reads, dstt):
        pt = big.next()
        for k in range(4):
            S.op("pe", lambda e: e.transpose(out=pt[:, k * 128:(k + 1) * 128], in_=src_fn(k), identity=ident_f[:, :]),
                 reads=reads + [ident_f], writes=[pt])
        evac(dst_ap, pt[:, 0:512], [pt], [dstt])

    def finish_branch(P, br, s, first):
        d_, r_, c_ = den.next(), rden.next(), cf.next()
        S.op("dve", lambda e: e.tensor_scalar(out=d_[:, :], in0=P[0:16, :, 128], scalar1=1e-30, scalar2=None, op0=ALU.max), reads=[P], writes=[d_])
        S.op("dve", lambda e: e.reciprocal(out=r_[:, :], in_=d_[:, :]), reads=[d_], writes=[r_])
        S.op("dve", lambda e: e.tensor_tensor(out=c_[:, :], in0=r_[:, :], in1=gsm[:, s, :, br], op=ALU.mult), reads=[r_, gsm], writes=[c_])
        cb = c_[:, :].unsqueeze(2).to_broadcast([16, 2, 128])
        if first:
            S.op("dve", lambda e: e.tensor_tensor(out=acc[:, :, :], in0=P[0:16, :, 0:128], in1=cb, op=ALU.mult), reads=[P, c_], writes=[acc])
        else:
            S.op("dve", lambda e: e.tensor_tensor(out=tmpo[:, :, :], in0=P[0:16, :, 0:128], in1=cb, op=ALU.mult), reads=[P, c_], writes=[tmpo])
            S.op("dve", lambda e: e.tensor_tensor(out=acc[:, :, :], in0=acc[:, :, :], in1=tmpo[:, :, :], op=ALU.add), reads=[acc, tmpo], writes=[acc])
        return r_

    def v3(bank):
        return View(bank[:, :].rearrange("p (g c) -> p g c", g=2), bank.b)

    for s in range(NS):
        q16 = [qTs[:, 4 * g:4 * g + 4, 4 * s:4 * s + 4] for g in range(2)]
        for hs in range(2):
            pv = gather(idxA, s, hs)
            for slot in range(2):
                dstt = xTk if slot == 0 else xTv
                for g in range(2):
                    for q4 in range(2):
                        c0 = (slot * 2 + g) * 128
                        transpose4(lambda k: pv[:, 4 * q4 + k, c0:c0 + 128],
                                   dstt[:, g, (8 * hs + 4 * q4) * 128:(8 * hs + 4 * q4 + 4) * 128], [pv], dstt)
        for kv in range(2):
            xT = xTk if kv == 0 else xTv
            for g in range(2):
                gt = gT.next()
                for half in range(2):
                    ph = big.next()
                    for rp in range(32):
                        st0 = rp
                        S.op("pe", lambda e: e.matmul(ph[:, 0:127], lhsT=w1[kv][:, rp, half * 128:(half + 1) * 128],
                                                      rhs=xT[:, g, st0:st0 + 16 * 126 + 1:16], start=(rp == 0), stop=(rp == 31)),
                             reads=[w1[kv], xT], writes=[ph])
                    S.op("act", lambda e: e.activation(out=gt[:, half, 0:127], in_=ph[:, 0:127], func=AF.Gelu_apprx_tanh,
                                                       bias=hb[:, kv, half:half + 1]), reads=[ph, hb], writes=[gt])
                po = big.next()
                if kv == 0:
                    for half in range(2):
                        S.op("pe", lambda e: e.matmul(po[:, 0:127], lhsT=w2b[:, 0, half, :], rhs=gt[:, half, 0:127],
                                                      start=(half == 0), stop=(half == 1)), reads=[w2b, gt], writes=[po])
                    evac(kcTs[:, g, 0:127], po[:, 0:127], [po], [kcTs])
                else:
                    for half in range(2):
                        S.op("pe", lambda e: e.matmul(po[0:127, 0:128], lhsT=gt[:, half, 0:127], rhs=w2b[:, 1, half, :],
                                                      start=(half == 0), stop=(half == 1)), reads=[w2b, gt], writes=[po])
                    evac(vcas[0:127, g, 0:128], po[0:127, 0:128], [po], [vcas])
        ps = small.next()
        for g in range(2):
            S.op("pe", lambda e: e.matmul(ps[0:127, g * 16:(g + 1) * 16].rearrange("p (r q) -> p r q", r=4), lhsT=kcTs[:, g, 0:127], rhs=q16[g],
                                          start=True, stop=True), reads=[kcTs, qTs], writes=[ps])
        S.op("act", lambda e: e.activation(out=ec[0:127, :, :], in_=ps[0:127, 0:32].rearrange("p (g c) -> p g c", g=2), func=AF.Exp, scale=SC),
             reads=[ps], writes=[ec])
        pc = v3(small.next())
        for g in range(2):
            S.op("pe", lambda e: e.matmul(pc[0:16, g, 0:162], lhsT=ec[0:127, g, :], rhs=vcas[0:127, g, 0:162], start=True, stop=True),
                 reads=[ec, vcas], writes=[pc])
        rd = finish_branch(pc, 0, s, True)
        S.op("dve", lambda e: e.tensor_tensor(out=impn[:, :, :], in0=pc[0:16, :, 129:162], in1=rd[:, :].unsqueeze(2).to_broadcast([16, 2, 33]),
                                              op=ALU.mult), reads=[pc, rd], writes=[impn])
        pi = small.next()
        S.op("pe", lambda e: e.matmul(pi[0:4, 0:66], lhsT=rsum[:, :], rhs=impn[:, :, :].rearrange("p g s -> p (g s)"), start=True, stop=True),
             reads=[rsum, impn], writes=[pi])
        S.op("dve", lambda e: e.tensor_copy(out=scs[:, :, :], in_=pi[0:4, 0:66].rearrange("p (g s) -> p g s", g=2)), reads=[pi], writes=[scs])
        S.op("dve", lambda e: e.memset(scs[:, :, 0:1], 1e9), reads=[], writes=[scs])
        S.op("dve", lambda e: e.memset(scs[:, :, 31:33], 1e9), reads=[], writes=[scs])
        for g in range(2):
            S.op("dve", lambda e: e.max(out=m8[:, 0:8], in_=scs[:, g, :]), reads=[scs], writes=[m8])
            S.op("dve", lambda e: e.match_replace(out=sc2[:, :], in_to_replace=m8[:, 0:8], in_values=scs[:, g, :], imm_value=-3e38),
                 reads=[scs, m8], writes=[sc2])
            S.op("dve", lambda e: e.max(out=m8[:, 8:16], in_=sc2[:, :]), reads=[sc2], writes=[m8])
            S.op("dve", lambda e: e.tensor_scalar(out=selm[:, g, :], in0=scs[:, g, :], scalar1=m8[:, 15:16], scalar2=None, op0=ALU.is_ge),
                 reads=[scs, m8], writes=[selm])
        S.op("dve", lambda e: e.tensor_copy(out=selx[:, :, :, :].rearrange("p g s k -> p (g s) k"),
                                            in_=selm[:, :, :].rearrange("p g s -> p (g s)").unsqueeze(2).to_broadcast([4, 66, 64])),
             reads=[selm], writes=[selx])
        pm = small.next()
        for g in range(2):
            for kc in range(16):
                S.op("pe", lambda e: e.matmul(pm[:, (g * 16 + kc) * 4:(g * 16 + kc) * 4 + 4],
                                              lhsT=selx[0:4, g, 2 * kc:2 * kc + 2, :].rearrange("p a b -> p (a b)"), rhs=ident_b[0:4, 0:4],
                                              start=True, stop=True), reads=[selx, ident_b], writes=[pm])
        S.op("act", lambda e: e.activation(out=maskT[:, :, :, :].rearrange("p g k t -> p (g k t)"), in_=pm[:, 0:128], func=AF.Copy),
             reads=[pm], writes=[maskT])
        for hs in range(2):
            pv = gather(idxB, s, hs)
            for g in range(2):
                for q4 in range(2):
                    transpose4(lambda k: pv[:, 4 * q4 + k, g * 128:(g + 1) * 128],
                               xTk[:, g, (8 * hs + 4 * q4) * 128:(8 * hs + 4 * q4 + 4) * 128], [pv], xTk)
            S.op("act", lambda e: e.activation(out=vsa[:, 8 * hs:8 * hs + 8, :, 0:128],
                                               in_=pv[:, :, 256:512].rearrange("p k (g d) -> p k g d", g=2), func=AF.Copy), reads=[pv], writes=[vsa])
        pss = v3(small.next())
        psn = v3(small.next())
        for g in range(2):
            for kc in range(16):
                S.op("pe", lambda e: e.matmul(pss[:, g, kc * 16:(kc + 1) * 16].rearrange("p (r q) -> p r q", r=4),
                                              lhsT=xTk[:, g, kc * 128:(kc + 1) * 128], rhs=q16[g], start=True, stop=True),
                     reads=[xTk, qTs], writes=[pss])
            S.op("pe", lambda e: e.matmul(psn[0:4, g, 0:16].rearrange("p (r q) -> p r q", r=4),
                                          lhsT=kselT[:, g, 4 * s:4 * s + 4], rhs=q16[g], start=True, stop=True),
                 reads=[kselT, qTs], writes=[psn])
        S.op("act", lambda e: e.activation(out=esel[:, :, 0:256], in_=pss[:, :, 0:256], func=AF.Exp, scale=SC), reads=[pss], writes=[esel])
        S.op("act", lambda e: e.activation(out=esel[0:4, :, 256:272], in_=psn[0:4, :, 0:16], func=AF.Exp, scale=SC), reads=[psn], writes=[esel])
        for g in range(2):
            S.op("dve", lambda e: e.tensor_tensor(out=esel[:, g, 0:256].rearrange("p (k r q) -> p k r q", k=16, r=4),
                                                  in0=esel[:, g, 0:256].rearrange("p (k r q) -> p k r q", k=16, r=4),
                                                  in1=maskT[:, g, :, :].unsqueeze(2).to_broadcast([128, 16, 4, 4]), op=ALU.mult),
                 reads=[esel, maskT], writes=[esel])
        S.op("dve", lambda e: e.tensor_tensor(out=esel[0:4, :, 256:272].rearrange("p g (r q) -> p g r q", r=4),
                                              in0=esel[0:4, :, 256:272].rearrange("p g (r q) -> p g r q", r=4),
                                              in1=mle_b[0:4, 0:4].unsqueeze(1).unsqueeze(1).to_broadcast([4, 2, 4, 4]), op=ALU.mult),
             reads=[esel, mle_b], writes=[esel])
        po_ = v3(small.next())
        for g in range(2):
            for kc in range(16):
                S.op("pe", lambda e: e.matmul(po_[0:16, g, 0:129], lhsT=esel[:, g, kc * 16:(kc + 1) * 16], rhs=vsa[:, kc, g, 0:129],
                                              start=(kc == 0), stop=False), reads=[esel, vsa], writes=[po_])
            S.op("pe", lambda e: e.matmul(po_[0:16, g, 0:129], lhsT=esel[0:4, g, 256:272], rhs=vnew[0:4, s, 0, g, 0:129],
                                          start=False, stop=True), reads=[esel, vnew], writes=[po_])
        finish_branch(po_, 1, s, False)
        S.dma("sp", wl[:, :, :], cwin[s, :, :].rearrange("(c p) f -> p c f", p=128), writes=[wl])
        for g in range(2):
            transpose4(lambda k: wl[:, k, g * 128:(g + 1) * 128], kwT[:, g, :], [wl], kwT)
        S.op("act", lambda e: e.activation(out=vwa[:, :, :, 0:128], in_=wl[:, :, 256:512].rearrange("p k (g d) -> p k g d", g=2), func=AF.Copy),
             reads=[wl], writes=[vwa])
        psw = v3(small.next())
        pwn = v3(small.next())
        for g in range(2):
            for kc in range(4):
                S.op("pe", lambda e: e.matmul(psw[:, g, kc * 16:(kc + 1) * 16].rearrange("p (r q) -> p r q", r=4),
                                              lhsT=kwT[:, g, kc * 128:(kc + 1) * 128], rhs=q16[g], start=True, stop=True),
                     reads=[kwT, qTs], writes=[psw])
            S.op("pe", lambda e: e.matmul(pwn[0:4, g, 0:16].rearrange("p (r q) -> p r q", r=4),
                                          lhsT=kwinT[:, g, 4 * s:4 * s + 4], rhs=q16[g], start=True, stop=True),
                 reads=[kwinT, qTs], writes=[pwn])
        S.op("act", lambda e: e.activation(out=ewin[:, :, 0:64], in_=psw[:, :, 0:64], func=AF.Exp, scale=SC), reads=[psw], writes=[ewin])
        S.op("act", lambda e: e.activation(out=ewin[0:4, :, 64:80], in_=pwn[0:4, :, 0:16], func=AF.Exp, scale=SC), reads=[pwn], writes=[ewin])
        S.op("dve", lambda e: e.tensor_tensor(out=ewin[:, :, 0:16].rearrange("p g (r q) -> p g r q", r=4),
                                              in0=ewin[:, :, 0:16].rearrange("p g (r q) -> p g r q", r=4),
                                              in1=mgt_b[:, 0:4].unsqueeze(1).unsqueeze(1).to_broadcast([128, 2, 4, 4]), op=ALU.mult),
             reads=[ewin, mgt_b], writes=[ewin])
        S.op("dve", lambda e: e.tensor_tensor(out=ewin[0:4, :, 64:80].rearrange("p g (r q) -> p g r q", r=4),
                                              in0=ewin[0:4, :, 64:80].rearrange("p g (r q) -> p g r q", r=4),
                                              in1=mle_b[0:4, 0:4].unsqueeze(1).unsqueeze(1).to_broadcast([4, 2, 4, 4]), op=ALU.mult),
             reads=[ewin, mle_b], writes=[ewin])
        pw_ = v3(small.next())
        for g in range(2):
            for kc in range(4):
                S.op("pe", lambda e: e.matmul(pw_[0:16, g, 0:129], lhsT=ewin[:, g, kc * 16:(kc + 1) * 16], rhs=vwa[:, kc, g, 0:129],
                                              start=(kc == 0), stop=False), reads=[ewin, vwa], writes=[pw_])
            S.op("pe", lambda e: e.matmul(pw_[0:16, g, 0:129], lhsT=ewin[0:4, g, 64:80], rhs=vnew[0:4, s, 1, g, 0:129],
                                          start=False, stop=True), reads=[ewin, vnew], writes=[pw_])
        finish_branch(pw_, 2, s, False)
        S.op("dve", lambda e: e.tensor_tensor(out=ob[:, :, :], in0=acc[:, :, :], in1=zr[:, s, :, :], op=ALU.mult), reads=[acc, zr], writes=[ob])
        pt = small.next()
        ptb_ = View(pt[:, :].bitcast(BF16), pt.b)
        for g in range(2):
            S.op("pe", lambda e: e.transpose(out=ptb_[:, g * 16:(g + 1) * 16], in_=ob[0:16, g, :], identity=ident_b[0:16, 0:16]),
                 reads=[ob, ident_b], writes=[ptb_])
        S.op("act", lambda e: e.activation(out=oTs[:, :, 4 * s:4 * s + 4], in_=ptb_[:, 0:32].rearrange("p (f q) -> p f q", f=8), func=AF.Copy),
             reads=[ptb_], writes=[oTs])
    S.dma("sp", N["oT_scr"][16, :, 0:8, 0:TS], oTs[:, :, :], reads=[oTs], writes=N["oT_b"][16][0:8])


def rec_setup(S, e2, R, V):
    G = dict(R)
    tl = lambda n, s, d=F32: S.tile(n, s, d, e2)
    G["cs"] = Rot([tl(f"cs{i}", [128, 128]) for i in range(2)])
    G["E1"] = Rot([tl(f"E1{i}", [128, 128]) for i in range(2)])
    G["E2"] = Rot([tl(f"E2{i}", [128, 128]) for i in range(2)])
    G["sm"] = Rot([tl(f"sm{i}", [128, 16]) for i in range(3)])
    G["E3"] = Rot([tl(f"E3{i}", [128, 128]) for i in range(2)])
    G["E4"] = Rot([tl(f"E4{i}", [128, 128]) for i in range(2)])
    G["qx"] = Rot([tl(f"qx{i}", [128, 64], BF16) for i in range(2)])
    G["qS"] = Rot([tl(f"qS{i}", [128, 128], BF16) for i in range(2)])
    G["kS"] = Rot([tl(f"kS{i}", [128, 128], BF16) for i in range(2)])
    G["KA"] = Rot([tl(f"KA{i}", [128, 128], BF16) for i in range(2)])
    G["KB"] = Rot([tl(f"KB{i}", [128, 128], BF16) for i in range(2)])
    for kk in ("KA", "KB"):
        for tt in G[kk].items:
            S.op("pool", lambda e: e.memset(tt[:, :], 0.0), writes=[tt])
    G["qp"] = Rot([tl(f"qp{i}", [128, 128], BF16) for i in range(2)])
    G["kp"] = Rot([tl(f"kp{i}", [128, 128], BF16) for i in range(2)])
    G["ktok"] = Rot([tl(f"ktok{i}", [128, 128], BF16) for i in range(2)])
    G["attm"] = Rot([tl(f"attm{i}", [128, 128], BF16) for i in range(2)])
    G["Sst"] = tl("Sst", [128, V])
    G["Sbf"] = Rot([tl(f"Sbf{i}", [128, V], BF16) for i in range(2)])
    G["ones"] = tl("ones", [128, 128])
    S.op("pool", lambda e: e.memset(G["ones"][:, :], 1.0), writes=[G["ones"]])
    G["bdm"] = tl("bdm", [64, 64], BF16)
    S.dma("pool", G["bdm"][:, :], R["cst"]["bd"][:, :], writes=[G["bdm"]])
    G["colm"] = tl("colm", [128, 16, 64], BF16)
    S.dma("pool", G["colm"][:, :, :], R["cst"]["colmask"][:, :, :], writes=[G["colm"]])
    G["rowm"] = tl("rowm", [64, 16], BF16)
    S.dma("pool", G["rowm"][:, :], R["cst"]["rowmask"][:, :], writes=[G["rowm"]])
    G["s0"] = Rot([tl(f"s0{i}", [128, V]) for i in range(3)])
    G["s0b"] = Rot([tl(f"s0b{i}", [128, V], BF16) for i in range(3)])
    G["sn"] = Rot([tl(f"sn{i}", [128, V]) for i in range(3)])
    G["qm"] = tl("qm", [128, 16, 64], BF16)
    G["km"] = tl("km", [64, 16, 128], BF16)
    G["oh"] = Rot([tl(f"oh{i}", [128, NT, V], BF16) for i in range(1 if V == 256 else 2)])
    G["pss"] = tl("pss", [128, NT])
    G["prs"] = tl("prs", [128, NT])
    G["pjunk"] = tl("pjunk", [128, V], BF16)
    G["ptmp"] = Rot([tl(f"ptmp{i}", [128, V]) for i in range(2)])
    G["pob"] = Rot([tl(f"pob{i}", [128, V], BF16) for i in range(2)])
    G["post"] = Rot([tl(f"post{i}", [128, V // 128, 128], BF16) for i in range(3)])
    return G


def rec_head(S, G, hs, V, COEF, QS, gate, state_in, out_p, out_s):
    qT, kT, v = hs["q"], hs["k"], hs["v"]
    mle_b, ident_b, psb, obuf = (G[k] for k in ("mle_b", "ident_b", "psb", "obuf"))
    p_att, p_o, p_ds = psb[4], psb[5], Rot([psb[0], psb[1]])
    Sst = G["Sst"]
    oh = G["oh"].next()
    for t in range(16):
        tok = slice(t * 128, (t + 1) * 128)
        c_, e1, e2_, e3, e4, s_ = (G[k].next() for k in ("cs", "E1", "E2", "E3", "E4", "sm"))
        q_, qx, qS, KA, KB, kS, kt_, at_ = (G[k].next() for k in ("qp", "qx", "qS", "KA", "KB", "kS", "ktok", "attm"))
        gap, gtt = gate(tok, 128)
        S.op("dve", lambda e: e.tensor_tensor_scan(out=c_[:, :], data0=G["ones"][:, :], data1=gap, initial=0.0,
                                                   op0=ALU.mult, op1=ALU.add), reads=[G["ones"], gtt], writes=[c_])
        S.op("dve", lambda e: e.tensor_scalar(out=s_[:, 4:8], in0=c_[:, 31:128:32], scalar1=-COEF, scalar2=None, op0=ALU.mult), reads=[c_], writes=[s_])
        S.op("dve", lambda e: e.tensor_scalar(out=s_[:, 8:12], in0=c_[:, 31:128:32], scalar1=COEF, scalar2=None, op0=ALU.mult), reads=[c_], writes=[s_])
        S.op("act", lambda e: e.activation(out=s_[:, 14:15], in_=c_[:, 95:96], func=AF.Exp, scale=COEF, bias=s_[:, 4:5]), reads=[c_, s_], writes=[s_])
        S.op("act", lambda e: e.activation(out=s_[:, 15:16], in_=c_[:, 127:128], func=AF.Exp, scale=COEF), reads=[c_], writes=[s_])
        lo, hi = slice(0, 64), slice(64, 128)
        S.op("act", lambda e: e.activation(out=e1[:, lo], in_=c_[:, lo], func=AF.Exp, scale=COEF, bias=s_[:, 4:5]), reads=[c_, s_], writes=[e1])
        S.op("act", lambda e: e.activation(out=e1[:, hi], in_=c_[:, hi], func=AF.Exp, scale=COEF, bias=s_[:, 6:7]), reads=[c_, s_], writes=[e1])
        S.op("act", lambda e: e.activation(out=e2_[:, lo], in_=c_[:, lo], func=AF.Exp, scale=-COEF, bias=s_[:, 8:9]), reads=[c_, s_], writes=[e2_])
        S.op("act", lambda e: e.activation(out=e2_[:, hi], in_=c_[:, hi], func=AF.Exp, scale=-COEF, bias=s_[:, 10:11]), reads=[c_, s_], writes=[e2_])
        S.op("act", lambda e: e.activation(out=e3[:, :], in_=c_[:, :], func=AF.Exp, scale=-COEF, bias=s_[:, 11:12]), reads=[c_, s_], writes=[e3])
        S.op("act", lambda e: e.activation(out=e4[:, :], in_=c_[:, :], func=AF.Exp, scale=COEF), reads=[c_], writes=[e4])
        S.op("dve", lambda e: e.scalar_tensor_tensor(out=q_[:, :], in0=qT[:, tok], scalar=QS, in1=e1[:, :],
                                                     op0=ALU.mult, op1=ALU.mult), reads=[qT, e1], writes=[q_])
        S.op("dve", lambda e: e.tensor_scalar(out=qx[:, :], in0=q_[:, hi], scalar1=s_[:, 14:15], scalar2=None, op0=ALU.mult),
             reads=[q_, s_], writes=[qx])
        S.op("dve", lambda e: e.tensor_tensor(out=KA[:, lo], in0=kT[:, t * 128:t * 128 + 64], in1=e2_[:, lo], op=ALU.mult),
             reads=[kT, e2_], writes=[KA])
        S.op("dve", lambda e: e.tensor_tensor(out=KB[:, hi], in0=kT[:, t * 128 + 64:(t + 1) * 128], in1=e2_[:, hi], op=ALU.mult),
             reads=[kT, e2_], writes=[KB])
        S.op("dve", lambda e: e.tensor_tensor(out=kS[:, :], in0=kT[:, tok], in1=e3[:, :], op=ALU.mult), reads=[kT, e3], writes=[kS])
        S.op("pe", lambda e: e.matmul(p_att[:, 0:64], lhsT=KA[:, :], rhs=q_[:, lo], start=True, stop=True),
             reads=[KA, q_], writes=[p_att])
        S.op("pe", lambda e: e.matmul(p_att[:, 64:128], lhsT=KA[:, :], rhs=qx[:, :], start=True, stop=False),
             reads=[KA, qx], writes=[p_att])
        S.op("pe", lambda e: e.matmul(p_att[:, 64:128], lhsT=KB[:, :], rhs=q_[:, hi], start=False, stop=True),
             reads=[KB, q_], writes=[p_att])
        S.op("dve", lambda e: e.tensor_tensor(out=at_[:, :], in0=p_att[:, 0:128], in1=mle_b[:, :], op=ALU.mult),
             reads=[p_att, mle_b], writes=[at_])
        pf = G["psbf"].next()
        S.op("pe", lambda e: e.transpose(out=pf[:, 0:128], in_=kS[:, :], identity=ident_b[:, :]),
             reads=[kS, ident_b], writes=[pf])
        S.op("act", lambda e: e.activation(out=kt_[:, :], in_=pf[:, 0:128], func=AF.Copy), reads=[pf], writes=[kt_])
        vv = v[:, t, :]
        if t > 0:
            sb_ = G["Sbf"].next()
            S.op("pool", lambda e: e.tensor_copy(out=sb_[:, :], in_=Sst[:, :]), reads=[Sst], writes=[sb_])
            S.op("dve", lambda e: e.scalar_tensor_tensor(out=qS[:, :], in0=qT[:, tok], scalar=QS, in1=e4[:, :],
                                                         op0=ALU.mult, op1=ALU.mult), reads=[qT, e4], writes=[qS])
        S.op("pe", lambda e: e.matmul(p_o[:, 0:V], lhsT=at_[:, :], rhs=vv, start=True, stop=(t == 0)),
             reads=[at_, v], writes=[p_o])
        if t > 0:
            S.op("pe", lambda e: e.matmul(p_o[:, 0:V], lhsT=qS[:, :], rhs=sb_[:, :], start=False, stop=True),
                 reads=[qS, sb_], writes=[p_o])
        S.op("act", lambda e: e.activation(out=oh[:, t, :], in_=p_o[:, 0:V], func=AF.Copy), reads=[p_o], writes=[oh])
        pd = p_ds.next()
        S.op("pe", lambda e: e.matmul(pd[:, 0:V], lhsT=kt_[:, :], rhs=vv, start=True, stop=True), reads=[kt_, v], writes=[pd])
        if t == 0:
            S.op("dve", lambda e: e.tensor_copy(out=Sst[:, :], in_=pd[:, 0:V]), reads=[pd], writes=[Sst])
        else:
            S.op("dve", lambda e: e.scalar_tensor_tensor(out=Sst[:, :], in0=Sst[:, :], scalar=s_[:, 15:16], in1=pd[:, 0:V],
                                                         op0=ALU.mult, op1=ALU.add), reads=[pd, s_, Sst], writes=[Sst])
    S.dma("sp", out_p, Sst[:, :], reads=[Sst], writes=[obuf("st_p")])

    tok = slice(T, TA)
    c_, e1, e2_ = G["cs"].next(), G["E1"].next(), G["E2"].next()
    q_, k_, kt_, at_ = G["qp"].next(), G["kp"].next(), G["ktok"].next(), G["attm"].next()
    qm, km, bdm, colm, rowm = G["qm"], G["km"], G["bdm"], G["colm"], G["rowm"]
    gap, gtt = gate(tok, TS)
    l3 = gap.rearrange("p (s t) -> p s t", t=4)
    c3 = c_[:, 0:TS].rearrange("p (s t) -> p s t", t=4)
    S.op("dve", lambda e: e.tensor_copy(out=c3[:, :, 0], in_=l3[:, :, 0]), reads=[gtt], writes=[c_])
    for i in range(1, 4):
        S.op("dve", lambda e: e.tensor_tensor(out=c3[:, :, i], in0=c3[:, :, i - 1], in1=l3[:, :, i], op=ALU.add),
             reads=[gtt, c_], writes=[c_])
    S.op("act", lambda e: e.activation(out=e1[:, 0:TS], in_=c_[:, 0:TS], func=AF.Exp, scale=COEF), reads=[c_], writes=[e1])
    S.op("act", lambda e: e.activation(out=e2_[:, 0:TS], in_=c_[:, 0:TS], func=AF.Exp, scale=-COEF), reads=[c_], writes=[e2_])
    S.op("dve", lambda e: e.scalar_tensor_tensor(out=q_[:, 0:TS], in0=qT[:, tok], scalar=QS, in1=e1[:, 0:TS],
                                                 op0=ALU.mult, op1=ALU.mult), reads=[qT, e1], writes=[q_])
    S.op("dve", lambda e: e.tensor_tensor(out=k_[:, 0:TS], in0=kT[:, tok], in1=e2_[:, 0:TS], op=ALU.mult),
         reads=[kT, e2_], writes=[k_])
    S.op("pe", lambda e: e.matmul(p_att[0:TS, 0:TS], lhsT=k_[:, 0:TS], rhs=q_[:, 0:TS], start=True, stop=True),
         reads=[k_, q_], writes=[p_att])
    S.op("dve", lambda e: e.tensor_tensor(out=at_[0:TS, 0:TS], in0=p_att[0:TS, 0:TS], in1=bdm[:, :], op=ALU.mult),
         reads=[p_att, bdm], writes=[at_])
    pf = G["psbf"].next()
    S.op("pe", lambda e: e.transpose(out=pf[0:TS, 0:128], in_=k_[:, 0:TS], identity=ident_b[:, :]),
         reads=[k_, ident_b], writes=[pf])
    S.op("act", lambda e: e.activation(out=kt_[0:TS, :], in_=pf[0:TS, 0:128], func=AF.Copy), reads=[pf], writes=[kt_])
    S.op("dve", lambda e: e.tensor_tensor(out=qm[:, :, :], in0=q_[:, 0:TS].unsqueeze(1).to_broadcast([128, 16, TS]),
                                          in1=colm[:, :, :], op=ALU.mult), reads=[q_, colm], writes=[qm])
    S.op("dve", lambda e: e.tensor_tensor(out=km[:, :, :], in0=kt_[0:TS, :].unsqueeze(1).to_broadcast([TS, 16, 128]),
                                          in1=rowm[:, :].unsqueeze(2).to_broadcast([TS, 16, 128]), op=ALU.mult),
         reads=[kt_, rowm], writes=[km])
    vv = v[0:TS, 16, :]
    S.op("pe", lambda e: e.matmul(p_o[0:TS, 0:V], lhsT=at_[0:TS, 0:TS], rhs=vv, start=True, stop=False),
         reads=[at_, v], writes=[p_o])
    for s in range(NS):
        a0, a0b, an = G["s0"].next(), G["s0b"].next(), G["sn"].next()
        S.dma("sp", a0[:, :], state_in(s), writes=[a0])
        S.op("pool", lambda e: e.tensor_copy(out=a0b[:, :], in_=a0[:, :]), reads=[a0], writes=[a0b])
        S.op("pe", lambda e: e.matmul(p_o[0:TS, 0:V], lhsT=qm[:, s, :], rhs=a0b[:, :], start=False, stop=(s == NS - 1)),
             reads=[qm, a0b], writes=[p_o])
        pd = p_ds.next()
        S.op("pe", lambda e: e.matmul(pd[:, 0:V], lhsT=km[:, s, :], rhs=vv, start=True, stop=True), reads=[km, v], writes=[pd])
        etot = e1[:, 4 * s + 3:4 * s + 4]
        S.op("act", lambda e: e.activation(out=a0[:, :], in_=a0[:, :], func=AF.Copy, scale=etot), reads=[a0, e1], writes=[a0])
        S.op("dve", lambda e: e.scalar_tensor_tensor(out=an[:, :], in0=pd[:, 0:V], scalar=etot, in1=a0[:, :],
                                                     op0=ALU.mult, op1=ALU.add), reads=[pd, e1, a0], writes=[an])
        S.dma("pool", out_s(s), an[:, :], reads=[an], writes=[obuf("st_s")])
    S.op("act", lambda e: e.activation(out=oh[0:TS, 16, :], in_=p_o[0:TS, 0:V], func=AF.Copy), reads=[p_o], writes=[oh])
    return oh


def post_head(S, G, oh, z, gnb, V, fc0, dbgh=False):
    pss, prs, pjunk = G["pss"], G["prs"], G["pjunk"]
    ident_b = G["ident_b"]
    nf = V // 128
    for t in range(NT):
        rows = 128 if t < 16 else TS
        S.op("act", lambda e: e.activation(out=pjunk[:rows, :], in_=oh[:rows, t, :], func=AF.Square, accum_out=pss[:rows, t:t + 1]),
             reads=[oh], writes=[pjunk, pss])
    S.op("act", lambda e: e.activation(out=prs[:TS, :], in_=pss[:TS, :], func=AF.Sqrt, scale=1.0 / V, bias=EPS), reads=[pss], writes=[prs])
    S.op("act", lambda e: e.activation(out=prs[TS:, 0:16], in_=pss[TS:, 0:16], func=AF.Sqrt, scale=1.0 / V, bias=EPS), reads=[pss], writes=[prs])
    S.op("dve", lambda e: e.reciprocal(out=prs[:TS, :], in_=prs[:TS, :]), reads=[prs], writes=[prs])
    S.op("dve", lambda e: e.reciprocal(out=prs[TS:, 0:16], in_=prs[TS:, 0:16]), reads=[prs], writes=[prs])
    for t in range(NT):
        rows = 128 if t < 16 else TS
        tmp, ob, st = G["ptmp"].next(), G["pob"].next(), G["post"].next()
        S.op("dve", lambda e: e.scalar_tensor_tensor(out=tmp[:rows, :], in0=oh[:rows, t, :], scalar=prs[:rows, t:t + 1], in1=gnb[:rows, :],
                                                     op0=ALU.mult, op1=ALU.mult), reads=[oh, prs, gnb], writes=[tmp])
        S.op("dve", lambda e: e.tensor_tensor(out=ob[:rows, :], in0=tmp[:rows, :], in1=z[:rows, t, :], op=ALU.mult),
             reads=[tmp, z], writes=[ob])
        pf = G["psbf"].next()
        for k in range(nf):
            S.op("pe", lambda e: e.transpose(out=pf[:, k * 128:k * 128 + rows], in_=ob[:rows, k * 128:(k + 1) * 128],
                                             identity=ident_b[:rows, :rows]), reads=[ob, ident_b], writes=[pf])
        S.op("act", lambda e: e.activation(out=st[:, :, 0:rows], in_=pf[:, 0:nf * 128].rearrange("p (k t) -> p k t", k=nf)[:, :, 0:rows],
                                           func=AF.Copy), reads=[pf], writes=[st])
        S.dma("sp", G["oT_scr"][t, :, fc0:fc0 + nf, 0:rows], st[:, :, 0:rows], reads=[st], writes=G["oT_b"][t][fc0:fc0 + nf])
        if dbgh and t == 0:
            G["dbg"]("pss", pss, pss[:, :], [128, NT])
            G["dbg"]("prs", prs, prs[:, :], [128, NT])
            G["dbg"]("tmp", tmp, tmp[:, :], [128, V])
            G["dbg"]("ob", ob, ob[:, :], [128, V], BF16)
            G["dbg"]("st", st, st[:, :, :], [128, nf, 128], BF16)


_PROG = {}


def _arr(w, ncols):
    return np.ascontiguousarray(w.reshape(16, 128, ncols).transpose(1, 0, 2))


def kernel(x_prompt, x_sample, cache_kv, cache_win, state_gla, state_hgrn, page_table,
           a_norm, a_w_in, a_gla_w2, a_gla_b, a_gla_gn, a_cmp_pe, a_cmp_w1, a_cmp_b1, a_cmp_w2, a_w_out,
           c_norm, c_w_in, c_lb_logits, c_gn, c_w_out, final_norm):
    f32 = np.float32
    asc = lambda a: np.ascontiguousarray(np.asarray(a, dtype=f32))
    x_prompt, x_sample = asc(x_prompt), asc(x_sample)
    if "nc" not in _PROG:
        _PROG["nc"] = build_program()
    nc = _PROG["nc"]
    consts = make_consts()
    w0 = np.asarray(a_w_in, f32)[0]
    wA = _arr(w0, 6696)
    cols = []
    for h in range(4):
        cols += [np.arange(3608 + h * 128, 3608 + (h + 1) * 128), np.arange(4120 + h * 128, 4120 + (h + 1) * 128),
                 np.arange(4632 + h * 256, 4632 + (h + 1) * 256), np.arange(5672 + h * 256, 5672 + (h + 1) * 256)]
    cols.append(np.arange(5656, 5672))
    wG = _arr(np.ascontiguousarray(w0[:, np.concatenate(cols)]), 3088)
    wc = np.asarray(c_w_in, f32)[0]
    cols = []
    for h in range(16):
        cols += [np.arange(k * 2048 + h * 128, k * 2048 + (h + 1) * 128) for k in range(4)]
    wC = _arr(np.ascontiguousarray(wc[:, np.concatenate(cols)]), 8192)
    wOA = _arr(np.asarray(a_w_out, f32)[0], 2048)
    wOC = _arr(np.asarray(c_w_out, f32)[0], 2048)
    a_norm_r = asc(np.asarray(a_norm, f32)[0].reshape(16, 128).T)
    c_norm_r = asc(np.asarray(c_norm, f32)[0].reshape(16, 128).T)
    lb_log = asc(np.asarray(c_lb_logits, f32).reshape(2, 16, 128).transpose(2, 0, 1))
    w2a = asc(np.concatenate([np.asarray(a_gla_w2, f32)[0], np.asarray(a_gla_b, f32)[0][None, :]], axis=0))
    cw1 = asc(np.asarray(a_cmp_w1, f32)[0].reshape(2, 32, 128, 256).transpose(0, 2, 1, 3))
    cw2 = asc(np.asarray(a_cmp_w2, f32)[0].reshape(2, 2, 128, 128).transpose(2, 0, 1, 3))
    cpe = asc(np.asarray(a_cmp_pe, f32)[0].transpose(2, 0, 1))
    cb1 = asc(np.asarray(a_cmp_b1, f32)[0].reshape(2, 2, 128).transpose(2, 0, 1))
    state_gla = np.asarray(state_gla, f32)
    state_hgrn = np.asarray(state_hgrn, f32)
    cache_win = np.asarray(cache_win, f32)
    ckv2 = asc(cache_kv).reshape(2560 * 128 * 2, 512)
    in_maps = []
    for c in range(NCORES):
        sl = slice(c * NS, (c + 1) * NS)
        m = {
            "x_p": x_prompt[c % 4],
            "x_s": asc(x_sample[sl].reshape(TS, D)),
            "cache_win": asc(cache_win[0, sl].reshape(NS, 512, 512)),
            "cache_kv": ckv2, "page_table": np.ascontiguousarray(np.asarray(page_table)[sl].astype(np.int32)),
            "state_gla": asc(state_gla[0, sl]),
            "state_hgrn": asc(state_hgrn[0, sl]),
            "a_norm": a_norm_r, "c_norm": c_norm_r, "f_norm": asc(final_norm),
            "wA": wA, "wG": wG, "wC": wC, "wOA": wOA, "wOC": wOC,
            "gla_w2a": w2a, "gla_gn": asc(np.asarray(a_gla_gn, f32)[0]), "c_gn": asc(np.asarray(c_gn, f32)[0]),
            "lb_log": lb_log, "cmp_w1": cw1, "cmp_w2": cw2, "cmp_pe": cpe, "cmp_b1": cb1,
        }
        for k, v in consts.items():
            m["c_" + k] = v
        in_maps.append(m)
    res = run_bass_kernel_spmd(nc, in_maps, core_ids=list(range(NCORES)))
    R = res.results
    B, SEQ, DB = 4, 2048, 128
    cat = lambda name, shp: np.concatenate([np.asarray(R[c][name], f32).reshape(shp) for c in range(NCORES)])
    stk = lambda name, shp: np.stack([np.asarray(R[c][name], f32).reshape(shp) for c in range(4)])
    y_p = stk("y_p", (SEQ, D))
    y_s = cat("y_s", (NS, 4, D))
    kv_p = stk("kv_p", (SEQ, 4, 2, 128))[None]
    kv_s = cat("kv_s", (NS, 4, 4, 2, 128))[None]
    win_p = stk("win_p", (512, 2, 2, 128))[None]
    win_s = cat("win_s", (NS, 512, 2, 2, 128))[None]
    gla_p = stk("gla_p", (4, 128, 256))[None]
    gla_s = cat("gla_s", (NS, 4, 128, 256))[None]
    hg_p = stk("hg_p", (16, 128, 128))[None]
    hg_s = cat("hg_s", (NS, 16, 128, 128))[None]
    return (y_p, y_s, kv_p, kv_s, win_p, win_s, gla_p, gla_s, hg_p, hg_s)
```

```python
import numpy as np
from contextlib import ExitStack
import concourse.bass as bass
import concourse.mybir as mybir
from concourse.bass_utils import run_bass_kernel_spmd

F32 = mybir.dt.float32
BF16 = mybir.dt.bfloat16
I32 = mybir.dt.int32
AF = mybir.ActivationFunctionType
ALU = mybir.AluOpType
AX = mybir.AxisListType

NCORES = 8
D = 2048
T = 2048
NS = 16
TS = 64
TA = T + TS
NT = 17
EPS = 1e-6
STAGE = 4
DEBUG = False
STOPAT = 99


class Buf:
    __slots__ = ("name", "w", "r")

    def __init__(self, name=""):
        self.name = name
        self.w = None
        self.r = []


class TT:
    def __init__(self, t, name):
        self.t = t
        self.b = Buf(name)

    def __getitem__(self, k):
        return self.t[k]


class Sched:
    ENG = ("pe", "act", "dve", "pool", "sp")
    NDMA = 12

    def __init__(self, nc, es):
        self.nc = nc
        self.es = es
        self.eng = {"pe": nc.tensor, "act": nc.scalar, "dve": nc.vector, "pool": nc.gpsimd, "sp": nc.sync}
        self.sems = {}
        self.cnt = {}
        for e in ("pe", "act", "dve", "pool"):
            self.sems[e] = es.enter_context(nc.semaphore("s_" + e))
            self.cnt[e] = 0
        for q in ("sp", "act", "pool"):
            for i in range(self.NDMA):
                k = f"d_{q}{i}"
                self.sems[k] = es.enter_context(nc.semaphore(k))
                self.cnt[k] = 0
        self.dma_rr = {"sp": 0, "act": 0, "pool": 0}
        self.waited = {e: {} for e in self.ENG}
        self.n_inst = 0
        self.n_wait = 0
        self.uid = 0
        self.freed = {}
        self._scopes = []

    def scope(self):
        from contextlib import contextmanager

        @contextmanager
        def cm():
            rec = []
            self._scopes.append(rec)
            try:
                with ExitStack() as es:
                    yield es
            finally:
                self._scopes.pop()
                for tt in rec:
                    toks = list(tt.b.r) + ([tt.b.w] if tt.b.w else [])
                    for k, v in toks:
                        if self.freed.get(k, 0) < v:
                            self.freed[k] = v
        return cm()

    def tile(self, name, shape, dtype, es=None):
        self.uid += 1
        t = (es or self.es).enter_context(self.nc.sbuf_tensor(f"{name}_{self.uid}", list(shape), dtype))
        tt = TT(t, name)
        tt.b.r = list(self.freed.items())
        if es is not None and self._scopes:
            self._scopes[-1].append(tt)
        return tt

    def ptile(self, name, shape, dtype=F32, es=None):
        self.uid += 1
        t = (es or self.es).enter_context(self.nc.psum_tensor(f"{name}_{self.uid}", list(shape), dtype))
        return TT(t, name)

    def _deps(self, reads, writes):
        deps = {}

        def add(t):
            if t is None:
                return
            k, v = t
            if deps.get(k, 0) < v:
                deps[k] = v
        for b in reads:
            add(b.w)
        for b in writes:
            add(b.w)
            for t in b.r:
                add(t)
        return deps

    def _wait(self, e, deps):
        eng = self.eng[e]
        wd = self.waited[e]
        for k, v in deps.items():
            if wd.get(k, 0) < v:
                eng.wait_ge(self.sems[k], v)
                wd[k] = v
                self.n_wait += 1

    def _commit(self, tok, reads, writes):
        for b in reads:
            b.r.append(tok)
            if len(b.r) > 64:
                m = {}
                for k, v in b.r:
                    if m.get(k, 0) < v:
                        m[k] = v
                b.r = list(m.items())
        for b in writes:
            b.w = tok
            b.r = []

    @staticmethod
    def _bl(xs):
        out = []
        for x in xs:
            b = x.b if isinstance(x, (TT, View)) else x
            if isinstance(b, (list, tuple)):
                out.extend(b)
            else:
                out.append(b)
        return out

    def op(self, e, fn, reads=(), writes=()):
        reads = self._bl(reads)
        writes = self._bl(writes)
        deps = self._deps(reads, writes)
        if e == "pe":
            deps.pop("pe", None)
        self._wait(e, deps)
        ins = fn(self.eng[e])
        self.cnt[e] += 1
        ins.then_inc(self.sems[e], 1)
        self._commit((e, self.cnt[e]), reads, writes)
        self.n_inst += 1
        return ins

    def dma(self, q, out, in_, reads=(), writes=(), fn=None, **kw):
        reads = self._bl(reads)
        writes = self._bl(writes)
        deps = self._deps(reads, writes)
        self._wait(q, deps)
        i = self.dma_rr[q]
        self.dma_rr[q] = (i + 1) % self.NDMA
        k = f"d_{q}{i}"
        if fn is None:
            ins = self.eng[q].dma_start(out=out, in_=in_, **kw)
        else:
            ins = fn(self.eng[q])
        self.cnt[k] += 16
        ins.then_inc(self.sems[k], 16)
        self._commit((k, self.cnt[k]), reads, writes)
        self.n_inst += 1

    def finish(self, bufs):
        deps = self._deps([], self._bl(bufs))
        self._wait("sp", deps)


class View:
    def __init__(self, ap, b):
        self.t = ap
        self.b = b

    def __getitem__(self, k):
        return self.t[k]


class Rot:
    def __init__(self, items):
        self.items = items
        self.i = 0

    def next(self):
        x = self.items[self.i]
        self.i = (self.i + 1) % len(self.items)
        return x


def make_consts():
    c = {}
    c["ident"] = np.eye(128, dtype=np.float32)
    j = np.arange(128)[:, None]
    i = np.arange(128)[None, :]
    c["mle"] = (j <= i).astype(np.float32)
    c["mgt"] = (j > i).astype(np.float32)
    j6 = np.arange(64)[:, None]
    i6 = np.arange(64)[None, :]
    c["bd"] = ((j6 // 4 == i6 // 4) & (j6 <= i6)).astype(np.float32)
    cm = (np.arange(64)[None, :] // 4 == np.arange(16)[:, None]).astype(np.float32)
    c["colmask"] = np.broadcast_to(cm[None], (128, 16, 64)).copy()
    c["rowmask"] = (np.arange(64)[:, None] // 4 == np.arange(16)[None, :]).astype(np.float32)
    def c2s(nc_, ns_):
        cst = np.arange(nc_)[:, None] * 16
        sst = np.arange(ns_)[None, :] * 64
        ov = np.clip(np.minimum(cst + 32, sst + 64) - np.maximum(cst, sst), 0, None)
        return (ov / 16).astype(np.float32)
    c["c2sp"] = c2s(127, 32)
    c["c2ss"] = c2s(127, 33)
    def eexp(ns_, nk_):
        key = np.arange(nk_ * 128)
        return (np.arange(ns_)[:, None] == (key[None, :] // 64)).astype(np.float32)
    c["eexp"] = eexp(32, 16)
    c["eexs"] = eexp(33, 17)
    x = np.arange(63)[None, :] - 31 - (np.arange(128)[:, None] >= 64)
    c["wsel"] = np.where(x > 0, -1e9, np.where(x >= -1, 1e9, 0.0)).astype(np.float32)
    c["rsum"] = (np.arange(16)[:, None] % 4 == np.arange(4)[None, :]).astype(np.float32)
    return c


CONST_SHAPES = {k: v.shape for k, v in make_consts().items()}


def build_program(stage=STAGE):
    nc = bass.Bass("TRN2", target_bir_lowering=False)

    def din(name, shape, dt=F32):
        return nc.dram_tensor(name, list(shape), dt, kind="ExternalInput").ap()

    def dout(name, shape, dt=F32):
        return nc.dram_tensor(name, list(shape), dt, kind="ExternalOutput").ap()

    def dscr(name, shape, dt=F32):
        return nc.dram_tensor(name, list(shape), dt, kind="ExternalOutput" if DEBUG else "Internal").ap()

    x_p = din("x_p", [T, D])
    x_s = din("x_s", [TS, D])
    cache_win = din("cache_win", [NS, 512, 512])
    cache_kv = din("cache_kv", [2560 * 128 * 2, 512])
    page_table = din("page_table", [NS, 16], I32)
    state_gla = din("state_gla", [NS, 4, 128, 256])
    state_hgrn = din("state_hgrn", [NS, 16, 128, 128])
    a_norm = din("a_norm", [128, 16])
    c_norm = din("c_norm", [128, 16])
    f_norm = din("f_norm", [D])
    wA = din("wA", [128, 16, 6696])
    wG = din("wG", [128, 16, 3088])
    wC = din("wC", [128, 16, 8192])
    wOA = din("wOA", [128, 16, 2048])
    wOC = din("wOC", [128, 16, 2048])
    gla_w2a = din("gla_w2a", [17, 512])
    gla_gn = din("gla_gn", [256])
    c_gn = din("c_gn", [128])
    lb_log = din("lb_log", [128, 2, 16])
    cmp_w1 = din("cmp_w1", [2, 128, 32, 256])
    cmp_w2 = din("cmp_w2", [128, 2, 2, 128])
    cmp_pe = din("cmp_pe", [128, 2, 32])
    cmp_b1 = din("cmp_b1", [128, 2, 2])
    cst = {k: din("c_" + k, list(s)) for k, s in CONST_SHAPES.items()}

    y_p = dout("y_p", [T, D])
    y_s = dout("y_s", [TS, D])
    kv_p = dout("kv_p", [T, 1024])
    kv_s = dout("kv_s", [TS, 1024])
    win_p = dout("win_p", [512, 512])
    win_s = dout("win_s", [NS, 512, 512])
    gla_p = dout("gla_p", [4, 128, 256])
    gla_s = dout("gla_s", [NS, 4, 128, 256])
    hg_p = dout("hg_p", [16, 128, 128])
    hg_s = dout("hg_s", [NS, 16, 128, 128])

    oT_scr = dscr("oT_scr", [NT, 128, 16, 128], BF16)
    g_scr = dscr("g_scr", [TS, 24])
    z_scr = dscr("z_scr", [TS, 1024], BF16)
    x1_scr = dscr("x1_scr", [TA, D])
    y_scr = dscr("y_scr", [TA, D])
    oT_b = [[Buf(f"oT{t}_{f}") for f in range(16)] for t in range(NT)]
    x1_b = [[Buf(f"x1_{t}_{k}") for k in range(4)] for t in range(NT)]
    ys_b = [[Buf(f"ys_{t}_{k}") for k in range(4)] for t in range(NT)]

    outs = []
    dbg_n = [0]

    with ExitStack() as es:
        S = Sched(nc, es)

        def dbg(name, tt, ap, shape, dt=F32):
            if not DEBUG:
                return
            dbg_n[0] += 1
            d = nc.dram_tensor(f"dbg_{name}", list(shape), dt, kind="ExternalOutput").ap()
            b = Buf(name)
            outs.append(b)
            S.dma("sp", d, ap, reads=[tt], writes=[b])

        def obuf(name):
            b = Buf(name)
            outs.append(b)
            return b

        hT_ref = [None]
        wbuf = Rot([S.tile(f"wbuf{i}", [128, 16, 512], BF16) for i in range(2)])
        ident_f = S.tile("ident_f", [128, 128], F32)
        ident_b = S.tile("ident_b", [128, 128], BF16)
        mle_b = S.tile("mle_b", [128, 128], BF16)
        normA = S.tile("normA", [128, 16], F32)
        normC = S.tile("normC", [128, 16], F32)
        PP = [S.ptile(f"pp{i}", [128, 1024], F32) for i in range(4)]
        bankb = [Buf(f"bank{i}") for i in range(8)]
        psb = [View(PP[i // 2][:, (i % 2) * 512:(i % 2) * 512 + 512], bankb[i]) for i in range(8)]
        psA = View(PP[2][:, :], [bankb[4], bankb[5]])
        psB = View(PP[3][:, :], [bankb[6], bankb[7]])
        psbf = Rot([View(PP[3][:, k * 512:(k + 1) * 512].bitcast(BF16), bankb[6 + k]) for k in range(2)])

        S.dma("sp", ident_f[:], cst["ident"][:, :], writes=[ident_f])
        S.dma("pool", ident_b[:], cst["ident"][:, :], writes=[ident_b])
        S.dma("pool", mle_b[:], cst["mle"][:, :], writes=[mle_b])
        S.dma("sp", normA[:], a_norm[:, :], writes=[normA])
        S.dma("sp", normC[:], c_norm[:, :], writes=[normC])

        for s in range(NS):
            S.dma("sp", win_s[s, 0:508, :], cache_win[s, 4:512, :], writes=[obuf("win_s")])

        def norm_pass(src_fn, normw, src_deps):
            with S.scope() as es1:
                xrot = Rot([S.tile(f"xt{i}", [128, D], F32, es1) for i in range(2)])
                ssr = Rot([S.tile(f"ss{i}", [128, 1], F32, es1) for i in range(2)])
                rsr = Rot([S.tile(f"rs{i}", [128, 1], F32, es1) for i in range(2)])
                rstdr = Rot([S.tile(f"rstd{i}", [128, 1], F32, es1) for i in range(2)])
                junk = S.tile("junk", [128, D], BF16, es1)
                prot = Rot(psb[0:4])
                for t in range(NT):
                    rows = 128 if t < 16 else TS
                    tok0 = t * 128
                    xt, ss, rs, rstd = xrot.next(), ssr.next(), rsr.next(), rstdr.next()
                    S.dma("sp", xt[:rows, :], src_fn(t, rows), reads=src_deps(t), writes=[xt])
                    S.op("act", lambda e: e.activation(out=junk[:rows, :], in_=xt[:rows, :], func=AF.Square,
                                                       accum_out=ss[:rows, 0:1]), reads=[xt], writes=[junk, ss])
                    S.op("act", lambda e: e.activation(out=rs[:rows, :], in_=ss[:rows, :], func=AF.Sqrt,
                                                       scale=1.0 / D, bias=EPS), reads=[ss], writes=[rs])
                    S.op("dve", lambda e: e.reciprocal(out=rstd[:rows, :], in_=rs[:rows, :]), reads=[rs], writes=[rstd])
                    S.op("act", lambda e: e.activation(out=xt[:rows, :], in_=xt[:rows, :], func=AF.Copy,
                                                       scale=rstd[:rows, 0:1]), reads=[xt, rstd], writes=[xt])
                    for cq in range(4):
                        pb = prot.next()
                        for k in range(4):
                            c = 4 * cq + k
                            S.op("pe", lambda e: e.transpose(out=pb[:, k * 128:k * 128 + rows],
                                                             in_=xt[:rows, c * 128:(c + 1) * 128],
                                                             identity=ident_f[:rows, :rows]),
                                 reads=[xt, ident_f], writes=[pb])
                        S.op("dve", lambda e: e.tensor_tensor(
                            out=hT_ref[0][:, 4 * cq:4 * cq + 4, tok0:tok0 + rows],
                            in0=pb[:, :].rearrange("p (k t) -> p k t", k=4)[:, :, :rows],
                            in1=normw[:, 4 * cq:4 * cq + 4].unsqueeze(2).to_broadcast([128, 4, rows]),
                            op=ALU.mult), reads=[pb, normw], writes=[hT_ref[0]])


        def load_w(col0, n, src):
            wb = wbuf.next()
            S.dma("pool", wb[:, :, 0:n], src[:, :, col0:col0 + n], writes=[wb])
            return wb

        prj = Rot(psb[0:4])
        evq = Rot(["act", "dve"])

        def proj_tok_g(wb, n, consume, tiles=range(NT), wc0=0):
            for t in tiles:
                rows = 128 if t < 16 else TS
                pb = prj.next()
                for c in range(16):
                    S.op("pe", lambda e: e.matmul(pb[:rows, 0:n], lhsT=hT_ref[0][:, c, t * 128:t * 128 + rows],
                                                  rhs=wb[:, c, wc0:wc0 + n], start=(c == 0), stop=(c == 15)),
                         reads=[hT_ref[0], wb], writes=[pb])
                consume(t, rows, pb)
                yield

        def proj_feat_g(wb, wc0, m, consume, chunks=range(5)):
            for j in chunks:
                n = 512 if j < 4 else TS
                pb = prj.next()
                for c in range(16):
                    S.op("pe", lambda e: e.matmul(pb[:m, 0:n], lhsT=wb[:, c, wc0:wc0 + m],
                                                  rhs=hT_ref[0][:, c, j * 512:j * 512 + n], start=(c == 0), stop=(c == 15)),
                         reads=[hT_ref[0], wb], writes=[pb])
                consume(j, n, pb)
                yield

        def proj_tok(*a, **k):
            for _ in proj_tok_g(*a, **k):
                pass

        def proj_feat(*a, **k):
            for _ in proj_feat_g(*a, **k):
                pass

        def evac(out_ap, in_ap, reads, writes, func=None):
            q = "act" if func is not None else evq.next()
            if q == "act":
                S.op("act", lambda e: e.activation(out=out_ap, in_=in_ap, func=func or AF.Copy), reads=reads, writes=writes)
            else:
                S.op("dve", lambda e: e.tensor_copy(out=out_ap, in_=in_ap), reads=reads, writes=writes)

        OQ, OKV, OG, OZA = 0, 1024, 2560, 2584

        SC = 128 ** -0.5
        with S.scope() as esN:
            gts = S.tile("gts", [128, NT, 24], F32, esN)
            qTs = S.tile("qTs", [128, 8, TS], BF16, esN)
            zs = S.tile("zs", [128, 1024], BF16, esN)
            vnew_src = S.tile("vnew_src", [TS, 2, 2, 130], BF16, esN)
            S.op("pool", lambda e: e.memset(vnew_src[:, :, :, 128:130], 1.0), writes=[vnew_src])
            mgt_b = S.tile("mgt_b", [128, 128], BF16, esN)
            S.dma("pool", mgt_b[:, :], cst["mgt"][:, :], writes=[mgt_b])
            kselTs = S.tile("kselTs", [128, 2, TS], BF16, esN)
            kwinTs = S.tile("kwinTs", [128, 2, TS], BF16, esN)
            with S.scope() as esH:
                hT_ref[0] = S.tile("hT", [128, 16, TA], BF16, esH)
                norm_pass(lambda t, rows: (x_p[t * 128:(t + 1) * 128, :] if t < 16 else x_s[:, :]), normA, lambda t: [])
                R = dict(mle_b=mle_b, ident_b=ident_b, ident_f=ident_f, psb=psb, psbf=psbf, cst=cst, obuf=obuf,
                         oT_scr=oT_scr, oT_b=oT_b, dbg=dbg)

                with S.scope() as esG:
                    lrT = S.tile("lrT", [17, TA], F32, esG)
                    w2a = S.tile("w2a", [17, 512], F32, esG)
                    S.dma("sp", w2a[:, :], gla_w2a[:, :], writes=[w2a])
                    for j in range(0, TA, 128):
                        n = min(128, TA - j)
                        S.dma("sp", lrT[16:17, j:j + n], cst["mle"][0:1, 0:n], writes=[lrT])
                    wb = load_w(3072, 16, wG)
                    proj_feat(wb, 0, 16, lambda j, n, pb: evac(lrT[0:16, j * 512:j * 512 + n], pb[0:16, 0:n], [pb], [lrT]))
                    zb_shared = S.tile("zb", [128, NT, 256], BF16, esG)
                    hsets = Rot([dict(q=S.tile(f"qbT{i}", [128, TA], BF16, esG), k=S.tile(f"kbT{i}", [128, TA], BF16, esG),
                                      v=S.tile(f"vb{i}", [128, NT, 256], BF16, esG), z=zb_shared)
                                 for i in range(2)])
                    G = rec_setup(S, esG, R, V=256)
                    gnb = S.tile("gnb", [128, 256], F32, esG)
                    S.dma("sp", gnb[:, :], gla_gn.partition_broadcast(128), writes=[gnb])
                    lg = Rot([S.tile(f"lg{i}", [128, 128], F32, esG) for i in range(2)])

                    for h in range(4):
                        hs = hsets.next()
                        wb = load_w(h * 768, 512, wG)
                        proj_feat(wb, 0, 128, lambda j, n, pb: evac(hs["q"][:, j * 512:j * 512 + n], pb[:, 0:n], [pb], [hs["q"]]))
                        proj_feat(wb, 128, 128, lambda j, n, pb: evac(hs["k"][:, j * 512:j * 512 + n], pb[:, 0:n], [pb], [hs["k"]]))
                        proj_tok(wb, 256, lambda t, rows, pb: evac(hs["v"][:rows, t, :], pb[:rows, 0:256], [pb], [hs["v"]]), wc0=256)
                        wb = load_w(h * 768 + 512, 256, wG)
                        proj_tok(wb, 256, lambda t, rows, pb: evac(hs["z"][:rows, t, :], pb[:rows, 0:256], [pb], [hs["z"]], func=AF.Silu))

                        def gate(tok, n, h=h):
                            l_ = lg.next()
                            p_g = psb[3]
                            S.op("pe", lambda e: e.matmul(p_g[:, 0:n], lhsT=w2a[0:17, h * 128:(h + 1) * 128], rhs=lrT[0:17, tok],
                                                          start=True, stop=True), reads=[w2a, lrT], writes=[p_g])
                            S.op("act", lambda e: e.activation(out=l_[:, 0:n], in_=p_g[:, 0:n], func=AF.Exp, scale=-1.0), reads=[p_g], writes=[l_])
                            S.op("act", lambda e: e.activation(out=l_[:, 0:n], in_=l_[:, 0:n], func=AF.Ln, bias=1.0, scale=1.0), reads=[l_], writes=[l_])
                            return l_[:, 0:n], l_
                        oh = rec_head(S, G, hs, V=256, COEF=-1.0 / 16.0, QS=128 ** -0.5, gate=gate,
                                      state_in=lambda s: state_gla[s, h, :, :], out_p=gla_p[h, :, :], out_s=lambda s: gla_s[s, h, :, :])
                        if h == 0:
                            dbg("gnb", gnb, gnb[:, :], [128, 256])
                            dbg("oh", oh, oh[:, 0, :], [128, 256], BF16)
                            dbg("zb", hs["z"], hs["z"][:, 0, :], [128, 256], BF16)
                        post_head(S, G, oh, hs["z"], gnb, V=256, fc0=8 + 2 * h, dbgh=(h == 0))
                        if STOPAT <= 1:
                            break

                with S.scope() as esNP:
                    vsel = S.tile("vsel", [128, NT, 2, 130], BF16, esNP)
                    vwin = S.tile("vwin", [128, NT, 2, 130], BF16, esNP)
                    S.op("pool", lambda e: e.memset(vsel[:, :, :, 128:130], 1.0), writes=[vsel])
                    S.op("pool", lambda e: e.memset(vwin[:, :, :, 128:130], 1.0), writes=[vwin])
                    kselT = S.tile("kselT", [128, 2, T], BF16, esNP)
                    kwinT = S.tile("kwinT", [128, 2, T], BF16, esNP)
                    kcT = S.tile("kcT", [128, 2, 128], BF16, esNP)
                    vca = S.tile("vca", [128, 2, 162], BF16, esNP)
                    eexp = S.tile("eexp", [32, 16, 128], BF16, esNP)
                    S.dma("pool", eexp[:, :, :], cst["eexp"].rearrange("s (k j) -> s k j", k=16), writes=[eexp])
                    wsel = S.tile("wsel", [128, 63], F32, esNP)
                    S.dma("sp", wsel[:, :], cst["wsel"][:, :], writes=[wsel])
                    S.op("pool", lambda e: e.memset(vca[:, :, 128:129], 1.0), writes=[vca])
                    for g in range(2):
                        S.dma("pool", vca[0:127, g, 129:161], cst["c2sp"][:, :], writes=[vca])

                    with S.scope() as esKV:
                        stg = Rot([S.tile(f"stg{i}", [128, 512], F32, esKV) for i in range(3)])
                        for blk in range(3):
                            wb = load_w(OKV + blk * 512, 512, wA)

                            def cons(t, rows, pb, blk=blk):
                                st = stg.next()
                                evac(st[:rows, :], pb[:rows, :], [pb], [st])
                                if blk < 2:
                                    dst = (kv_p[t * 128:t * 128 + rows, blk * 512:(blk + 1) * 512] if t < 16
                                           else kv_s[:, blk * 512:(blk + 1) * 512])
                                    S.dma("sp", dst, st[:rows, :], reads=[st], writes=[obuf("kv")])
                                else:
                                    if t >= 12 and t < 16:
                                        S.dma("sp", win_p[(t - 12) * 128:(t - 11) * 128, :], st[:rows, :], reads=[st], writes=[obuf("win")])
                                    elif t == 16:
                                        for s in range(NS):
                                            S.dma("sp", win_s[s, 508:512, :], st[4 * s:4 * s + 4, :], reads=[st], writes=[obuf("wins")])
                                if blk >= 1:
                                    vt = vsel if blk == 1 else vwin
                                    S.op("pool", lambda e: e.tensor_copy(out=vt[:rows, t, :, 0:128],
                                                                         in_=st[:rows, 256:512].rearrange("p (g d) -> p g d", g=2)),
                                         reads=[st], writes=[vt])
                                    if t == 16:
                                        S.op("pool", lambda e: e.tensor_copy(out=vnew_src[:, blk - 1, :, 0:128],
                                                                             in_=st[:rows, 256:512].rearrange("p (g d) -> p g d", g=2)),
                                             reads=[st], writes=[vnew_src])
                            proj_tok(wb, 512, cons)

                    if stage >= 3:
                        with S.scope() as esCmp:
                            kcmpT = S.tile("kcmpT", [128, 2, T], BF16, esCmp)
                            vcmpT = S.tile("vcmpT", [128, 2, T], BF16, esCmp)
                            wb = load_w(OKV, 512, wA)
                            for i, dstt in enumerate((kcmpT, kcmpT, vcmpT, vcmpT)):
                                proj_feat(wb, i * 128, 128, lambda j, n, pb, dstt=dstt, i=i: evac(dstt[:, i % 2, j * 512:j * 512 + n], pb[:, 0:n], [pb], [dstt]),
                                          chunks=range(4))
                            wb = load_w(OKV + 512, 256, wA)
                            for g in range(2):
                                proj_feat(wb, g * 128, 128, lambda j, n, pb, g=g: (evac(kselT[:, g, j * 512:j * 512 + n], pb[:, 0:n], [pb], [kselT]) if j < 4
                                                                                  else evac(kselTs[:, g, :], pb[:, 0:n], [pb], [kselTs])))
                            wb = load_w(OKV + 1024, 256, wA)
                            for g in range(2):
                                proj_feat(wb, g * 128, 128, lambda j, n, pb, g=g: (evac(kwinT[:, g, j * 512:j * 512 + n], pb[:, 0:n], [pb], [kwinT]) if j < 4
                                                                                  else evac(kwinTs[:, g, :], pb[:, 0:n], [pb], [kwinTs])))
                            w1b = S.tile("w1b", [128, 32, 256], BF16, esCmp)
                            w2b = S.tile("w2b", [128, 2, 2, 128], BF16, esCmp)
                            peb = S.tile("peb", [128, 2, 32], BF16, esCmp)
                            b1t = S.tile("b1t", [128, 2, 2], F32, esCmp)
                            hb = S.tile("hb", [128, 2, 2], F32, esCmp)
                            gT = Rot([S.tile(f"gT{i}", [128, 2, 128], BF16, esCmp) for i in range(2)])
                            S.dma("pool", w2b[:, :, :, :], cmp_w2[:, :, :, :], writes=[w2b])
                            S.dma("pool", peb[:, :, :], cmp_pe[:, :, :], writes=[peb])
                            S.dma("sp", b1t[:, :, :], cmp_b1[:, :, :], writes=[b1t])
                            for kv in range(2):
                                S.dma("pool", w1b[:, :, :], cmp_w1[kv, :, :, :], writes=[w1b])
                                xT = kcmpT if kv == 0 else vcmpT
                                pp = psb[3]
                                for half in range(2):
                                    for rp in range(32):
                                        S.op("pe", lambda e: e.matmul(pp[:, half:half + 1], lhsT=w1b[:, rp, half * 128:(half + 1) * 128],
                                                                      rhs=peb[:, kv, rp:rp + 1], start=(rp == 0), stop=(rp == 31)),
                                             reads=[w1b, peb], writes=[pp])
                                S.op("dve", lambda e: e.tensor_tensor(out=hb[:, kv, :], in0=pp[:, 0:2], in1=b1t[:, kv, :], op=ALU.add),
                                     reads=[pp, b1t], writes=[hb])
                                for g in range(2):
                                    gt = gT.next()
                                    for half in range(2):
                                        ph = prj.next()
                                        for rp in range(32):
                                            r_, p_ = rp // 16, rp % 16
                                            st0 = 16 * r_ + p_
                                            S.op("pe", lambda e: e.matmul(ph[:, 0:127], lhsT=w1b[:, rp, half * 128:(half + 1) * 128],
                                                                          rhs=xT[:, g, st0:st0 + 16 * 126 + 1:16], start=(rp == 0), stop=(rp == 31)),
                                                 reads=[w1b, xT], writes=[ph])
                                        S.op("act", lambda e: e.activation(out=gt[:, half, 0:127], in_=ph[:, 0:127], func=AF.Gelu_apprx_tanh,
                                                                           bias=hb[:, kv, half:half + 1]), reads=[ph, hb], writes=[gt])
                                    po = prj.next()
                                    if kv == 0:
                                        for half in range(2):
                                            S.op("pe", lambda e: e.matmul(po[:, 0:127], lhsT=w2b[:, 0, half, :], rhs=gt[:, half, 0:127],
                                                                          start=(half == 0), stop=(half == 1)), reads=[w2b, gt], writes=[po])
                                        evac(kcT[:, g, 0:127], po[:, 0:127], [po], [kcT])
                                    else:
                                        for half in range(2):
                                            S.op("pe", lambda e: e.matmul(po[0:127, 0:128], lhsT=gt[:, half, 0:127], rhs=w2b[:, 1, half, :],
                                                                          start=(half == 0), stop=(half == 1)), reads=[w2b, gt], writes=[po])
                                        evac(vca[0:127, g, 0:128], po[0:127, 0:128], [po], [vca])

                        wb = load_w(OG, 24, wA)
                        proj_tok(wb, 24, lambda t, rows, pb: evac(gts[:rows, t, :], pb[:rows, 0:24], [pb], [gts], func=AF.Sigmoid))

                        for g in range(2):
                            with S.scope() as esQ:
                                qTg = S.tile("qTg", [128, 4, T], BF16, esQ)
                                zg = S.tile("zg", [128, 16, 512], BF16, esQ)
                                wb = load_w(OQ + g * 512, 512, wA)
                                for r in range(4):
                                    def qcons(j, n, pb, r=r):
                                        if j < 4:
                                            evac(qTg[:, r, j * 512:j * 512 + n], pb[:, 0:n], [pb], [qTg])
                                        else:
                                            evac(qTs[:, 4 * g + r, :], pb[:, 0:n], [pb], [qTs])
                                    proj_feat(wb, r * 128, 128, qcons)
                                wb = load_w(OZA + g * 512, 512, wA)

                                def zcons(t, rows, pb):
                                    if t < 16:
                                        evac(zg[:, t, :], pb[:, :], [pb], [zg], func=AF.Silu)
                                    else:
                                        evac(zs[:rows, g * 512:(g + 1) * 512], pb[:rows, :], [pb], [zs], func=AF.Silu)
                                proj_tok(wb, 512, zcons)
                                nsa_prompt(S, esQ, g, dict(qTg=qTg, zg=zg, kcT=kcT, vca=vca, kselT=kselT, kwinT=kwinT, vsel=vsel, vwin=vwin,
                                                           gts=gts, wsel=wsel, eexp=eexp, mle_b=mle_b, mgt_b=mgt_b, ident_b=ident_b,
                                                           psb=psb, psA=psA, psB=psB, oT_scr=oT_scr, oT_b=oT_b, SC=SC))
            if stage >= 4:
                with S.scope() as esS:
                    nsa_sample(S, esS, nc, dict(qTs=qTs, zs=zs, gts=gts, kselT=kselTs, kwinT=kwinTs, vnew_src=vnew_src, mle_b=mle_b,
                                                mgt_b=mgt_b, ident_b=ident_b, ident_f=ident_f, psb=psb, wbuf=wbuf, cst=cst, SC=SC,
                                                cache_kv=cache_kv, cache_win=cache_win, page_table=page_table, cmp_w1=cmp_w1,
                                                cmp_w2=cmp_w2, cmp_pe=cmp_pe, cmp_b1=cmp_b1, g_scr=g_scr, z_scr=z_scr,
                                                oT_scr=oT_scr, oT_b=oT_b, evac=evac))

            if stage < 3:
                with S.scope() as esZ:
                    zt = S.tile("zt", [128, 8, 128], BF16, esZ)
                    S.op("pool", lambda e: e.memset(zt[:, :, :], 0.0), writes=[zt])
                    for t in range(NT):
                        S.dma("sp", oT_scr[t, :, 0:8, :], zt[:, :, :], reads=[zt], writes=oT_b[t][0:8])
            elif stage < 4:
                with S.scope() as esZ:
                    zt = S.tile("zt", [128, 8, 128], BF16, esZ)
                    S.op("pool", lambda e: e.memset(zt[:, :, :], 0.0), writes=[zt])
                    S.dma("sp", oT_scr[16, :, 0:8, :], zt[:, :, :], reads=[zt], writes=oT_b[16][0:8])

        def wout_phase(wsrc, res_fn, res_deps, dst, dst_b):
            with S.scope() as e3:
                otr = Rot([S.tile(f"ot{i}", [128, 16, 128], BF16, e3) for i in range(4)])
                xr = Rot([S.tile(f"xr{i}", [128, 512], F32, e3) for i in range(5)])
                for blk in range(4):
                    wb = load_w(blk * 512, 512, wsrc)
                    for t in range(NT):
                        rows = 128 if t < 16 else TS
                        ot, xt = otr.next(), xr.next()
                        S.dma("sp", ot[:, :, :], oT_scr[t, :, :, :], reads=oT_b[t], writes=[ot])
                        S.dma("sp", xt[:rows, :], res_fn(t, rows, blk), reads=res_deps(t, blk), writes=[xt])
                        pb = prj.next()
                        for c in range(16):
                            S.op("pe", lambda e: e.matmul(pb[:rows, 0:512], lhsT=ot[:, c, 0:rows], rhs=wb[:, c, 0:512],
                                                          start=(c == 0), stop=(c == 15)), reads=[ot, wb], writes=[pb])
                        S.op("dve", lambda e: e.tensor_tensor(out=xt[:rows, :], in0=pb[:rows, 0:512], in1=xt[:rows, :], op=ALU.add),
                             reads=[pb, xt], writes=[xt])
                        S.dma("pool", dst[t * 128:t * 128 + rows, blk * 512:(blk + 1) * 512], xt[:rows, :], reads=[xt], writes=[dst_b[t][blk]])

        def xsrc(t, rows, blk):
            return (x_p[t * 128:(t + 1) * 128, blk * 512:(blk + 1) * 512] if t < 16 else x_s[:, blk * 512:(blk + 1) * 512])

        esH2 = es.enter_context(S.scope())
        hT_ref[0] = S.tile("hT2", [128, 16, TA], BF16, esH2)
        if STOPAT > 1:
            wout_phase(wOA, xsrc, lambda t, blk: [], x1_scr, x1_b)
        run_c = STOPAT > 2

        if run_c:
          norm_pass(lambda t, rows: x1_scr[t * 128:t * 128 + rows, :], normC, lambda t: x1_b[t])
        with S.scope() as esC:
          if run_c:
            lbt = S.tile("lbt", [128, 2, 16], F32, esC)
            lb = S.tile("lb", [128, 16], F32, esC)
            oml = S.tile("oml", [128, 16], F32, esC)
            S.dma("sp", lbt[:, :, :], lb_log[:, :, :], writes=[lbt])
            S.op("dve", lambda e: e.tensor_tensor(out=lb[:, :], in0=lbt[:, 1, :], in1=lbt[:, 0, :], op=ALU.subtract), reads=[lbt], writes=[lb])
            S.op("act", lambda e: e.activation(out=lb[:, :], in_=lb[:, :], func=AF.Sigmoid), reads=[lb], writes=[lb])
            S.op("dve", lambda e: e.tensor_scalar(out=oml[:, :], in0=lb[:, :], scalar1=-1.0, scalar2=1.0, op0=ALU.mult, op1=ALU.add),
                 reads=[lb], writes=[oml])
            hsets = Rot([dict(q=S.tile(f"qcT{i}", [128, TA], BF16, esC), k=S.tile(f"kcT{i}", [128, TA], BF16, esC),
                              g=S.tile(f"gcT{i}", [128, TA], F32, esC),
                              v=S.tile(f"vc{i}", [128, NT, 128], BF16, esC), z=S.tile(f"zc{i}", [128, NT, 128], BF16, esC))
                         for i in range(2)])
            G = rec_setup(S, esC, R, V=128)
            gnc = S.tile("gnc", [128, 128], F32, esC)
            S.dma("sp", gnc[:, :], c_gn.partition_broadcast(128), writes=[gnc])
            sgr = Rot([S.tile(f"sg{i}", [128, 512], F32, esC) for i in range(2)])
            def head_proj(h, hs):
                wb = load_w(h * 512, 512, wC)
                yield from proj_feat_g(wb, 0, 128, lambda j, n, pb: evac(hs["q"][:, j * 512:j * 512 + n], pb[:, 0:n], [pb], [hs["q"]], func=AF.Silu))

                def fgate(j, n, pb):
                    sg = sgr.next()
                    sl = slice(j * 512, j * 512 + n)
                    S.op("act", lambda e: e.activation(out=sg[:, 0:n], in_=pb[:, 0:n], func=AF.Sigmoid), reads=[pb], writes=[sg])
                    S.op("dve", lambda e: e.tensor_scalar(out=sg[:, 0:n], in0=sg[:, 0:n], scalar1=oml[:, h:h + 1], scalar2=lb[:, h:h + 1],
                                                          op0=ALU.mult, op1=ALU.add), reads=[sg, oml, lb], writes=[sg])
                    S.op("act", lambda e: e.activation(out=hs["g"][:, sl], in_=sg[:, 0:n], func=AF.Ln), reads=[sg], writes=[hs["g"]])
                    S.op("dve", lambda e: e.tensor_scalar(out=hs["k"][:, sl], in0=sg[:, 0:n], scalar1=-1.0, scalar2=1.0,
                                                          op0=ALU.mult, op1=ALU.add), reads=[sg], writes=[hs["k"]])
                yield from proj_feat_g(wb, 128, 128, fgate)
                yield from proj_tok_g(wb, 128, lambda t, rows, pb: evac(hs["v"][:rows, t, :], pb[:rows, 0:128], [pb], [hs["v"]]), wc0=256)
                yield from proj_tok_g(wb, 128, lambda t, rows, pb: evac(hs["z"][:rows, t, :], pb[:rows, 0:128], [pb], [hs["z"]], func=AF.Silu), wc0=384)

            def step(gen, k):
                if gen is None:
                    return
                for _ in range(k):
                    try:
                        next(gen)
                    except StopIteration:
                        return

            hs_cur = hsets.next()
            step(head_proj(0, hs_cur), 1000)
            for h in range(16):
                hs = hs_cur
                if h + 1 < 16:
                    hs_cur = hsets.next()
                    gen = head_proj(h + 1, hs_cur)
                else:
                    gen = None
                oh = rec_head(S, G, hs, V=128, COEF=1.0, QS=1.0, gate=lambda tok, n, hs=hs: (hs["g"][:, tok], hs["g"]),
                              state_in=lambda s, h=h: state_hgrn[s, h, :, :], out_p=hg_p[h, :, :], out_s=lambda s, h=h: hg_s[s, h, :, :],
                              hook=lambda: step(gen, 3))
                step(gen, 1000)
                post_head(S, G, oh, hs["z"], gnc, V=128, fc0=h)

        if run_c:
          wout_phase(wOC, lambda t, rows, blk: x1_scr[t * 128:t * 128 + rows, blk * 512:(blk + 1) * 512],
                     lambda t, blk: [x1_b[t][blk]], y_scr, ys_b)

        with S.scope() as e4:
          if run_c:
            xrot = Rot([S.tile(f"yt{i}", [128, D], F32, e4) for i in range(3)])
            ssr = Rot([S.tile(f"yss{i}", [128, 1], F32, e4) for i in range(2)])
            rsr = Rot([S.tile(f"yrs{i}", [128, 1], F32, e4) for i in range(2)])
            rstdr = Rot([S.tile(f"yrstd{i}", [128, 1], F32, e4) for i in range(2)])
            junk = S.tile("yjunk", [128, D], BF16, e4)
            fnb = S.tile("fnb", [128, D], F32, e4)
            S.dma("sp", fnb[:, :], f_norm.partition_broadcast(128), writes=[fnb])
            for t in range(NT):
                rows = 128 if t < 16 else TS
                xt, ss, rs, rstd = xrot.next(), ssr.next(), rsr.next(), rstdr.next()
                S.dma("sp", xt[:rows, :], y_scr[t * 128:t * 128 + rows, :], reads=ys_b[t], writes=[xt])
                S.op("act", lambda e: e.activation(out=junk[:rows, :], in_=xt[:rows, :], func=AF.Square,
                                                   accum_out=ss[:rows, 0:1]), reads=[xt], writes=[junk, ss])
                S.op("act", lambda e: e.activation(out=rs[:rows, :], in_=ss[:rows, :], func=AF.Sqrt,
                                                   scale=1.0 / D, bias=EPS), reads=[ss], writes=[rs])
                S.op("dve", lambda e: e.reciprocal(out=rstd[:rows, :], in_=rs[:rows, :]), reads=[rs], writes=[rstd])
                S.op("dve", lambda e: e.scalar_tensor_tensor(out=xt[:rows, :], in0=xt[:rows, :], scalar=rstd[:rows, 0:1], in1=fnb[:rows, :],
                                                             op0=ALU.mult, op1=ALU.mult), reads=[xt, rstd, fnb], writes=[xt])
                dst = y_p[t * 128:(t + 1) * 128, :] if t < 16 else y_s[:, :]
                S.dma("pool", dst, xt[:rows, :], reads=[xt], writes=[obuf("y")])

        S.finish(outs)
        print("instructions", S.n_inst, "waits", S.n_wait)
    return nc


def nsa_prompt(S, es, g, N):
    qTg, zg, kcT, vca, kselT, kwinT, vsel, vwin, gts = (N[k] for k in ("qTg", "zg", "kcT", "vca", "kselT", "kwinT", "vsel", "vwin", "gts"))
    wsel, eexp, mle_b, mgt_b, ident_b, psb, psA, psB, SC = (N[k] for k in ("wsel", "eexp", "mle_b", "mgt_b", "ident_b", "psb", "psA", "psB", "SC"))
    tl = lambda n, s, d=F32: S.tile(n, s, d, es)
    et = Rot([tl(f"et{i}", [128, 512], BF16) for i in range(3)])
    den = Rot([tl(f"den{i}", [128, 4]) for i in range(3)])
    rden = Rot([tl(f"rden{i}", [128, 4]) for i in range(3)])
    cf = Rot([tl(f"cf{i}", [128, 4]) for i in range(3)])
    acc = Rot([tl(f"acc{i}", [128, 4, 128]) for i in range(2)])
    tmpo = Rot([tl(f"tmpo{i}", [128, 4, 128]) for i in range(2)])
    impn = tl("impn", [128, 4, 32])
    sc = Rot([tl(f"sc{i}", [128, 32]) for i in range(2)])
    sc2 = tl("sc2", [128, 32])
    m8 = tl("m8", [128, 16])
    selm = tl("selm", [128, 32], BF16)
    selT = Rot([tl(f"selT{i}", [32, 128], BF16) for i in range(2)])
    m2 = Rot([tl(f"m2{i}", [128, 128], BF16) for i in range(2)])
    ob = Rot([tl(f"nob{i}", [128, 512], BF16) for i in range(2)])
    st = Rot([tl(f"nst{i}", [128, 4, 128], BF16) for i in range(2)])
    scb = Rot([psb[0], psb[1]])
    pmk, pmisc = psb[2], psb[3]
    pmisc_bf = View(pmisc[:, :].bitcast(BF16), pmisc.b)
    A3 = View(psA[:, :].rearrange("p (r c) -> p r c", r=4), psA.b)
    B3 = View(psB[:, :].rearrange("p (r c) -> p r c", r=4), psB.b)

    def v4(x):
        return x[:, :].rearrange("p (r q) -> p r q", r=4)

    def finish_branch(P3, br, t, a_, first):
        d_, r_, c_ = den.next(), rden.next(), cf.next()
        S.op("dve", lambda e: e.tensor_scalar(out=d_[:, :], in0=P3[:, :, 128], scalar1=1e-30, scalar2=None, op0=ALU.max), reads=[P3], writes=[d_])
        S.op("dve", lambda e: e.reciprocal(out=r_[:, :], in_=d_[:, :]), reads=[d_], writes=[r_])
        S.op("dve", lambda e: e.tensor_tensor(out=c_[:, :], in0=r_[:, :], in1=gts[:, t, 12 * g + br:12 * g + 12:3], op=ALU.mult),
             reads=[r_, gts], writes=[c_])
        cb = c_[:, :].unsqueeze(2).to_broadcast([128, 4, 128])
        if first:
            S.op("dve", lambda e: e.tensor_tensor(out=a_[:, :, :], in0=P3[:, :, 0:128], in1=cb, op=ALU.mult), reads=[P3, c_], writes=[a_])
        else:
            tm = tmpo.next()
            S.op("dve", lambda e: e.tensor_tensor(out=tm[:, :, :], in0=P3[:, :, 0:128], in1=cb, op=ALU.mult), reads=[P3, c_], writes=[tm])
            S.op("pool", lambda e: e.tensor_tensor(out=a_[:, :, :], in0=a_[:, :, :], in1=tm[:, :, :], op=ALU.add), reads=[a_, tm], writes=[a_])
        return r_

    for t in range(16):
        t0 = 128 * t
        qrhs = qTg[:, :, t0:t0 + 128]
        a_ = acc.next()
        ps = scb.next()
        S.op("pe", lambda e: e.matmul(v4(ps)[0:127], lhsT=kcT[:, g, 0:127], rhs=qrhs, start=True, stop=True), reads=[kcT, qTg], writes=[ps])
        ec = et.next()
        S.op("act", lambda e: e.activation(out=ec[0:127, :], in_=ps[0:127, :], func=AF.Exp, scale=SC), reads=[ps], writes=[ec])
        S.op("pool", lambda e: e.affine_select(out=v4(ec)[0:127], in_=v4(ec)[0:127], pattern=[[0, 4], [1, 128]], compare_op=ALU.is_ge,
                                               fill=0.0, base=t0 - 31, channel_multiplier=-16), reads=[ec], writes=[ec])
        for r in range(4):
            S.op("pe", lambda e: e.matmul(A3[:, r, 0:161], lhsT=ec[0:127, r * 128:(r + 1) * 128], rhs=vca[0:127, g, 0:161],
                                          start=True, stop=True), reads=[ec, vca], writes=[A3])
        rd = finish_branch(A3, 0, t, a_, True)
        sel = t >= 8
        if sel:
            s_ = sc.next()
            S.op("dve", lambda e: e.tensor_tensor(out=impn[:, :, :], in0=A3[:, :, 129:161], in1=rd[:, :].unsqueeze(2).to_broadcast([128, 4, 32]),
                                                  op=ALU.mult), reads=[A3, rd], writes=[impn])
            S.op("dve", lambda e: e.tensor_reduce(out=s_[:, :], in_=impn[:, :, :].rearrange("p r s -> p s r"), axis=AX.X, op=ALU.add),
                 reads=[impn], writes=[s_])
            S.op("dve", lambda e: e.tensor_tensor(out=s_[:, :], in0=s_[:, :], in1=wsel[:, 31 - 2 * t:63 - 2 * t], op=ALU.add),
                 reads=[s_, wsel], writes=[s_])
            S.op("dve", lambda e: e.memset(s_[:, 0:1], 1e9), reads=[], writes=[s_])
            S.op("dve", lambda e: e.max(out=m8[:, 0:8], in_=s_[:, :]), reads=[s_], writes=[m8])
            S.op("dve", lambda e: e.match_replace(out=sc2[:, :], in_to_replace=m8[:, 0:8], in_values=s_[:, :], imm_value=-3e38),
                 reads=[s_, m8], writes=[sc2])
            S.op("dve", lambda e: e.max(out=m8[:, 8:16], in_=sc2[:, :]), reads=[sc2], writes=[m8])
            S.op("dve", lambda e: e.tensor_scalar(out=selm[:, :], in0=s_[:, :], scalar1=m8[:, 15:16], scalar2=None, op0=ALU.is_ge),
                 reads=[s_, m8], writes=[selm])
            S.op("pe", lambda e: e.transpose(out=pmisc_bf[0:32, 0:128], in_=selm[:, 0:32], identity=ident_b[:, :]),
                 reads=[selm, ident_b], writes=[pmisc_bf])
            sT = selT.next()
            S.op("act", lambda e: e.activation(out=sT[:, :], in_=pmisc_bf[0:32, 0:128], func=AF.Copy), reads=[pmisc_bf], writes=[sT])
        S.op("dve", lambda e: e.memset(B3[:, :, 0:129], 0.0), reads=[], writes=[B3])
        for kc in range(t + 1):
            ps = scb.next()
            S.op("pe", lambda e: e.matmul(v4(ps), lhsT=kselT[:, g, kc * 128:(kc + 1) * 128], rhs=qrhs, start=True, stop=True),
                 reads=[kselT, qTg], writes=[ps])
            e_ = et.next()
            S.op("act", lambda e: e.activation(out=e_[:, :], in_=ps[:, :], func=AF.Exp, scale=SC), reads=[ps], writes=[e_])
            if sel:
                S.op("pe", lambda e: e.matmul(pmk[:, 0:128], lhsT=eexp[0:32, kc, :], rhs=sT[0:32, :], start=True, stop=True),
                     reads=[eexp, sT], writes=[pmk])
                if kc == t:
                    mm = m2.next()
                    S.op("dve", lambda e: e.tensor_tensor(out=mm[:, :], in0=pmk[:, 0:128], in1=mle_b[:, :], op=ALU.mult),
                         reads=[pmk, mle_b], writes=[mm])
                    msrc = mm
                else:
                    msrc = pmk
                S.op("dve", lambda e: e.tensor_tensor(out=v4(e_), in0=v4(e_), in1=msrc[:, 0:128].unsqueeze(1).to_broadcast([128, 4, 128]),
                                                      op=ALU.mult), reads=[e_, msrc], writes=[e_])
            elif kc == t:
                S.op("pool", lambda e: e.tensor_tensor(out=v4(e_), in0=v4(e_), in1=mle_b[:, :].unsqueeze(1).to_broadcast([128, 4, 128]),
                                                       op=ALU.mult), reads=[e_, mle_b], writes=[e_])
            for r in range(4):
                S.op("pe", lambda e: e.matmul(B3[:, r, 0:129], lhsT=e_[:, r * 128:(r + 1) * 128], rhs=vsel[:, kc, g, 0:129],
                                              start=False, stop=(kc == t), skip_group_check=True), reads=[e_, vsel], writes=[B3])
        finish_branch(B3, 1, t, a_, False)
        S.op("dve", lambda e: e.memset(A3[:, :, 0:129], 0.0), reads=[], writes=[A3])
        k0 = max(0, t - 4)
        for kc in range(k0, t + 1):
            ps = scb.next()
            S.op("pe", lambda e: e.matmul(v4(ps), lhsT=kwinT[:, g, kc * 128:(kc + 1) * 128], rhs=qrhs, start=True, stop=True),
                 reads=[kwinT, qTg], writes=[ps])
            e_ = et.next()
            S.op("act", lambda e: e.activation(out=e_[:, :], in_=ps[:, :], func=AF.Exp, scale=SC), reads=[ps], writes=[e_])
            mk = mle_b if kc == t else (mgt_b if kc == t - 4 else None)
            if mk is not None:
                S.op("pool", lambda e: e.tensor_tensor(out=v4(e_), in0=v4(e_), in1=mk[:, :].unsqueeze(1).to_broadcast([128, 4, 128]),
                                                       op=ALU.mult), reads=[e_, mk], writes=[e_])
            for r in range(4):
                S.op("pe", lambda e: e.matmul(A3[:, r, 0:129], lhsT=e_[:, r * 128:(r + 1) * 128], rhs=vwin[:, kc, g, 0:129],
                                              start=False, stop=(kc == t), skip_group_check=True), reads=[e_, vwin], writes=[A3])
        finish_branch(A3, 2, t, a_, False)
        o_ = ob.next()
        S.op("dve", lambda e: e.tensor_tensor(out=o_[:, :], in0=a_[:, :, :].rearrange("p r d -> p (r d)"), in1=zg[:, t, :], op=ALU.mult),
             reads=[a_, zg], writes=[o_])
        for r in range(4):
            S.op("pe", lambda e: e.transpose(out=pmisc_bf[:, 128 + r * 128:256 + r * 128], in_=o_[:, r * 128:(r + 1) * 128], identity=ident_b[:, :]),
                 reads=[o_, ident_b], writes=[pmisc_bf])
        s_t = st.next()
        S.op("act", lambda e: e.activation(out=s_t[:, :, :], in_=pmisc_bf[:, 128:640].rearrange("p (r q) -> p r q", r=4), func=AF.Copy),
             reads=[pmisc_bf], writes=[s_t])
        S.dma("sp", N["oT_scr"][t, :, 4 * g:4 * g + 4, :], s_t[:, :, :], reads=[s_t], writes=N["oT_b"][t][4 * g:4 * g + 4])


def nsa_sample(S, es, nc, N):
    qTs, zs, gts, kselT, kwinT, vnew_src, mle_b, mgt_b, ident_b, ident_f, psb, wbuf, cst, SC, evac = (
        N[k] for k in ("qTs", "zs", "gts", "kselT", "kwinT", "vnew_src", "mle_b", "mgt_b", "ident_b", "ident_f", "psb", "wbuf", "cst", "SC", "evac"))
    ckv, cwin = N["cache_kv"], N["cache_win"]
    tl = lambda n, s, d=F32: S.tile(n, s, d, es)
    big = Rot(psb[0:4])
    small = Rot(psb[4:8])
    ptb = tl("ptb", [128, 256], I32)
    S.dma("sp", ptb[:, :], N["page_table"].rearrange("s p -> (s p)").partition_broadcast(128), writes=[ptb])
    pci = tl("pci", [128, 1], I32)
    S.op("pool", lambda e: e.iota(pci[:, :], pattern=[[0, 1]], base=0, channel_multiplier=2), writes=[pci])
    pcf = tl("pcf", [128, 1])
    S.op("dve", lambda e: e.tensor_copy(out=pcf[:, :], in_=pci[:, :]), reads=[pci], writes=[pcf])
    idxA = tl("idxA", [128, 256], I32)
    idxB = tl("idxB", [128, 256], I32)
    S.op("dve", lambda e: e.tensor_scalar(out=idxA[:, :], in0=ptb[:, :], scalar1=256.0, scalar2=pcf[:, 0:1], op0=ALU.mult, op1=ALU.add),
         reads=[ptb, pcf], writes=[idxA])
    S.op("dve", lambda e: e.tensor_scalar(out=idxB[:, :], in0=idxA[:, :], scalar1=1.0, scalar2=None, op0=ALU.add), reads=[idxA], writes=[idxB])
    bg, bz = Buf("gscr"), Buf("zscr")
    S.dma("sp", N["g_scr"][:, :], gts[:TS, 16, :], reads=[gts], writes=[bg])
    S.dma("sp", N["z_scr"][:, :], zs[:TS, :], reads=[zs], writes=[bz])
    gsm = tl("gsm", [16, NS, 2, 3])
    zr = tl("zr", [16, NS, 2, 128], BF16)
    gv = N["g_scr"].rearrange("(s t) (g r b) -> r t s g b", t=4, g=2, r=4)
    zv = N["z_scr"].rearrange("(s t) (g r d) -> r t s g d", t=4, g=2, r=4)
    for r in range(4):
        for g in range(2):
            S.dma("sp", gsm[4 * r:4 * r + 4, :, g, :], gv[r][:, :, g, :], reads=[bg], writes=[gsm])
            S.dma("sp", zr[4 * r:4 * r + 4, :, g, :], zv[r][:, :, g, :], reads=[bz], writes=[zr])
    vnew = tl("vnew", [4, NS, 2, 2, 130], BF16)
    for s in range(NS):
        S.dma("sp", vnew[0:4, s, :, :, :], vnew_src[4 * s:4 * s + 4, :, :, :], reads=[vnew_src], writes=[vnew])
    w1 = [tl(f"w1_{kv}", [128, 32, 256], BF16) for kv in range(2)]
    w2b = tl("w2b", [128, 2, 2, 128], BF16)
    peb = tl("peb", [128, 2, 32], BF16)
    b1t = tl("b1t", [128, 2, 2])
    hb = tl("hb", [128, 2, 2])
    for kv in range(2):
        S.dma("pool", w1[kv][:, :, :], N["cmp_w1"][kv, :, :, :], writes=[w1[kv]])
    S.dma("pool", w2b[:, :, :, :], N["cmp_w2"][:, :, :, :], writes=[w2b])
    S.dma("pool", peb[:, :, :], N["cmp_pe"][:, :, :], writes=[peb])
    S.dma("sp", b1t[:, :, :], N["cmp_b1"][:, :, :], writes=[b1t])
    for kv in range(2):
        pp = small.next()
        for half in range(2):
            for rp in range(32):
                S.op("pe", lambda e: e.matmul(pp[:, half:half + 1], lhsT=w1[kv][:, rp, half * 128:(half + 1) * 128],
                                              rhs=peb[:, kv, rp:rp + 1], start=(rp == 0), stop=(rp == 31)), reads=[w1[kv], peb], writes=[pp])
        S.op("dve", lambda e: e.tensor_tensor(out=hb[:, kv, :], in0=pp[:, 0:2], in1=b1t[:, kv, :], op=ALU.add), reads=[pp, b1t], writes=[hb])
    rsum = tl("rsum", [16, 4])
    S.dma("sp", rsum[:, :], cst["rsum"][:, :], writes=[rsum])
    xTk = tl("xTk", [128, 2, 2048], BF16)
    xTv = tl("xTv", [128, 2, 2048], BF16)
    gT = Rot([tl(f"sgT{i}", [128, 2, 128], BF16) for i in range(2)])
    kcTs = tl("kcTs", [128, 2, 128], BF16)
    vcas = tl("vcas", [128, 2, 162], BF16)
    S.op("pool", lambda e: e.memset(vcas[:, :, 128:129], 1.0), writes=[vcas])
    for g in range(2):
        S.dma("pool", vcas[0:127, g, 129:162], cst["c2ss"][:, :], writes=[vcas])
    ec = tl("sec", [128, 2, 16], BF16)
    den = Rot([tl(f"sden{i}", [16, 2]) for i in range(3)])
    rden = Rot([tl(f"srden{i}", [16, 2]) for i in range(3)])
    cf = Rot([tl(f"scf{i}", [16, 2]) for i in range(3)])
    acc = tl("sacc", [16, 2, 128])
    tmpo = tl("stmpo", [16, 2, 128])
    impn = tl("simpn", [16, 2, 33])
    scs = tl("sscs", [4, 2, 33])
    sc2 = tl("ssc2", [4, 33])
    m8 = tl("sm8", [4, 16])
    selm = tl("sselm", [4, 2, 33], BF16)
    selx = tl("sselx", [4, 2, 33, 64], BF16)
    maskT = tl("smaskT", [128, 2, 16, 4], BF16)
    vsa = tl("vsa", [128, 16, 2, 130], BF16)
    S.op("pool", lambda e: e.memset(vsa[:, :, :, 128:130], 1.0), writes=[vsa])
    esel = tl("esel", [128, 2, 272], BF16)
    wl = tl("wl", [128, 4, 512])
    kwT = tl("kwT", [128, 2, 512], BF16)
    vwa = tl("vwa", [128, 4, 2, 130], BF16)
    S.op("pool", lambda e: e.memset(vwa[:, :, :, 128:130], 1.0), writes=[vwa])
    ewin = tl("ewin", [128, 2, 80], BF16)
    ob = tl("sob", [16, 2, 128], BF16)
    oTs = tl("oTs", [128, 8, TS], BF16)

    def pbview(wb):
        return View(wb[:, :, :].bitcast(F32).rearrange("p c (a f) -> p (c a) f", a=1).rearrange("p (k two) f -> p k (two f)", two=2), wb.b)

    pgbufs = Rot(list(wbuf.items) + [tl(f"pgx{i}", [128, 16, 512], BF16) for i in range(2)])

    def gather(idx, s, hs):
        wb = pgbufs.next()
        pv = pbview(wb)
        for pgl in range(8):
            col = s * 16 + hs * 8 + pgl
            S.dma("pool", None, None, reads=[idx], writes=[pv],
                  fn=lambda e: e.indirect_dma_start(out=pv[:, pgl, :], out_offset=None, in_=ckv[:, :],
                                                    in_offset=bass.IndirectOffsetOnAxis(ap=idx[:, col:col + 1], axis=0)))
        return pv

    def transpose4(src_fn, dst_ap, reads, dstt):
        pt = big.next()
        for k in range(4):
            S.op("pe", lambda e: e.transpose(out=pt[:, k * 128:(k + 1) * 128], in_=src_fn(k), identity=ident_f[:, :]),
                 reads=reads + [ident_f], writes=[pt])
        evac(dst_ap, pt[:, 0:512], [pt], [dstt])

    def finish_branch(P, br, s, first):
        d_, r_, c_ = den.next(), rden.next(), cf.next()
        S.op("dve", lambda e: e.tensor_scalar(out=d_[:, :], in0=P[0:16, :, 128], scalar1=1e-30, scalar2=None, op0=ALU.max), reads=[P], writes=[d_])
        S.op("dve", lambda e: e.reciprocal(out=r_[:, :], in_=d_[:, :]), reads=[d_], writes=[r_])
        S.op("dve", lambda e: e.tensor_tensor(out=c_[:, :], in0=r_[:, :], in1=gsm[:, s, :, br], op=ALU.mult), reads=[r_, gsm], writes=[c_])
        cb = c_[:, :].unsqueeze(2).to_broadcast([16, 2, 128])
        if first:
            S.op("dve", lambda e: e.tensor_tensor(out=acc[:, :, :], in0=P[0:16, :, 0:128], in1=cb, op=ALU.mult), reads=[P, c_], writes=[acc])
        else:
            S.op("dve", lambda e: e.tensor_tensor(out=tmpo[:, :, :], in0=P[0:16, :, 0:128], in1=cb, op=ALU.mult), reads=[P, c_], writes=[tmpo])
            S.op("dve", lambda e: e.tensor_tensor(out=acc[:, :, :], in0=acc[:, :, :], in1=tmpo[:, :, :], op=ALU.add), reads=[acc, tmpo], writes=[acc])
        return r_

    def v3(bank):
        return View(bank[:, :].rearrange("p (g c) -> p g c", g=2), bank.b)

    for s in range(NS):
        q16 = [qTs[:, 4 * g:4 * g + 4, 4 * s:4 * s + 4] for g in range(2)]
        for hs in range(2):
            pv = gather(idxA, s, hs)
            for slot in range(2):
                dstt = xTk if slot == 0 else xTv
                for g in range(2):
                    for q4 in range(2):
                        c0 = (slot * 2 + g) * 128
                        transpose4(lambda k: pv[:, 4 * q4 + k, c0:c0 + 128],
                                   dstt[:, g, (8 * hs + 4 * q4) * 128:(8 * hs + 4 * q4 + 4) * 128], [pv], dstt)
        for kv in range(2):
            xT = xTk if kv == 0 else xTv
            for g in range(2):
                gt = gT.next()
                for half in range(2):
                    ph = big.next()
                    for rp in range(32):
                        st0 = rp
                        S.op("pe", lambda e: e.matmul(ph[:, 0:127], lhsT=w1[kv][:, rp, half * 128:(half + 1) * 128],
                                                      rhs=xT[:, g, st0:st0 + 16 * 126 + 1:16], start=(rp == 0), stop=(rp == 31)),
                             reads=[w1[kv], xT], writes=[ph])
                    S.op("act", lambda e: e.activation(out=gt[:, half, 0:127], in_=ph[:, 0:127], func=AF.Gelu_apprx_tanh,
                                                       bias=hb[:, kv, half:half + 1]), reads=[ph, hb], writes=[gt])
                po = big.next()
                if kv == 0:
                    for half in range(2):
                        S.op("pe", lambda e: e.matmul(po[:, 0:127], lhsT=w2b[:, 0, half, :], rhs=gt[:, half, 0:127],
                                                      start=(half == 0), stop=(half == 1)), reads=[w2b, gt], writes=[po])
                    evac(kcTs[:, g, 0:127], po[:, 0:127], [po], [kcTs])
                else:
                    for half in range(2):
                        S.op("pe", lambda e: e.matmul(po[0:127, 0:128], lhsT=gt[:, half, 0:127], rhs=w2b[:, 1, half, :],
                                                      start=(half == 0), stop=(half == 1)), reads=[w2b, gt], writes=[po])
                    evac(vcas[0:127, g, 0:128], po[0:127, 0:128], [po], [vcas])
        ps = small.next()
        for g in range(2):
            S.op("pe", lambda e: e.matmul(ps[0:127, g * 16:(g + 1) * 16].rearrange("p (r q) -> p r q", r=4), lhsT=kcTs[:, g, 0:127], rhs=q16[g],
                                          start=True, stop=True), reads=[kcTs, qTs], writes=[ps])
        S.op("act", lambda e: e.activation(out=ec[0:127, :, :], in_=ps[0:127, 0:32].rearrange("p (g c) -> p g c", g=2), func=AF.Exp, scale=SC),
             reads=[ps], writes=[ec])
        pc = v3(small.next())
        for g in range(2):
            S.op("pe", lambda e: e.matmul(pc[0:16, g, 0:162], lhsT=ec[0:127, g, :], rhs=vcas[0:127, g, 0:162], start=True, stop=True),
                 reads=[ec, vcas], writes=[pc])
        rd = finish_branch(pc, 0, s, True)
        S.op("dve", lambda e: e.tensor_tensor(out=impn[:, :, :], in0=pc[0:16, :, 129:162], in1=rd[:, :].unsqueeze(2).to_broadcast([16, 2, 33]),
                                              op=ALU.mult), reads=[pc, rd], writes=[impn])
        pi = small.next()
        S.op("pe", lambda e: e.matmul(pi[0:4, 0:66], lhsT=rsum[:, :], rhs=impn[:, :, :].rearrange("p g s -> p (g s)"), start=True, stop=True),
             reads=[rsum, impn], writes=[pi])
        S.op("dve", lambda e: e.tensor_copy(out=scs[:, :, :], in_=pi[0:4, 0:66].rearrange("p (g s) -> p g s", g=2)), reads=[pi], writes=[scs])
        S.op("dve", lambda e: e.memset(scs[:, :, 0:1], 1e9), reads=[], writes=[scs])
        S.op("dve", lambda e: e.memset(scs[:, :, 31:33], 1e9), reads=[], writes=[scs])
        for g in range(2):
            S.op("dve", lambda e: e.max(out=m8[:, 0:8], in_=scs[:, g, :]), reads=[scs], writes=[m8])
            S.op("dve", lambda e: e.match_replace(out=sc2[:, :], in_to_replace=m8[:, 0:8], in_values=scs[:, g, :], imm_value=-3e38),
                 reads=[scs, m8], writes=[sc2])
            S.op("dve", lambda e: e.max(out=m8[:, 8:16], in_=sc2[:, :]), reads=[sc2], writes=[m8])
            S.op("dve", lambda e: e.tensor_scalar(out=selm[:, g, :], in0=scs[:, g, :], scalar1=m8[:, 15:16], scalar2=None, op0=ALU.is_ge),
                 reads=[scs, m8], writes=[selm])
        S.op("dve", lambda e: e.tensor_copy(out=selx[:, :, :, :].rearrange("p g s k -> p (g s) k"),
                                            in_=selm[:, :, :].rearrange("p g s -> p (g s)").unsqueeze(2).to_broadcast([4, 66, 64])),
             reads=[selm], writes=[selx])
        pm = small.next()
        for g in range(2):
            for kc in range(16):
                S.op("pe", lambda e: e.matmul(pm[:, (g * 16 + kc) * 4:(g * 16 + kc) * 4 + 4],
                                              lhsT=selx[0:4, g, 2 * kc:2 * kc + 2, :].rearrange("p a b -> p (a b)"), rhs=ident_b[0:4, 0:4],
                                              start=True, stop=True), reads=[selx, ident_b], writes=[pm])
        S.op("act", lambda e: e.activation(out=maskT[:, :, :, :].rearrange("p g k t -> p (g k t)"), in_=pm[:, 0:128], func=AF.Copy),
             reads=[pm], writes=[maskT])
        for hs in range(2):
            pv = gather(idxB, s, hs)
            for g in range(2):
                for q4 in range(2):
                    transpose4(lambda k: pv[:, 4 * q4 + k, g * 128:(g + 1) * 128],
                               xTk[:, g, (8 * hs + 4 * q4) * 128:(8 * hs + 4 * q4 + 4) * 128], [pv], xTk)
            S.op("act", lambda e: e.activation(out=vsa[:, 8 * hs:8 * hs + 8, :, 0:128],
                                               in_=pv[:, :, 256:512].rearrange("p k (g d) -> p k g d", g=2), func=AF.Copy), reads=[pv], writes=[vsa])
        pss = v3(small.next())
        psn = v3(small.next())
        for g in range(2):
            for kc in range(16):
                S.op("pe", lambda e: e.matmul(pss[:, g, kc * 16:(kc + 1) * 16].rearrange("p (r q) -> p r q", r=4),
                                              lhsT=xTk[:, g, kc * 128:(kc + 1) * 128], rhs=q16[g], start=True, stop=True),
                     reads=[xTk, qTs], writes=[pss])
            S.op("pe", lambda e: e.matmul(psn[0:4, g, 0:16].rearrange("p (r q) -> p r q", r=4),
                                          lhsT=kselT[:, g, 4 * s:4 * s + 4], rhs=q16[g], start=True, stop=True),
                 reads=[kselT, qTs], writes=[psn])
        S.op("act", lambda e: e.activation(out=esel[:, :, 0:256], in_=pss[:, :, 0:256], func=AF.Exp, scale=SC), reads=[pss], writes=[esel])
        S.op("act", lambda e: e.activation(out=esel[0:4, :, 256:272], in_=psn[0:4, :, 0:16], func=AF.Exp, scale=SC), reads=[psn], writes=[esel])
        for g in range(2):
            S.op("dve", lambda e: e.tensor_tensor(out=esel[:, g, 0:256].rearrange("p (k r q) -> p k r q", k=16, r=4),
                                                  in0=esel[:, g, 0:256].rearrange("p (k r q) -> p k r q", k=16, r=4),
                                                  in1=maskT[:, g, :, :].unsqueeze(2).to_broadcast([128, 16, 4, 4]), op=ALU.mult),
                 reads=[esel, maskT], writes=[esel])
        S.op("dve", lambda e: e.tensor_tensor(out=esel[0:4, :, 256:272].rearrange("p g (r q) -> p g r q", r=4),
                                              in0=esel[0:4, :, 256:272].rearrange("p g (r q) -> p g r q", r=4),
                                              in1=mle_b[0:4, 0:4].unsqueeze(1).unsqueeze(1).to_broadcast([4, 2, 4, 4]), op=ALU.mult),
             reads=[esel, mle_b], writes=[esel])
        po_ = v3(small.next())
        for g in range(2):
            for kc in range(16):
                S.op("pe", lambda e: e.matmul(po_[0:16, g, 0:129], lhsT=esel[:, g, kc * 16:(kc + 1) * 16], rhs=vsa[:, kc, g, 0:129],
                                              start=(kc == 0), stop=False), reads=[esel, vsa], writes=[po_])
            S.op("pe", lambda e: e.matmul(po_[0:16, g, 0:129], lhsT=esel[0:4, g, 256:272], rhs=vnew[0:4, s, 0, g, 0:129],
                                          start=False, stop=True), reads=[esel, vnew], writes=[po_])
        finish_branch(po_, 1, s, False)
        S.dma("sp", wl[:, :, :], cwin[s, :, :].rearrange("(c p) f -> p c f", p=128), writes=[wl])
        for g in range(2):
            transpose4(lambda k: wl[:, k, g * 128:(g + 1) * 128], kwT[:, g, :], [wl], kwT)
        S.op("act", lambda e: e.activation(out=vwa[:, :, :, 0:128], in_=wl[:, :, 256:512].rearrange("p k (g d) -> p k g d", g=2), func=AF.Copy),
             reads=[wl], writes=[vwa])
        psw = v3(small.next())
        pwn = v3(small.next())
        for g in range(2):
            for kc in range(4):
                S.op("pe", lambda e: e.matmul(psw[:, g, kc * 16:(kc + 1) * 16].rearrange("p (r q) -> p r q", r=4),
                                              lhsT=kwT[:, g, kc * 128:(kc + 1) * 128], rhs=q16[g], start=True, stop=True),
                     reads=[kwT, qTs], writes=[psw])
            S.op("pe", lambda e: e.matmul(pwn[0:4, g, 0:16].rearrange("p (r q) -> p r q", r=4),
                                          lhsT=kwinT[:, g, 4 * s:4 * s + 4], rhs=q16[g], start=True, stop=True),
                 reads=[kwinT, qTs], writes=[pwn])
        S.op("act", lambda e: e.activation(out=ewin[:, :, 0:64], in_=psw[:, :, 0:64], func=AF.Exp, scale=SC), reads=[psw], writes=[ewin])
        S.op("act", lambda e: e.activation(out=ewin[0:4, :, 64:80], in_=pwn[0:4, :, 0:16], func=AF.Exp, scale=SC), reads=[pwn], writes=[ewin])
        S.op("dve", lambda e: e.tensor_tensor(out=ewin[:, :, 0:16].rearrange("p g (r q) -> p g r q", r=4),
                                              in0=ewin[:, :, 0:16].rearrange("p g (r q) -> p g r q", r=4),
                                              in1=mgt_b[:, 0:4].unsqueeze(1).unsqueeze(1).to_broadcast([128, 2, 4, 4]), op=ALU.mult),
             reads=[ewin, mgt_b], writes=[ewin])
        S.op("dve", lambda e: e.tensor_tensor(out=ewin[0:4, :, 64:80].rearrange("p g (r q) -> p g r q", r=4),
                                              in0=ewin[0:4, :, 64:80].rearrange("p g (r q) -> p g r q", r=4),
                                              in1=mle_b[0:4, 0:4].unsqueeze(1).unsqueeze(1).to_broadcast([4, 2, 4, 4]), op=ALU.mult),
             reads=[ewin, mle_b], writes=[ewin])
        pw_ = v3(small.next())
        for g in range(2):
            for kc in range(4):
                S.op("pe", lambda e: e.matmul(pw_[0:16, g, 0:129], lhsT=ewin[:, g, kc * 16:(kc + 1) * 16], rhs=vwa[:, kc, g, 0:129],
                                              start=(kc == 0), stop=False), reads=[ewin, vwa], writes=[pw_])
            S.op("pe", lambda e: e.matmul(pw_[0:16, g, 0:129], lhsT=ewin[0:4, g, 64:80], rhs=vnew[0:4, s, 1, g, 0:129],
                                          start=False, stop=True), reads=[ewin, vnew], writes=[pw_])
        finish_branch(pw_, 2, s, False)
        S.op("dve", lambda e: e.tensor_tensor(out=ob[:, :, :], in0=acc[:, :, :], in1=zr[:, s, :, :], op=ALU.mult), reads=[acc, zr], writes=[ob])
        pt = small.next()
        ptb_ = View(pt[:, :].bitcast(BF16), pt.b)
        for g in range(2):
            S.op("pe", lambda e: e.transpose(out=ptb_[:, g * 16:(g + 1) * 16], in_=ob[0:16, g, :], identity=ident_b[0:16, 0:16]),
                 reads=[ob, ident_b], writes=[ptb_])
        S.op("act", lambda e: e.activation(out=oTs[:, :, 4 * s:4 * s + 4], in_=ptb_[:, 0:32].rearrange("p (f q) -> p f q", f=8), func=AF.Copy),
             reads=[ptb_], writes=[oTs])
    S.dma("sp", N["oT_scr"][16, :, 0:8, 0:TS], oTs[:, :, :], reads=[oTs], writes=N["oT_b"][16][0:8])


def rec_setup(S, e2, R, V):
    G = dict(R)
    tl = lambda n, s, d=F32: S.tile(n, s, d, e2)
    G["cs"] = Rot([tl(f"cs{i}", [128, 128]) for i in range(2)])
    G["E1"] = Rot([tl(f"E1{i}", [128, 128]) for i in range(2)])
    G["E2"] = Rot([tl(f"E2{i}", [128, 128]) for i in range(2)])
    G["sm"] = Rot([tl(f"sm{i}", [128, 16]) for i in range(3)])
    G["E3"] = Rot([tl(f"E3{i}", [128, 128]) for i in range(2)])
    G["E4"] = Rot([tl(f"E4{i}", [128, 128]) for i in range(2)])
    G["qx"] = Rot([tl(f"qx{i}", [128, 64], BF16) for i in range(2)])
    G["qS"] = Rot([tl(f"qS{i}", [128, 128], BF16) for i in range(2)])
    G["kS"] = Rot([tl(f"kS{i}", [128, 128], BF16) for i in range(2)])
    G["KA"] = Rot([tl(f"KA{i}", [128, 128], BF16) for i in range(2)])
    G["KB"] = Rot([tl(f"KB{i}", [128, 128], BF16) for i in range(2)])
    for kk in ("KA", "KB"):
        for tt in G[kk].items:
            S.op("pool", lambda e: e.memset(tt[:, :], 0.0), writes=[tt])
    G["qp"] = Rot([tl(f"qp{i}", [128, 128], BF16) for i in range(2)])
    G["kp"] = Rot([tl(f"kp{i}", [128, 128], BF16) for i in range(2)])
    G["ktok"] = Rot([tl(f"ktok{i}", [128, 128], BF16) for i in range(2)])
    G["attm"] = Rot([tl(f"attm{i}", [128, 128], BF16) for i in range(2)])
    G["Sst"] = tl("Sst", [128, V])
    G["Sbf"] = Rot([tl(f"Sbf{i}", [128, V], BF16) for i in range(2)])
    G["ones"] = tl("ones", [128, 128])
    S.op("pool", lambda e: e.memset(G["ones"][:, :], 1.0), writes=[G["ones"]])
    G["bdm"] = tl("bdm", [64, 64], BF16)
    S.dma("pool", G["bdm"][:, :], R["cst"]["bd"][:, :], writes=[G["bdm"]])
    G["colm"] = tl("colm", [128, 16, 64], BF16)
    S.dma("pool", G["colm"][:, :, :], R["cst"]["colmask"][:, :, :], writes=[G["colm"]])
    G["rowm"] = tl("rowm", [64, 16], BF16)
    S.dma("pool", G["rowm"][:, :], R["cst"]["rowmask"][:, :], writes=[G["rowm"]])
    G["s0"] = Rot([tl(f"s0{i}", [128, V]) for i in range(3)])
    G["s0b"] = Rot([tl(f"s0b{i}", [128, V], BF16) for i in range(3)])
    G["sn"] = Rot([tl(f"sn{i}", [128, V]) for i in range(3)])
    G["qm"] = tl("qm", [128, 16, 64], BF16)
    G["km"] = tl("km", [64, 16, 128], BF16)
    G["oh"] = Rot([tl(f"oh{i}", [128, NT, V], BF16) for i in range(1 if V == 256 else 2)])
    G["pss"] = tl("pss", [128, NT])
    G["prs"] = tl("prs", [128, NT])
    G["pjunk"] = tl("pjunk", [128, V], BF16)
    G["ptmp"] = Rot([tl(f"ptmp{i}", [128, V]) for i in range(2)])
    G["pob"] = Rot([tl(f"pob{i}", [128, V], BF16) for i in range(2)])
    G["post"] = Rot([tl(f"post{i}", [128, V // 128, 128], BF16) for i in range(3)])
    return G


def rec_head(S, G, hs, V, COEF, QS, gate, state_in, out_p, out_s, hook=None):
    qT, kT, v = hs["q"], hs["k"], hs["v"]
    mle_b, ident_b, psb, obuf = (G[k] for k in ("mle_b", "ident_b", "psb", "obuf"))
    p_att, p_o, p_ds = psb[4], psb[5], Rot([psb[0], psb[1]])
    Sst = G["Sst"]
    oh = G["oh"].next()
    for t in range(16):
        if hook is not None:
            hook()
        tok = slice(t * 128, (t + 1) * 128)
        c_, e1, e2_, e3, e4, s_ = (G[k].next() for k in ("cs", "E1", "E2", "E3", "E4", "sm"))
        q_, qx, qS, KA, KB, kS, kt_, at_ = (G[k].next() for k in ("qp", "qx", "qS", "KA", "KB", "kS", "ktok", "attm"))
        gap, gtt = gate(tok, 128)
        S.op("dve", lambda e: e.tensor_tensor_scan(out=c_[:, :], data0=G["ones"][:, :], data1=gap, initial=0.0,
                                                   op0=ALU.mult, op1=ALU.add), reads=[G["ones"], gtt], writes=[c_])
        S.op("dve", lambda e: e.tensor_scalar(out=s_[:, 4:8], in0=c_[:, 31:128:32], scalar1=-COEF, scalar2=None, op0=ALU.mult), reads=[c_], writes=[s_])
        S.op("dve", lambda e: e.tensor_scalar(out=s_[:, 8:12], in0=c_[:, 31:128:32], scalar1=COEF, scalar2=None, op0=ALU.mult), reads=[c_], writes=[s_])
        S.op("act", lambda e: e.activation(out=s_[:, 14:15], in_=c_[:, 95:96], func=AF.Exp, scale=COEF, bias=s_[:, 4:5]), reads=[c_, s_], writes=[s_])
        S.op("act", lambda e: e.activation(out=s_[:, 15:16], in_=c_[:, 127:128], func=AF.Exp, scale=COEF), reads=[c_], writes=[s_])
        lo, hi = slice(0, 64), slice(64, 128)
        S.op("act", lambda e: e.activation(out=e1[:, lo], in_=c_[:, lo], func=AF.Exp, scale=COEF, bias=s_[:, 4:5]), reads=[c_, s_], writes=[e1])
        S.op("act", lambda e: e.activation(out=e1[:, hi], in_=c_[:, hi], func=AF.Exp, scale=COEF, bias=s_[:, 6:7]), reads=[c_, s_], writes=[e1])
        S.op("act", lambda e: e.activation(out=e2_[:, lo], in_=c_[:, lo], func=AF.Exp, scale=-COEF, bias=s_[:, 8:9]), reads=[c_, s_], writes=[e2_])
        S.op("act", lambda e: e.activation(out=e2_[:, hi], in_=c_[:, hi], func=AF.Exp, scale=-COEF, bias=s_[:, 10:11]), reads=[c_, s_], writes=[e2_])
        S.op("act", lambda e: e.activation(out=e3[:, :], in_=c_[:, :], func=AF.Exp, scale=-COEF, bias=s_[:, 11:12]), reads=[c_, s_], writes=[e3])
        S.op("act", lambda e: e.activation(out=e4[:, :], in_=c_[:, :], func=AF.Exp, scale=COEF), reads=[c_], writes=[e4])
        S.op("dve", lambda e: e.scalar_tensor_tensor(out=q_[:, :], in0=qT[:, tok], scalar=QS, in1=e1[:, :],
                                                     op0=ALU.mult, op1=ALU.mult), reads=[qT, e1], writes=[q_])
        S.op("dve", lambda e: e.tensor_scalar(out=qx[:, :], in0=q_[:, hi], scalar1=s_[:, 14:15], scalar2=None, op0=ALU.mult),
             reads=[q_, s_], writes=[qx])
        S.op("dve", lambda e: e.tensor_tensor(out=KA[:, lo], in0=kT[:, t * 128:t * 128 + 64], in1=e2_[:, lo], op=ALU.mult),
             reads=[kT, e2_], writes=[KA])
        S.op("dve", lambda e: e.tensor_tensor(out=KB[:, hi], in0=kT[:, t * 128 + 64:(t + 1) * 128], in1=e2_[:, hi], op=ALU.mult),
             reads=[kT, e2_], writes=[KB])
        S.op("dve", lambda e: e.tensor_tensor(out=kS[:, :], in0=kT[:, tok], in1=e3[:, :], op=ALU.mult), reads=[kT, e3], writes=[kS])
        S.op("pe", lambda e: e.matmul(p_att[:, 0:64], lhsT=KA[:, :], rhs=q_[:, lo], start=True, stop=True),
             reads=[KA, q_], writes=[p_att])
        S.op("pe", lambda e: e.matmul(p_att[:, 64:128], lhsT=KA[:, :], rhs=qx[:, :], start=True, stop=False),
             reads=[KA, qx], writes=[p_att])
        S.op("pe", lambda e: e.matmul(p_att[:, 64:128], lhsT=KB[:, :], rhs=q_[:, hi], start=False, stop=True),
             reads=[KB, q_], writes=[p_att])
        S.op("dve", lambda e: e.tensor_tensor(out=at_[:, :], in0=p_att[:, 0:128], in1=mle_b[:, :], op=ALU.mult),
             reads=[p_att, mle_b], writes=[at_])
        pf = G["psbf"].next()
        S.op("pe", lambda e: e.transpose(out=pf[:, 0:128], in_=kS[:, :], identity=ident_b[:, :]),
             reads=[kS, ident_b], writes=[pf])
        S.op("act", lambda e: e.activation(out=kt_[:, :], in_=pf[:, 0:128], func=AF.Copy), reads=[pf], writes=[kt_])
        vv = v[:, t, :]
        if t > 0:
            sb_ = G["Sbf"].next()
            S.op("pool", lambda e: e.tensor_copy(out=sb_[:, :], in_=Sst[:, :]), reads=[Sst], writes=[sb_])
            S.op("dve", lambda e: e.scalar_tensor_tensor(out=qS[:, :], in0=qT[:, tok], scalar=QS, in1=e4[:, :],
                                                         op0=ALU.mult, op1=ALU.mult), reads=[qT, e4], writes=[qS])
        S.op("pe", lambda e: e.matmul(p_o[:, 0:V], lhsT=at_[:, :], rhs=vv, start=True, stop=(t == 0)),
             reads=[at_, v], writes=[p_o])
        if t > 0:
            S.op("pe", lambda e: e.matmul(p_o[:, 0:V], lhsT=qS[:, :], rhs=sb_[:, :], start=False, stop=True),
                 reads=[qS, sb_], writes=[p_o])
        S.op("act", lambda e: e.activation(out=oh[:, t, :], in_=p_o[:, 0:V], func=AF.Copy), reads=[p_o], writes=[oh])
        pd = p_ds.next()
        S.op("pe", lambda e: e.matmul(pd[:, 0:V], lhsT=kt_[:, :], rhs=vv, start=True, stop=True), reads=[kt_, v], writes=[pd])
        if t == 0:
            S.op("dve", lambda e: e.tensor_copy(out=Sst[:, :], in_=pd[:, 0:V]), reads=[pd], writes=[Sst])
        else:
            S.op("dve", lambda e: e.scalar_tensor_tensor(out=Sst[:, :], in0=Sst[:, :], scalar=s_[:, 15:16], in1=pd[:, 0:V],
                                                         op0=ALU.mult, op1=ALU.add), reads=[pd, s_, Sst], writes=[Sst])
    S.dma("sp", out_p, Sst[:, :], reads=[Sst], writes=[obuf("st_p")])

    tok = slice(T, TA)
    c_, e1, e2_ = G["cs"].next(), G["E1"].next(), G["E2"].next()
    q_, k_, kt_, at_ = G["qp"].next(), G["kp"].next(), G["ktok"].next(), G["attm"].next()
    qm, km, bdm, colm, rowm = G["qm"], G["km"], G["bdm"], G["colm"], G["rowm"]
    gap, gtt = gate(tok, TS)
    l3 = gap.rearrange("p (s t) -> p s t", t=4)
    c3 = c_[:, 0:TS].rearrange("p (s t) -> p s t", t=4)
    S.op("dve", lambda e: e.tensor_copy(out=c3[:, :, 0], in_=l3[:, :, 0]), reads=[gtt], writes=[c_])
    for i in range(1, 4):
        S.op("dve", lambda e: e.tensor_tensor(out=c3[:, :, i], in0=c3[:, :, i - 1], in1=l3[:, :, i], op=ALU.add),
             reads=[gtt, c_], writes=[c_])
    S.op("act", lambda e: e.activation(out=e1[:, 0:TS], in_=c_[:, 0:TS], func=AF.Exp, scale=COEF), reads=[c_], writes=[e1])
    S.op("act", lambda e: e.activation(out=e2_[:, 0:TS], in_=c_[:, 0:TS], func=AF.Exp, scale=-COEF), reads=[c_], writes=[e2_])
    S.op("dve", lambda e: e.scalar_tensor_tensor(out=q_[:, 0:TS], in0=qT[:, tok], scalar=QS, in1=e1[:, 0:TS],
                                                 op0=ALU.mult, op1=ALU.mult), reads=[qT, e1], writes=[q_])
    S.op("dve", lambda e: e.tensor_tensor(out=k_[:, 0:TS], in0=kT[:, tok], in1=e2_[:, 0:TS], op=ALU.mult),
         reads=[kT, e2_], writes=[k_])
    S.op("pe", lambda e: e.matmul(p_att[0:TS, 0:TS], lhsT=k_[:, 0:TS], rhs=q_[:, 0:TS], start=True, stop=True),
         reads=[k_, q_], writes=[p_att])
    S.op("dve", lambda e: e.tensor_tensor(out=at_[0:TS, 0:TS], in0=p_att[0:TS, 0:TS], in1=bdm[:, :], op=ALU.mult),
         reads=[p_att, bdm], writes=[at_])
    pf = G["psbf"].next()
    S.op("pe", lambda e: e.transpose(out=pf[0:TS, 0:128], in_=k_[:, 0:TS], identity=ident_b[:, :]),
         reads=[k_, ident_b], writes=[pf])
    S.op("act", lambda e: e.activation(out=kt_[0:TS, :], in_=pf[0:TS, 0:128], func=AF.Copy), reads=[pf], writes=[kt_])
    S.op("dve", lambda e: e.tensor_tensor(out=qm[:, :, :], in0=q_[:, 0:TS].unsqueeze(1).to_broadcast([128, 16, TS]),
                                          in1=colm[:, :, :], op=ALU.mult), reads=[q_, colm], writes=[qm])
    S.op("dve", lambda e: e.tensor_tensor(out=km[:, :, :], in0=kt_[0:TS, :].unsqueeze(1).to_broadcast([TS, 16, 128]),
                                          in1=rowm[:, :].unsqueeze(2).to_broadcast([TS, 16, 128]), op=ALU.mult),
         reads=[kt_, rowm], writes=[km])
    vv = v[0:TS, 16, :]
    S.op("pe", lambda e: e.matmul(p_o[0:TS, 0:V], lhsT=at_[0:TS, 0:TS], rhs=vv, start=True, stop=False),
         reads=[at_, v], writes=[p_o])
    for s in range(NS):
        a0, a0b, an = G["s0"].next(), G["s0b"].next(), G["sn"].next()
        S.dma("sp", a0[:, :], state_in(s), writes=[a0])
        S.op("pool", lambda e: e.tensor_copy(out=a0b[:, :], in_=a0[:, :]), reads=[a0], writes=[a0b])
        S.op("pe", lambda e: e.matmul(p_o[0:TS, 0:V], lhsT=qm[:, s, :], rhs=a0b[:, :], start=False, stop=(s == NS - 1)),
             reads=[qm, a0b], writes=[p_o])
        pd = p_ds.next()
        S.op("pe", lambda e: e.matmul(pd[:, 0:V], lhsT=km[:, s, :], rhs=vv, start=True, stop=True), reads=[km, v], writes=[pd])
        etot = e1[:, 4 * s + 3:4 * s + 4]
        S.op("act", lambda e: e.activation(out=a0[:, :], in_=a0[:, :], func=AF.Copy, scale=etot), reads=[a0, e1], writes=[a0])
        S.op("dve", lambda e: e.scalar_tensor_tensor(out=an[:, :], in0=pd[:, 0:V], scalar=etot, in1=a0[:, :],
                                                     op0=ALU.mult, op1=ALU.add), reads=[pd, e1, a0], writes=[an])
        S.dma("pool", out_s(s), an[:, :], reads=[an], writes=[obuf("st_s")])
    S.op("act", lambda e: e.activation(out=oh[0:TS, 16, :], in_=p_o[0:TS, 0:V], func=AF.Copy), reads=[p_o], writes=[oh])
    return oh


def post_head(S, G, oh, z, gnb, V, fc0, dbgh=False):
    pss, prs, pjunk = G["pss"], G["prs"], G["pjunk"]
    ident_b = G["ident_b"]
    nf = V // 128
    for t in range(NT):
        rows = 128 if t < 16 else TS
        S.op("act", lambda e: e.activation(out=pjunk[:rows, :], in_=oh[:rows, t, :], func=AF.Square, accum_out=pss[:rows, t:t + 1]),
             reads=[oh], writes=[pjunk, pss])
    S.op("act", lambda e: e.activation(out=prs[:TS, :], in_=pss[:TS, :], func=AF.Sqrt, scale=1.0 / V, bias=EPS), reads=[pss], writes=[prs])
    S.op("act", lambda e: e.activation(out=prs[TS:, 0:16], in_=pss[TS:, 0:16], func=AF.Sqrt, scale=1.0 / V, bias=EPS), reads=[pss], writes=[prs])
    S.op("dve", lambda e: e.reciprocal(out=prs[:TS, :], in_=prs[:TS, :]), reads=[prs], writes=[prs])
    S.op("dve", lambda e: e.reciprocal(out=prs[TS:, 0:16], in_=prs[TS:, 0:16]), reads=[prs], writes=[prs])
    for t in range(NT):
        rows = 128 if t < 16 else TS
        tmp, ob, st = G["ptmp"].next(), G["pob"].next(), G["post"].next()
        S.op("dve", lambda e: e.scalar_tensor_tensor(out=tmp[:rows, :], in0=oh[:rows, t, :], scalar=prs[:rows, t:t + 1], in1=gnb[:rows, :],
                                                     op0=ALU.mult, op1=ALU.mult), reads=[oh, prs, gnb], writes=[tmp])
        S.op("dve", lambda e: e.tensor_tensor(out=ob[:rows, :], in0=tmp[:rows, :], in1=z[:rows, t, :], op=ALU.mult),
             reads=[tmp, z], writes=[ob])
        pf = G["psbf"].next()
        for k in range(nf):
            S.op("pe", lambda e: e.transpose(out=pf[:, k * 128:k * 128 + rows], in_=ob[:rows, k * 128:(k + 1) * 128],
                                             identity=ident_b[:rows, :rows]), reads=[ob, ident_b], writes=[pf])
        S.op("act", lambda e: e.activation(out=st[:, :, 0:rows], in_=pf[:, 0:nf * 128].rearrange("p (k t) -> p k t", k=nf)[:, :, 0:rows],
                                           func=AF.Copy), reads=[pf], writes=[st])
        S.dma("sp", G["oT_scr"][t, :, fc0:fc0 + nf, 0:rows], st[:, :, 0:rows], reads=[st], writes=G["oT_b"][t][fc0:fc0 + nf])
        if dbgh and t == 0:
            G["dbg"]("pss", pss, pss[:, :], [128, NT])
            G["dbg"]("prs", prs, prs[:, :], [128, NT])
            G["dbg"]("tmp", tmp, tmp[:, :], [128, V])
            G["dbg"]("ob", ob, ob[:, :], [128, V], BF16)
            G["dbg"]("st", st, st[:, :, :], [128, nf, 128], BF16)


_PROG = {}


def _arr(w, ncols):
    return np.ascontiguousarray(w.reshape(16, 128, ncols).transpose(1, 0, 2))


def kernel(x_prompt, x_sample, cache_kv, cache_win, state_gla, state_hgrn, page_table,
           a_norm, a_w_in, a_gla_w2, a_gla_b, a_gla_gn, a_cmp_pe, a_cmp_w1, a_cmp_b1, a_cmp_w2, a_w_out,
           c_norm, c_w_in, c_lb_logits, c_gn, c_w_out, final_norm):
    f32 = np.float32
    asc = lambda a: np.ascontiguousarray(np.asarray(a, dtype=f32))
    x_prompt, x_sample = asc(x_prompt), asc(x_sample)
    if "nc" not in _PROG:
        _PROG["nc"] = build_program()
    nc = _PROG["nc"]
    consts = make_consts()
    w0 = np.asarray(a_w_in, f32)[0]
    wA = _arr(w0, 6696)
    cols = []
    for h in range(4):
        cols += [np.arange(3608 + h * 128, 3608 + (h + 1) * 128), np.arange(4120 + h * 128, 4120 + (h + 1) * 128),
                 np.arange(4632 + h * 256, 4632 + (h + 1) * 256), np.arange(5672 + h * 256, 5672 + (h + 1) * 256)]
    cols.append(np.arange(5656, 5672))
    wG = _arr(np.ascontiguousarray(w0[:, np.concatenate(cols)]), 3088)
    wc = np.asarray(c_w_in, f32)[0]
    cols = []
    for h in range(16):
        cols += [np.arange(k * 2048 + h * 128, k * 2048 + (h + 1) * 128) for k in range(4)]
    wC = _arr(np.ascontiguousarray(wc[:, np.concatenate(cols)]), 8192)
    wOA = _arr(np.asarray(a_w_out, f32)[0], 2048)
    wOC = _arr(np.asarray(c_w_out, f32)[0], 2048)
    a_norm_r = asc(np.asarray(a_norm, f32)[0].reshape(16, 128).T)
    c_norm_r = asc(np.asarray(c_norm, f32)[0].reshape(16, 128).T)
    lb_log = asc(np.asarray(c_lb_logits, f32).reshape(2, 16, 128).transpose(2, 0, 1))
    w2a = asc(np.concatenate([np.asarray(a_gla_w2, f32)[0], np.asarray(a_gla_b, f32)[0][None, :]], axis=0))
    cw1 = asc(np.asarray(a_cmp_w1, f32)[0].reshape(2, 32, 128, 256).transpose(0, 2, 1, 3))
    cw2 = asc(np.asarray(a_cmp_w2, f32)[0].reshape(2, 2, 128, 128).transpose(2, 0, 1, 3))
    cpe = asc(np.asarray(a_cmp_pe, f32)[0].transpose(2, 0, 1))
    cb1 = asc(np.asarray(a_cmp_b1, f32)[0].reshape(2, 2, 128).transpose(2, 0, 1))
    state_gla = np.asarray(state_gla, f32)
    state_hgrn = np.asarray(state_hgrn, f32)
    cache_win = np.asarray(cache_win, f32)
    ckv2 = asc(cache_kv).reshape(2560 * 128 * 2, 512)
    in_maps = []
    for c in range(NCORES):
        sl = slice(c * NS, (c + 1) * NS)
        m = {
            "x_p": x_prompt[c % 4],
            "x_s": asc(x_sample[sl].reshape(TS, D)),
            "cache_win": asc(cache_win[0, sl].reshape(NS, 512, 512)),
            "cache_kv": ckv2, "page_table": np.ascontiguousarray(np.asarray(page_table)[sl].astype(np.int32)),
            "state_gla": asc(state_gla[0, sl]),
            "state_hgrn": asc(state_hgrn[0, sl]),
            "a_norm": a_norm_r, "c_norm": c_norm_r, "f_norm": asc(final_norm),
            "wA": wA, "wG": wG, "wC": wC, "wOA": wOA, "wOC": wOC,
            "gla_w2a": w2a, "gla_gn": asc(np.asarray(a_gla_gn, f32)[0]), "c_gn": asc(np.asarray(c_gn, f32)[0]),
            "lb_log": lb_log, "cmp_w1": cw1, "cmp_w2": cw2, "cmp_pe": cpe, "cmp_b1": cb1,
        }
        for k, v in consts.items():
            m["c_" + k] = v
        in_maps.append(m)
    res = run_bass_kernel_spmd(nc, in_maps, core_ids=list(range(NCORES)))
    R = res.results
    B, SEQ, DB = 4, 2048, 128
    cat = lambda name, shp: np.concatenate([np.asarray(R[c][name], f32).reshape(shp) for c in range(NCORES)])
    stk = lambda name, shp: np.stack([np.asarray(R[c][name], f32).reshape(shp) for c in range(4)])
    y_p = stk("y_p", (SEQ, D))
    y_s = cat("y_s", (NS, 4, D))
    kv_p = stk("kv_p", (SEQ, 4, 2, 128))[None]
    kv_s = cat("kv_s", (NS, 4, 4, 2, 128))[None]
    win_p = stk("win_p", (512, 2, 2, 128))[None]
    win_s = cat("win_s", (NS, 512, 2, 2, 128))[None]
    gla_p = stk("gla_p", (4, 128, 256))[None]
    gla_s = cat("gla_s", (NS, 4, 128, 256))[None]
    hg_p = stk("hg_p", (16, 128, 128))[None]
    hg_s = cat("hg_s", (NS, 16, 128, 128))[None]
    return (y_p, y_s, kv_p, kv_s, win_p, win_s, gla_p, gla_s, hg_p, hg_s)
```

```python
import numpy as np
from contextlib import ExitStack
import concourse.bass as bass
import concourse.mybir as mybir
from concourse.bass_utils import run_bass_kernel_spmd

F32 = mybir.dt.float32
BF16 = mybir.dt.bfloat16
I32 = mybir.dt.int32
AF = mybir.ActivationFunctionType
ALU = mybir.AluOpType
AX = mybir.AxisListType

NCORES = 8
D = 2048
T = 2048
NS = 16
TS = 64
TA = T + TS
NT = 17
EPS = 1e-6
STAGE = 4
DEBUG = False
STOPAT = 99


class Buf:
    __slots__ = ("name", "w", "r")

    def __init__(self, name=""):
        self.name = name
        self.w = None
        self.r = []


class TT:
    def __init__(self, t, name):
        self.t = t
        self.b = Buf(name)

    def __getitem__(self, k):
        return self.t[k]


class Sched:
    ENG = ("pe", "act", "dve", "pool", "sp")
    NDMA = 12

    def __init__(self, nc, es):
        self.nc = nc
        self.es = es
        self.eng = {"pe": nc.tensor, "act": nc.scalar, "dve": nc.vector, "pool": nc.gpsimd, "sp": nc.sync}
        self.sems = {}
        self.cnt = {}
        for e in ("pe", "act", "dve", "pool"):
            self.sems[e] = es.enter_context(nc.semaphore("s_" + e))
            self.cnt[e] = 0
        for q in ("sp", "act", "pool"):
            for i in range(self.NDMA):
                k = f"d_{q}{i}"
                self.sems[k] = es.enter_context(nc.semaphore(k))
                self.cnt[k] = 0
        self.dma_rr = {"sp": 0, "act": 0, "pool": 0}
        self.waited = {e: {} for e in self.ENG}
        self.n_inst = 0
        self.n_wait = 0
        self.uid = 0
        self.freed = {}
        self._scopes = []

    def scope(self):
        from contextlib import contextmanager

        @contextmanager
        def cm():
            rec = []
            self._scopes.append(rec)
            try:
                with ExitStack() as es:
                    yield es
            finally:
                self._scopes.pop()
                for tt in rec:
                    toks = list(tt.b.r) + ([tt.b.w] if tt.b.w else [])
                    for k, v in toks:
                        if self.freed.get(k, 0) < v:
                            self.freed[k] = v
        return cm()

    def tile(self, name, shape, dtype, es=None):
        self.uid += 1
        t = (es or self.es).enter_context(self.nc.sbuf_tensor(f"{name}_{self.uid}", list(shape), dtype))
        tt = TT(t, name)
        tt.b.r = list(self.freed.items())
        if es is not None and self._scopes:
            self._scopes[-1].append(tt)
        return tt

    def ptile(self, name, shape, dtype=F32, es=None):
        self.uid += 1
        t = (es or self.es).enter_context(self.nc.psum_tensor(f"{name}_{self.uid}", list(shape), dtype))
        return TT(t, name)

    def _deps(self, reads, writes):
        deps = {}

        def add(t):
            if t is None:
                return
            k, v = t
            if deps.get(k, 0) < v:
                deps[k] = v
        for b in reads:
            add(b.w)
        for b in writes:
            add(b.w)
            for t in b.r:
                add(t)
        return deps

    def _wait(self, e, deps):
        eng = self.eng[e]
        wd = self.waited[e]
        for k, v in deps.items():
            if wd.get(k, 0) < v:
                eng.wait_ge(self.sems[k], v)
                wd[k] = v
                self.n_wait += 1

    def _commit(self, tok, reads, writes):
        for b in reads:
            b.r.append(tok)
            if len(b.r) > 64:
                m = {}
                for k, v in b.r:
                    if m.get(k, 0) < v:
                        m[k] = v
                b.r = list(m.items())
        for b in writes:
            b.w = tok
            b.r = []

    @staticmethod
    def _bl(xs):
        out = []
        for x in xs:
            b = x.b if isinstance(x, (TT, View)) else x
            if isinstance(b, (list, tuple)):
                out.extend(b)
            else:
                out.append(b)
        return out

    def op(self, e, fn, reads=(), writes=()):
        reads = self._bl(reads)
        writes = self._bl(writes)
        deps = self._deps(reads, writes)
        if e == "pe":
            deps.pop("pe", None)
        self._wait(e, deps)
        ins = fn(self.eng[e])
        self.cnt[e] += 1
        ins.then_inc(self.sems[e], 1)
        self._commit((e, self.cnt[e]), reads, writes)
        self.n_inst += 1
        return ins

    def dma(self, q, out, in_, reads=(), writes=(), fn=None, **kw):
        reads = self._bl(reads)
        writes = self._bl(writes)
        deps = self._deps(reads, writes)
        self._wait(q, deps)
        i = self.dma_rr[q]
        self.dma_rr[q] = (i + 1) % self.NDMA
        k = f"d_{q}{i}"
        if fn is None:
            ins = self.eng[q].dma_start(out=out, in_=in_, **kw)
        else:
            ins = fn(self.eng[q])
        self.cnt[k] += 16
        ins.then_inc(self.sems[k], 16)
        self._commit((k, self.cnt[k]), reads, writes)
        self.n_inst += 1

    def finish(self, bufs):
        deps = self._deps([], self._bl(bufs))
        self._wait("sp", deps)


class View:
    def __init__(self, ap, b):
        self.t = ap
        self.b = b

    def __getitem__(self, k):
        return self.t[k]


class Rot:
    def __init__(self, items):
        self.items = items
        self.i = 0

    def next(self):
        x = self.items[self.i]
        self.i = (self.i + 1) % len(self.items)
        return x


def make_consts():
    c = {}
    c["ident"] = np.eye(128, dtype=np.float32)
    j = np.arange(128)[:, None]
    i = np.arange(128)[None, :]
    c["mle"] = (j <= i).astype(np.float32)
    c["mgt"] = (j > i).astype(np.float32)
    j6 = np.arange(64)[:, None]
    i6 = np.arange(64)[None, :]
    c["bd"] = ((j6 // 4 == i6 // 4) & (j6 <= i6)).astype(np.float32)
    cm = (np.arange(64)[None, :] // 4 == np.arange(16)[:, None]).astype(np.float32)
    c["colmask"] = np.broadcast_to(cm[None], (128, 16, 64)).copy()
    c["rowmask"] = (np.arange(64)[:, None] // 4 == np.arange(16)[None, :]).astype(np.float32)
    def c2s(nc_, ns_):
        cst = np.arange(nc_)[:, None] * 16
        sst = np.arange(ns_)[None, :] * 64
        ov = np.clip(np.minimum(cst + 32, sst + 64) - np.maximum(cst, sst), 0, None)
        return (ov / 16).astype(np.float32)
    c["c2sp"] = c2s(127, 32)
    c["c2ss"] = c2s(127, 33)
    def eexp(ns_, nk_):
        key = np.arange(nk_ * 128)
        return (np.arange(ns_)[:, None] == (key[None, :] // 64)).astype(np.float32)
    c["eexp"] = eexp(32, 16)
    c["eexs"] = eexp(33, 17)
    x = np.arange(63)[None, :] - 31 - (np.arange(128)[:, None] >= 64)
    c["wsel"] = np.where(x > 0, -1e9, np.where(x >= -1, 1e9, 0.0)).astype(np.float32)
    c["rsum"] = (np.arange(16)[:, None] % 4 == np.arange(4)[None, :]).astype(np.float32)
    return c


CONST_SHAPES = {k: v.shape for k, v in make_consts().items()}


def build_program(stage=STAGE):
    nc = bass.Bass("TRN2", target_bir_lowering=False)

    def din(name, shape, dt=F32):
        return nc.dram_tensor(name, list(shape), dt, kind="ExternalInput").ap()

    def dout(name, shape, dt=F32):
        return nc.dram_tensor(name, list(shape), dt, kind="ExternalOutput").ap()

    def dscr(name, shape, dt=F32):
        return nc.dram_tensor(name, list(shape), dt, kind="ExternalOutput" if DEBUG else "Internal").ap()

    x_p = din("x_p", [T, D])
    x_s = din("x_s", [TS, D])
    cache_win = din("cache_win", [NS, 512, 512])
    cache_kv = din("cache_kv", [2560 * 128 * 2, 512])
    page_table = din("page_table", [NS, 16], I32)
    state_gla = din("state_gla", [NS, 4, 128, 256])
    state_hgrn = din("state_hgrn", [NS, 16, 128, 128])
    a_norm = din("a_norm", [128, 16])
    c_norm = din("c_norm", [128, 16])
    f_norm = din("f_norm", [D])
    wA = din("wA", [128, 16, 6696])
    wG = din("wG", [128, 16, 3088])
    wC = din("wC", [128, 16, 8192])
    wOA = din("wOA", [128, 16, 2048])
    wOC = din("wOC", [128, 16, 2048])
    gla_w2a = din("gla_w2a", [17, 512])
    gla_gn = din("gla_gn", [256])
    c_gn = din("c_gn", [128])
    lb_log = din("lb_log", [128, 2, 16])
    cmp_w1 = din("cmp_w1", [2, 128, 32, 256])
    cmp_w2 = din("cmp_w2", [128, 2, 2, 128])
    cmp_pe = din("cmp_pe", [128, 2, 32])
    cmp_b1 = din("cmp_b1", [128, 2, 2])
    cst = {k: din("c_" + k, list(s)) for k, s in CONST_SHAPES.items()}

    y_p = dout("y_p", [T, D])
    y_s = dout("y_s", [TS, D])
    kv_p = dout("kv_p", [T, 1024])
    kv_s = dout("kv_s", [TS, 1024])
    win_p = dout("win_p", [512, 512])
    win_s = dout("win_s", [NS, 512, 512])
    gla_p = dout("gla_p", [4, 128, 256])
    gla_s = dout("gla_s", [NS, 4, 128, 256])
    hg_p = dout("hg_p", [16, 128, 128])
    hg_s = dout("hg_s", [NS, 16, 128, 128])

    oT_scr = dscr("oT_scr", [NT, 128, 16, 128], BF16)
    g_scr = dscr("g_scr", [TS, 24])
    z_scr = dscr("z_scr", [TS, 1024], BF16)
    x1_scr = dscr("x1_scr", [TA, D])
    y_scr = dscr("y_scr", [TA, D])
    oT_b = [[Buf(f"oT{t}_{f}") for f in range(16)] for t in range(NT)]
    x1_b = [[Buf(f"x1_{t}_{k}") for k in range(4)] for t in range(NT)]
    ys_b = [[Buf(f"ys_{t}_{k}") for k in range(4)] for t in range(NT)]

    outs = []
    dbg_n = [0]

    with ExitStack() as es:
        S = Sched(nc, es)

        def dbg(name, tt, ap, shape, dt=F32):
            if not DEBUG:
                return
            dbg_n[0] += 1
            d = nc.dram_tensor(f"dbg_{name}", list(shape), dt, kind="ExternalOutput").ap()
            b = Buf(name)
            outs.append(b)
            S.dma("sp", d, ap, reads=[tt], writes=[b])

        def obuf(name):
            b = Buf(name)
            outs.append(b)
            return b

        hT_ref = [None]
        wbuf = Rot([S.tile(f"wbuf{i}", [128, 16, 512], BF16) for i in range(2)])
        ident_f = S.tile("ident_f", [128, 128], F32)
        ident_b = S.tile("ident_b", [128, 128], BF16)
        mle_b = S.tile("mle_b", [128, 128], BF16)
        normA = S.tile("normA", [128, 16], F32)
        normC = S.tile("normC", [128, 16], F32)
        PP = [S.ptile(f"pp{i}", [128, 1024], F32) for i in range(4)]
        bankb = [Buf(f"bank{i}") for i in range(8)]
        psb = [View(PP[i // 2][:, (i % 2) * 512:(i % 2) * 512 + 512], bankb[i]) for i in range(8)]
        psA = View(PP[2][:, :], [bankb[4], bankb[5]])
        psB = View(PP[3][:, :], [bankb[6], bankb[7]])
        psbf = Rot([View(PP[3][:, k * 512:(k + 1) * 512].bitcast(BF16), bankb[6 + k]) for k in range(2)])

        S.dma("sp", ident_f[:], cst["ident"][:, :], writes=[ident_f])
        S.dma("pool", ident_b[:], cst["ident"][:, :], writes=[ident_b])
        S.dma("pool", mle_b[:], cst["mle"][:, :], writes=[mle_b])
        S.dma("sp", normA[:], a_norm[:, :], writes=[normA])
        S.dma("sp", normC[:], c_norm[:, :], writes=[normC])

        for s in range(NS):
            S.dma("sp", win_s[s, 0:508, :], cache_win[s, 4:512, :], writes=[obuf("win_s")])

        def norm_pass(src_fn, normw, src_deps):
            with S.scope() as es1:
                xrot = Rot([S.tile(f"xt{i}", [128, D], F32, es1) for i in range(2)])
                ssr = Rot([S.tile(f"ss{i}", [128, 1], F32, es1) for i in range(2)])
                rsr = Rot([S.tile(f"rs{i}", [128, 1], F32, es1) for i in range(2)])
                rstdr = Rot([S.tile(f"rstd{i}", [128, 1], F32, es1) for i in range(2)])
                junk = S.tile("junk", [128, D], BF16, es1)
                prot = Rot(psb[0:4])
                for t in range(NT):
                    rows = 128 if t < 16 else TS
                    tok0 = t * 128
                    xt, ss, rs, rstd = xrot.next(), ssr.next(), rsr.next(), rstdr.next()
                    S.dma("sp", xt[:rows, :], src_fn(t, rows), reads=src_deps(t), writes=[xt])
                    S.op("act", lambda e: e.activation(out=junk[:rows, :], in_=xt[:rows, :], func=AF.Square,
                                                       accum_out=ss[:rows, 0:1]), reads=[xt], writes=[junk, ss])
                    S.op("act", lambda e: e.activation(out=rs[:rows, :], in_=ss[:rows, :], func=AF.Sqrt,
                                                       scale=1.0 / D, bias=EPS), reads=[ss], writes=[rs])
                    S.op("dve", lambda e: e.reciprocal(out=rstd[:rows, :], in_=rs[:rows, :]), reads=[rs], writes=[rstd])
                    S.op("act", lambda e: e.activation(out=xt[:rows, :], in_=xt[:rows, :], func=AF.Copy,
                                                       scale=rstd[:rows, 0:1]), reads=[xt, rstd], writes=[xt])
                    for cq in range(4):
                        pb = prot.next()
                        for k in range(4):
                            c = 4 * cq + k
                            S.op("pe", lambda e: e.transpose(out=pb[:, k * 128:k * 128 + rows],
                                                             in_=xt[:rows, c * 128:(c + 1) * 128],
                                                             identity=ident_f[:rows, :rows]),
                                 reads=[xt, ident_f], writes=[pb])
                        S.op("dve", lambda e: e.tensor_tensor(
                            out=hT_ref[0][:, 4 * cq:4 * cq + 4, tok0:tok0 + rows],
                            in0=pb[:, :].rearrange("p (k t) -> p k t", k=4)[:, :, :rows],
                            in1=normw[:, 4 * cq:4 * cq + 4].unsqueeze(2).to_broadcast([128, 4, rows]),
                            op=ALU.mult), reads=[pb, normw], writes=[hT_ref[0]])


        def load_w(col0, n, src):
            wb = wbuf.next()
            S.dma("pool", wb[:, :, 0:n], src[:, :, col0:col0 + n], writes=[wb])
            return wb

        prj = Rot(psb[0:4])
        evq = Rot(["act", "dve"])

        def proj_tok_g(wb, n, consume, tiles=range(NT), wc0=0):
            for t in tiles:
                rows = 128 if t < 16 else TS
                pb = prj.next()
                for c in range(16):
                    S.op("pe", lambda e: e.matmul(pb[:rows, 0:n], lhsT=hT_ref[0][:, c, t * 128:t * 128 + rows],
                                                  rhs=wb[:, c, wc0:wc0 + n], start=(c == 0), stop=(c == 15)),
                         reads=[hT_ref[0], wb], writes=[pb])
                consume(t, rows, pb)
                yield

        def proj_feat_g(wb, wc0, m, consume, chunks=range(5)):
            for j in chunks:
                n = 512 if j < 4 else TS
                pb = prj.next()
                for c in range(16):
                    S.op("pe", lambda e: e.matmul(pb[:m, 0:n], lhsT=wb[:, c, wc0:wc0 + m],
                                                  rhs=hT_ref[0][:, c, j * 512:j * 512 + n], start=(c == 0), stop=(c == 15)),
                         reads=[hT_ref[0], wb], writes=[pb])
                consume(j, n, pb)
                yield

        def proj_tok(*a, **k):
            for _ in proj_tok_g(*a, **k):
                pass

        def proj_feat(*a, **k):
            for _ in proj_feat_g(*a, **k):
                pass

        def evac(out_ap, in_ap, reads, writes, func=None):
            q = "act" if func is not None else evq.next()
            if q == "act":
                S.op("act", lambda e: e.activation(out=out_ap, in_=in_ap, func=func or AF.Copy), reads=reads, writes=writes)
            else:
                S.op("dve", lambda e: e.tensor_copy(out=out_ap, in_=in_ap), reads=reads, writes=writes)

        OQ, OKV, OG, OZA = 0, 1024, 2560, 2584

        SC = 128 ** -0.5
        with S.scope() as esN:
            gts = S.tile("gts", [128, NT, 24], F32, esN)
            qTs = S.tile("qTs", [128, 8, TS], BF16, esN)
            zs = S.tile("zs", [128, 1024], BF16, esN)
            vnew_src = S.tile("vnew_src", [TS, 2, 2, 130], BF16, esN)
            S.op("pool", lambda e: e.memset(vnew_src[:, :, :, 128:130], 1.0), writes=[vnew_src])
            mgt_b = S.tile("mgt_b", [128, 128], BF16, esN)
            S.dma("pool", mgt_b[:, :], cst["mgt"][:, :], writes=[mgt_b])
            kselTs = S.tile("kselTs", [128, 2, TS], BF16, esN)
            kwinTs = S.tile("kwinTs", [128, 2, TS], BF16, esN)
            with S.scope() as esH:
                hT_ref[0] = S.tile("hT", [128, 16, TA], BF16, esH)
                norm_pass(lambda t, rows: (x_p[t * 128:(t + 1) * 128, :] if t < 16 else x_s[:, :]), normA, lambda t: [])
                R = dict(mle_b=mle_b, ident_b=ident_b, ident_f=ident_f, psb=psb, psbf=psbf, cst=cst, obuf=obuf,
                         oT_scr=oT_scr, oT_b=oT_b, dbg=dbg)

                with S.scope() as esG:
                    lrT = S.tile("lrT", [17, TA], F32, esG)
                    w2a = S.tile("w2a", [17, 512], F32, esG)
                    S.dma("sp", w2a[:, :], gla_w2a[:, :], writes=[w2a])
                    for j in range(0, TA, 128):
                        n = min(128, TA - j)
                        S.dma("sp", lrT[16:17, j:j + n], cst["mle"][0:1, 0:n], writes=[lrT])
                    wb = load_w(3072, 16, wG)
                    proj_feat(wb, 0, 16, lambda j, n, pb: evac(lrT[0:16, j * 512:j * 512 + n], pb[0:16, 0:n], [pb], [lrT]))
                    zb_shared = S.tile("zb", [128, NT, 256], BF16, esG)
                    hsets = Rot([dict(q=S.tile(f"qbT{i}", [128, TA], BF16, esG), k=S.tile(f"kbT{i}", [128, TA], BF16, esG),
                                      v=S.tile(f"vb{i}", [128, NT, 256], BF16, esG), z=zb_shared)
                                 for i in range(2)])
                    G = rec_setup(S, esG, R, V=256)
                    gnb = S.tile("gnb", [128, 256], F32, esG)
                    S.dma("sp", gnb[:, :], gla_gn.partition_broadcast(128), writes=[gnb])
                    lg = Rot([S.tile(f"lg{i}", [128, 128], F32, esG) for i in range(2)])

                    for h in range(4):
                        hs = hsets.next()
                        wb = load_w(h * 768, 512, wG)
                        proj_feat(wb, 0, 128, lambda j, n, pb: evac(hs["q"][:, j * 512:j * 512 + n], pb[:, 0:n], [pb], [hs["q"]]))
                        proj_feat(wb, 128, 128, lambda j, n, pb: evac(hs["k"][:, j * 512:j * 512 + n], pb[:, 0:n], [pb], [hs["k"]]))
                        proj_tok(wb, 256, lambda t, rows, pb: evac(hs["v"][:rows, t, :], pb[:rows, 0:256], [pb], [hs["v"]]), wc0=256)
                        wb = load_w(h * 768 + 512, 256, wG)
                        proj_tok(wb, 256, lambda t, rows, pb: evac(hs["z"][:rows, t, :], pb[:rows, 0:256], [pb], [hs["z"]], func=AF.Silu))

                        def gate(tok, n, h=h):
                            l_ = lg.next()
                            p_g = psb[3]
                            S.op("pe", lambda e: e.matmul(p_g[:, 0:n], lhsT=w2a[0:17, h * 128:(h + 1) * 128], rhs=lrT[0:17, tok],
                                                          start=True, stop=True), reads=[w2a, lrT], writes=[p_g])
                            S.op("act", lambda e: e.activation(out=l_[:, 0:n], in_=p_g[:, 0:n], func=AF.Exp, scale=-1.0), reads=[p_g], writes=[l_])
                            S.op("act", lambda e: e.activation(out=l_[:, 0:n], in_=l_[:, 0:n], func=AF.Ln, bias=1.0, scale=1.0), reads=[l_], writes=[l_])
                            return l_[:, 0:n], l_
                        oh = rec_head(S, G, hs, V=256, COEF=-1.0 / 16.0, QS=128 ** -0.5, gate=gate,
                                      state_in=lambda s: state_gla[s, h, :, :], out_p=gla_p[h, :, :], out_s=lambda s: gla_s[s, h, :, :])
                        if h == 0:
                            dbg("gnb", gnb, gnb[:, :], [128, 256])
                            dbg("oh", oh, oh[:, 0, :], [128, 256], BF16)
                            dbg("zb", hs["z"], hs["z"][:, 0, :], [128, 256], BF16)
                        post_head(S, G, oh, hs["z"], gnb, V=256, fc0=8 + 2 * h, dbgh=(h == 0))
                        if STOPAT <= 1:
                            break

                with S.scope() as esNP:
                    vsel = S.tile("vsel", [128, NT, 2, 130], BF16, esNP)
                    vwin = S.tile("vwin", [128, NT, 2, 130], BF16, esNP)
                    S.op("pool", lambda e: e.memset(vsel[:, :, :, 128:130], 1.0), writes=[vsel])
                    S.op("pool", lambda e: e.memset(vwin[:, :, :, 128:130], 1.0), writes=[vwin])
                    kselT = S.tile("kselT", [128, 2, T], BF16, esNP)
                    kwinT = S.tile("kwinT", [128, 2, T], BF16, esNP)
                    kcT = S.tile("kcT", [128, 2, 128], BF16, esNP)
                    vca = S.tile("vca", [128, 2, 162], BF16, esNP)
                    eexp = S.tile("eexp", [32, 16, 128], BF16, esNP)
                    S.dma("pool", eexp[:, :, :], cst["eexp"].rearrange("s (k j) -> s k j", k=16), writes=[eexp])
                    wsel = S.tile("wsel", [128, 63], F32, esNP)
                    S.dma("sp", wsel[:, :], cst["wsel"][:, :], writes=[wsel])
                    S.op("pool", lambda e: e.memset(vca[:, :, 128:129], 1.0), writes=[vca])
                    for g in range(2):
                        S.dma("pool", vca[0:127, g, 129:161], cst["c2sp"][:, :], writes=[vca])

                    with S.scope() as esKV:
                        stg = Rot([S.tile(f"stg{i}", [128, 512], F32, esKV) for i in range(3)])
                        for blk in range(3):
                            wb = load_w(OKV + blk * 512, 512, wA)

                            def cons(t, rows, pb, blk=blk):
                                st = stg.next()
                                evac(st[:rows, :], pb[:rows, :], [pb], [st])
                                if blk < 2:
                                    dst = (kv_p[t * 128:t * 128 + rows, blk * 512:(blk + 1) * 512] if t < 16
                                           else kv_s[:, blk * 512:(blk + 1) * 512])
                                    S.dma("sp", dst, st[:rows, :], reads=[st], writes=[obuf("kv")])
                                else:
                                    if t >= 12 and t < 16:
                                        S.dma("sp", win_p[(t - 12) * 128:(t - 11) * 128, :], st[:rows, :], reads=[st], writes=[obuf("win")])
                                    elif t == 16:
                                        for s in range(NS):
                                            S.dma("sp", win_s[s, 508:512, :], st[4 * s:4 * s + 4, :], reads=[st], writes=[obuf("wins")])
                                if blk >= 1:
                                    vt = vsel if blk == 1 else vwin
                                    S.op("pool", lambda e: e.tensor_copy(out=vt[:rows, t, :, 0:128],
                                                                         in_=st[:rows, 256:512].rearrange("p (g d) -> p g d", g=2)),
                                         reads=[st], writes=[vt])
                                    if t == 16:
                                        S.op("pool", lambda e: e.tensor_copy(out=vnew_src[:, blk - 1, :, 0:128],
                                                                             in_=st[:rows, 256:512].rearrange("p (g d) -> p g d", g=2)),
                                             reads=[st], writes=[vnew_src])
                            proj_tok(wb, 512, cons)

                    if stage >= 3:
                        with S.scope() as esCmp:
                            kcmpT = S.tile("kcmpT", [128, 2, T], BF16, esCmp)
                            vcmpT = S.tile("vcmpT", [128, 2, T], BF16, esCmp)
                            wb = load_w(OKV, 512, wA)
                            for i, dstt in enumerate((kcmpT, kcmpT, vcmpT, vcmpT)):
                                proj_feat(wb, i * 128, 128, lambda j, n, pb, dstt=dstt, i=i: evac(dstt[:, i % 2, j * 512:j * 512 + n], pb[:, 0:n], [pb], [dstt]),
                                          chunks=range(4))
                            wb = load_w(OKV + 512, 256, wA)
                            for g in range(2):
                                proj_feat(wb, g * 128, 128, lambda j, n, pb, g=g: (evac(kselT[:, g, j * 512:j * 512 + n], pb[:, 0:n], [pb], [kselT]) if j < 4
                                                                                  else evac(kselTs[:, g, :], pb[:, 0:n], [pb], [kselTs])))
                            wb = load_w(OKV + 1024, 256, wA)
                            for g in range(2):
                                proj_feat(wb, g * 128, 128, lambda j, n, pb, g=g: (evac(kwinT[:, g, j * 512:j * 512 + n], pb[:, 0:n], [pb], [kwinT]) if j < 4
                                                                                  else evac(kwinTs[:, g, :], pb[:, 0:n], [pb], [kwinTs])))
                            w1b = S.tile("w1b", [128, 32, 256], BF16, esCmp)
                            w2b = S.tile("w2b", [128, 2, 2, 128], BF16, esCmp)
                            peb = S.tile("peb", [128, 2, 32], BF16, esCmp)
                            b1t = S.tile("b1t", [128, 2, 2], F32, esCmp)
                            hb = S.tile("hb", [128, 2, 2], F32, esCmp)
                            gT = Rot([S.tile(f"gT{i}", [128, 2, 128], BF16, esCmp) for i in range(2)])
                            S.dma("pool", w2b[:, :, :, :], cmp_w2[:, :, :, :], writes=[w2b])
                            S.dma("pool", peb[:, :, :], cmp_pe[:, :, :], writes=[peb])
                            S.dma("sp", b1t[:, :, :], cmp_b1[:, :, :], writes=[b1t])
                            for kv in range(2):
                                S.dma("pool", w1b[:, :, :], cmp_w1[kv, :, :, :], writes=[w1b])
                                xT = kcmpT if kv == 0 else vcmpT
                                pp = psb[3]
                                for half in range(2):
                                    for rp in range(32):
                                        S.op("pe", lambda e: e.matmul(pp[:, half:half + 1], lhsT=w1b[:, rp, half * 128:(half + 1) * 128],
                                                                      rhs=peb[:, kv, rp:rp + 1], start=(rp == 0), stop=(rp == 31)),
                                             reads=[w1b, peb], writes=[pp])
                                S.op("dve", lambda e: e.tensor_tensor(out=hb[:, kv, :], in0=pp[:, 0:2], in1=b1t[:, kv, :], op=ALU.add),
                                     reads=[pp, b1t], writes=[hb])
                                for g in range(2):
                                    gt = gT.next()
                                    for half in range(2):
                                        ph = prj.next()
                                        for rp in range(32):
                                            r_, p_ = rp // 16, rp % 16
                                            st0 = 16 * r_ + p_
                                            S.op("pe", lambda e: e.matmul(ph[:, 0:127], lhsT=w1b[:, rp, half * 128:(half + 1) * 128],
                                                                          rhs=xT[:, g, st0:st0 + 16 * 126 + 1:16], start=(rp == 0), stop=(rp == 31)),
                                                 reads=[w1b, xT], writes=[ph])
                                        S.op("act", lambda e: e.activation(out=gt[:, half, 0:127], in_=ph[:, 0:127], func=AF.Gelu_apprx_tanh,
                                                                           bias=hb[:, kv, half:half + 1]), reads=[ph, hb], writes=[gt])
                                    po = prj.next()
                                    if kv == 0:
                                        for half in range(2):
                                            S.op("pe", lambda e: e.matmul(po[:, 0:127], lhsT=w2b[:, 0, half, :], rhs=gt[:, half, 0:127],
                                                                          start=(half == 0), stop=(half == 1)), reads=[w2b, gt], writes=[po])
                                        evac(kcT[:, g, 0:127], po[:, 0:127], [po], [kcT])
                                    else:
                                        for half in range(2):
                                            S.op("pe", lambda e: e.matmul(po[0:127, 0:128], lhsT=gt[:, half, 0:127], rhs=w2b[:, 1, half, :],
                                                                          start=(half == 0), stop=(half == 1)), reads=[w2b, gt], writes=[po])
                                        evac(vca[0:127, g, 0:128], po[0:127, 0:128], [po], [vca])

                        wb = load_w(OG, 24, wA)
                        proj_tok(wb, 24, lambda t, rows, pb: evac(gts[:rows, t, :], pb[:rows, 0:24], [pb], [gts], func=AF.Sigmoid))

                        for g in range(2):
                            with S.scope() as esQ:
                                qTg = S.tile("qTg", [128, 4, T], BF16, esQ)
                                zg = S.tile("zg", [128, 16, 512], BF16, esQ)
                                wb = load_w(OQ + g * 512, 512, wA)
                                for r in range(4):
                                    def qcons(j, n, pb, r=r):
                                        if j < 4:
                                            evac(qTg[:, r, j * 512:j * 512 + n], pb[:, 0:n], [pb], [qTg])
                                        else:
                                            evac(qTs[:, 4 * g + r, :], pb[:, 0:n], [pb], [qTs])
                                    proj_feat(wb, r * 128, 128, qcons)
                                wb = load_w(OZA + g * 512, 512, wA)

                                def zcons(t, rows, pb):
                                    if t < 16:
                                        evac(zg[:, t, :], pb[:, :], [pb], [zg], func=AF.Silu)
                                    else:
                                        evac(zs[:rows, g * 512:(g + 1) * 512], pb[:rows, :], [pb], [zs], func=AF.Silu)
                                proj_tok(wb, 512, zcons)
                                nsa_prompt(S, esQ, g, dict(qTg=qTg, zg=zg, kcT=kcT, vca=vca, kselT=kselT, kwinT=kwinT, vsel=vsel, vwin=vwin,
                                                           gts=gts, wsel=wsel, eexp=eexp, mle_b=mle_b, mgt_b=mgt_b, ident_b=ident_b,
                                                           psb=psb, psA=psA, psB=psB, oT_scr=oT_scr, oT_b=oT_b, SC=SC))
            if stage >= 4:
                with S.scope() as esS:
                    nsa_sample(S, esS, nc, dict(qTs=qTs, zs=zs, gts=gts, kselT=kselTs, kwinT=kwinTs, vnew_src=vnew_src, mle_b=mle_b,
                                                mgt_b=mgt_b, ident_b=ident_b, ident_f=ident_f, psb=psb, wbuf=wbuf, cst=cst, SC=SC,
                                                cache_kv=cache_kv, cache_win=cache_win, page_table=page_table, cmp_w1=cmp_w1,
                                                cmp_w2=cmp_w2, cmp_pe=cmp_pe, cmp_b1=cmp_b1, g_scr=g_scr, z_scr=z_scr,
                                                oT_scr=oT_scr, oT_b=oT_b, evac=evac))

            if stage < 3:
                with S.scope() as esZ:
                    zt = S.tile("zt", [128, 8, 128], BF16, esZ)
                    S.op("pool", lambda e: e.memset(zt[:, :, :], 0.0), writes=[zt])
                    for t in range(NT):
                        S.dma("sp", oT_scr[t, :, 0:8, :], zt[:, :, :], reads=[zt], writes=oT_b[t][0:8])
            elif stage < 4:
                with S.scope() as esZ:
                    zt = S.tile("zt", [128, 8, 128], BF16, esZ)
                    S.op("pool", lambda e: e.memset(zt[:, :, :], 0.0), writes=[zt])
                    S.dma("sp", oT_scr[16, :, 0:8, :], zt[:, :, :], reads=[zt], writes=oT_b[16][0:8])

        def wout_phase(wsrc, res_fn, res_deps, dst, dst_b):
            with S.scope() as e3:
                otr = Rot([S.tile(f"ot{i}", [128, 16, 128], BF16, e3) for i in range(4)])
                xr = Rot([S.tile(f"xr{i}", [128, 512], F32, e3) for i in range(5)])
                for blk in range(4):
                    wb = load_w(blk * 512, 512, wsrc)
                    for t in range(NT):
                        rows = 128 if t < 16 else TS
                        ot, xt = otr.next(), xr.next()
                        S.dma("sp", ot[:, :, :], oT_scr[t, :, :, :], reads=oT_b[t], writes=[ot])
                        S.dma("sp", xt[:rows, :], res_fn(t, rows, blk), reads=res_deps(t, blk), writes=[xt])
                        pb = prj.next()
                        for c in range(16):
                            S.op("pe", lambda e: e.matmul(pb[:rows, 0:512], lhsT=ot[:, c, 0:rows], rhs=wb[:, c, 0:512],
                                                          start=(c == 0), stop=(c == 15)), reads=[ot, wb], writes=[pb])
                        S.op("dve", lambda e: e.tensor_tensor(out=xt[:rows, :], in0=pb[:rows, 0:512], in1=xt[:rows, :], op=ALU.add),
                             reads=[pb, xt], writes=[xt])
                        S.dma("pool", dst[t * 128:t * 128 + rows, blk * 512:(blk + 1) * 512], xt[:rows, :], reads=[xt], writes=[dst_b[t][blk]])

        def xsrc(t, rows, blk):
            return (x_p[t * 128:(t + 1) * 128, blk * 512:(blk + 1) * 512] if t < 16 else x_s[:, blk * 512:(blk + 1) * 512])

        esH2 = es.enter_context(S.scope())
        hT_ref[0] = S.tile("hT2", [128, 16, TA], BF16, esH2)
        if STOPAT > 1:
            wout_phase(wOA, xsrc, lambda t, blk: [], x1_scr, x1_b)
        run_c = STOPAT > 2

        if run_c:
          norm_pass(lambda t, rows: x1_scr[t * 128:t * 128 + rows, :], normC, lambda t: x1_b[t])
        with S.scope() as esC:
          if run_c:
            lbt = S.tile("lbt", [128, 2, 16], F32, esC)
            lb = S.tile("lb", [128, 16], F32, esC)
            oml = S.tile("oml", [128, 16], F32, esC)
            S.dma("sp", lbt[:, :, :], lb_log[:, :, :], writes=[lbt])
            S.op("dve", lambda e: e.tensor_tensor(out=lb[:, :], in0=lbt[:, 1, :], in1=lbt[:, 0, :], op=ALU.subtract), reads=[lbt], writes=[lb])
            S.op("act", lambda e: e.activation(out=lb[:, :], in_=lb[:, :], func=AF.Sigmoid), reads=[lb], writes=[lb])
            S.op("dve", lambda e: e.tensor_scalar(out=oml[:, :], in0=lb[:, :], scalar1=-1.0, scalar2=1.0, op0=ALU.mult, op1=ALU.add),
                 reads=[lb], writes=[oml])
            hsets = Rot([dict(q=S.tile(f"qcT{i}", [128, TA], BF16, esC), k=S.tile(f"kcT{i}", [128, TA], BF16, esC),
                              g=S.tile(f"gcT{i}", [128, TA], F32, esC),
                              v=S.tile(f"vc{i}", [128, NT, 128], BF16, esC), z=S.tile(f"zc{i}", [128, NT, 128], BF16, esC))
                         for i in range(2)])
            G = rec_setup(S, esC, R, V=128)
            gnc = S.tile("gnc", [128, 128], F32, esC)
            S.dma("sp", gnc[:, :], c_gn.partition_broadcast(128), writes=[gnc])
            sgr = Rot([S.tile(f"sg{i}", [128, 512], F32, esC) for i in range(3)])
            fa = S.tile("fa", [128, 16], F32, esC)
            fb = S.tile("fb", [128, 16], F32, esC)
            S.op("dve", lambda e: e.tensor_scalar(out=fa[:, :], in0=oml[:, :], scalar1=0.5, scalar2=None, op0=ALU.mult), reads=[oml], writes=[fa])
            S.op("dve", lambda e: e.tensor_tensor(out=fb[:, :], in0=fa[:, :], in1=lb[:, :], op=ALU.add), reads=[fa, lb], writes=[fb])
            S.op("dve", lambda e: e.tensor_scalar(out=gnc[:, :], in0=gnc[:, :], scalar1=0.5, scalar2=None, op0=ALU.mult), reads=[gnc], writes=[gnc])
            def head_proj(h, hs):
                wb = load_w(h * 512, 512, wC)

                def qcons(j, n, pb):
                    th = sgr.next()
                    S.op("act", lambda e: e.activation(out=th[:, 0:n], in_=pb[:, 0:n], func=AF.Tanh, scale=0.5), reads=[pb], writes=[th])
                    S.op("dve", lambda e: e.scalar_tensor_tensor(out=hs["q"][:, j * 512:j * 512 + n], in0=th[:, 0:n], scalar=1.0, in1=pb[:, 0:n],
                                                                 op0=ALU.add, op1=ALU.mult), reads=[th, pb], writes=[hs["q"]])
                yield from proj_feat_g(wb, 0, 128, qcons)

                def fgate(j, n, pb):
                    th = sgr.next()
                    sl = slice(j * 512, j * 512 + n)
                    S.op("act", lambda e: e.activation(out=th[:, 0:n], in_=pb[:, 0:n], func=AF.Tanh, scale=0.5), reads=[pb], writes=[th])
                    S.op("dve", lambda e: e.tensor_scalar(out=hs["g"][:, sl], in0=th[:, 0:n], scalar1=fa[:, h:h + 1], scalar2=fb[:, h:h + 1],
                                                          op0=ALU.mult, op1=ALU.add), reads=[th, fa, fb], writes=[hs["g"]])
                    S.op("dve", lambda e: e.tensor_scalar(out=hs["k"][:, sl], in0=hs["g"][:, sl], scalar1=-1.0, scalar2=1.0,
                                                          op0=ALU.mult, op1=ALU.add), reads=[hs["g"]], writes=[hs["k"]])
                yield from proj_feat_g(wb, 128, 128, fgate)
                S.op("act", lambda e: e.activation(out=hs["g"][:, :], in_=hs["g"][:, :], func=AF.Ln), reads=[hs["g"]], writes=[hs["g"]])
                yield
                yield from proj_tok_g(wb, 128, lambda t, rows, pb: evac(hs["v"][:rows, t, :], pb[:rows, 0:128], [pb], [hs["v"]]), wc0=256)

                def zcons(t, rows, pb):
                    th = sgr.next()
                    S.op("act", lambda e: e.activation(out=th[:rows, 0:128], in_=pb[:rows, 0:128], func=AF.Tanh, scale=0.5), reads=[pb], writes=[th])
                    S.op("dve", lambda e: e.scalar_tensor_tensor(out=hs["z"][:rows, t, :], in0=th[:rows, 0:128], scalar=1.0, in1=pb[:rows, 0:128],
                                                                 op0=ALU.add, op1=ALU.mult), reads=[th, pb], writes=[hs["z"]])
                yield from proj_tok_g(wb, 128, zcons, wc0=384)

            def step(gen, k):
                if gen is None:
                    return
                for _ in range(k):
                    try:
                        next(gen)
                    except StopIteration:
                        return

            hs_cur = hsets.next()
            step(head_proj(0, hs_cur), 1000)
            for h in range(16):
                hs = hs_cur
                if h + 1 < 16:
                    hs_cur = hsets.next()
                    gen = head_proj(h + 1, hs_cur)
                else:
                    gen = None
                oh = rec_head(S, G, hs, V=128, COEF=1.0, QS=0.5, gate=lambda tok, n, hs=hs: (hs["g"][:, tok], hs["g"]),
                              state_in=lambda s, h=h: state_hgrn[s, h, :, :], out_p=hg_p[h, :, :], out_s=lambda s, h=h: hg_s[s, h, :, :],
                              hook=lambda: step(gen, 3))
                step(gen, 1000)
                post_head(S, G, oh, hs["z"], gnc, V=128, fc0=h)

        if run_c:
          wout_phase(wOC, lambda t, rows, blk: x1_scr[t * 128:t * 128 + rows, blk * 512:(blk + 1) * 512],
                     lambda t, blk: [x1_b[t][blk]], y_scr, ys_b)

        with S.scope() as e4:
          if run_c:
            xrot = Rot([S.tile(f"yt{i}", [128, D], F32, e4) for i in range(3)])
            ssr = Rot([S.tile(f"yss{i}", [128, 1], F32, e4) for i in range(2)])
            rsr = Rot([S.tile(f"yrs{i}", [128, 1], F32, e4) for i in range(2)])
            rstdr = Rot([S.tile(f"yrstd{i}", [128, 1], F32, e4) for i in range(2)])
            junk = S.tile("yjunk", [128, D], BF16, e4)
            fnb = S.tile("fnb", [128, D], F32, e4)
            S.dma("sp", fnb[:, :], f_norm.partition_broadcast(128), writes=[fnb])
            for t in range(NT):
                rows = 128 if t < 16 else TS
                xt, ss, rs, rstd = xrot.next(), ssr.next(), rsr.next(), rstdr.next()
                S.dma("sp", xt[:rows, :], y_scr[t * 128:t * 128 + rows, :], reads=ys_b[t], writes=[xt])
                S.op("act", lambda e: e.activation(out=junk[:rows, :], in_=xt[:rows, :], func=AF.Square,
                                                   accum_out=ss[:rows, 0:1]), reads=[xt], writes=[junk, ss])
                S.op("act", lambda e: e.activation(out=rs[:rows, :], in_=ss[:rows, :], func=AF.Sqrt,
                                                   scale=1.0 / D, bias=EPS), reads=[ss], writes=[rs])
                S.op("dve", lambda e: e.reciprocal(out=rstd[:rows, :], in_=rs[:rows, :]), reads=[rs], writes=[rstd])
                S.op("dve", lambda e: e.scalar_tensor_tensor(out=xt[:rows, :], in0=xt[:rows, :], scalar=rstd[:rows, 0:1], in1=fnb[:rows, :],
                                                             op0=ALU.mult, op1=ALU.mult), reads=[xt, rstd, fnb], writes=[xt])
                dst = y_p[t * 128:(t + 1) * 128, :] if t < 16 else y_s[:, :]
                S.dma("pool", dst, xt[:rows, :], reads=[xt], writes=[obuf("y")])

        S.finish(outs)
        print("instructions", S.n_inst, "waits", S.n_wait)
    return nc


def nsa_prompt(S, es, g, N):
    qTg, zg, kcT, vca, kselT, kwinT, vsel, vwin, gts = (N[k] for k in ("qTg", "zg", "kcT", "vca", "kselT", "kwinT", "vsel", "vwin", "gts"))
    wsel, eexp, mle_b, mgt_b, ident_b, psb, psA, psB, SC = (N[k] for k in ("wsel", "eexp", "mle_b", "mgt_b", "ident_b", "psb", "psA", "psB", "SC"))
    tl = lambda n, s, d=F32: S.tile(n, s, d, es)
    et = Rot([tl(f"et{i}", [128, 512], BF16) for i in range(3)])
    den = Rot([tl(f"den{i}", [128, 4]) for i in range(3)])
    rden = Rot([tl(f"rden{i}", [128, 4]) for i in range(3)])
    cf = Rot([tl(f"cf{i}", [128, 4]) for i in range(3)])
    acc = Rot([tl(f"acc{i}", [128, 4, 128]) for i in range(2)])
    tmpo = Rot([tl(f"tmpo{i}", [128, 4, 128]) for i in range(2)])
    impn = tl("impn", [128, 4, 32])
    sc = Rot([tl(f"sc{i}", [128, 32]) for i in range(2)])
    sc2 = tl("sc2", [128, 32])
    m8 = tl("m8", [128, 16])
    selm = tl("selm", [128, 32], BF16)
    selT = Rot([tl(f"selT{i}", [32, 128], BF16) for i in range(2)])
    m2 = Rot([tl(f"m2{i}", [128, 128], BF16) for i in range(2)])
    ob = Rot([tl(f"nob{i}", [128, 512], BF16) for i in range(2)])
    st = Rot([tl(f"nst{i}", [128, 4, 128], BF16) for i in range(2)])
    scb = Rot([psb[0], psb[1]])
    pmk, pmisc = psb[2], psb[3]
    pmisc_bf = View(pmisc[:, :].bitcast(BF16), pmisc.b)
    A3 = View(psA[:, :].rearrange("p (r c) -> p r c", r=4), psA.b)
    B3 = View(psB[:, :].rearrange("p (r c) -> p r c", r=4), psB.b)

    def v4(x):
        return x[:, :].rearrange("p (r q) -> p r q", r=4)

    def finish_branch(P3, br, t, a_, first):
        d_, r_, c_ = den.next(), rden.next(), cf.next()
        S.op("dve", lambda e: e.tensor_scalar(out=d_[:, :], in0=P3[:, :, 128], scalar1=1e-30, scalar2=None, op0=ALU.max), reads=[P3], writes=[d_])
        S.op("dve", lambda e: e.reciprocal(out=r_[:, :], in_=d_[:, :]), reads=[d_], writes=[r_])
        S.op("dve", lambda e: e.tensor_tensor(out=c_[:, :], in0=r_[:, :], in1=gts[:, t, 12 * g + br:12 * g + 12:3], op=ALU.mult),
             reads=[r_, gts], writes=[c_])
        cb = c_[:, :].unsqueeze(2).to_broadcast([128, 4, 128])
        if first:
            S.op("dve", lambda e: e.tensor_tensor(out=a_[:, :, :], in0=P3[:, :, 0:128], in1=cb, op=ALU.mult), reads=[P3, c_], writes=[a_])
        else:
            tm = tmpo.next()
            S.op("dve", lambda e: e.tensor_tensor(out=tm[:, :, :], in0=P3[:, :, 0:128], in1=cb, op=ALU.mult), reads=[P3, c_], writes=[tm])
            S.op("pool", lambda e: e.tensor_tensor(out=a_[:, :, :], in0=a_[:, :, :], in1=tm[:, :, :], op=ALU.add), reads=[a_, tm], writes=[a_])
        return r_

    for t in range(16):
        t0 = 128 * t
        qrhs = qTg[:, :, t0:t0 + 128]
        a_ = acc.next()
        ps = scb.next()
        S.op("pe", lambda e: e.matmul(v4(ps)[0:127], lhsT=kcT[:, g, 0:127], rhs=qrhs, start=True, stop=True), reads=[kcT, qTg], writes=[ps])
        ec = et.next()
        S.op("act", lambda e: e.activation(out=ec[0:127, :], in_=ps[0:127, :], func=AF.Exp, scale=SC), reads=[ps], writes=[ec])
        S.op("pool", lambda e: e.affine_select(out=v4(ec)[0:127], in_=v4(ec)[0:127], pattern=[[0, 4], [1, 128]], compare_op=ALU.is_ge,
                                               fill=0.0, base=t0 - 31, channel_multiplier=-16), reads=[ec], writes=[ec])
        for r in range(4):
            S.op("pe", lambda e: e.matmul(A3[:, r, 0:161], lhsT=ec[0:127, r * 128:(r + 1) * 128], rhs=vca[0:127, g, 0:161],
                                          start=True, stop=True), reads=[ec, vca], writes=[A3])
        rd = finish_branch(A3, 0, t, a_, True)
        sel = t >= 8
        if sel:
            s_ = sc.next()
            S.op("dve", lambda e: e.tensor_tensor(out=impn[:, :, :], in0=A3[:, :, 129:161], in1=rd[:, :].unsqueeze(2).to_broadcast([128, 4, 32]),
                                                  op=ALU.mult), reads=[A3, rd], writes=[impn])
            S.op("dve", lambda e: e.tensor_reduce(out=s_[:, :], in_=impn[:, :, :].rearrange("p r s -> p s r"), axis=AX.X, op=ALU.add),
                 reads=[impn], writes=[s_])
            S.op("dve", lambda e: e.tensor_tensor(out=s_[:, :], in0=s_[:, :], in1=wsel[:, 31 - 2 * t:63 - 2 * t], op=ALU.add),
                 reads=[s_, wsel], writes=[s_])
            S.op("dve", lambda e: e.memset(s_[:, 0:1], 1e9), reads=[], writes=[s_])
            S.op("dve", lambda e: e.max(out=m8[:, 0:8], in_=s_[:, :]), reads=[s_], writes=[m8])
            S.op("dve", lambda e: e.match_replace(out=sc2[:, :], in_to_replace=m8[:, 0:8], in_values=s_[:, :], imm_value=-3e38),
                 reads=[s_, m8], writes=[sc2])
            S.op("dve", lambda e: e.max(out=m8[:, 8:16], in_=sc2[:, :]), reads=[sc2], writes=[m8])
            S.op("dve", lambda e: e.tensor_scalar(out=selm[:, :], in0=s_[:, :], scalar1=m8[:, 15:16], scalar2=None, op0=ALU.is_ge),
                 reads=[s_, m8], writes=[selm])
            S.op("pe", lambda e: e.transpose(out=pmisc_bf[0:32, 0:128], in_=selm[:, 0:32], identity=ident_b[:, :]),
                 reads=[selm, ident_b], writes=[pmisc_bf])
            sT = selT.next()
            S.op("act", lambda e: e.activation(out=sT[:, :], in_=pmisc_bf[0:32, 0:128], func=AF.Copy), reads=[pmisc_bf], writes=[sT])
        S.op("dve", lambda e: e.memset(B3[:, :, 0:129], 0.0), reads=[], writes=[B3])
        for kc in range(t + 1):
            ps = scb.next()
            S.op("pe", lambda e: e.matmul(v4(ps), lhsT=kselT[:, g, kc * 128:(kc + 1) * 128], rhs=qrhs, start=True, stop=True),
                 reads=[kselT, qTg], writes=[ps])
            e_ = et.next()
            S.op("act", lambda e: e.activation(out=e_[:, :], in_=ps[:, :], func=AF.Exp, scale=SC), reads=[ps], writes=[e_])
            if sel:
                S.op("pe", lambda e: e.matmul(pmk[:, 0:128], lhsT=eexp[0:32, kc, :], rhs=sT[0:32, :], start=True, stop=True),
                     reads=[eexp, sT], writes=[pmk])
                if kc == t:
                    mm = m2.next()
                    S.op("dve", lambda e: e.tensor_tensor(out=mm[:, :], in0=pmk[:, 0:128], in1=mle_b[:, :], op=ALU.mult),
                         reads=[pmk, mle_b], writes=[mm])
                    msrc = mm
                else:
                    msrc = pmk
                S.op("dve", lambda e: e.tensor_tensor(out=v4(e_), in0=v4(e_), in1=msrc[:, 0:128].unsqueeze(1).to_broadcast([128, 4, 128]),
                                                      op=ALU.mult), reads=[e_, msrc], writes=[e_])
            elif kc == t:
                S.op("pool", lambda e: e.tensor_tensor(out=v4(e_), in0=v4(e_), in1=mle_b[:, :].unsqueeze(1).to_broadcast([128, 4, 128]),
                                                       op=ALU.mult), reads=[e_, mle_b], writes=[e_])
            for r in range(4):
                S.op("pe", lambda e: e.matmul(B3[:, r, 0:129], lhsT=e_[:, r * 128:(r + 1) * 128], rhs=vsel[:, kc, g, 0:129],
                                              start=False, stop=(kc == t), skip_group_check=True), reads=[e_, vsel], writes=[B3])
        finish_branch(B3, 1, t, a_, False)
        S.op("dve", lambda e: e.memset(A3[:, :, 0:129], 0.0), reads=[], writes=[A3])
        k0 = max(0, t - 4)
        for kc in range(k0, t + 1):
            ps = scb.next()
            S.op("pe", lambda e: e.matmul(v4(ps), lhsT=kwinT[:, g, kc * 128:(kc + 1) * 128], rhs=qrhs, start=True, stop=True),
                 reads=[kwinT, qTg], writes=[ps])
            e_ = et.next()
            S.op("act", lambda e: e.activation(out=e_[:, :], in_=ps[:, :], func=AF.Exp, scale=SC), reads=[ps], writes=[e_])
            mk = mle_b if kc == t else (mgt_b if kc == t - 4 else None)
            if mk is not None:
                S.op("pool", lambda e: e.tensor_tensor(out=v4(e_), in0=v4(e_), in1=mk[:, :].unsqueeze(1).to_broadcast([128, 4, 128]),
                                                       op=ALU.mult), reads=[e_, mk], writes=[e_])
            for r in range(4):
                S.op("pe", lambda e: e.matmul(A3[:, r, 0:129], lhsT=e_[:, r * 128:(r + 1) * 128], rhs=vwin[:, kc, g, 0:129],
                                              start=False, stop=(kc == t), skip_group_check=True), reads=[e_, vwin], writes=[A3])
        finish_branch(A3, 2, t, a_, False)
        o_ = ob.next()
        S.op("dve", lambda e: e.tensor_tensor(out=o_[:, :], in0=a_[:, :, :].rearrange("p r d -> p (r d)"), in1=zg[:, t, :], op=ALU.mult),
             reads=[a_, zg], writes=[o_])
        for r in range(4):
            S.op("pe", lambda e: e.transpose(out=pmisc_bf[:, 128 + r * 128:256 + r * 128], in_=o_[:, r * 128:(r + 1) * 128], identity=ident_b[:, :]),
                 reads=[o_, ident_b], writes=[pmisc_bf])
        s_t = st.next()
        S.op("act", lambda e: e.activation(out=s_t[:, :, :], in_=pmisc_bf[:, 128:640].rearrange("p (r q) -> p r q", r=4), func=AF.Copy),
             reads=[pmisc_bf], writes=[s_t])
        S.dma("sp", N["oT_scr"][t, :, 4 * g:4 * g + 4, :], s_t[:, :, :], reads=[s_t], writes=N["oT_b"][t][4 * g:4 * g + 4])


def nsa_sample(S, es, nc, N):
    qTs, zs, gts, kselT, kwinT, vnew_src, mle_b, mgt_b, ident_b, ident_f, psb, wbuf, cst, SC, evac = (
        N[k] for k in ("qTs", "zs", "gts", "kselT", "kwinT", "vnew_src", "mle_b", "mgt_b", "ident_b", "ident_f", "psb", "wbuf", "cst", "SC", "evac"))
    ckv, cwin = N["cache_kv"], N["cache_win"]
    tl = lambda n, s, d=F32: S.tile(n, s, d, es)
    big = Rot(psb[0:4])
    small = Rot(psb[4:8])
    ptb = tl("ptb", [128, 256], I32)
    S.dma("sp", ptb[:, :], N["page_table"].rearrange("s p -> (s p)").partition_broadcast(128), writes=[ptb])
    pci = tl("pci", [128, 1], I32)
    S.op("pool", lambda e: e.iota(pci[:, :], pattern=[[0, 1]], base=0, channel_multiplier=2), writes=[pci])
    pcf = tl("pcf", [128, 1])
    S.op("dve", lambda e: e.tensor_copy(out=pcf[:, :], in_=pci[:, :]), reads=[pci], writes=[pcf])
    idxA = tl("idxA", [128, 256], I32)
    idxB = tl("idxB", [128, 256], I32)
    S.op("dve", lambda e: e.tensor_scalar(out=idxA[:, :], in0=ptb[:, :], scalar1=256.0, scalar2=pcf[:, 0:1], op0=ALU.mult, op1=ALU.add),
         reads=[ptb, pcf], writes=[idxA])
    S.op("dve", lambda e: e.tensor_scalar(out=idxB[:, :], in0=idxA[:, :], scalar1=1.0, scalar2=None, op0=ALU.add), reads=[idxA], writes=[idxB])
    bg, bz = Buf("gscr"), Buf("zscr")
    S.dma("sp", N["g_scr"][:, :], gts[:TS, 16, :], reads=[gts], writes=[bg])
    S.dma("sp", N["z_scr"][:, :], zs[:TS, :], reads=[zs], writes=[bz])
    gsm = tl("gsm", [16, NS, 2, 3])
    zr = tl("zr", [16, NS, 2, 128], BF16)
    gv = N["g_scr"].rearrange("(s t) (g r b) -> r t s g b", t=4, g=2, r=4)
    zv = N["z_scr"].rearrange("(s t) (g r d) -> r t s g d", t=4, g=2, r=4)
    for r in range(4):
        for g in range(2):
            S.dma("sp", gsm[4 * r:4 * r + 4, :, g, :], gv[r][:, :, g, :], reads=[bg], writes=[gsm])
            S.dma("sp", zr[4 * r:4 * r + 4, :, g, :], zv[r][:, :, g, :], reads=[bz], writes=[zr])
    vnew = tl("vnew", [4, NS, 2, 2, 130], BF16)
    for s in range(NS):
        S.dma("sp", vnew[0:4, s, :, :, :], vnew_src[4 * s:4 * s + 4, :, :, :], reads=[vnew_src], writes=[vnew])
    w1 = [tl(f"w1_{kv}", [128, 32, 256], BF16) for kv in range(2)]
    w2b = tl("w2b", [128, 2, 2, 128], BF16)
    peb = tl("peb", [128, 2, 32], BF16)
    b1t = tl("b1t", [128, 2, 2])
    hb = tl("hb", [128, 2, 2])
    for kv in range(2):
        S.dma("pool", w1[kv][:, :, :], N["cmp_w1"][kv, :, :, :], writes=[w1[kv]])
    S.dma("pool", w2b[:, :, :, :], N["cmp_w2"][:, :, :, :], writes=[w2b])
    S.dma("pool", peb[:, :, :], N["cmp_pe"][:, :, :], writes=[peb])
    S.dma("sp", b1t[:, :, :], N["cmp_b1"][:, :, :], writes=[b1t])
    for kv in range(2):
        pp = small.next()
        for half in range(2):
            for rp in range(32):
                S.op("pe", lambda e: e.matmul(pp[:, half:half + 1], lhsT=w1[kv][:, rp, half * 128:(half + 1) * 128],
                                              rhs=peb[:, kv, rp:rp + 1], start=(rp == 0), stop=(rp == 31)), reads=[w1[kv], peb], writes=[pp])
        S.op("dve", lambda e: e.tensor_tensor(out=hb[:, kv, :], in0=pp[:, 0:2], in1=b1t[:, kv, :], op=ALU.add), reads=[pp, b1t], writes=[hb])
    rsum = tl("rsum", [16, 4])
    S.dma("sp", rsum[:, :], cst["rsum"][:, :], writes=[rsum])
    xTk = tl("xTk", [128, 2, 2048], BF16)
    xTv = tl("xTv", [128, 2, 2048], BF16)
    gT = Rot([tl(f"sgT{i}", [128, 2, 128], BF16) for i in range(2)])
    kcTs = tl("kcTs", [128, 2, 128], BF16)
    vcas = tl("vcas", [128, 2, 162], BF16)
    S.op("pool", lambda e: e.memset(vcas[:, :, 128:129], 1.0), writes=[vcas])
    for g in range(2):
        S.dma("pool", vcas[0:127, g, 129:162], cst["c2ss"][:, :], writes=[vcas])
    ec = tl("sec", [128, 2, 16], BF16)
    den = Rot([tl(f"sden{i}", [16, 2]) for i in range(3)])
    rden = Rot([tl(f"srden{i}", [16, 2]) for i in range(3)])
    cf = Rot([tl(f"scf{i}", [16, 2]) for i in range(3)])
    acc = tl("sacc", [16, 2, 128])
    tmpo = tl("stmpo", [16, 2, 128])
    impn = tl("simpn", [16, 2, 33])
    scs = tl("sscs", [4, 2, 33])
    sc2 = tl("ssc2", [4, 33])
    m8 = tl("sm8", [4, 16])
    selm = tl("sselm", [4, 2, 33], BF16)
    selx = tl("sselx", [4, 2, 33, 64], BF16)
    maskT = tl("smaskT", [128, 2, 16, 4], BF16)
    vsa = tl("vsa", [128, 16, 2, 130], BF16)
    S.op("pool", lambda e: e.memset(vsa[:, :, :, 128:130], 1.0), writes=[vsa])
    esel = tl("esel", [128, 2, 272], BF16)
    wl = tl("wl", [128, 4, 512])
    kwT = tl("kwT", [128, 2, 512], BF16)
    vwa = tl("vwa", [128, 4, 2, 130], BF16)
    S.op("pool", lambda e: e.memset(vwa[:, :, :, 128:130], 1.0), writes=[vwa])
    ewin = tl("ewin", [128, 2, 80], BF16)
    ob = tl("sob", [16, 2, 128], BF16)
    oTs = tl("oTs", [128, 8, TS], BF16)

    def pbview(wb):
        return View(wb[:, :, :].bitcast(F32).rearrange("p c (a f) -> p (c a) f", a=1).rearrange("p (k two) f -> p k (two f)", two=2), wb.b)

    pgbufs = Rot(list(wbuf.items) + [tl(f"pgx{i}", [128, 16, 512], BF16) for i in range(2)])

    def gather(idx, s, hs):
        wb = pgbufs.next()
        pv = pbview(wb)
        for pgl in range(8):
            col = s * 16 + hs * 8 + pgl
            S.dma("pool", None, None, reads=[idx], writes=[pv],
                  fn=lambda e: e.indirect_dma_start(out=pv[:, pgl, :], out_offset=None, in_=ckv[:, :],
                                                    in_offset=bass.IndirectOffsetOnAxis(ap=idx[:, col:col + 1], axis=0)))
        return pv

    def transpose4(src_fn, dst_ap, reads, dstt):
        pt = big.next()
        for k in range(4):
            S.op("pe", lambda e: e.transpose(out=pt[:, k * 128:(k + 1) * 128], in_=src_fn(k), identity=ident_f[:, :]),
                 reads=reads + [ident_f], writes=[pt])
        evac(dst_ap, pt[:, 0:512], [pt], [dstt])

    def finish_branch(P, br, s, first):
        d_, r_, c_ = den.next(), rden.next(), cf.next()
        S.op("dve", lambda e: e.tensor_scalar(out=d_[:, :], in0=P[0:16, :, 128], scalar1=1e-30, scalar2=None, op0=ALU.max), reads=[P], writes=[d_])
        S.op("dve", lambda e: e.reciprocal(out=r_[:, :], in_=d_[:, :]), reads=[d_], writes=[r_])
        S.op("dve", lambda e: e.tensor_tensor(out=c_[:, :], in0=r_[:, :], in1=gsm[:, s, :, br], op=ALU.mult), reads=[r_, gsm], writes=[c_])
        cb = c_[:, :].unsqueeze(2).to_broadcast([16, 2, 128])
        if first:
            S.op("dve", lambda e: e.tensor_tensor(out=acc[:, :, :], in0=P[0:16, :, 0:128], in1=cb, op=ALU.mult), reads=[P, c_], writes=[acc])
        else:
            S.op("dve", lambda e: e.tensor_tensor(out=tmpo[:, :, :], in0=P[0:16, :, 0:128], in1=cb, op=ALU.mult), reads=[P, c_], writes=[tmpo])
            S.op("dve", lambda e: e.tensor_tensor(out=acc[:, :, :], in0=acc[:, :, :], in1=tmpo[:, :, :], op=ALU.add), reads=[acc, tmpo], writes=[acc])
        return r_

    def v3(bank):
        return View(bank[:, :].rearrange("p (g c) -> p g c", g=2), bank.b)

    for s in range(NS):
        q16 = [qTs[:, 4 * g:4 * g + 4, 4 * s:4 * s + 4] for g in range(2)]
        for hs in range(2):
            pv = gather(idxA, s, hs)
            for slot in range(2):
                dstt = xTk if slot == 0 else xTv
                for g in range(2):
                    for q4 in range(2):
                        c0 = (slot * 2 + g) * 128
                        transpose4(lambda k: pv[:, 4 * q4 + k, c0:c0 + 128],
                                   dstt[:, g, (8 * hs + 4 * q4) * 128:(8 * hs + 4 * q4 + 4) * 128], [pv], dstt)
        for kv in range(2):
            xT = xTk if kv == 0 else xTv
            for g in range(2):
                gt = gT.next()
                for half in range(2):
                    ph = big.next()
                    for rp in range(32):
                        st0 = rp
                        S.op("pe", lambda e: e.matmul(ph[:, 0:127], lhsT=w1[kv][:, rp, half * 128:(half + 1) * 128],
                                                      rhs=xT[:, g, st0:st0 + 16 * 126 + 1:16], start=(rp == 0), stop=(rp == 31)),
                             reads=[w1[kv], xT], writes=[ph])
                    S.op("act", lambda e: e.activation(out=gt[:, half, 0:127], in_=ph[:, 0:127], func=AF.Gelu_apprx_tanh,
                                                       bias=hb[:, kv, half:half + 1]), reads=[ph, hb], writes=[gt])
                po = big.next()
                if kv == 0:
                    for half in range(2):
                        S.op("pe", lambda e: e.matmul(po[:, 0:127], lhsT=w2b[:, 0, half, :], rhs=gt[:, half, 0:127],
                                                      start=(half == 0), stop=(half == 1)), reads=[w2b, gt], writes=[po])
                    evac(kcTs[:, g, 0:127], po[:, 0:127], [po], [kcTs])
                else:
                    for half in range(2):
                        S.op("pe", lambda e: e.matmul(po[0:127, 0:128], lhsT=gt[:, half, 0:127], rhs=w2b[:, 1, half, :],
                                                      start=(half == 0), stop=(half == 1)), reads=[w2b, gt], writes=[po])
                    evac(vcas[0:127, g, 0:128], po[0:127, 0:128], [po], [vcas])
        ps = small.next()
        for g in range(2):
            S.op("pe", lambda e: e.matmul(ps[0:127, g * 16:(g + 1) * 16].rearrange("p (r q) -> p r q", r=4), lhsT=kcTs[:, g, 0:127], rhs=q16[g],
                                          start=True, stop=True), reads=[kcTs, qTs], writes=[ps])
        S.op("act", lambda e: e.activation(out=ec[0:127, :, :], in_=ps[0:127, 0:32].rearrange("p (g c) -> p g c", g=2), func=AF.Exp, scale=SC),
             reads=[ps], writes=[ec])
        pc = v3(small.next())
        for g in range(2):
            S.op("pe", lambda e: e.matmul(pc[0:16, g, 0:162], lhsT=ec[0:127, g, :], rhs=vcas[0:127, g, 0:162], start=True, stop=True),
                 reads=[ec, vcas], writes=[pc])
        rd = finish_branch(pc, 0, s, True)
        S.op("dve", lambda e: e.tensor_tensor(out=impn[:, :, :], in0=pc[0:16, :, 129:162], in1=rd[:, :].unsqueeze(2).to_broadcast([16, 2, 33]),
                                              op=ALU.mult), reads=[pc, rd], writes=[impn])
        pi = small.next()
        S.op("pe", lambda e: e.matmul(pi[0:4, 0:66], lhsT=rsum[:, :], rhs=impn[:, :, :].rearrange("p g s -> p (g s)"), start=True, stop=True),
             reads=[rsum, impn], writes=[pi])
        S.op("dve", lambda e: e.tensor_copy(out=scs[:, :, :], in_=pi[0:4, 0:66].rearrange("p (g s) -> p g s", g=2)), reads=[pi], writes=[scs])
        S.op("dve", lambda e: e.memset(scs[:, :, 0:1], 1e9), reads=[], writes=[scs])
        S.op("dve", lambda e: e.memset(scs[:, :, 31:33], 1e9), reads=[], writes=[scs])
        for g in range(2):
            S.op("dve", lambda e: e.max(out=m8[:, 0:8], in_=scs[:, g, :]), reads=[scs], writes=[m8])
            S.op("dve", lambda e: e.match_replace(out=sc2[:, :], in_to_replace=m8[:, 0:8], in_values=scs[:, g, :], imm_value=-3e38),
                 reads=[scs, m8], writes=[sc2])
            S.op("dve", lambda e: e.max(out=m8[:, 8:16], in_=sc2[:, :]), reads=[sc2], writes=[m8])
            S.op("dve", lambda e: e.tensor_scalar(out=selm[:, g, :], in0=scs[:, g, :], scalar1=m8[:, 15:16], scalar2=None, op0=ALU.is_ge),
                 reads=[scs, m8], writes=[selm])
        S.op("dve", lambda e: e.tensor_copy(out=selx[:, :, :, :].rearrange("p g s k -> p (g s) k"),
                                            in_=selm[:, :, :].rearrange("p g s -> p (g s)").unsqueeze(2).to_broadcast([4, 66, 64])),
             reads=[selm], writes=[selx])
        pm = small.next()
        for g in range(2):
            for kc in range(16):
                S.op("pe", lambda e: e.matmul(pm[:, (g * 16 + kc) * 4:(g * 16 + kc) * 4 + 4],
                                              lhsT=selx[0:4, g, 2 * kc:2 * kc + 2, :].rearrange("p a b -> p (a b)"), rhs=ident_b[0:4, 0:4],
                                              start=True, stop=True), reads=[selx, ident_b], writes=[pm])
        S.op("act", lambda e: e.activation(out=maskT[:, :, :, :].rearrange("p g k t -> p (g k t)"), in_=pm[:, 0:128], func=AF.Copy),
             reads=[pm], writes=[maskT])
        for hs in range(2):
            pv = gather(idxB, s, hs)
            for g in range(2):
                for q4 in range(2):
                    transpose4(lambda k: pv[:, 4 * q4 + k, g * 128:(g + 1) * 128],
                               xTk[:, g, (8 * hs + 4 * q4) * 128:(8 * hs + 4 * q4 + 4) * 128], [pv], xTk)
            S.op("act", lambda e: e.activation(out=vsa[:, 8 * hs:8 * hs + 8, :, 0:128],
                                               in_=pv[:, :, 256:512].rearrange("p k (g d) -> p k g d", g=2), func=AF.Copy), reads=[pv], writes=[vsa])
        pss = v3(small.next())
        psn = v3(small.next())
        for g in range(2):
            for kc in range(16):
                S.op("pe", lambda e: e.matmul(pss[:, g, kc * 16:(kc + 1) * 16].rearrange("p (r q) -> p r q", r=4),
                                              lhsT=xTk[:, g, kc * 128:(kc + 1) * 128], rhs=q16[g], start=True, stop=True),
                     reads=[xTk, qTs], writes=[pss])
            S.op("pe", lambda e: e.matmul(psn[0:4, g, 0:16].rearrange("p (r q) -> p r q", r=4),
                                          lhsT=kselT[:, g, 4 * s:4 * s + 4], rhs=q16[g], start=True, stop=True),
                 reads=[kselT, qTs], writes=[psn])
        S.op("act", lambda e: e.activation(out=esel[:, :, 0:256], in_=pss[:, :, 0:256], func=AF.Exp, scale=SC), reads=[pss], writes=[esel])
        S.op("act", lambda e: e.activation(out=esel[0:4, :, 256:272], in_=psn[0:4, :, 0:16], func=AF.Exp, scale=SC), reads=[psn], writes=[esel])
        for g in range(2):
            S.op("dve", lambda e: e.tensor_tensor(out=esel[:, g, 0:256].rearrange("p (k r q) -> p k r q", k=16, r=4),
                                                  in0=esel[:, g, 0:256].rearrange("p (k r q) -> p k r q", k=16, r=4),
                                                  in1=maskT[:, g, :, :].unsqueeze(2).to_broadcast([128, 16, 4, 4]), op=ALU.mult),
                 reads=[esel, maskT], writes=[esel])
        S.op("dve", lambda e: e.tensor_tensor(out=esel[0:4, :, 256:272].rearrange("p g (r q) -> p g r q", r=4),
                                              in0=esel[0:4, :, 256:272].rearrange("p g (r q) -> p g r q", r=4),
                                              in1=mle_b[0:4, 0:4].unsqueeze(1).unsqueeze(1).to_broadcast([4, 2, 4, 4]), op=ALU.mult),
             reads=[esel, mle_b], writes=[esel])
        po_ = v3(small.next())
        for g in range(2):
            for kc in range(16):
                S.op("pe", lambda e: e.matmul(po_[0:16, g, 0:129], lhsT=esel[:, g, kc * 16:(kc + 1) * 16], rhs=vsa[:, kc, g, 0:129],
                                              start=(kc == 0), stop=False), reads=[esel, vsa], writes=[po_])
            S.op("pe", lambda e: e.matmul(po_[0:16, g, 0:129], lhsT=esel[0:4, g, 256:272], rhs=vnew[0:4, s, 0, g, 0:129],
                                          start=False, stop=True), reads=[esel, vnew], writes=[po_])
        finish_branch(po_, 1, s, False)
        S.dma("sp", wl[:, :, :], cwin[s, :, :].rearrange("(c p) f -> p c f", p=128), writes=[wl])
        for g in range(2):
            transpose4(lambda k: wl[:, k, g * 128:(g + 1) * 128], kwT[:, g, :], [wl], kwT)
        S.op("act", lambda e: e.activation(out=vwa[:, :, :, 0:128], in_=wl[:, :, 256:512].rearrange("p k (g d) -> p k g d", g=2), func=AF.Copy),
             reads=[wl], writes=[vwa])
        psw = v3(small.next())
        pwn = v3(small.next())
        for g in range(2):
            for kc in range(4):
                S.op("pe", lambda e: e.matmul(psw[:, g, kc * 16:(kc + 1) * 16].rearrange("p (r q) -> p r q", r=4),
                                              lhsT=kwT[:, g, kc * 128:(kc + 1) * 128], rhs=q16[g], start=True, stop=True),
                     reads=[kwT, qTs], writes=[psw])
            S.op("pe", lambda e: e.matmul(pwn[0:4, g, 0:16].rearrange("p (r q) -> p r q", r=4),
                                          lhsT=kwinT[:, g, 4 * s:4 * s + 4], rhs=q16[g], start=True, stop=True),
                 reads=[kwinT, qTs], writes=[pwn])
        S.op("act", lambda e: e.activation(out=ewin[:, :, 0:64], in_=psw[:, :, 0:64], func=AF.Exp, scale=SC), reads=[psw], writes=[ewin])
        S.op("act", lambda e: e.activation(out=ewin[0:4, :, 64:80], in_=pwn[0:4, :, 0:16], func=AF.Exp, scale=SC), reads=[pwn], writes=[ewin])
        S.op("dve", lambda e: e.tensor_tensor(out=ewin[:, :, 0:16].rearrange("p g (r q) -> p g r q", r=4),
                                              in0=ewin[:, :, 0:16].rearrange("p g (r q) -> p g r q", r=4),
                                              in1=mgt_b[:, 0:4].unsqueeze(1).unsqueeze(1).to_broadcast([128, 2, 4, 4]), op=ALU.mult),
             reads=[ewin, mgt_b], writes=[ewin])
        S.op("dve", lambda e: e.tensor_tensor(out=ewin[0:4, :, 64:80].rearrange("p g (r q) -> p g r q", r=4),
                                              in0=ewin[0:4, :, 64:80].rearrange("p g (r q) -> p g r q", r=4),
                                              in1=mle_b[0:4, 0:4].unsqueeze(1).unsqueeze(1).to_broadcast([4, 2, 4, 4]), op=ALU.mult),
             reads=[ewin, mle_b], writes=[ewin])
        pw_ = v3(small.next())
        for g in range(2):
            for kc in range(4):
                S.op("pe", lambda e: e.matmul(pw_[0:16, g, 0:129], lhsT=ewin[:, g, kc * 16:(kc + 1) * 16], rhs=vwa[:, kc, g, 0:129],
                                              start=(kc == 0), stop=False), reads=[ewin, vwa], writes=[pw_])
            S.op("pe", lambda e: e.matmul(pw_[0:16, g, 0:129], lhsT=ewin[0:4, g, 64:80], rhs=vnew[0:4, s, 1, g, 0:129],
                                          start=False, stop=True), reads=[ewin, vnew], writes=[pw_])
        finish_branch(pw_, 2, s, False)
        S.op("dve", lambda e: e.tensor_tensor(out=ob[:, :, :], in0=acc[:, :, :], in1=zr[:, s, :, :], op=ALU.mult), reads=[acc, zr], writes=[ob])
        pt = small.next()
        ptb_ = View(pt[:, :].bitcast(BF16), pt.b)
        for g in range(2):
            S.op("pe", lambda e: e.transpose(out=ptb_[:, g * 16:(g + 1) * 16], in_=ob[0:16, g, :], identity=ident_b[0:16, 0:16]),
                 reads=[ob, ident_b], writes=[ptb_])
        S.op("act", lambda e: e.activation(out=oTs[:, :, 4 * s:4 * s + 4], in_=ptb_[:, 0:32].rearrange("p (f q) -> p f q", f=8), func=AF.Copy),
             reads=[ptb_], writes=[oTs])
    S.dma("sp", N["oT_scr"][16, :, 0:8, 0:TS], oTs[:, :, :], reads=[oTs], writes=N["oT_b"][16][0:8])


def rec_setup(S, e2, R, V):
    G = dict(R)
    tl = lambda n, s, d=F32: S.tile(n, s, d, e2)
    G["cs"] = Rot([tl(f"cs{i}", [128, 128]) for i in range(2)])
    G["E1"] = Rot([tl(f"E1{i}", [128, 128]) for i in range(2)])
    G["E2"] = Rot([tl(f"E2{i}", [128, 128]) for i in range(2)])
    G["sm"] = Rot([tl(f"sm{i}", [128, 16]) for i in range(3)])
    G["E3"] = Rot([tl(f"E3{i}", [128, 128]) for i in range(2)])
    G["E4"] = Rot([tl(f"E4{i}", [128, 128]) for i in range(2)])
    G["qx"] = Rot([tl(f"qx{i}", [128, 64], BF16) for i in range(2)])
    G["qS"] = Rot([tl(f"qS{i}", [128, 128], BF16) for i in range(2)])
    G["kS"] = Rot([tl(f"kS{i}", [128, 128], BF16) for i in range(2)])
    G["KA"] = Rot([tl(f"KA{i}", [128, 128], BF16) for i in range(2)])
    G["KB"] = Rot([tl(f"KB{i}", [128, 128], BF16) for i in range(2)])
    for kk in ("KA", "KB"):
        for tt in G[kk].items:
            S.op("pool", lambda e: e.memset(tt[:, :], 0.0), writes=[tt])
    G["qp"] = Rot([tl(f"qp{i}", [128, 128], BF16) for i in range(2)])
    G["kp"] = Rot([tl(f"kp{i}", [128, 128], BF16) for i in range(2)])
    G["ktok"] = Rot([tl(f"ktok{i}", [128, 128], BF16) for i in range(2)])
    G["attm"] = Rot([tl(f"attm{i}", [128, 128], BF16) for i in range(2)])
    G["Sst"] = tl("Sst", [128, V])
    G["Sbf"] = Rot([tl(f"Sbf{i}", [128, V], BF16) for i in range(2)])
    G["ones"] = tl("ones", [128, 128])
    S.op("pool", lambda e: e.memset(G["ones"][:, :], 1.0), writes=[G["ones"]])
    G["bdm"] = tl("bdm", [64, 64], BF16)
    S.dma("pool", G["bdm"][:, :], R["cst"]["bd"][:, :], writes=[G["bdm"]])
    G["colm"] = tl("colm", [128, 16, 64], BF16)
    S.dma("pool", G["colm"][:, :, :], R["cst"]["colmask"][:, :, :], writes=[G["colm"]])
    G["rowm"] = tl("rowm", [64, 16], BF16)
    S.dma("pool", G["rowm"][:, :], R["cst"]["rowmask"][:, :], writes=[G["rowm"]])
    G["s0"] = Rot([tl(f"s0{i}", [128, V]) for i in range(3)])
    G["s0b"] = Rot([tl(f"s0b{i}", [128, V], BF16) for i in range(3)])
    G["sn"] = Rot([tl(f"sn{i}", [128, V]) for i in range(3)])
    G["qm"] = tl("qm", [128, 16, 64], BF16)
    G["km"] = tl("km", [64, 16, 128], BF16)
    G["oh"] = Rot([tl(f"oh{i}", [128, NT, V], BF16) for i in range(1 if V == 256 else 2)])
    G["pss"] = tl("pss", [128, NT])
    G["prs"] = tl("prs", [128, NT])
    G["pjunk"] = tl("pjunk", [128, V], BF16)
    G["ptmp"] = Rot([tl(f"ptmp{i}", [128, V]) for i in range(2)])
    G["pob"] = Rot([tl(f"pob{i}", [128, V], BF16) for i in range(2)])
    G["post"] = Rot([tl(f"post{i}", [128, V // 128, 128], BF16) for i in range(3)])
    return G


def rec_head(S, G, hs, V, COEF, QS, gate, state_in, out_p, out_s, hook=None):
    qT, kT, v = hs["q"], hs["k"], hs["v"]
    mle_b, ident_b, psb, obuf = (G[k] for k in ("mle_b", "ident_b", "psb", "obuf"))
    p_att, p_o, p_ds = psb[4], psb[5], Rot([psb[0], psb[1]])
    Sst = G["Sst"]
    oh = G["oh"].next()
    for t in range(16):
        if hook is not None:
            hook()
        tok = slice(t * 128, (t + 1) * 128)
        c_, e1, e2_, e3, e4, s_ = (G[k].next() for k in ("cs", "E1", "E2", "E3", "E4", "sm"))
        q_, qx, qS, KA, KB, kS, kt_, at_ = (G[k].next() for k in ("qp", "qx", "qS", "KA", "KB", "kS", "ktok", "attm"))
        gap, gtt = gate(tok, 128)
        S.op("dve", lambda e: e.tensor_tensor_scan(out=c_[:, :], data0=G["ones"][:, :], data1=gap, initial=0.0,
                                                   op0=ALU.mult, op1=ALU.add), reads=[G["ones"], gtt], writes=[c_])
        S.op("dve", lambda e: e.tensor_scalar(out=s_[:, 4:8], in0=c_[:, 31:128:32], scalar1=-COEF, scalar2=None, op0=ALU.mult), reads=[c_], writes=[s_])
        S.op("dve", lambda e: e.tensor_scalar(out=s_[:, 8:12], in0=c_[:, 31:128:32], scalar1=COEF, scalar2=None, op0=ALU.mult), reads=[c_], writes=[s_])
        S.op("act", lambda e: e.activation(out=s_[:, 14:15], in_=c_[:, 95:96], func=AF.Exp, scale=COEF, bias=s_[:, 4:5]), reads=[c_, s_], writes=[s_])
        S.op("act", lambda e: e.activation(out=s_[:, 15:16], in_=c_[:, 127:128], func=AF.Exp, scale=COEF), reads=[c_], writes=[s_])
        lo, hi = slice(0, 64), slice(64, 128)
        S.op("act", lambda e: e.activation(out=e1[:, lo], in_=c_[:, lo], func=AF.Exp, scale=COEF, bias=s_[:, 4:5]), reads=[c_, s_], writes=[e1])
        S.op("act", lambda e: e.activation(out=e1[:, hi], in_=c_[:, hi], func=AF.Exp, scale=COEF, bias=s_[:, 6:7]), reads=[c_, s_], writes=[e1])
        S.op("act", lambda e: e.activation(out=e2_[:, lo], in_=c_[:, lo], func=AF.Exp, scale=-COEF, bias=s_[:, 8:9]), reads=[c_, s_], writes=[e2_])
        S.op("act", lambda e: e.activation(out=e2_[:, hi], in_=c_[:, hi], func=AF.Exp, scale=-COEF, bias=s_[:, 10:11]), reads=[c_, s_], writes=[e2_])
        S.op("act", lambda e: e.activation(out=e3[:, :], in_=c_[:, :], func=AF.Exp, scale=-COEF, bias=s_[:, 11:12]), reads=[c_, s_], writes=[e3])
        S.op("act", lambda e: e.activation(out=e4[:, :], in_=c_[:, :], func=AF.Exp, scale=COEF), reads=[c_], writes=[e4])
        S.op("dve", lambda e: e.scalar_tensor_tensor(out=q_[:, :], in0=qT[:, tok], scalar=QS, in1=e1[:, :],
                                                     op0=ALU.mult, op1=ALU.mult), reads=[qT, e1], writes=[q_])
        S.op("dve", lambda e: e.tensor_scalar(out=qx[:, :], in0=q_[:, hi], scalar1=s_[:, 14:15], scalar2=None, op0=ALU.mult),
             reads=[q_, s_], writes=[qx])
        S.op("dve", lambda e: e.tensor_tensor(out=KA[:, lo], in0=kT[:, t * 128:t * 128 + 64], in1=e2_[:, lo], op=ALU.mult),
             reads=[kT, e2_], writes=[KA])
        S.op("dve", lambda e: e.tensor_tensor(out=KB[:, hi], in0=kT[:, t * 128 + 64:(t + 1) * 128], in1=e2_[:, hi], op=ALU.mult),
             reads=[kT, e2_], writes=[KB])
        S.op("dve", lambda e: e.tensor_tensor(out=kS[:, :], in0=kT[:, tok], in1=e3[:, :], op=ALU.mult), reads=[kT, e3], writes=[kS])
        S.op("pe", lambda e: e.matmul(p_att[:, 0:64], lhsT=KA[:, :], rhs=q_[:, lo], start=True, stop=True),
             reads=[KA, q_], writes=[p_att])
        S.op("pe", lambda e: e.matmul(p_att[:, 64:128], lhsT=KA[:, :], rhs=qx[:, :], start=True, stop=False),
             reads=[KA, qx], writes=[p_att])
        S.op("pe", lambda e: e.matmul(p_att[:, 64:128], lhsT=KB[:, :], rhs=q_[:, hi], start=False, stop=True),
             reads=[KB, q_], writes=[p_att])
        S.op("dve", lambda e: e.tensor_tensor(out=at_[:, :], in0=p_att[:, 0:128], in1=mle_b[:, :], op=ALU.mult),
             reads=[p_att, mle_b], writes=[at_])
        pf = G["psbf"].next()
        S.op("pe", lambda e: e.transpose(out=pf[:, 0:128], in_=kS[:, :], identity=ident_b[:, :]),
             reads=[kS, ident_b], writes=[pf])
        S.op("act", lambda e: e.activation(out=kt_[:, :], in_=pf[:, 0:128], func=AF.Copy), reads=[pf], writes=[kt_])
        vv = v[:, t, :]
        if t > 0:
            sb_ = G["Sbf"].next()
            S.op("pool", lambda e: e.tensor_copy(out=sb_[:, :], in_=Sst[:, :]), reads=[Sst], writes=[sb_])
            S.op("dve", lambda e: e.scalar_tensor_tensor(out=qS[:, :], in0=qT[:, tok], scalar=QS, in1=e4[:, :],
                                                         op0=ALU.mult, op1=ALU.mult), reads=[qT, e4], writes=[qS])
        S.op("pe", lambda e: e.matmul(p_o[:, 0:V], lhsT=at_[:, :], rhs=vv, start=True, stop=(t == 0)),
             reads=[at_, v], writes=[p_o])
        if t > 0:
            S.op("pe", lambda e: e.matmul(p_o[:, 0:V], lhsT=qS[:, :], rhs=sb_[:, :], start=False, stop=True),
                 reads=[qS, sb_], writes=[p_o])
        S.op("act", lambda e: e.activation(out=oh[:, t, :], in_=p_o[:, 0:V], func=AF.Copy), reads=[p_o], writes=[oh])
        pd = p_ds.next()
        S.op("pe", lambda e: e.matmul(pd[:, 0:V], lhsT=kt_[:, :], rhs=vv, start=True, stop=True), reads=[kt_, v], writes=[pd])
        if t == 0:
            S.op("dve", lambda e: e.tensor_copy(out=Sst[:, :], in_=pd[:, 0:V]), reads=[pd], writes=[Sst])
        else:
            S.op("dve", lambda e: e.scalar_tensor_tensor(out=Sst[:, :], in0=Sst[:, :], scalar=s_[:, 15:16], in1=pd[:, 0:V],
                                                         op0=ALU.mult, op1=ALU.add), reads=[pd, s_, Sst], writes=[Sst])
    S.dma("sp", out_p, Sst[:, :], reads=[Sst], writes=[obuf("st_p")])

    tok = slice(T, TA)
    c_, e1, e2_ = G["cs"].next(), G["E1"].next(), G["E2"].next()
    q_, k_, kt_, at_ = G["qp"].next(), G["kp"].next(), G["ktok"].next(), G["attm"].next()
    qm, km, bdm, colm, rowm = G["qm"], G["km"], G["bdm"], G["colm"], G["rowm"]
    gap, gtt = gate(tok, TS)
    l3 = gap.rearrange("p (s t) -> p s t", t=4)
    c3 = c_[:, 0:TS].rearrange("p (s t) -> p s t", t=4)
    S.op("dve", lambda e: e.tensor_copy(out=c3[:, :, 0], in_=l3[:, :, 0]), reads=[gtt], writes=[c_])
    for i in range(1, 4):
        S.op("dve", lambda e: e.tensor_tensor(out=c3[:, :, i], in0=c3[:, :, i - 1], in1=l3[:, :, i], op=ALU.add),
             reads=[gtt, c_], writes=[c_])
    S.op("act", lambda e: e.activation(out=e1[:, 0:TS], in_=c_[:, 0:TS], func=AF.Exp, scale=COEF), reads=[c_], writes=[e1])
    S.op("act", lambda e: e.activation(out=e2_[:, 0:TS], in_=c_[:, 0:TS], func=AF.Exp, scale=-COEF), reads=[c_], writes=[e2_])
    S.op("dve", lambda e: e.scalar_tensor_tensor(out=q_[:, 0:TS], in0=qT[:, tok], scalar=QS, in1=e1[:, 0:TS],
                                                 op0=ALU.mult, op1=ALU.mult), reads=[qT, e1], writes=[q_])
    S.op("dve", lambda e: e.tensor_tensor(out=k_[:, 0:TS], in0=kT[:, tok], in1=e2_[:, 0:TS], op=ALU.mult),
         reads=[kT, e2_], writes=[k_])
    S.op("pe", lambda e: e.matmul(p_att[0:TS, 0:TS], lhsT=k_[:, 0:TS], rhs=q_[:, 0:TS], start=True, stop=True),
         reads=[k_, q_], writes=[p_att])
    S.op("dve", lambda e: e.tensor_tensor(out=at_[0:TS, 0:TS], in0=p_att[0:TS, 0:TS], in1=bdm[:, :], op=ALU.mult),
         reads=[p_att, bdm], writes=[at_])
    pf = G["psbf"].next()
    S.op("pe", lambda e: e.transpose(out=pf[0:TS, 0:128], in_=k_[:, 0:TS], identity=ident_b[:, :]),
         reads=[k_, ident_b], writes=[pf])
    S.op("act", lambda e: e.activation(out=kt_[0:TS, :], in_=pf[0:TS, 0:128], func=AF.Copy), reads=[pf], writes=[kt_])
    S.op("dve", lambda e: e.tensor_tensor(out=qm[:, :, :], in0=q_[:, 0:TS].unsqueeze(1).to_broadcast([128, 16, TS]),
                                          in1=colm[:, :, :], op=ALU.mult), reads=[q_, colm], writes=[qm])
    S.op("dve", lambda e: e.tensor_tensor(out=km[:, :, :], in0=kt_[0:TS, :].unsqueeze(1).to_broadcast([TS, 16, 128]),
                                          in1=rowm[:, :].unsqueeze(2).to_broadcast([TS, 16, 128]), op=ALU.mult),
         reads=[kt_, rowm], writes=[km])
    vv = v[0:TS, 16, :]
    S.op("pe", lambda e: e.matmul(p_o[0:TS, 0:V], lhsT=at_[0:TS, 0:TS], rhs=vv, start=True, stop=False),
         reads=[at_, v], writes=[p_o])
    for s in range(NS):
        a0, a0b, an = G["s0"].next(), G["s0b"].next(), G["sn"].next()
        S.dma("sp", a0[:, :], state_in(s), writes=[a0])
        S.op("pool", lambda e: e.tensor_copy(out=a0b[:, :], in_=a0[:, :]), reads=[a0], writes=[a0b])
        S.op("pe", lambda e: e.matmul(p_o[0:TS, 0:V], lhsT=qm[:, s, :], rhs=a0b[:, :], start=False, stop=(s == NS - 1)),
             reads=[qm, a0b], writes=[p_o])
        pd = p_ds.next()
        S.op("pe", lambda e: e.matmul(pd[:, 0:V], lhsT=km[:, s, :], rhs=vv, start=True, stop=True), reads=[km, v], writes=[pd])
        etot = e1[:, 4 * s + 3:4 * s + 4]
        S.op("act", lambda e: e.activation(out=a0[:, :], in_=a0[:, :], func=AF.Copy, scale=etot), reads=[a0, e1], writes=[a0])
        S.op("dve", lambda e: e.scalar_tensor_tensor(out=an[:, :], in0=pd[:, 0:V], scalar=etot, in1=a0[:, :],
                                                     op0=ALU.mult, op1=ALU.add), reads=[pd, e1, a0], writes=[an])
        S.dma("pool", out_s(s), an[:, :], reads=[an], writes=[obuf("st_s")])
    S.op("act", lambda e: e.activation(out=oh[0:TS, 16, :], in_=p_o[0:TS, 0:V], func=AF.Copy), reads=[p_o], writes=[oh])
    return oh


def post_head(S, G, oh, z, gnb, V, fc0, dbgh=False):
    pss, prs, pjunk = G["pss"], G["prs"], G["pjunk"]
    ident_b = G["ident_b"]
    nf = V // 128
    for t in range(NT):
        rows = 128 if t < 16 else TS
        S.op("act", lambda e: e.activation(out=pjunk[:rows, :], in_=oh[:rows, t, :], func=AF.Square, accum_out=pss[:rows, t:t + 1]),
             reads=[oh], writes=[pjunk, pss])
    S.op("act", lambda e: e.activation(out=prs[:TS, :], in_=pss[:TS, :], func=AF.Sqrt, scale=1.0 / V, bias=EPS), reads=[pss], writes=[prs])
    S.op("act", lambda e: e.activation(out=prs[TS:, 0:16], in_=pss[TS:, 0:16], func=AF.Sqrt, scale=1.0 / V, bias=EPS), reads=[pss], writes=[prs])
    S.op("dve", lambda e: e.reciprocal(out=prs[:TS, :], in_=prs[:TS, :]), reads=[prs], writes=[prs])
    S.op("dve", lambda e: e.reciprocal(out=prs[TS:, 0:16], in_=prs[TS:, 0:16]), reads=[prs], writes=[prs])
    for t in range(NT):
        rows = 128 if t < 16 else TS
        tmp, ob, st = G["ptmp"].next(), G["pob"].next(), G["post"].next()
        S.op("dve", lambda e: e.scalar_tensor_tensor(out=tmp[:rows, :], in0=oh[:rows, t, :], scalar=prs[:rows, t:t + 1], in1=gnb[:rows, :],
                                                     op0=ALU.mult, op1=ALU.mult), reads=[oh, prs, gnb], writes=[tmp])
        S.op("dve", lambda e: e.tensor_tensor(out=ob[:rows, :], in0=tmp[:rows, :], in1=z[:rows, t, :], op=ALU.mult),
             reads=[tmp, z], writes=[ob])
        pf = G["psbf"].next()
        for k in range(nf):
            S.op("pe", lambda e: e.transpose(out=pf[:, k * 128:k * 128 + rows], in_=ob[:rows, k * 128:(k + 1) * 128],
                                             identity=ident_b[:rows, :rows]), reads=[ob, ident_b], writes=[pf])
        S.op("act", lambda e: e.activation(out=st[:, :, 0:rows], in_=pf[:, 0:nf * 128].rearrange("p (k t) -> p k t", k=nf)[:, :, 0:rows],
                                           func=AF.Copy), reads=[pf], writes=[st])
        S.dma("sp", G["oT_scr"][t, :, fc0:fc0 + nf, 0:rows], st[:, :, 0:rows], reads=[st], writes=G["oT_b"][t][fc0:fc0 + nf])
        if dbgh and t == 0:
            G["dbg"]("pss", pss, pss[:, :], [128, NT])
            G["dbg"]("prs", prs, prs[:, :], [128, NT])
            G["dbg"]("tmp", tmp, tmp[:, :], [128, V])
            G["dbg"]("ob", ob, ob[:, :], [128, V], BF16)
            G["dbg"]("st", st, st[:, :, :], [128, nf, 128], BF16)


_PROG = {}


def _arr(w, ncols):
    return np.ascontiguousarray(w.reshape(16, 128, ncols).transpose(1, 0, 2))


def kernel(x_prompt, x_sample, cache_kv, cache_win, state_gla, state_hgrn, page_table,
           a_norm, a_w_in, a_gla_w2, a_gla_b, a_gla_gn, a_cmp_pe, a_cmp_w1, a_cmp_b1, a_cmp_w2, a_w_out,
           c_norm, c_w_in, c_lb_logits, c_gn, c_w_out, final_norm):
    f32 = np.float32
    asc = lambda a: np.ascontiguousarray(np.asarray(a, dtype=f32))
    x_prompt, x_sample = asc(x_prompt), asc(x_sample)
    if "nc" not in _PROG:
        _PROG["nc"] = build_program()
    nc = _PROG["nc"]
    consts = make_consts()
    w0 = np.asarray(a_w_in, f32)[0]
    wA = _arr(w0, 6696)
    cols = []
    for h in range(4):
        cols += [np.arange(3608 + h * 128, 3608 + (h + 1) * 128), np.arange(4120 + h * 128, 4120 + (h + 1) * 128),
                 np.arange(4632 + h * 256, 4632 + (h + 1) * 256), np.arange(5672 + h * 256, 5672 + (h + 1) * 256)]
    cols.append(np.arange(5656, 5672))
    wG = _arr(np.ascontiguousarray(w0[:, np.concatenate(cols)]), 3088)
    wc = np.asarray(c_w_in, f32)[0]
    cols = []
    for h in range(16):
        cols += [np.arange(k * 2048 + h * 128, k * 2048 + (h + 1) * 128) for k in range(4)]
    wC = _arr(np.ascontiguousarray(wc[:, np.concatenate(cols)]), 8192)
    wOA = _arr(np.asarray(a_w_out, f32)[0], 2048)
    wOC = _arr(np.asarray(c_w_out, f32)[0], 2048)
    a_norm_r = asc(np.asarray(a_norm, f32)[0].reshape(16, 128).T)
    c_norm_r = asc(np.asarray(c_norm, f32)[0].reshape(16, 128).T)
    lb_log = asc(np.asarray(c_lb_logits, f32).reshape(2, 16, 128).transpose(2, 0, 1))
    w2a = asc(np.concatenate([np.asarray(a_gla_w2, f32)[0], np.asarray(a_gla_b, f32)[0][None, :]], axis=0))
    cw1 = asc(np.asarray(a_cmp_w1, f32)[0].reshape(2, 32, 128, 256).transpose(0, 2, 1, 3))
    cw2 = asc(np.asarray(a_cmp_w2, f32)[0].reshape(2, 2, 128, 128).transpose(2, 0, 1, 3))
    cpe = asc(np.asarray(a_cmp_pe, f32)[0].transpose(2, 0, 1))
    cb1 = asc(np.asarray(a_cmp_b1, f32)[0].reshape(2, 2, 128).transpose(2, 0, 1))
    state_gla = np.asarray(state_gla, f32)
    state_hgrn = np.asarray(state_hgrn, f32)
    cache_win = np.asarray(cache_win, f32)
    ckv2 = asc(cache_kv).reshape(2560 * 128 * 2, 512)
    in_maps = []
    for c in range(NCORES):
        sl = slice(c * NS, (c + 1) * NS)
        m = {
            "x_p": x_prompt[c % 4],
            "x_s": asc(x_sample[sl].reshape(TS, D)),
            "cache_win": asc(cache_win[0, sl].reshape(NS, 512, 512)),
            "cache_kv": ckv2, "page_table": np.ascontiguousarray(np.asarray(page_table)[sl].astype(np.int32)),
            "state_gla": asc(state_gla[0, sl]),
            "state_hgrn": asc(state_hgrn[0, sl]),
            "a_norm": a_norm_r, "c_norm": c_norm_r, "f_norm": asc(final_norm),
            "wA": wA, "wG": wG, "wC": wC, "wOA": wOA, "wOC": wOC,
            "gla_w2a": w2a, "gla_gn": asc(np.asarray(a_gla_gn, f32)[0]), "c_gn": asc(np.asarray(c_gn, f32)[0]),
            "lb_log": lb_log, "cmp_w1": cw1, "cmp_w2": cw2, "cmp_pe": cpe, "cmp_b1": cb1,
        }
        for k, v in consts.items():
            m["c_" + k] = v
        in_maps.append(m)
    res = run_bass_kernel_spmd(nc, in_maps, core_ids=list(range(NCORES)))
    R = res.results
    B, SEQ, DB = 4, 2048, 128
    cat = lambda name, shp: np.concatenate([np.asarray(R[c][name], f32).reshape(shp) for c in range(NCORES)])
    stk = lambda name, shp: np.stack([np.asarray(R[c][name], f32).reshape(shp) for c in range(4)])
    y_p = stk("y_p", (SEQ, D))
    y_s = cat("y_s", (NS, 4, D))
    kv_p = stk("kv_p", (SEQ, 4, 2, 128))[None]
    kv_s = cat("kv_s", (NS, 4, 4, 2, 128))[None]
    win_p = stk("win_p", (512, 2, 2, 128))[None]
    win_s = cat("win_s", (NS, 512, 2, 2, 128))[None]
    gla_p = stk("gla_p", (4, 128, 256))[None]
    gla_s = cat("gla_s", (NS, 4, 128, 256))[None]
    hg_p = stk("hg_p", (16, 128, 128))[None]
    hg_s = cat("hg_s", (NS, 16, 128, 128))[None]
    return (y_p, y_s, kv_p, kv_s, win_p, win_s, gla_p, gla_s, hg_p, hg_s)
```

```python
import numpy as np
from contextlib import ExitStack
import concourse.bass as bass
import concourse.mybir as mybir
from concourse.bass_utils import run_bass_kernel_spmd

F32 = mybir.dt.float32
BF16 = mybir.dt.bfloat16
I32 = mybir.dt.int32
AF = mybir.ActivationFunctionType
ALU = mybir.AluOpType
AX = mybir.AxisListType

NCORES = 8
D = 2048
T = 2048
NS = 16
TS = 64
TA = T + TS
NT = 17
EPS = 1e-6
STAGE = 4
DEBUG = False
STOPAT = 99


class Buf:
    __slots__ = ("name", "w", "r")

    def __init__(self, name=""):
        self.name = name
        self.w = None
        self.r = []


class TT:
    def __init__(self, t, name):
        self.t = t
        self.b = Buf(name)

    def __getitem__(self, k):
        return self.t[k]


class Sched:
    ENG = ("pe", "act", "dve", "pool", "sp")
    NDMA = 12

    def __init__(self, nc, es):
        self.nc = nc
        self.es = es
        self.eng = {"pe": nc.tensor, "act": nc.scalar, "dve": nc.vector, "pool": nc.gpsimd, "sp": nc.sync}
        self.sems = {}
        self.cnt = {}
        for e in ("pe", "act", "dve", "pool"):
            self.sems[e] = es.enter_context(nc.semaphore("s_" + e))
            self.cnt[e] = 0
        for q in ("sp", "act", "pool"):
            for i in range(self.NDMA):
                k = f"d_{q}{i}"
                self.sems[k] = es.enter_context(nc.semaphore(k))
                self.cnt[k] = 0
        self.dma_rr = {"sp": 0, "act": 0, "pool": 0}
        self.waited = {e: {} for e in self.ENG}
        self.n_inst = 0
        self.n_wait = 0
        self.uid = 0
        self.freed = {}
        self._scopes = []

    def scope(self):
        from contextlib import contextmanager

        @contextmanager
        def cm():
            rec = []
            self._scopes.append(rec)
            try:
                with ExitStack() as es:
                    yield es
            finally:
                self._scopes.pop()
                for tt in rec:
                    toks = list(tt.b.r) + ([tt.b.w] if tt.b.w else [])
                    for k, v in toks:
                        if self.freed.get(k, 0) < v:
                            self.freed[k] = v
        return cm()

    def tile(self, name, shape, dtype, es=None):
        self.uid += 1
        t = (es or self.es).enter_context(self.nc.sbuf_tensor(f"{name}_{self.uid}", list(shape), dtype))
        tt = TT(t, name)
        tt.b.r = list(self.freed.items())
        if es is not None and self._scopes:
            self._scopes[-1].append(tt)
        return tt

    def ptile(self, name, shape, dtype=F32, es=None):
        self.uid += 1
        t = (es or self.es).enter_context(self.nc.psum_tensor(f"{name}_{self.uid}", list(shape), dtype))
        return TT(t, name)

    def _deps(self, reads, writes):
        deps = {}

        def add(t):
            if t is None:
                return
            k, v = t
            if deps.get(k, 0) < v:
                deps[k] = v
        for b in reads:
            add(b.w)
        for b in writes:
            add(b.w)
            for t in b.r:
                add(t)
        return deps

    def _wait(self, e, deps):
        eng = self.eng[e]
        wd = self.waited[e]
        for k, v in deps.items():
            if wd.get(k, 0) < v:
                eng.wait_ge(self.sems[k], v)
                wd[k] = v
                self.n_wait += 1

    def _commit(self, tok, reads, writes):
        for b in reads:
            b.r.append(tok)
            if len(b.r) > 64:
                m = {}
                for k, v in b.r:
                    if m.get(k, 0) < v:
                        m[k] = v
                b.r = list(m.items())
        for b in writes:
            b.w = tok
            b.r = []

    @staticmethod
    def _bl(xs):
        out = []
        for x in xs:
            b = x.b if isinstance(x, (TT, View)) else x
            if isinstance(b, (list, tuple)):
                out.extend(b)
            else:
                out.append(b)
        return out

    def op(self, e, fn, reads=(), writes=(), sig=True):
        reads = self._bl(reads)
        writes = self._bl(writes)
        deps = self._deps(reads, writes)
        if e == "pe":
            deps.pop("pe", None)
        self._wait(e, deps)
        ins = fn(self.eng[e])
        if sig or e != "pe":
            self.cnt[e] += 1
            ins.then_inc(self.sems[e], 1)
            tok = (e, self.cnt[e])
        else:
            tok = (e, self.cnt[e] + 1)
        self._commit(tok, reads, writes)
        self.n_inst += 1
        return ins

    def dma(self, q, out, in_, reads=(), writes=(), fn=None, **kw):
        reads = self._bl(reads)
        writes = self._bl(writes)
        deps = self._deps(reads, writes)
        self._wait(q, deps)
        i = self.dma_rr[q]
        self.dma_rr[q] = (i + 1) % self.NDMA
        k = f"d_{q}{i}"
        if fn is None:
            ins = self.eng[q].dma_start(out=out, in_=in_, **kw)
        else:
            ins = fn(self.eng[q])
        self.cnt[k] += 16
        ins.then_inc(self.sems[k], 16)
        self._commit((k, self.cnt[k]), reads, writes)
        self.n_inst += 1

    def finish(self, bufs):
        deps = self._deps([], self._bl(bufs))
        self._wait("sp", deps)


class View:
    def __init__(self, ap, b):
        self.t = ap
        self.b = b

    def __getitem__(self, k):
        return self.t[k]


class Rot:
    def __init__(self, items):
        self.items = items
        self.i = 0

    def next(self):
        x = self.items[self.i]
        self.i = (self.i + 1) % len(self.items)
        return x


def make_consts():
    c = {}
    c["ident"] = np.eye(128, dtype=np.float32)
    j = np.arange(128)[:, None]
    i = np.arange(128)[None, :]
    c["mle"] = (j <= i).astype(np.float32)
    c["mgt"] = (j > i).astype(np.float32)
    j6 = np.arange(64)[:, None]
    i6 = np.arange(64)[None, :]
    c["bd"] = ((j6 // 4 == i6 // 4) & (j6 <= i6)).astype(np.float32)
    cm = (np.arange(64)[None, :] // 4 == np.arange(16)[:, None]).astype(np.float32)
    c["colmask"] = np.broadcast_to(cm[None], (128, 16, 64)).copy()
    c["rowmask"] = (np.arange(64)[:, None] // 4 == np.arange(16)[None, :]).astype(np.float32)
    def c2s(nc_, ns_):
        cst = np.arange(nc_)[:, None] * 16
        sst = np.arange(ns_)[None, :] * 64
        ov = np.clip(np.minimum(cst + 32, sst + 64) - np.maximum(cst, sst), 0, None)
        return (ov / 16).astype(np.float32)
    c["c2sp"] = c2s(127, 32)
    c["c2ss"] = c2s(127, 33)
    def eexp(ns_, nk_):
        key = np.arange(nk_ * 128)
        return (np.arange(ns_)[:, None] == (key[None, :] // 64)).astype(np.float32)
    c["eexp"] = eexp(32, 16)
    c["eexs"] = eexp(33, 17)
    x = np.arange(63)[None, :] - 31 - (np.arange(128)[:, None] >= 64)
    c["wsel"] = np.where(x > 0, -1e9, np.where(x >= -1, 1e9, 0.0)).astype(np.float32)
    c["rsum"] = (np.arange(16)[:, None] % 4 == np.arange(4)[None, :]).astype(np.float32)
    return c


CONST_SHAPES = {k: v.shape for k, v in make_consts().items()}


def build_program(stage=STAGE):
    nc = bass.Bass("TRN2", target_bir_lowering=False)

    def din(name, shape, dt=F32):
        return nc.dram_tensor(name, list(shape), dt, kind="ExternalInput").ap()

    def dout(name, shape, dt=F32):
        return nc.dram_tensor(name, list(shape), dt, kind="ExternalOutput").ap()

    def dscr(name, shape, dt=F32):
        return nc.dram_tensor(name, list(shape), dt, kind="ExternalOutput" if DEBUG else "Internal").ap()

    x_p = din("x_p", [T, D])
    x_s = din("x_s", [TS, D])
    cache_win = din("cache_win", [NS, 512, 512])
    cache_kv = din("cache_kv", [2560 * 128 * 2, 512])
    page_table = din("page_table", [NS, 16], I32)
    state_gla = din("state_gla", [NS, 4, 128, 256])
    state_hgrn = din("state_hgrn", [NS, 16, 128, 128])
    a_norm = din("a_norm", [128, 16])
    c_norm = din("c_norm", [128, 16])
    f_norm = din("f_norm", [D])
    wA = din("wA", [128, 16, 6696])
    wG = din("wG", [128, 16, 3088])
    wC = din("wC", [128, 16, 8192])
    wOA = din("wOA", [128, 16, 2048])
    wOC = din("wOC", [128, 16, 2048])
    gla_w2a = din("gla_w2a", [17, 512])
    gla_gn = din("gla_gn", [256])
    c_gn = din("c_gn", [128])
    lb_log = din("lb_log", [128, 2, 16])
    cmp_w1 = din("cmp_w1", [2, 128, 32, 256])
    cmp_w2 = din("cmp_w2", [128, 2, 2, 128])
    cmp_pe = din("cmp_pe", [128, 2, 32])
    cmp_b1 = din("cmp_b1", [128, 2, 2])
    cst = {k: din("c_" + k, list(s)) for k, s in CONST_SHAPES.items()}

    y_p = dout("y_p", [T, D])
    y_s = dout("y_s", [TS, D])
    kv_p = dout("kv_p", [T, 1024])
    kv_s = dout("kv_s", [TS, 1024])
    win_p = dout("win_p", [512, 512])
    win_s = dout("win_s", [NS, 512, 512])
    gla_p = dout("gla_p", [4, 128, 256])
    gla_s = dout("gla_s", [NS, 4, 128, 256])
    hg_p = dout("hg_p", [16, 128, 128])
    hg_s = dout("hg_s", [NS, 16, 128, 128])

    oT_scr = dscr("oT_scr", [NT, 128, 16, 128], BF16)
    g_scr = dscr("g_scr", [TS, 24])
    z_scr = dscr("z_scr", [TS, 1024], BF16)
    x1_scr = dscr("x1_scr", [TA, D])
    y_scr = dscr("y_scr", [TA, D])
    oT_b = [[Buf(f"oT{t}_{f}") for f in range(16)] for t in range(NT)]
    x1_b = [[Buf(f"x1_{t}_{k}") for k in range(4)] for t in range(NT)]
    ys_b = [[Buf(f"ys_{t}_{k}") for k in range(4)] for t in range(NT)]

    outs = []
    dbg_n = [0]

    with ExitStack() as es:
        S = Sched(nc, es)

        def dbg(name, tt, ap, shape, dt=F32):
            if not DEBUG:
                return
            dbg_n[0] += 1
            d = nc.dram_tensor(f"dbg_{name}", list(shape), dt, kind="ExternalOutput").ap()
            b = Buf(name)
            outs.append(b)
            S.dma("sp", d, ap, reads=[tt], writes=[b])

        def obuf(name):
            b = Buf(name)
            outs.append(b)
            return b

        hT_ref = [None]
        wbuf = Rot([S.tile(f"wbuf{i}", [128, 16, 512], BF16) for i in range(2)])
        ident_f = S.tile("ident_f", [128, 128], F32)
        ident_b = S.tile("ident_b", [128, 128], BF16)
        mle_b = S.tile("mle_b", [128, 128], BF16)
        normA = S.tile("normA", [128, 16], F32)
        normC = S.tile("normC", [128, 16], F32)
        PP = [S.ptile(f"pp{i}", [128, 1024], F32) for i in range(4)]
        bankb = [Buf(f"bank{i}") for i in range(8)]
        psb = [View(PP[i // 2][:, (i % 2) * 512:(i % 2) * 512 + 512], bankb[i]) for i in range(8)]
        psA = View(PP[2][:, :], [bankb[4], bankb[5]])
        psB = View(PP[3][:, :], [bankb[6], bankb[7]])
        psbf = Rot([View(PP[3][:, k * 512:(k + 1) * 512].bitcast(BF16), bankb[6 + k]) for k in range(2)])

        S.dma("sp", ident_f[:], cst["ident"][:, :], writes=[ident_f])
        S.dma("pool", ident_b[:], cst["ident"][:, :], writes=[ident_b])
        S.dma("pool", mle_b[:], cst["mle"][:, :], writes=[mle_b])
        S.dma("sp", normA[:], a_norm[:, :], writes=[normA])
        S.dma("sp", normC[:], c_norm[:, :], writes=[normC])

        for s in range(NS):
            S.dma("sp", win_s[s, 0:508, :], cache_win[s, 4:512, :], writes=[obuf("win_s")])

        def norm_pass(src_fn, normw, src_deps):
            with S.scope() as es1:
                xrot = Rot([S.tile(f"xt{i}", [128, D], F32, es1) for i in range(2)])
                ssr = Rot([S.tile(f"ss{i}", [128, 1], F32, es1) for i in range(2)])
                rsr = Rot([S.tile(f"rs{i}", [128, 1], F32, es1) for i in range(2)])
                rstdr = Rot([S.tile(f"rstd{i}", [128, 1], F32, es1) for i in range(2)])
                junk = S.tile("junk", [128, D], BF16, es1)
                prot = Rot(psb[0:4])
                for t in range(NT):
                    rows = 128 if t < 16 else TS
                    tok0 = t * 128
                    xt, ss, rs, rstd = xrot.next(), ssr.next(), rsr.next(), rstdr.next()
                    S.dma("sp", xt[:rows, :], src_fn(t, rows), reads=src_deps(t), writes=[xt])
                    S.op("act", lambda e: e.activation(out=junk[:rows, :], in_=xt[:rows, :], func=AF.Square,
                                                       accum_out=ss[:rows, 0:1]), reads=[xt], writes=[junk, ss])
                    S.op("act", lambda e: e.activation(out=rs[:rows, :], in_=ss[:rows, :], func=AF.Sqrt,
                                                       scale=1.0 / D, bias=EPS), reads=[ss], writes=[rs])
                    S.op("dve", lambda e: e.reciprocal(out=rstd[:rows, :], in_=rs[:rows, :]), reads=[rs], writes=[rstd])
                    S.op("act", lambda e: e.activation(out=xt[:rows, :], in_=xt[:rows, :], func=AF.Copy,
                                                       scale=rstd[:rows, 0:1]), reads=[xt, rstd], writes=[xt])
                    for cq in range(4):
                        pb = prot.next()
                        for k in range(4):
                            c = 4 * cq + k
                            S.op("pe", lambda e: e.transpose(out=pb[:, k * 128:k * 128 + rows],
                                                             in_=xt[:rows, c * 128:(c + 1) * 128],
                                                             identity=ident_f[:rows, :rows]),
                                 reads=[xt, ident_f], writes=[pb], sig=(k == 3))
                        S.op("dve", lambda e: e.tensor_tensor(
                            out=hT_ref[0][:, 4 * cq:4 * cq + 4, tok0:tok0 + rows],
                            in0=pb[:, :].rearrange("p (k t) -> p k t", k=4)[:, :, :rows],
                            in1=normw[:, 4 * cq:4 * cq + 4].unsqueeze(2).to_broadcast([128, 4, rows]),
                            op=ALU.mult), reads=[pb, normw], writes=[hT_ref[0]])


        def load_w(col0, n, src):
            wb = wbuf.next()
            S.dma("pool", wb[:, :, 0:n], src[:, :, col0:col0 + n], writes=[wb])
            return wb

        prj = Rot(psb[0:4])
        evq = Rot(["act", "dve"])

        def proj_tok_g(wb, n, consume, tiles=range(NT), wc0=0):
            for t in tiles:
                rows = 128 if t < 16 else TS
                pb = prj.next()
                for c in range(16):
                    S.op("pe", lambda e: e.matmul(pb[:rows, 0:n], lhsT=hT_ref[0][:, c, t * 128:t * 128 + rows],
                                                  rhs=wb[:, c, wc0:wc0 + n], start=(c == 0), stop=(c == 15)),
                         reads=[hT_ref[0], wb], writes=[pb], sig=(c == 15))
                consume(t, rows, pb)
                yield

        def proj_feat_g(wb, wc0, m, consume, chunks=range(5)):
            for j in chunks:
                n = 512 if j < 4 else TS
                pb = prj.next()
                for c in range(16):
                    S.op("pe", lambda e: e.matmul(pb[:m, 0:n], lhsT=wb[:, c, wc0:wc0 + m],
                                                  rhs=hT_ref[0][:, c, j * 512:j * 512 + n], start=(c == 0), stop=(c == 15)),
                         reads=[hT_ref[0], wb], writes=[pb], sig=(c == 15))
                consume(j, n, pb)
                yield

        def proj_tok(*a, **k):
            for _ in proj_tok_g(*a, **k):
                pass

        def proj_feat(*a, **k):
            for _ in proj_feat_g(*a, **k):
                pass

        def evac(out_ap, in_ap, reads, writes, func=None):
            q = "act" if func is not None else evq.next()
            if q == "act":
                S.op("act", lambda e: e.activation(out=out_ap, in_=in_ap, func=func or AF.Copy), reads=reads, writes=writes)
            else:
                S.op("dve", lambda e: e.tensor_copy(out=out_ap, in_=in_ap), reads=reads, writes=writes)

        OQ, OKV, OG, OZA = 0, 1024, 2560, 2584

        SC = 128 ** -0.5
        with S.scope() as esN:
            gts = S.tile("gts", [128, NT, 24], F32, esN)
            qTs = S.tile("qTs", [128, 8, TS], BF16, esN)
            zs = S.tile("zs", [128, 1024], BF16, esN)
            vnew_src = S.tile("vnew_src", [TS, 2, 2, 130], BF16, esN)
            S.op("pool", lambda e: e.memset(vnew_src[:, :, :, 128:130], 1.0), writes=[vnew_src])
            mgt_b = S.tile("mgt_b", [128, 128], BF16, esN)
            S.dma("pool", mgt_b[:, :], cst["mgt"][:, :], writes=[mgt_b])
            kselTs = S.tile("kselTs", [128, 2, TS], BF16, esN)
            kwinTs = S.tile("kwinTs", [128, 2, TS], BF16, esN)
            with S.scope() as esH:
                hT_ref[0] = S.tile("hT", [128, 16, TA], BF16, esH)
                norm_pass(lambda t, rows: (x_p[t * 128:(t + 1) * 128, :] if t < 16 else x_s[:, :]), normA, lambda t: [])
                R = dict(mle_b=mle_b, ident_b=ident_b, ident_f=ident_f, psb=psb, psbf=psbf, cst=cst, obuf=obuf,
                         oT_scr=oT_scr, oT_b=oT_b, dbg=dbg)

                with S.scope() as esG:
                    lrT = S.tile("lrT", [17, TA], F32, esG)
                    w2a = S.tile("w2a", [17, 512], F32, esG)
                    S.dma("sp", w2a[:, :], gla_w2a[:, :], writes=[w2a])
                    for j in range(0, TA, 128):
                        n = min(128, TA - j)
                        S.dma("sp", lrT[16:17, j:j + n], cst["mle"][0:1, 0:n], writes=[lrT])
                    wb = load_w(3072, 16, wG)
                    proj_feat(wb, 0, 16, lambda j, n, pb: evac(lrT[0:16, j * 512:j * 512 + n], pb[0:16, 0:n], [pb], [lrT]))
                    zb_shared = S.tile("zb", [128, NT, 256], BF16, esG)
                    hsets = Rot([dict(q=S.tile(f"qbT{i}", [128, TA], BF16, esG), k=S.tile(f"kbT{i}", [128, TA], BF16, esG),
                                      v=S.tile(f"vb{i}", [128, NT, 256], BF16, esG), z=zb_shared)
                                 for i in range(2)])
                    G = rec_setup(S, esG, R, V=256)
                    gnb = S.tile("gnb", [128, 256], F32, esG)
                    S.dma("sp", gnb[:, :], gla_gn.partition_broadcast(128), writes=[gnb])
                    lg = Rot([S.tile(f"lg{i}", [128, 128], F32, esG) for i in range(2)])

                    for h in range(4):
                        hs = hsets.next()
                        wb = load_w(h * 768, 512, wG)
                        proj_feat(wb, 0, 128, lambda j, n, pb: evac(hs["q"][:, j * 512:j * 512 + n], pb[:, 0:n], [pb], [hs["q"]]))
                        proj_feat(wb, 128, 128, lambda j, n, pb: evac(hs["k"][:, j * 512:j * 512 + n], pb[:, 0:n], [pb], [hs["k"]]))
                        proj_tok(wb, 256, lambda t, rows, pb: evac(hs["v"][:rows, t, :], pb[:rows, 0:256], [pb], [hs["v"]]), wc0=256)
                        wb = load_w(h * 768 + 512, 256, wG)
                        proj_tok(wb, 256, lambda t, rows, pb: evac(hs["z"][:rows, t, :], pb[:rows, 0:256], [pb], [hs["z"]], func=AF.Silu))

                        def gate(tok, n, h=h):
                            l_ = lg.next()
                            p_g = psb[3]
                            S.op("pe", lambda e: e.matmul(p_g[:, 0:n], lhsT=w2a[0:17, h * 128:(h + 1) * 128], rhs=lrT[0:17, tok],
                                                          start=True, stop=True), reads=[w2a, lrT], writes=[p_g])
                            S.op("act", lambda e: e.activation(out=l_[:, 0:n], in_=p_g[:, 0:n], func=AF.Exp, scale=-1.0), reads=[p_g], writes=[l_])
                            S.op("act", lambda e: e.activation(out=l_[:, 0:n], in_=l_[:, 0:n], func=AF.Ln, bias=1.0, scale=1.0), reads=[l_], writes=[l_])
                            return l_[:, 0:n], l_
                        oh = rec_head(S, G, hs, V=256, COEF=-1.0 / 16.0, QS=128 ** -0.5, gate=gate,
                                      state_in=lambda s: state_gla[s, h, :, :], out_p=gla_p[h, :, :], out_s=lambda s: gla_s[s, h, :, :])
                        if h == 0:
                            dbg("gnb", gnb, gnb[:, :], [128, 256])
                            dbg("oh", oh, oh[:, 0, :], [128, 256], BF16)
                            dbg("zb", hs["z"], hs["z"][:, 0, :], [128, 256], BF16)
                        post_head(S, G, oh, hs["z"], gnb, V=256, fc0=8 + 2 * h, dbgh=(h == 0))
                        if STOPAT <= 1:
                            break

                with S.scope() as esNP:
                    vsel = S.tile("vsel", [128, NT, 2, 130], BF16, esNP)
                    vwin = S.tile("vwin", [128, NT, 2, 130], BF16, esNP)
                    S.op("pool", lambda e: e.memset(vsel[:, :, :, 128:130], 1.0), writes=[vsel])
                    S.op("pool", lambda e: e.memset(vwin[:, :, :, 128:130], 1.0), writes=[vwin])
                    kselT = S.tile("kselT", [128, 2, T], BF16, esNP)
                    kwinT = S.tile("kwinT", [128, 2, T], BF16, esNP)
                    kcT = S.tile("kcT", [128, 2, 128], BF16, esNP)
                    vca = S.tile("vca", [128, 2, 162], BF16, esNP)
                    eexp = S.tile("eexp", [32, 16, 128], BF16, esNP)
                    S.dma("pool", eexp[:, :, :], cst["eexp"].rearrange("s (k j) -> s k j", k=16), writes=[eexp])
                    wsel = S.tile("wsel", [128, 63], F32, esNP)
                    S.dma("sp", wsel[:, :], cst["wsel"][:, :], writes=[wsel])
                    S.op("pool", lambda e: e.memset(vca[:, :, 128:129], 1.0), writes=[vca])
                    for g in range(2):
                        S.dma("pool", vca[0:127, g, 129:161], cst["c2sp"][:, :], writes=[vca])

                    with S.scope() as esKV:
                        stg = Rot([S.tile(f"stg{i}", [128, 512], F32, esKV) for i in range(3)])
                        for blk in range(3):
                            wb = load_w(OKV + blk * 512, 512, wA)

                            def cons(t, rows, pb, blk=blk):
                                st = stg.next()
                                evac(st[:rows, :], pb[:rows, :], [pb], [st])
                                if blk < 2:
                                    dst = (kv_p[t * 128:t * 128 + rows, blk * 512:(blk + 1) * 512] if t < 16
                                           else kv_s[:, blk * 512:(blk + 1) * 512])
                                    S.dma("sp", dst, st[:rows, :], reads=[st], writes=[obuf("kv")])
                                else:
                                    if t >= 12 and t < 16:
                                        S.dma("sp", win_p[(t - 12) * 128:(t - 11) * 128, :], st[:rows, :], reads=[st], writes=[obuf("win")])
                                    elif t == 16:
                                        for s in range(NS):
                                            S.dma("sp", win_s[s, 508:512, :], st[4 * s:4 * s + 4, :], reads=[st], writes=[obuf("wins")])
                                if blk >= 1:
                                    vt = vsel if blk == 1 else vwin
                                    S.op("pool", lambda e: e.tensor_copy(out=vt[:rows, t, :, 0:128],
                                                                         in_=st[:rows, 256:512].rearrange("p (g d) -> p g d", g=2)),
                                         reads=[st], writes=[vt])
                                    if t == 16:
                                        S.op("pool", lambda e: e.tensor_copy(out=vnew_src[:, blk - 1, :, 0:128],
                                                                             in_=st[:rows, 256:512].rearrange("p (g d) -> p g d", g=2)),
                                             reads=[st], writes=[vnew_src])
                            proj_tok(wb, 512, cons)

                    if stage >= 3:
                        with S.scope() as esCmp:
                            kcmpT = S.tile("kcmpT", [128, 2, T], BF16, esCmp)
                            vcmpT = S.tile("vcmpT", [128, 2, T], BF16, esCmp)
                            wb = load_w(OKV, 512, wA)
                            for i, dstt in enumerate((kcmpT, kcmpT, vcmpT, vcmpT)):
                                proj_feat(wb, i * 128, 128, lambda j, n, pb, dstt=dstt, i=i: evac(dstt[:, i % 2, j * 512:j * 512 + n], pb[:, 0:n], [pb], [dstt]),
                                          chunks=range(4))
                            wb = load_w(OKV + 512, 256, wA)
                            for g in range(2):
                                proj_feat(wb, g * 128, 128, lambda j, n, pb, g=g: (evac(kselT[:, g, j * 512:j * 512 + n], pb[:, 0:n], [pb], [kselT]) if j < 4
                                                                                  else evac(kselTs[:, g, :], pb[:, 0:n], [pb], [kselTs])))
                            wb = load_w(OKV + 1024, 256, wA)
                            for g in range(2):
                                proj_feat(wb, g * 128, 128, lambda j, n, pb, g=g: (evac(kwinT[:, g, j * 512:j * 512 + n], pb[:, 0:n], [pb], [kwinT]) if j < 4
                                                                                  else evac(kwinTs[:, g, :], pb[:, 0:n], [pb], [kwinTs])))
                            w1b = S.tile("w1b", [128, 32, 256], BF16, esCmp)
                            w2b = S.tile("w2b", [128, 2, 2, 128], BF16, esCmp)
                            peb = S.tile("peb", [128, 2, 32], BF16, esCmp)
                            b1t = S.tile("b1t", [128, 2, 2], F32, esCmp)
                            hb = S.tile("hb", [128, 2, 2], F32, esCmp)
                            gT = Rot([S.tile(f"gT{i}", [128, 2, 128], BF16, esCmp) for i in range(2)])
                            S.dma("pool", w2b[:, :, :, :], cmp_w2[:, :, :, :], writes=[w2b])
                            S.dma("pool", peb[:, :, :], cmp_pe[:, :, :], writes=[peb])
                            S.dma("sp", b1t[:, :, :], cmp_b1[:, :, :], writes=[b1t])
                            for kv in range(2):
                                S.dma("pool", w1b[:, :, :], cmp_w1[kv, :, :, :], writes=[w1b])
                                xT = kcmpT if kv == 0 else vcmpT
                                pp = psb[3]
                                for half in range(2):
                                    for rp in range(32):
                                        S.op("pe", lambda e: e.matmul(pp[:, half:half + 1], lhsT=w1b[:, rp, half * 128:(half + 1) * 128],
                                                                      rhs=peb[:, kv, rp:rp + 1], start=(rp == 0), stop=(rp == 31)),
                                             reads=[w1b, peb], writes=[pp])
                                S.op("dve", lambda e: e.tensor_tensor(out=hb[:, kv, :], in0=pp[:, 0:2], in1=b1t[:, kv, :], op=ALU.add),
                                     reads=[pp, b1t], writes=[hb])
                                for g in range(2):
                                    gt = gT.next()
                                    for half in range(2):
                                        ph = prj.next()
                                        for rp in range(32):
                                            r_, p_ = rp // 16, rp % 16
                                            st0 = 16 * r_ + p_
                                            S.op("pe", lambda e: e.matmul(ph[:, 0:127], lhsT=w1b[:, rp, half * 128:(half + 1) * 128],
                                                                          rhs=xT[:, g, st0:st0 + 16 * 126 + 1:16], start=(rp == 0), stop=(rp == 31)),
                                                 reads=[w1b, xT], writes=[ph], sig=(rp == 31))
                                        S.op("act", lambda e: e.activation(out=gt[:, half, 0:127], in_=ph[:, 0:127], func=AF.Gelu_apprx_tanh,
                                                                           bias=hb[:, kv, half:half + 1]), reads=[ph, hb], writes=[gt])
                                    po = prj.next()
                                    if kv == 0:
                                        for half in range(2):
                                            S.op("pe", lambda e: e.matmul(po[:, 0:127], lhsT=w2b[:, 0, half, :], rhs=gt[:, half, 0:127],
                                                                          start=(half == 0), stop=(half == 1)), reads=[w2b, gt], writes=[po])
                                        evac(kcT[:, g, 0:127], po[:, 0:127], [po], [kcT])
                                    else:
                                        for half in range(2):
                                            S.op("pe", lambda e: e.matmul(po[0:127, 0:128], lhsT=gt[:, half, 0:127], rhs=w2b[:, 1, half, :],
                                                                          start=(half == 0), stop=(half == 1)), reads=[w2b, gt], writes=[po])
                                        evac(vca[0:127, g, 0:128], po[0:127, 0:128], [po], [vca])

                        wb = load_w(OG, 24, wA)
                        proj_tok(wb, 24, lambda t, rows, pb: evac(gts[:rows, t, :], pb[:rows, 0:24], [pb], [gts], func=AF.Sigmoid))

                        for g in range(2):
                            with S.scope() as esQ:
                                qTg = S.tile("qTg", [128, 4, T], BF16, esQ)
                                zg = S.tile("zg", [128, 16, 512], BF16, esQ)
                                wb = load_w(OQ + g * 512, 512, wA)
                                for r in range(4):
                                    def qcons(j, n, pb, r=r):
                                        if j < 4:
                                            evac(qTg[:, r, j * 512:j * 512 + n], pb[:, 0:n], [pb], [qTg])
                                        else:
                                            evac(qTs[:, 4 * g + r, :], pb[:, 0:n], [pb], [qTs])
                                    proj_feat(wb, r * 128, 128, qcons)
                                wb = load_w(OZA + g * 512, 512, wA)

                                def zcons(t, rows, pb):
                                    if t < 16:
                                        evac(zg[:, t, :], pb[:, :], [pb], [zg], func=AF.Silu)
                                    else:
                                        evac(zs[:rows, g * 512:(g + 1) * 512], pb[:rows, :], [pb], [zs], func=AF.Silu)
                                proj_tok(wb, 512, zcons)
                                nsa_prompt(S, esQ, g, dict(qTg=qTg, zg=zg, kcT=kcT, vca=vca, kselT=kselT, kwinT=kwinT, vsel=vsel, vwin=vwin,
                                                           gts=gts, wsel=wsel, eexp=eexp, mle_b=mle_b, mgt_b=mgt_b, ident_b=ident_b,
                                                           psb=psb, psA=psA, psB=psB, oT_scr=oT_scr, oT_b=oT_b, SC=SC))
            if stage >= 4:
                with S.scope() as esS:
                    nsa_sample(S, esS, nc, dict(qTs=qTs, zs=zs, gts=gts, kselT=kselTs, kwinT=kwinTs, vnew_src=vnew_src, mle_b=mle_b,
                                                mgt_b=mgt_b, ident_b=ident_b, ident_f=ident_f, psb=psb, wbuf=wbuf, cst=cst, SC=SC,
                                                cache_kv=cache_kv, cache_win=cache_win, page_table=page_table, cmp_w1=cmp_w1,
                                                cmp_w2=cmp_w2, cmp_pe=cmp_pe, cmp_b1=cmp_b1, g_scr=g_scr, z_scr=z_scr,
                                                oT_scr=oT_scr, oT_b=oT_b, evac=evac))

            if stage < 3:
                with S.scope() as esZ:
                    zt = S.tile("zt", [128, 8, 128], BF16, esZ)
                    S.op("pool", lambda e: e.memset(zt[:, :, :], 0.0), writes=[zt])
                    for t in range(NT):
                        S.dma("sp", oT_scr[t, :, 0:8, :], zt[:, :, :], reads=[zt], writes=oT_b[t][0:8])
            elif stage < 4:
                with S.scope() as esZ:
                    zt = S.tile("zt", [128, 8, 128], BF16, esZ)
                    S.op("pool", lambda e: e.memset(zt[:, :, :], 0.0), writes=[zt])
                    S.dma("sp", oT_scr[16, :, 0:8, :], zt[:, :, :], reads=[zt], writes=oT_b[16][0:8])

        def wout_phase(wsrc, res_fn, res_deps, dst, dst_b):
            with S.scope() as e3:
                otr = Rot([S.tile(f"ot{i}", [128, 16, 128], BF16, e3) for i in range(4)])
                xr = Rot([S.tile(f"xr{i}", [128, 512], F32, e3) for i in range(5)])
                for blk in range(4):
                    wb = load_w(blk * 512, 512, wsrc)
                    for t in range(NT):
                        rows = 128 if t < 16 else TS
                        ot, xt = otr.next(), xr.next()
                        S.dma("sp", ot[:, :, :], oT_scr[t, :, :, :], reads=oT_b[t], writes=[ot])
                        S.dma("sp", xt[:rows, :], res_fn(t, rows, blk), reads=res_deps(t, blk), writes=[xt])
                        pb = prj.next()
                        for c in range(16):
                            S.op("pe", lambda e: e.matmul(pb[:rows, 0:512], lhsT=ot[:, c, 0:rows], rhs=wb[:, c, 0:512],
                                                          start=(c == 0), stop=(c == 15)), reads=[ot, wb], writes=[pb], sig=(c == 15))
                        S.op("dve", lambda e: e.tensor_tensor(out=xt[:rows, :], in0=pb[:rows, 0:512], in1=xt[:rows, :], op=ALU.add),
                             reads=[pb, xt], writes=[xt])
                        S.dma("pool", dst[t * 128:t * 128 + rows, blk * 512:(blk + 1) * 512], xt[:rows, :], reads=[xt], writes=[dst_b[t][blk]])

        def xsrc(t, rows, blk):
            return (x_p[t * 128:(t + 1) * 128, blk * 512:(blk + 1) * 512] if t < 16 else x_s[:, blk * 512:(blk + 1) * 512])

        esH2 = es.enter_context(S.scope())
        hT_ref[0] = S.tile("hT2", [128, 16, TA], BF16, esH2)
        if STOPAT > 1:
            wout_phase(wOA, xsrc, lambda t, blk: [], x1_scr, x1_b)
        run_c = STOPAT > 2

        if run_c:
          norm_pass(lambda t, rows: x1_scr[t * 128:t * 128 + rows, :], normC, lambda t: x1_b[t])
        with S.scope() as esC:
          if run_c:
            lbt = S.tile("lbt", [128, 2, 16], F32, esC)
            lb = S.tile("lb", [128, 16], F32, esC)
            oml = S.tile("oml", [128, 16], F32, esC)
            S.dma("sp", lbt[:, :, :], lb_log[:, :, :], writes=[lbt])
            S.op("dve", lambda e: e.tensor_tensor(out=lb[:, :], in0=lbt[:, 1, :], in1=lbt[:, 0, :], op=ALU.subtract), reads=[lbt], writes=[lb])
            S.op("act", lambda e: e.activation(out=lb[:, :], in_=lb[:, :], func=AF.Sigmoid), reads=[lb], writes=[lb])
            S.op("dve", lambda e: e.tensor_scalar(out=oml[:, :], in0=lb[:, :], scalar1=-1.0, scalar2=1.0, op0=ALU.mult, op1=ALU.add),
                 reads=[lb], writes=[oml])
            hsets = Rot([dict(q=S.tile(f"qcT{i}", [128, TA], BF16, esC), k=S.tile(f"kcT{i}", [128, TA], BF16, esC),
                              g=S.tile(f"gcT{i}", [128, TA], F32, esC),
                              v=S.tile(f"vc{i}", [128, NT, 128], BF16, esC), z=S.tile(f"zc{i}", [128, NT, 128], BF16, esC))
                         for i in range(2)])
            G = rec_setup(S, esC, R, V=128)
            gnc = S.tile("gnc", [128, 128], F32, esC)
            S.dma("sp", gnc[:, :], c_gn.partition_broadcast(128), writes=[gnc])
            sgr = Rot([S.tile(f"sg{i}", [128, 512], F32, esC) for i in range(3)])
            fa = S.tile("fa", [128, 16], F32, esC)
            fb = S.tile("fb", [128, 16], F32, esC)
            S.op("dve", lambda e: e.tensor_scalar(out=fa[:, :], in0=oml[:, :], scalar1=0.5, scalar2=None, op0=ALU.mult), reads=[oml], writes=[fa])
            S.op("dve", lambda e: e.tensor_tensor(out=fb[:, :], in0=fa[:, :], in1=lb[:, :], op=ALU.add), reads=[fa, lb], writes=[fb])
            S.op("dve", lambda e: e.tensor_scalar(out=gnc[:, :], in0=gnc[:, :], scalar1=0.5, scalar2=None, op0=ALU.mult), reads=[gnc], writes=[gnc])
            def head_proj(h, hs):
                wb = load_w(h * 512, 512, wC)

                def qcons(j, n, pb):
                    th = sgr.next()
                    S.op("act", lambda e: e.activation(out=th[:, 0:n], in_=pb[:, 0:n], func=AF.Tanh, scale=0.5), reads=[pb], writes=[th])
                    S.op("dve", lambda e: e.scalar_tensor_tensor(out=hs["q"][:, j * 512:j * 512 + n], in0=th[:, 0:n], scalar=1.0, in1=pb[:, 0:n],
                                                                 op0=ALU.add, op1=ALU.mult), reads=[th, pb], writes=[hs["q"]])
                yield from proj_feat_g(wb, 0, 128, qcons)

                def fgate(j, n, pb):
                    th = sgr.next()
                    sl = slice(j * 512, j * 512 + n)
                    S.op("act", lambda e: e.activation(out=th[:, 0:n], in_=pb[:, 0:n], func=AF.Tanh, scale=0.5), reads=[pb], writes=[th])
                    S.op("dve", lambda e: e.tensor_scalar(out=hs["g"][:, sl], in0=th[:, 0:n], scalar1=fa[:, h:h + 1], scalar2=fb[:, h:h + 1],
                                                          op0=ALU.mult, op1=ALU.add), reads=[th, fa, fb], writes=[hs["g"]])
                    S.op("dve", lambda e: e.tensor_scalar(out=hs["k"][:, sl], in0=hs["g"][:, sl], scalar1=-1.0, scalar2=1.0,
                                                          op0=ALU.mult, op1=ALU.add), reads=[hs["g"]], writes=[hs["k"]])
                yield from proj_feat_g(wb, 128, 128, fgate)
                S.op("act", lambda e: e.activation(out=hs["g"][:, :], in_=hs["g"][:, :], func=AF.Ln), reads=[hs["g"]], writes=[hs["g"]])
                yield
                yield from proj_tok_g(wb, 128, lambda t, rows, pb: evac(hs["v"][:rows, t, :], pb[:rows, 0:128], [pb], [hs["v"]]), wc0=256)

                def zcons(t, rows, pb):
                    th = sgr.next()
                    S.op("act", lambda e: e.activation(out=th[:rows, 0:128], in_=pb[:rows, 0:128], func=AF.Tanh, scale=0.5), reads=[pb], writes=[th])
                    S.op("dve", lambda e: e.scalar_tensor_tensor(out=hs["z"][:rows, t, :], in0=th[:rows, 0:128], scalar=1.0, in1=pb[:rows, 0:128],
                                                                 op0=ALU.add, op1=ALU.mult), reads=[th, pb], writes=[hs["z"]])
                yield from proj_tok_g(wb, 128, zcons, wc0=384)

            def step(gen, k):
                if gen is None:
                    return
                for _ in range(k):
                    try:
                        next(gen)
                    except StopIteration:
                        return

            hs_cur = hsets.next()
            step(head_proj(0, hs_cur), 1000)
            for h in range(16):
                hs = hs_cur
                if h + 1 < 16:
                    hs_cur = hsets.next()
                    gen = head_proj(h + 1, hs_cur)
                else:
                    gen = None
                oh = rec_head(S, G, hs, V=128, COEF=1.0, QS=0.5, gate=lambda tok, n, hs=hs: (hs["g"][:, tok], hs["g"]),
                              state_in=lambda s, h=h: state_hgrn[s, h, :, :], out_p=hg_p[h, :, :], out_s=lambda s, h=h: hg_s[s, h, :, :],
                              hook=lambda: step(gen, 3))
                step(gen, 1000)
                post_head(S, G, oh, hs["z"], gnc, V=128, fc0=h)

        if run_c:
          wout_phase(wOC, lambda t, rows, blk: x1_scr[t * 128:t * 128 + rows, blk * 512:(blk + 1) * 512],
                     lambda t, blk: [x1_b[t][blk]], y_scr, ys_b)

        with S.scope() as e4:
          if run_c:
            xrot = Rot([S.tile(f"yt{i}", [128, D], F32, e4) for i in range(3)])
            ssr = Rot([S.tile(f"yss{i}", [128, 1], F32, e4) for i in range(2)])
            rsr = Rot([S.tile(f"yrs{i}", [128, 1], F32, e4) for i in range(2)])
            rstdr = Rot([S.tile(f"yrstd{i}", [128, 1], F32, e4) for i in range(2)])
            junk = S.tile("yjunk", [128, D], BF16, e4)
            fnb = S.tile("fnb", [128, D], F32, e4)
            S.dma("sp", fnb[:, :], f_norm.partition_broadcast(128), writes=[fnb])
            for t in range(NT):
                rows = 128 if t < 16 else TS
                xt, ss, rs, rstd = xrot.next(), ssr.next(), rsr.next(), rstdr.next()
                S.dma("sp", xt[:rows, :], y_scr[t * 128:t * 128 + rows, :], reads=ys_b[t], writes=[xt])
                S.op("act", lambda e: e.activation(out=junk[:rows, :], in_=xt[:rows, :], func=AF.Square,
                                                   accum_out=ss[:rows, 0:1]), reads=[xt], writes=[junk, ss])
                S.op("act", lambda e: e.activation(out=rs[:rows, :], in_=ss[:rows, :], func=AF.Sqrt,
                                                   scale=1.0 / D, bias=EPS), reads=[ss], writes=[rs])
                S.op("dve", lambda e: e.reciprocal(out=rstd[:rows, :], in_=rs[:rows, :]), reads=[rs], writes=[rstd])
                S.op("dve", lambda e: e.scalar_tensor_tensor(out=xt[:rows, :], in0=xt[:rows, :], scalar=rstd[:rows, 0:1], in1=fnb[:rows, :],
                                                             op0=ALU.mult, op1=ALU.mult), reads=[xt, rstd, fnb], writes=[xt])
                dst = y_p[t * 128:(t + 1) * 128, :] if t < 16 else y_s[:, :]
                S.dma("pool", dst, xt[:rows, :], reads=[xt], writes=[obuf("y")])

        S.finish(outs)
        print("instructions", S.n_inst, "waits", S.n_wait)
    return nc


def nsa_prompt(S, es, g, N):
    qTg, zg, kcT, vca, kselT, kwinT, vsel, vwin, gts = (N[k] for k in ("qTg", "zg", "kcT", "vca", "kselT", "kwinT", "vsel", "vwin", "gts"))
    wsel, eexp, mle_b, mgt_b, ident_b, psb, psA, psB, SC = (N[k] for k in ("wsel", "eexp", "mle_b", "mgt_b", "ident_b", "psb", "psA", "psB", "SC"))
    tl = lambda n, s, d=F32: S.tile(n, s, d, es)
    et = Rot([tl(f"et{i}", [128, 512], BF16) for i in range(3)])
    den = Rot([tl(f"den{i}", [128, 4]) for i in range(3)])
    rden = Rot([tl(f"rden{i}", [128, 4]) for i in range(3)])
    cf = Rot([tl(f"cf{i}", [128, 4]) for i in range(3)])
    acc = Rot([tl(f"acc{i}", [128, 4, 128]) for i in range(2)])
    tmpo = Rot([tl(f"tmpo{i}", [128, 4, 128]) for i in range(2)])
    impn = tl("impn", [128, 4, 32])
    sc = Rot([tl(f"sc{i}", [128, 32]) for i in range(2)])
    sc2 = tl("sc2", [128, 32])
    m8 = tl("m8", [128, 16])
    selm = tl("selm", [128, 32], BF16)
    selT = Rot([tl(f"selT{i}", [32, 128], BF16) for i in range(2)])
    m2 = Rot([tl(f"m2{i}", [128, 128], BF16) for i in range(2)])
    ob = Rot([tl(f"nob{i}", [128, 512], BF16) for i in range(2)])
    st = Rot([tl(f"nst{i}", [128, 4, 128], BF16) for i in range(2)])
    scb = Rot([psb[0], psb[1]])
    pmk, pmisc = psb[2], psb[3]
    pmisc_bf = View(pmisc[:, :].bitcast(BF16), pmisc.b)
    A3 = View(psA[:, :].rearrange("p (r c) -> p r c", r=4), psA.b)
    B3 = View(psB[:, :].rearrange("p (r c) -> p r c", r=4), psB.b)

    def v4(x):
        return x[:, :].rearrange("p (r q) -> p r q", r=4)

    def finish_branch(P3, br, t, a_, first):
        d_, r_, c_ = den.next(), rden.next(), cf.next()
        S.op("dve", lambda e: e.tensor_scalar(out=d_[:, :], in0=P3[:, :, 128], scalar1=1e-30, scalar2=None, op0=ALU.max), reads=[P3], writes=[d_])
        S.op("dve", lambda e: e.reciprocal(out=r_[:, :], in_=d_[:, :]), reads=[d_], writes=[r_])
        S.op("dve", lambda e: e.tensor_tensor(out=c_[:, :], in0=r_[:, :], in1=gts[:, t, 12 * g + br:12 * g + 12:3], op=ALU.mult),
             reads=[r_, gts], writes=[c_])
        cb = c_[:, :].unsqueeze(2).to_broadcast([128, 4, 128])
        if first:
            S.op("dve", lambda e: e.tensor_tensor(out=a_[:, :, :], in0=P3[:, :, 0:128], in1=cb, op=ALU.mult), reads=[P3, c_], writes=[a_])
        else:
            tm = tmpo.next()
            S.op("dve", lambda e: e.tensor_tensor(out=tm[:, :, :], in0=P3[:, :, 0:128], in1=cb, op=ALU.mult), reads=[P3, c_], writes=[tm])
            S.op("pool", lambda e: e.tensor_tensor(out=a_[:, :, :], in0=a_[:, :, :], in1=tm[:, :, :], op=ALU.add), reads=[a_, tm], writes=[a_])
        return r_

    for t in range(16):
        t0 = 128 * t
        qrhs = qTg[:, :, t0:t0 + 128]
        a_ = acc.next()
        ps = scb.next()
        S.op("pe", lambda e: e.matmul(v4(ps)[0:127], lhsT=kcT[:, g, 0:127], rhs=qrhs, start=True, stop=True), reads=[kcT, qTg], writes=[ps])
        ec = et.next()
        S.op("act", lambda e: e.activation(out=ec[0:127, :], in_=ps[0:127, :], func=AF.Exp, scale=SC), reads=[ps], writes=[ec])
        S.op("pool", lambda e: e.affine_select(out=v4(ec)[0:127], in_=v4(ec)[0:127], pattern=[[0, 4], [1, 128]], compare_op=ALU.is_ge,
                                               fill=0.0, base=t0 - 31, channel_multiplier=-16), reads=[ec], writes=[ec])
        for r in range(4):
            S.op("pe", lambda e: e.matmul(A3[:, r, 0:161], lhsT=ec[0:127, r * 128:(r + 1) * 128], rhs=vca[0:127, g, 0:161],
                                          start=True, stop=True), reads=[ec, vca], writes=[A3])
        rd = finish_branch(A3, 0, t, a_, True)
        sel = t >= 8
        if sel:
            s_ = sc.next()
            S.op("dve", lambda e: e.tensor_tensor(out=impn[:, :, :], in0=A3[:, :, 129:161], in1=rd[:, :].unsqueeze(2).to_broadcast([128, 4, 32]),
                                                  op=ALU.mult), reads=[A3, rd], writes=[impn])
            S.op("dve", lambda e: e.tensor_reduce(out=s_[:, :], in_=impn[:, :, :].rearrange("p r s -> p s r"), axis=AX.X, op=ALU.add),
                 reads=[impn], writes=[s_])
            S.op("dve", lambda e: e.tensor_tensor(out=s_[:, :], in0=s_[:, :], in1=wsel[:, 31 - 2 * t:63 - 2 * t], op=ALU.add),
                 reads=[s_, wsel], writes=[s_])
            S.op("dve", lambda e: e.memset(s_[:, 0:1], 1e9), reads=[], writes=[s_])
            S.op("dve", lambda e: e.max(out=m8[:, 0:8], in_=s_[:, :]), reads=[s_], writes=[m8])
            S.op("dve", lambda e: e.match_replace(out=sc2[:, :], in_to_replace=m8[:, 0:8], in_values=s_[:, :], imm_value=-3e38),
                 reads=[s_, m8], writes=[sc2])
            S.op("dve", lambda e: e.max(out=m8[:, 8:16], in_=sc2[:, :]), reads=[sc2], writes=[m8])
            S.op("dve", lambda e: e.tensor_scalar(out=selm[:, :], in0=s_[:, :], scalar1=m8[:, 15:16], scalar2=None, op0=ALU.is_ge),
                 reads=[s_, m8], writes=[selm])
            S.op("pe", lambda e: e.transpose(out=pmisc_bf[0:32, 0:128], in_=selm[:, 0:32], identity=ident_b[:, :]),
                 reads=[selm, ident_b], writes=[pmisc_bf])
            sT = selT.next()
            S.op("act", lambda e: e.activation(out=sT[:, :], in_=pmisc_bf[0:32, 0:128], func=AF.Copy), reads=[pmisc_bf], writes=[sT])
        S.op("dve", lambda e: e.memset(B3[:, :, 0:129], 0.0), reads=[], writes=[B3])
        for kc in range(t + 1):
            ps = scb.next()
            S.op("pe", lambda e: e.matmul(v4(ps), lhsT=kselT[:, g, kc * 128:(kc + 1) * 128], rhs=qrhs, start=True, stop=True),
                 reads=[kselT, qTg], writes=[ps])
            e_ = et.next()
            S.op("act", lambda e: e.activation(out=e_[:, :], in_=ps[:, :], func=AF.Exp, scale=SC), reads=[ps], writes=[e_])
            if sel:
                S.op("pe", lambda e: e.matmul(pmk[:, 0:128], lhsT=eexp[0:32, kc, :], rhs=sT[0:32, :], start=True, stop=True),
                     reads=[eexp, sT], writes=[pmk])
                if kc == t:
                    mm = m2.next()
                    S.op("dve", lambda e: e.tensor_tensor(out=mm[:, :], in0=pmk[:, 0:128], in1=mle_b[:, :], op=ALU.mult),
                         reads=[pmk, mle_b], writes=[mm])
                    msrc = mm
                else:
                    msrc = pmk
                S.op("dve", lambda e: e.tensor_tensor(out=v4(e_), in0=v4(e_), in1=msrc[:, 0:128].unsqueeze(1).to_broadcast([128, 4, 128]),
                                                      op=ALU.mult), reads=[e_, msrc], writes=[e_])
            elif kc == t:
                S.op("pool", lambda e: e.tensor_tensor(out=v4(e_), in0=v4(e_), in1=mle_b[:, :].unsqueeze(1).to_broadcast([128, 4, 128]),
                                                       op=ALU.mult), reads=[e_, mle_b], writes=[e_])
            for r in range(4):
                S.op("pe", lambda e: e.matmul(B3[:, r, 0:129], lhsT=e_[:, r * 128:(r + 1) * 128], rhs=vsel[:, kc, g, 0:129],
                                              start=False, stop=(kc == t), skip_group_check=True), reads=[e_, vsel], writes=[B3])
        finish_branch(B3, 1, t, a_, False)
        S.op("dve", lambda e: e.memset(A3[:, :, 0:129], 0.0), reads=[], writes=[A3])
        k0 = max(0, t - 4)
        for kc in range(k0, t + 1):
            ps = scb.next()
            S.op("pe", lambda e: e.matmul(v4(ps), lhsT=kwinT[:, g, kc * 128:(kc + 1) * 128], rhs=qrhs, start=True, stop=True),
                 reads=[kwinT, qTg], writes=[ps])
            e_ = et.next()
            S.op("act", lambda e: e.activation(out=e_[:, :], in_=ps[:, :], func=AF.Exp, scale=SC), reads=[ps], writes=[e_])
            mk = mle_b if kc == t else (mgt_b if kc == t - 4 else None)
            if mk is not None:
                S.op("pool", lambda e: e.tensor_tensor(out=v4(e_), in0=v4(e_), in1=mk[:, :].unsqueeze(1).to_broadcast([128, 4, 128]),
                                                       op=ALU.mult), reads=[e_, mk], writes=[e_])
            for r in range(4):
                S.op("pe", lambda e: e.matmul(A3[:, r, 0:129], lhsT=e_[:, r * 128:(r + 1) * 128], rhs=vwin[:, kc, g, 0:129],
                                              start=False, stop=(kc == t), skip_group_check=True), reads=[e_, vwin], writes=[A3])
        finish_branch(A3, 2, t, a_, False)
        o_ = ob.next()
        S.op("dve", lambda e: e.tensor_tensor(out=o_[:, :], in0=a_[:, :, :].rearrange("p r d -> p (r d)"), in1=zg[:, t, :], op=ALU.mult),
             reads=[a_, zg], writes=[o_])
        for r in range(4):
            S.op("pe", lambda e: e.transpose(out=pmisc_bf[:, 128 + r * 128:256 + r * 128], in_=o_[:, r * 128:(r + 1) * 128], identity=ident_b[:, :]),
                 reads=[o_, ident_b], writes=[pmisc_bf])
        s_t = st.next()
        S.op("act", lambda e: e.activation(out=s_t[:, :, :], in_=pmisc_bf[:, 128:640].rearrange("p (r q) -> p r q", r=4), func=AF.Copy),
             reads=[pmisc_bf], writes=[s_t])
        S.dma("sp", N["oT_scr"][t, :, 4 * g:4 * g + 4, :], s_t[:, :, :], reads=[s_t], writes=N["oT_b"][t][4 * g:4 * g + 4])


def nsa_sample(S, es, nc, N):
    qTs, zs, gts, kselT, kwinT, vnew_src, mle_b, mgt_b, ident_b, ident_f, psb, wbuf, cst, SC, evac = (
        N[k] for k in ("qTs", "zs", "gts", "kselT", "kwinT", "vnew_src", "mle_b", "mgt_b", "ident_b", "ident_f", "psb", "wbuf", "cst", "SC", "evac"))
    ckv, cwin = N["cache_kv"], N["cache_win"]
    tl = lambda n, s, d=F32: S.tile(n, s, d, es)
    big = Rot(psb[0:4])
    small = Rot(psb[4:8])
    ptb = tl("ptb", [128, 256], I32)
    S.dma("sp", ptb[:, :], N["page_table"].rearrange("s p -> (s p)").partition_broadcast(128), writes=[ptb])
    pci = tl("pci", [128, 1], I32)
    S.op("pool", lambda e: e.iota(pci[:, :], pattern=[[0, 1]], base=0, channel_multiplier=2), writes=[pci])
    pcf = tl("pcf", [128, 1])
    S.op("dve", lambda e: e.tensor_copy(out=pcf[:, :], in_=pci[:, :]), reads=[pci], writes=[pcf])
    idxA = tl("idxA", [128, 256], I32)
    idxB = tl("idxB", [128, 256], I32)
    S.op("dve", lambda e: e.tensor_scalar(out=idxA[:, :], in0=ptb[:, :], scalar1=256.0, scalar2=pcf[:, 0:1], op0=ALU.mult, op1=ALU.add),
         reads=[ptb, pcf], writes=[idxA])
    S.op("dve", lambda e: e.tensor_scalar(out=idxB[:, :], in0=idxA[:, :], scalar1=1.0, scalar2=None, op0=ALU.add), reads=[idxA], writes=[idxB])
    bg, bz = Buf("gscr"), Buf("zscr")
    S.dma("sp", N["g_scr"][:, :], gts[:TS, 16, :], reads=[gts], writes=[bg])
    S.dma("sp", N["z_scr"][:, :], zs[:TS, :], reads=[zs], writes=[bz])
    gsm = tl("gsm", [16, NS, 2, 3])
    zr = tl("zr", [16, NS, 2, 128], BF16)
    gv = N["g_scr"].rearrange("(s t) (g r b) -> r t s g b", t=4, g=2, r=4)
    zv = N["z_scr"].rearrange("(s t) (g r d) -> r t s g d", t=4, g=2, r=4)
    for r in range(4):
        for g in range(2):
            S.dma("sp", gsm[4 * r:4 * r + 4, :, g, :], gv[r][:, :, g, :], reads=[bg], writes=[gsm])
            S.dma("sp", zr[4 * r:4 * r + 4, :, g, :], zv[r][:, :, g, :], reads=[bz], writes=[zr])
    vnew = tl("vnew", [4, NS, 2, 2, 130], BF16)
    for s in range(NS):
        S.dma("sp", vnew[0:4, s, :, :, :], vnew_src[4 * s:4 * s + 4, :, :, :], reads=[vnew_src], writes=[vnew])
    w1 = [tl(f"w1_{kv}", [128, 32, 256], BF16) for kv in range(2)]
    w2b = tl("w2b", [128, 2, 2, 128], BF16)
    peb = tl("peb", [128, 2, 32], BF16)
    b1t = tl("b1t", [128, 2, 2])
    hb = tl("hb", [128, 2, 2])
    for kv in range(2):
        S.dma("pool", w1[kv][:, :, :], N["cmp_w1"][kv, :, :, :], writes=[w1[kv]])
    S.dma("pool", w2b[:, :, :, :], N["cmp_w2"][:, :, :, :], writes=[w2b])
    S.dma("pool", peb[:, :, :], N["cmp_pe"][:, :, :], writes=[peb])
    S.dma("sp", b1t[:, :, :], N["cmp_b1"][:, :, :], writes=[b1t])
    for kv in range(2):
        pp = small.next()
        for half in range(2):
            for rp in range(32):
                S.op("pe", lambda e: e.matmul(pp[:, half:half + 1], lhsT=w1[kv][:, rp, half * 128:(half + 1) * 128],
                                              rhs=peb[:, kv, rp:rp + 1], start=(rp == 0), stop=(rp == 31)), reads=[w1[kv], peb], writes=[pp])
        S.op("dve", lambda e: e.tensor_tensor(out=hb[:, kv, :], in0=pp[:, 0:2], in1=b1t[:, kv, :], op=ALU.add), reads=[pp, b1t], writes=[hb])
    rsum = tl("rsum", [16, 4])
    S.dma("sp", rsum[:, :], cst["rsum"][:, :], writes=[rsum])
    xTk = tl("xTk", [128, 2, 2048], BF16)
    xTv = tl("xTv", [128, 2, 2048], BF16)
    gT = Rot([tl(f"sgT{i}", [128, 2, 128], BF16) for i in range(2)])
    kcTs = tl("kcTs", [128, 2, 128], BF16)
    vcas = tl("vcas", [128, 2, 162], BF16)
    S.op("pool", lambda e: e.memset(vcas[:, :, 128:129], 1.0), writes=[vcas])
    for g in range(2):
        S.dma("pool", vcas[0:127, g, 129:162], cst["c2ss"][:, :], writes=[vcas])
    ec = tl("sec", [128, 2, 16], BF16)
    den = Rot([tl(f"sden{i}", [16, 2]) for i in range(3)])
    rden = Rot([tl(f"srden{i}", [16, 2]) for i in range(3)])
    cf = Rot([tl(f"scf{i}", [16, 2]) for i in range(3)])
    acc = tl("sacc", [16, 2, 128])
    tmpo = tl("stmpo", [16, 2, 128])
    impn = tl("simpn", [16, 2, 33])
    scs = tl("sscs", [4, 2, 33])
    sc2 = tl("ssc2", [4, 33])
    m8 = tl("sm8", [4, 16])
    selm = tl("sselm", [4, 2, 33], BF16)
    selx = tl("sselx", [4, 2, 33, 64], BF16)
    maskT = tl("smaskT", [128, 2, 16, 4], BF16)
    vsa = tl("vsa", [128, 16, 2, 130], BF16)
    S.op("pool", lambda e: e.memset(vsa[:, :, :, 128:130], 1.0), writes=[vsa])
    esel = tl("esel", [128, 2, 272], BF16)
    wl = tl("wl", [128, 4, 512])
    kwT = tl("kwT", [128, 2, 512], BF16)
    vwa = tl("vwa", [128, 4, 2, 130], BF16)
    S.op("pool", lambda e: e.memset(vwa[:, :, :, 128:130], 1.0), writes=[vwa])
    ewin = tl("ewin", [128, 2, 80], BF16)
    ob = tl("sob", [16, 2, 128], BF16)
    oTs = tl("oTs", [128, 8, TS], BF16)

    def pbview(wb):
        return View(wb[:, :, :].bitcast(F32).rearrange("p c (a f) -> p (c a) f", a=1).rearrange("p (k two) f -> p k (two f)", two=2), wb.b)

    pgbufs = Rot(list(wbuf.items) + [tl(f"pgx{i}", [128, 16, 512], BF16) for i in range(2)])

    def gather(idx, s, hs):
        wb = pgbufs.next()
        pv = pbview(wb)
        for pgl in range(8):
            col = s * 16 + hs * 8 + pgl
            S.dma("pool", None, None, reads=[idx], writes=[pv],
                  fn=lambda e: e.indirect_dma_start(out=pv[:, pgl, :], out_offset=None, in_=ckv[:, :],
                                                    in_offset=bass.IndirectOffsetOnAxis(ap=idx[:, col:col + 1], axis=0)))
        return pv

    def transpose4(src_fn, dst_ap, reads, dstt):
        pt = big.next()
        for k in range(4):
            S.op("pe", lambda e: e.transpose(out=pt[:, k * 128:(k + 1) * 128], in_=src_fn(k), identity=ident_f[:, :]),
                 reads=reads + [ident_f], writes=[pt], sig=(k == 3))
        evac(dst_ap, pt[:, 0:512], [pt], [dstt])

    def finish_branch(P, br, s, first):
        d_, r_, c_ = den.next(), rden.next(), cf.next()
        S.op("dve", lambda e: e.tensor_scalar(out=d_[:, :], in0=P[0:16, :, 128], scalar1=1e-30, scalar2=None, op0=ALU.max), reads=[P], writes=[d_])
        S.op("dve", lambda e: e.reciprocal(out=r_[:, :], in_=d_[:, :]), reads=[d_], writes=[r_])
        S.op("dve", lambda e: e.tensor_tensor(out=c_[:, :], in0=r_[:, :], in1=gsm[:, s, :, br], op=ALU.mult), reads=[r_, gsm], writes=[c_])
        cb = c_[:, :].unsqueeze(2).to_broadcast([16, 2, 128])
        if first:
            S.op("dve", lambda e: e.tensor_tensor(out=acc[:, :, :], in0=P[0:16, :, 0:128], in1=cb, op=ALU.mult), reads=[P, c_], writes=[acc])
        else:
            S.op("dve", lambda e: e.tensor_tensor(out=tmpo[:, :, :], in0=P[0:16, :, 0:128], in1=cb, op=ALU.mult), reads=[P, c_], writes=[tmpo])
            S.op("dve", lambda e: e.tensor_tensor(out=acc[:, :, :], in0=acc[:, :, :], in1=tmpo[:, :, :], op=ALU.add), reads=[acc, tmpo], writes=[acc])
        return r_

    def v3(bank):
        return View(bank[:, :].rearrange("p (g c) -> p g c", g=2), bank.b)

    for s in range(NS):
        q16 = [qTs[:, 4 * g:4 * g + 4, 4 * s:4 * s + 4] for g in range(2)]
        for hs in range(2):
            pv = gather(idxA, s, hs)
            for slot in range(2):
                dstt = xTk if slot == 0 else xTv
                for g in range(2):
                    for q4 in range(2):
                        c0 = (slot * 2 + g) * 128
                        transpose4(lambda k: pv[:, 4 * q4 + k, c0:c0 + 128],
                                   dstt[:, g, (8 * hs + 4 * q4) * 128:(8 * hs + 4 * q4 + 4) * 128], [pv], dstt)
        for kv in range(2):
            xT = xTk if kv == 0 else xTv
            for g in range(2):
                gt = gT.next()
                for half in range(2):
                    ph = big.next()
                    for rp in range(32):
                        st0 = rp
                        S.op("pe", lambda e: e.matmul(ph[:, 0:127], lhsT=w1[kv][:, rp, half * 128:(half + 1) * 128],
                                                      rhs=xT[:, g, st0:st0 + 16 * 126 + 1:16], start=(rp == 0), stop=(rp == 31)),
                             reads=[w1[kv], xT], writes=[ph], sig=(rp == 31))
                    S.op("act", lambda e: e.activation(out=gt[:, half, 0:127], in_=ph[:, 0:127], func=AF.Gelu_apprx_tanh,
                                                       bias=hb[:, kv, half:half + 1]), reads=[ph, hb], writes=[gt])
                po = big.next()
                if kv == 0:
                    for half in range(2):
                        S.op("pe", lambda e: e.matmul(po[:, 0:127], lhsT=w2b[:, 0, half, :], rhs=gt[:, half, 0:127],
                                                      start=(half == 0), stop=(half == 1)), reads=[w2b, gt], writes=[po])
                    evac(kcTs[:, g, 0:127], po[:, 0:127], [po], [kcTs])
                else:
                    for half in range(2):
                        S.op("pe", lambda e: e.matmul(po[0:127, 0:128], lhsT=gt[:, half, 0:127], rhs=w2b[:, 1, half, :],
                                                      start=(half == 0), stop=(half == 1)), reads=[w2b, gt], writes=[po])
                    evac(vcas[0:127, g, 0:128], po[0:127, 0:128], [po], [vcas])
        ps = small.next()
        for g in range(2):
            S.op("pe", lambda e: e.matmul(ps[0:127, g * 16:(g + 1) * 16].rearrange("p (r q) -> p r q", r=4), lhsT=kcTs[:, g, 0:127], rhs=q16[g],
                                          start=True, stop=True), reads=[kcTs, qTs], writes=[ps])
        S.op("act", lambda e: e.activation(out=ec[0:127, :, :], in_=ps[0:127, 0:32].rearrange("p (g c) -> p g c", g=2), func=AF.Exp, scale=SC),
             reads=[ps], writes=[ec])
        pc = v3(small.next())
        for g in range(2):
            S.op("pe", lambda e: e.matmul(pc[0:16, g, 0:162], lhsT=ec[0:127, g, :], rhs=vcas[0:127, g, 0:162], start=True, stop=True),
                 reads=[ec, vcas], writes=[pc])
        rd = finish_branch(pc, 0, s, True)
        S.op("dve", lambda e: e.tensor_tensor(out=impn[:, :, :], in0=pc[0:16, :, 129:162], in1=rd[:, :].unsqueeze(2).to_broadcast([16, 2, 33]),
                                              op=ALU.mult), reads=[pc, rd], writes=[impn])
        pi = small.next()
        S.op("pe", lambda e: e.matmul(pi[0:4, 0:66], lhsT=rsum[:, :], rhs=impn[:, :, :].rearrange("p g s -> p (g s)"), start=True, stop=True),
             reads=[rsum, impn], writes=[pi])
        S.op("dve", lambda e: e.tensor_copy(out=scs[:, :, :], in_=pi[0:4, 0:66].rearrange("p (g s) -> p g s", g=2)), reads=[pi], writes=[scs])
        S.op("dve", lambda e: e.memset(scs[:, :, 0:1], 1e9), reads=[], writes=[scs])
        S.op("dve", lambda e: e.memset(scs[:, :, 31:33], 1e9), reads=[], writes=[scs])
        for g in range(2):
            S.op("dve", lambda e: e.max(out=m8[:, 0:8], in_=scs[:, g, :]), reads=[scs], writes=[m8])
            S.op("dve", lambda e: e.match_replace(out=sc2[:, :], in_to_replace=m8[:, 0:8], in_values=scs[:, g, :], imm_value=-3e38),
                 reads=[scs, m8], writes=[sc2])
            S.op("dve", lambda e: e.max(out=m8[:, 8:16], in_=sc2[:, :]), reads=[sc2], writes=[m8])
            S.op("dve", lambda e: e.tensor_scalar(out=selm[:, g, :], in0=scs[:, g, :], scalar1=m8[:, 15:16], scalar2=None, op0=ALU.is_ge),
                 reads=[scs, m8], writes=[selm])
        S.op("dve", lambda e: e.tensor_copy(out=selx[:, :, :, :].rearrange("p g s k -> p (g s) k"),
                                            in_=selm[:, :, :].rearrange("p g s -> p (g s)").unsqueeze(2).to_broadcast([4, 66, 64])),
             reads=[selm], writes=[selx])
        pm = small.next()
        for g in range(2):
            for kc in range(16):
                S.op("pe", lambda e: e.matmul(pm[:, (g * 16 + kc) * 4:(g * 16 + kc) * 4 + 4],
                                              lhsT=selx[0:4, g, 2 * kc:2 * kc + 2, :].rearrange("p a b -> p (a b)"), rhs=ident_b[0:4, 0:4],
                                              start=True, stop=True), reads=[selx, ident_b], writes=[pm])
        S.op("act", lambda e: e.activation(out=maskT[:, :, :, :].rearrange("p g k t -> p (g k t)"), in_=pm[:, 0:128], func=AF.Copy),
             reads=[pm], writes=[maskT])
        for hs in range(2):
            pv = gather(idxB, s, hs)
            for g in range(2):
                for q4 in range(2):
                    transpose4(lambda k: pv[:, 4 * q4 + k, g * 128:(g + 1) * 128],
                               xTk[:, g, (8 * hs + 4 * q4) * 128:(8 * hs + 4 * q4 + 4) * 128], [pv], xTk)
            S.op("act", lambda e: e.activation(out=vsa[:, 8 * hs:8 * hs + 8, :, 0:128],
                                               in_=pv[:, :, 256:512].rearrange("p k (g d) -> p k g d", g=2), func=AF.Copy), reads=[pv], writes=[vsa])
        pss = v3(small.next())
        psn = v3(small.next())
        for g in range(2):
            for kc in range(16):
                S.op("pe", lambda e: e.matmul(pss[:, g, kc * 16:(kc + 1) * 16].rearrange("p (r q) -> p r q", r=4),
                                              lhsT=xTk[:, g, kc * 128:(kc + 1) * 128], rhs=q16[g], start=True, stop=True),
                     reads=[xTk, qTs], writes=[pss])
            S.op("pe", lambda e: e.matmul(psn[0:4, g, 0:16].rearrange("p (r q) -> p r q", r=4),
                                          lhsT=kselT[:, g, 4 * s:4 * s + 4], rhs=q16[g], start=True, stop=True),
                 reads=[kselT, qTs], writes=[psn])
        S.op("act", lambda e: e.activation(out=esel[:, :, 0:256], in_=pss[:, :, 0:256], func=AF.Exp, scale=SC), reads=[pss], writes=[esel])
        S.op("act", lambda e: e.activation(out=esel[0:4, :, 256:272], in_=psn[0:4, :, 0:16], func=AF.Exp, scale=SC), reads=[psn], writes=[esel])
        for g in range(2):
            S.op("dve", lambda e: e.tensor_tensor(out=esel[:, g, 0:256].rearrange("p (k r q) -> p k r q", k=16, r=4),
                                                  in0=esel[:, g, 0:256].rearrange("p (k r q) -> p k r q", k=16, r=4),
                                                  in1=maskT[:, g, :, :].unsqueeze(2).to_broadcast([128, 16, 4, 4]), op=ALU.mult),
                 reads=[esel, maskT], writes=[esel])
        S.op("dve", lambda e: e.tensor_tensor(out=esel[0:4, :, 256:272].rearrange("p g (r q) -> p g r q", r=4),
                                              in0=esel[0:4, :, 256:272].rearrange("p g (r q) -> p g r q", r=4),
                                              in1=mle_b[0:4, 0:4].unsqueeze(1).unsqueeze(1).to_broadcast([4, 2, 4, 4]), op=ALU.mult),
             reads=[esel, mle_b], writes=[esel])
        po_ = v3(small.next())
        for g in range(2):
            for kc in range(16):
                S.op("pe", lambda e: e.matmul(po_[0:16, g, 0:129], lhsT=esel[:, g, kc * 16:(kc + 1) * 16], rhs=vsa[:, kc, g, 0:129],
                                              start=(kc == 0), stop=False), reads=[esel, vsa], writes=[po_])
            S.op("pe", lambda e: e.matmul(po_[0:16, g, 0:129], lhsT=esel[0:4, g, 256:272], rhs=vnew[0:4, s, 0, g, 0:129],
                                          start=False, stop=True), reads=[esel, vnew], writes=[po_])
        finish_branch(po_, 1, s, False)
        S.dma("sp", wl[:, :, :], cwin[s, :, :].rearrange("(c p) f -> p c f", p=128), writes=[wl])
        for g in range(2):
            transpose4(lambda k: wl[:, k, g * 128:(g + 1) * 128], kwT[:, g, :], [wl], kwT)
        S.op("act", lambda e: e.activation(out=vwa[:, :, :, 0:128], in_=wl[:, :, 256:512].rearrange("p k (g d) -> p k g d", g=2), func=AF.Copy),
             reads=[wl], writes=[vwa])
        psw = v3(small.next())
        pwn = v3(small.next())
        for g in range(2):
            for kc in range(4):
                S.op("pe", lambda e: e.matmul(psw[:, g, kc * 16:(kc + 1) * 16].rearrange("p (r q) -> p r q", r=4),
                                              lhsT=kwT[:, g, kc * 128:(kc + 1) * 128], rhs=q16[g], start=True, stop=True),
                     reads=[kwT, qTs], writes=[psw])
            S.op("pe", lambda e: e.matmul(pwn[0:4, g, 0:16].rearrange("p (r q) -> p r q", r=4),
                                          lhsT=kwinT[:, g, 4 * s:4 * s + 4], rhs=q16[g], start=True, stop=True),
                 reads=[kwinT, qTs], writes=[pwn])
        S.op("act", lambda e: e.activation(out=ewin[:, :, 0:64], in_=psw[:, :, 0:64], func=AF.Exp, scale=SC), reads=[psw], writes=[ewin])
        S.op("act", lambda e: e.activation(out=ewin[0:4, :, 64:80], in_=pwn[0:4, :, 0:16], func=AF.Exp, scale=SC), reads=[pwn], writes=[ewin])
        S.op("dve", lambda e: e.tensor_tensor(out=ewin[:, :, 0:16].rearrange("p g (r q) -> p g r q", r=4),
                                              in0=ewin[:, :, 0:16].rearrange("p g (r q) -> p g r q", r=4),
                                              in1=mgt_b[:, 0:4].unsqueeze(1).unsqueeze(1).to_broadcast([128, 2, 4, 4]), op=ALU.mult),
             reads=[ewin, mgt_b], writes=[ewin])
        S.op("dve", lambda e: e.tensor_tensor(out=ewin[0:4, :, 64:80].rearrange("p g (r q) -> p g r q", r=4),
                                              in0=ewin[0:4, :, 64:80].rearrange("p g (r q) -> p g r q", r=4),
                                              in1=mle_b[0:4, 0:4].unsqueeze(1).unsqueeze(1).to_broadcast([4, 2, 4, 4]), op=ALU.mult),
             reads=[ewin, mle_b], writes=[ewin])
        pw_ = v3(small.next())
        for g in range(2):
            for kc in range(4):
                S.op("pe", lambda e: e.matmul(pw_[0:16, g, 0:129], lhsT=ewin[:, g, kc * 16:(kc + 1) * 16], rhs=vwa[:, kc, g, 0:129],
                                              start=(kc == 0), stop=False), reads=[ewin, vwa], writes=[pw_])
            S.op("pe", lambda e: e.matmul(pw_[0:16, g, 0:129], lhsT=ewin[0:4, g, 64:80], rhs=vnew[0:4, s, 1, g, 0:129],
                                          start=False, stop=True), reads=[ewin, vnew], writes=[pw_])
        finish_branch(pw_, 2, s, False)
        S.op("dve", lambda e: e.tensor_tensor(out=ob[:, :, :], in0=acc[:, :, :], in1=zr[:, s, :, :], op=ALU.mult), reads=[acc, zr], writes=[ob])
        pt = small.next()
        ptb_ = View(pt[:, :].bitcast(BF16), pt.b)
        for g in range(2):
            S.op("pe", lambda e: e.transpose(out=ptb_[:, g * 16:(g + 1) * 16], in_=ob[0:16, g, :], identity=ident_b[0:16, 0:16]),
                 reads=[ob, ident_b], writes=[ptb_])
        S.op("act", lambda e: e.activation(out=oTs[:, :, 4 * s:4 * s + 4], in_=ptb_[:, 0:32].rearrange("p (f q) -> p f q", f=8), func=AF.Copy),
             reads=[ptb_], writes=[oTs])
    S.dma("sp", N["oT_scr"][16, :, 0:8, 0:TS], oTs[:, :, :], reads=[oTs], writes=N["oT_b"][16][0:8])


def rec_setup(S, e2, R, V):
    G = dict(R)
    tl = lambda n, s, d=F32: S.tile(n, s, d, e2)
    G["cs"] = Rot([tl(f"cs{i}", [128, 128]) for i in range(2)])
    G["E1"] = Rot([tl(f"E1{i}", [128, 128]) for i in range(2)])
    G["E2"] = Rot([tl(f"E2{i}", [128, 128]) for i in range(2)])
    G["sm"] = Rot([tl(f"sm{i}", [128, 16]) for i in range(3)])
    G["E3"] = Rot([tl(f"E3{i}", [128, 128]) for i in range(2)])
    G["E4"] = Rot([tl(f"E4{i}", [128, 128]) for i in range(2)])
    G["qx"] = Rot([tl(f"qx{i}", [128, 64], BF16) for i in range(2)])
    G["qS"] = Rot([tl(f"qS{i}", [128, 128], BF16) for i in range(2)])
    G["kS"] = Rot([tl(f"kS{i}", [128, 128], BF16) for i in range(2)])
    G["KA"] = Rot([tl(f"KA{i}", [128, 128], BF16) for i in range(2)])
    G["KB"] = Rot([tl(f"KB{i}", [128, 128], BF16) for i in range(2)])
    for kk in ("KA", "KB"):
        for tt in G[kk].items:
            S.op("pool", lambda e: e.memset(tt[:, :], 0.0), writes=[tt])
    G["qp"] = Rot([tl(f"qp{i}", [128, 128], BF16) for i in range(2)])
    G["kp"] = Rot([tl(f"kp{i}", [128, 128], BF16) for i in range(2)])
    G["ktok"] = Rot([tl(f"ktok{i}", [128, 128], BF16) for i in range(2)])
    G["attm"] = Rot([tl(f"attm{i}", [128, 128], BF16) for i in range(2)])
    G["Sst"] = tl("Sst", [128, V])
    G["Sbf"] = Rot([tl(f"Sbf{i}", [128, V], BF16) for i in range(2)])
    G["ones"] = tl("ones", [128, 128])
    S.op("pool", lambda e: e.memset(G["ones"][:, :], 1.0), writes=[G["ones"]])
    G["bdm"] = tl("bdm", [64, 64], BF16)
    S.dma("pool", G["bdm"][:, :], R["cst"]["bd"][:, :], writes=[G["bdm"]])
    G["colm"] = tl("colm", [128, 16, 64], BF16)
    S.dma("pool", G["colm"][:, :, :], R["cst"]["colmask"][:, :, :], writes=[G["colm"]])
    G["rowm"] = tl("rowm", [64, 16], BF16)
    S.dma("pool", G["rowm"][:, :], R["cst"]["rowmask"][:, :], writes=[G["rowm"]])
    G["s0"] = Rot([tl(f"s0{i}", [128, V]) for i in range(3)])
    G["s0b"] = Rot([tl(f"s0b{i}", [128, V], BF16) for i in range(3)])
    G["sn"] = Rot([tl(f"sn{i}", [128, V]) for i in range(3)])
    G["qm"] = tl("qm", [128, 16, 64], BF16)
    G["km"] = tl("km", [64, 16, 128], BF16)
    G["oh"] = Rot([tl(f"oh{i}", [128, NT, V], BF16) for i in range(1 if V == 256 else 2)])
    G["pss"] = tl("pss", [128, NT])
    G["prs"] = tl("prs", [128, NT])
    G["pjunk"] = tl("pjunk", [128, V], BF16)
    G["ptmp"] = Rot([tl(f"ptmp{i}", [128, V]) for i in range(2)])
    G["pob"] = Rot([tl(f"pob{i}", [128, V], BF16) for i in range(2)])
    G["post"] = Rot([tl(f"post{i}", [128, V // 128, 128], BF16) for i in range(3)])
    return G


def rec_head(S, G, hs, V, COEF, QS, gate, state_in, out_p, out_s, hook=None):
    qT, kT, v = hs["q"], hs["k"], hs["v"]
    mle_b, ident_b, psb, obuf = (G[k] for k in ("mle_b", "ident_b", "psb", "obuf"))
    p_att, p_o, p_ds = psb[4], psb[5], Rot([psb[0], psb[1]])
    Sst = G["Sst"]
    oh = G["oh"].next()
    for t in range(16):
        if hook is not None:
            hook()
        tok = slice(t * 128, (t + 1) * 128)
        c_, e1, e2_, e3, e4, s_ = (G[k].next() for k in ("cs", "E1", "E2", "E3", "E4", "sm"))
        q_, qx, qS, KA, KB, kS, kt_, at_ = (G[k].next() for k in ("qp", "qx", "qS", "KA", "KB", "kS", "ktok", "attm"))
        gap, gtt = gate(tok, 128)
        S.op("dve", lambda e: e.tensor_tensor_scan(out=c_[:, :], data0=G["ones"][:, :], data1=gap, initial=0.0,
                                                   op0=ALU.mult, op1=ALU.add), reads=[G["ones"], gtt], writes=[c_])
        S.op("dve", lambda e: e.tensor_scalar(out=s_[:, 4:8], in0=c_[:, 31:128:32], scalar1=-COEF, scalar2=None, op0=ALU.mult), reads=[c_], writes=[s_])
        S.op("dve", lambda e: e.tensor_scalar(out=s_[:, 8:12], in0=c_[:, 31:128:32], scalar1=COEF, scalar2=None, op0=ALU.mult), reads=[c_], writes=[s_])
        S.op("act", lambda e: e.activation(out=s_[:, 14:15], in_=c_[:, 95:96], func=AF.Exp, scale=COEF, bias=s_[:, 4:5]), reads=[c_, s_], writes=[s_])
        S.op("act", lambda e: e.activation(out=s_[:, 15:16], in_=c_[:, 127:128], func=AF.Exp, scale=COEF), reads=[c_], writes=[s_])
        lo, hi = slice(0, 64), slice(64, 128)
        S.op("act", lambda e: e.activation(out=e1[:, lo], in_=c_[:, lo], func=AF.Exp, scale=COEF, bias=s_[:, 4:5]), reads=[c_, s_], writes=[e1])
        S.op("act", lambda e: e.activation(out=e1[:, hi], in_=c_[:, hi], func=AF.Exp, scale=COEF, bias=s_[:, 6:7]), reads=[c_, s_], writes=[e1])
        S.op("act", lambda e: e.activation(out=e2_[:, lo], in_=c_[:, lo], func=AF.Exp, scale=-COEF, bias=s_[:, 8:9]), reads=[c_, s_], writes=[e2_])
        S.op("act", lambda e: e.activation(out=e2_[:, hi], in_=c_[:, hi], func=AF.Exp, scale=-COEF, bias=s_[:, 10:11]), reads=[c_, s_], writes=[e2_])
        S.op("act", lambda e: e.activation(out=e3[:, :], in_=c_[:, :], func=AF.Exp, scale=-COEF, bias=s_[:, 11:12]), reads=[c_, s_], writes=[e3])
        S.op("act", lambda e: e.activation(out=e4[:, :], in_=c_[:, :], func=AF.Exp, scale=COEF), reads=[c_], writes=[e4])
        S.op("dve", lambda e: e.scalar_tensor_tensor(out=q_[:, :], in0=qT[:, tok], scalar=QS, in1=e1[:, :],
                                                     op0=ALU.mult, op1=ALU.mult), reads=[qT, e1], writes=[q_])
        S.op("dve", lambda e: e.tensor_scalar(out=qx[:, :], in0=q_[:, hi], scalar1=s_[:, 14:15], scalar2=None, op0=ALU.mult),
             reads=[q_, s_], writes=[qx])
        S.op("dve", lambda e: e.tensor_tensor(out=KA[:, lo], in0=kT[:, t * 128:t * 128 + 64], in1=e2_[:, lo], op=ALU.mult),
             reads=[kT, e2_], writes=[KA])
        S.op("dve", lambda e: e.tensor_tensor(out=KB[:, hi], in0=kT[:, t * 128 + 64:(t + 1) * 128], in1=e2_[:, hi], op=ALU.mult),
             reads=[kT, e2_], writes=[KB])
        S.op("dve", lambda e: e.tensor_tensor(out=kS[:, :], in0=kT[:, tok], in1=e3[:, :], op=ALU.mult), reads=[kT, e3], writes=[kS])
        S.op("pe", lambda e: e.matmul(p_att[:, 0:64], lhsT=KA[:, :], rhs=q_[:, lo], start=True, stop=True),
             reads=[KA, q_], writes=[p_att])
        S.op("pe", lambda e: e.matmul(p_att[:, 64:128], lhsT=KA[:, :], rhs=qx[:, :], start=True, stop=False),
             reads=[KA, qx], writes=[p_att])
        S.op("pe", lambda e: e.matmul(p_att[:, 64:128], lhsT=KB[:, :], rhs=q_[:, hi], start=False, stop=True),
             reads=[KB, q_], writes=[p_att])
        S.op("dve", lambda e: e.tensor_tensor(out=at_[:, :], in0=p_att[:, 0:128], in1=mle_b[:, :], op=ALU.mult),
             reads=[p_att, mle_b], writes=[at_])
        pf = G["psbf"].next()
        S.op("pe", lambda e: e.transpose(out=pf[:, 0:128], in_=kS[:, :], identity=ident_b[:, :]),
             reads=[kS, ident_b], writes=[pf])
        S.op("act", lambda e: e.activation(out=kt_[:, :], in_=pf[:, 0:128], func=AF.Copy), reads=[pf], writes=[kt_])
        vv = v[:, t, :]
        if t > 0:
            sb_ = G["Sbf"].next()
            S.op("pool", lambda e: e.tensor_copy(out=sb_[:, :], in_=Sst[:, :]), reads=[Sst], writes=[sb_])
            S.op("dve", lambda e: e.scalar_tensor_tensor(out=qS[:, :], in0=qT[:, tok], scalar=QS, in1=e4[:, :],
                                                         op0=ALU.mult, op1=ALU.mult), reads=[qT, e4], writes=[qS])
        S.op("pe", lambda e: e.matmul(p_o[:, 0:V], lhsT=at_[:, :], rhs=vv, start=True, stop=(t == 0)),
             reads=[at_, v], writes=[p_o])
        if t > 0:
            S.op("pe", lambda e: e.matmul(p_o[:, 0:V], lhsT=qS[:, :], rhs=sb_[:, :], start=False, stop=True),
                 reads=[qS, sb_], writes=[p_o])
        S.op("act", lambda e: e.activation(out=oh[:, t, :], in_=p_o[:, 0:V], func=AF.Copy), reads=[p_o], writes=[oh])
        pd = p_ds.next()
        S.op("pe", lambda e: e.matmul(pd[:, 0:V], lhsT=kt_[:, :], rhs=vv, start=True, stop=True), reads=[kt_, v], writes=[pd])
        if t == 0:
            S.op("dve", lambda e: e.tensor_copy(out=Sst[:, :], in_=pd[:, 0:V]), reads=[pd], writes=[Sst])
        else:
            S.op("dve", lambda e: e.scalar_tensor_tensor(out=Sst[:, :], in0=Sst[:, :], scalar=s_[:, 15:16], in1=pd[:, 0:V],
                                                         op0=ALU.mult, op1=ALU.add), reads=[pd, s_, Sst], writes=[Sst])
    S.dma("sp", out_p, Sst[:, :], reads=[Sst], writes=[obuf("st_p")])

    tok = slice(T, TA)
    c_, e1, e2_ = G["cs"].next(), G["E1"].next(), G["E2"].next()
    q_, k_, kt_, at_ = G["qp"].next(), G["kp"].next(), G["ktok"].next(), G["attm"].next()
    qm, km, bdm, colm, rowm = G["qm"], G["km"], G["bdm"], G["colm"], G["rowm"]
    gap, gtt = gate(tok, TS)
    l3 = gap.rearrange("p (s t) -> p s t", t=4)
    c3 = c_[:, 0:TS].rearrange("p (s t) -> p s t", t=4)
    S.op("dve", lambda e: e.tensor_copy(out=c3[:, :, 0], in_=l3[:, :, 0]), reads=[gtt], writes=[c_])
    for i in range(1, 4):
        S.op("dve", lambda e: e.tensor_tensor(out=c3[:, :, i], in0=c3[:, :, i - 1], in1=l3[:, :, i], op=ALU.add),
             reads=[gtt, c_], writes=[c_])
    S.op("act", lambda e: e.activation(out=e1[:, 0:TS], in_=c_[:, 0:TS], func=AF.Exp, scale=COEF), reads=[c_], writes=[e1])
    S.op("act", lambda e: e.activation(out=e2_[:, 0:TS], in_=c_[:, 0:TS], func=AF.Exp, scale=-COEF), reads=[c_], writes=[e2_])
    S.op("dve", lambda e: e.scalar_tensor_tensor(out=q_[:, 0:TS], in0=qT[:, tok], scalar=QS, in1=e1[:, 0:TS],
                                                 op0=ALU.mult, op1=ALU.mult), reads=[qT, e1], writes=[q_])
    S.op("dve", lambda e: e.tensor_tensor(out=k_[:, 0:TS], in0=kT[:, tok], in1=e2_[:, 0:TS], op=ALU.mult),
         reads=[kT, e2_], writes=[k_])
    S.op("pe", lambda e: e.matmul(p_att[0:TS, 0:TS], lhsT=k_[:, 0:TS], rhs=q_[:, 0:TS], start=True, stop=True),
         reads=[k_, q_], writes=[p_att])
    S.op("dve", lambda e: e.tensor_tensor(out=at_[0:TS, 0:TS], in0=p_att[0:TS, 0:TS], in1=bdm[:, :], op=ALU.mult),
         reads=[p_att, bdm], writes=[at_])
    pf = G["psbf"].next()
    S.op("pe", lambda e: e.transpose(out=pf[0:TS, 0:128], in_=k_[:, 0:TS], identity=ident_b[:, :]),
         reads=[k_, ident_b], writes=[pf])
    S.op("act", lambda e: e.activation(out=kt_[0:TS, :], in_=pf[0:TS, 0:128], func=AF.Copy), reads=[pf], writes=[kt_])
    S.op("dve", lambda e: e.tensor_tensor(out=qm[:, :, :], in0=q_[:, 0:TS].unsqueeze(1).to_broadcast([128, 16, TS]),
                                          in1=colm[:, :, :], op=ALU.mult), reads=[q_, colm], writes=[qm])
    S.op("dve", lambda e: e.tensor_tensor(out=km[:, :, :], in0=kt_[0:TS, :].unsqueeze(1).to_broadcast([TS, 16, 128]),
                                          in1=rowm[:, :].unsqueeze(2).to_broadcast([TS, 16, 128]), op=ALU.mult),
         reads=[kt_, rowm], writes=[km])
    vv = v[0:TS, 16, :]
    S.op("pe", lambda e: e.matmul(p_o[0:TS, 0:V], lhsT=at_[0:TS, 0:TS], rhs=vv, start=True, stop=False),
         reads=[at_, v], writes=[p_o])
    for s in range(NS):
        a0, a0b, an = G["s0"].next(), G["s0b"].next(), G["sn"].next()
        S.dma("sp", a0[:, :], state_in(s), writes=[a0])
        S.op("pool", lambda e: e.tensor_copy(out=a0b[:, :], in_=a0[:, :]), reads=[a0], writes=[a0b])
        S.op("pe", lambda e: e.matmul(p_o[0:TS, 0:V], lhsT=qm[:, s, :], rhs=a0b[:, :], start=False, stop=(s == NS - 1)),
             reads=[qm, a0b], writes=[p_o])
        pd = p_ds.next()
        S.op("pe", lambda e: e.matmul(pd[:, 0:V], lhsT=km[:, s, :], rhs=vv, start=True, stop=True), reads=[km, v], writes=[pd])
        etot = e1[:, 4 * s + 3:4 * s + 4]
        S.op("act", lambda e: e.activation(out=a0[:, :], in_=a0[:, :], func=AF.Copy, scale=etot), reads=[a0, e1], writes=[a0])
        S.op("dve", lambda e: e.scalar_tensor_tensor(out=an[:, :], in0=pd[:, 0:V], scalar=etot, in1=a0[:, :],
                                                     op0=ALU.mult, op1=ALU.add), reads=[pd, e1, a0], writes=[an])
        S.dma("pool", out_s(s), an[:, :], reads=[an], writes=[obuf("st_s")])
    S.op("act", lambda e: e.activation(out=oh[0:TS, 16, :], in_=p_o[0:TS, 0:V], func=AF.Copy), reads=[p_o], writes=[oh])
    return oh


def post_head(S, G, oh, z, gnb, V, fc0, dbgh=False):
    pss, prs, pjunk = G["pss"], G["prs"], G["pjunk"]
    ident_b = G["ident_b"]
    nf = V // 128
    for t in range(NT):
        rows = 128 if t < 16 else TS
        S.op("act", lambda e: e.activation(out=pjunk[:rows, :], in_=oh[:rows, t, :], func=AF.Square, accum_out=pss[:rows, t:t + 1]),
             reads=[oh], writes=[pjunk, pss])
    S.op("act", lambda e: e.activation(out=prs[:TS, :], in_=pss[:TS, :], func=AF.Sqrt, scale=1.0 / V, bias=EPS), reads=[pss], writes=[prs])
    S.op("act", lambda e: e.activation(out=prs[TS:, 0:16], in_=pss[TS:, 0:16], func=AF.Sqrt, scale=1.0 / V, bias=EPS), reads=[pss], writes=[prs])
    S.op("dve", lambda e: e.reciprocal(out=prs[:TS, :], in_=prs[:TS, :]), reads=[prs], writes=[prs])
    S.op("dve", lambda e: e.reciprocal(out=prs[TS:, 0:16], in_=prs[TS:, 0:16]), reads=[prs], writes=[prs])
    for t in range(NT):
        rows = 128 if t < 16 else TS
        tmp, ob, st = G["ptmp"].next(), G["pob"].next(), G["post"].next()
        S.op("dve", lambda e: e.scalar_tensor_tensor(out=tmp[:rows, :], in0=oh[:rows, t, :], scalar=prs[:rows, t:t + 1], in1=gnb[:rows, :],
                                                     op0=ALU.mult, op1=ALU.mult), reads=[oh, prs, gnb], writes=[tmp])
        S.op("dve", lambda e: e.tensor_tensor(out=ob[:rows, :], in0=tmp[:rows, :], in1=z[:rows, t, :], op=ALU.mult),
             reads=[tmp, z], writes=[ob])
        pf = G["psbf"].next()
        for k in range(nf):
            S.op("pe", lambda e: e.transpose(out=pf[:, k * 128:k * 128 + rows], in_=ob[:rows, k * 128:(k + 1) * 128],
                                             identity=ident_b[:rows, :rows]), reads=[ob, ident_b], writes=[pf])
        S.op("act", lambda e: e.activation(out=st[:, :, 0:rows], in_=pf[:, 0:nf * 128].rearrange("p (k t) -> p k t", k=nf)[:, :, 0:rows],
                                           func=AF.Copy), reads=[pf], writes=[st])
        S.dma("sp", G["oT_scr"][t, :, fc0:fc0 + nf, 0:rows], st[:, :, 0:rows], reads=[st], writes=G["oT_b"][t][fc0:fc0 + nf])
        if dbgh and t == 0:
            G["dbg"]("pss", pss, pss[:, :], [128, NT])
            G["dbg"]("prs", prs, prs[:, :], [128, NT])
            G["dbg"]("tmp", tmp, tmp[:, :], [128, V])
            G["dbg"]("ob", ob, ob[:, :], [128, V], BF16)
            G["dbg"]("st", st, st[:, :, :], [128, nf, 128], BF16)


_PROG = {}


def _arr(w, ncols):
    return np.ascontiguousarray(w.reshape(16, 128, ncols).transpose(1, 0, 2))


def kernel(x_prompt, x_sample, cache_kv, cache_win, state_gla, state_hgrn, page_table,
           a_norm, a_w_in, a_gla_w2, a_gla_b, a_gla_gn, a_cmp_pe, a_cmp_w1, a_cmp_b1, a_cmp_w2, a_w_out,
           c_norm, c_w_in, c_lb_logits, c_gn, c_w_out, final_norm):
    f32 = np.float32
    asc = lambda a: np.ascontiguousarray(np.asarray(a, dtype=f32))
    x_prompt, x_sample = asc(x_prompt), asc(x_sample)
    if "nc" not in _PROG:
        _PROG["nc"] = build_program()
    nc = _PROG["nc"]
    consts = make_consts()
    w0 = np.asarray(a_w_in, f32)[0]
    wA = _arr(w0, 6696)
    cols = []
    for h in range(4):
        cols += [np.arange(3608 + h * 128, 3608 + (h + 1) * 128), np.arange(4120 + h * 128, 4120 + (h + 1) * 128),
                 np.arange(4632 + h * 256, 4632 + (h + 1) * 256), np.arange(5672 + h * 256, 5672 + (h + 1) * 256)]
    cols.append(np.arange(5656, 5672))
    wG = _arr(np.ascontiguousarray(w0[:, np.concatenate(cols)]), 3088)
    wc = np.asarray(c_w_in, f32)[0]
    cols = []
    for h in range(16):
        cols += [np.arange(k * 2048 + h * 128, k * 2048 + (h + 1) * 128) for k in range(4)]
    wC = _arr(np.ascontiguousarray(wc[:, np.concatenate(cols)]), 8192)
    wOA = _arr(np.asarray(a_w_out, f32)[0], 2048)
    wOC = _arr(np.asarray(c_w_out, f32)[0], 2048)
    a_norm_r = asc(np.asarray(a_norm, f32)[0].reshape(16, 128).T)
    c_norm_r = asc(np.asarray(c_norm, f32)[0].reshape(16, 128).T)
    lb_log = asc(np.asarray(c_lb_logits, f32).reshape(2, 16, 128).transpose(2, 0, 1))
    w2a = asc(np.concatenate([np.asarray(a_gla_w2, f32)[0], np.asarray(a_gla_b, f32)[0][None, :]], axis=0))
    cw1 = asc(np.asarray(a_cmp_w1, f32)[0].reshape(2, 32, 128, 256).transpose(0, 2, 1, 3))
    cw2 = asc(np.asarray(a_cmp_w2, f32)[0].reshape(2, 2, 128, 128).transpose(2, 0, 1, 3))
    cpe = asc(np.asarray(a_cmp_pe, f32)[0].transpose(2, 0, 1))
    cb1 = asc(np.asarray(a_cmp_b1, f32)[0].reshape(2, 2, 128).transpose(2, 0, 1))
    state_gla = np.asarray(state_gla, f32)
    state_hgrn = np.asarray(state_hgrn, f32)
    cache_win = np.asarray(cache_win, f32)
    ckv2 = asc(cache_kv).reshape(2560 * 128 * 2, 512)
    in_maps = []
    for c in range(NCORES):
        sl = slice(c * NS, (c + 1) * NS)
        m = {
            "x_p": x_prompt[c % 4],
            "x_s": asc(x_sample[sl].reshape(TS, D)),
            "cache_win": asc(cache_win[0, sl].reshape(NS, 512, 512)),
            "cache_kv": ckv2, "page_table": np.ascontiguousarray(np.asarray(page_table)[sl].astype(np.int32)),
            "state_gla": asc(state_gla[0, sl]),
            "state_hgrn": asc(state_hgrn[0, sl]),
            "a_norm": a_norm_r, "c_norm": c_norm_r, "f_norm": asc(final_norm),
            "wA": wA, "wG": wG, "wC": wC, "wOA": wOA, "wOC": wOC,
            "gla_w2a": w2a, "gla_gn": asc(np.asarray(a_gla_gn, f32)[0]), "c_gn": asc(np.asarray(c_gn, f32)[0]),
            "lb_log": lb_log, "cmp_w1": cw1, "cmp_w2": cw2, "cmp_pe": cpe, "cmp_b1": cb1,
        }
        for k, v in consts.items():
            m["c_" + k] = v
        in_maps.append(m)
    res = run_bass_kernel_spmd(nc, in_maps, core_ids=list(range(NCORES)))
    R = res.results
    B, SEQ, DB = 4, 2048, 128
    cat = lambda name, shp: np.concatenate([np.asarray(R[c][name], f32).reshape(shp) for c in range(NCORES)])
    stk = lambda name, shp: np.stack([np.asarray(R[c][name], f32).reshape(shp) for c in range(4)])
    y_p = stk("y_p", (SEQ, D))
    y_s = cat("y_s", (NS, 4, D))
    kv_p = stk("kv_p", (SEQ, 4, 2, 128))[None]
    kv_s = cat("kv_s", (NS, 4, 4, 2, 128))[None]
    win_p = stk("win_p", (512, 2, 2, 128))[None]
    win_s = cat("win_s", (NS, 512, 2, 2, 128))[None]
    gla_p = stk("gla_p", (4, 128, 256))[None]
    gla_s = cat("gla_s", (NS, 4, 128, 256))[None]
    hg_p = stk("hg_p", (16, 128, 128))[None]
    hg_s = cat("hg_s", (NS, 16, 128, 128))[None]
    return (y_p, y_s, kv_p, kv_s, win_p, win_s, gla_p, gla_s, hg_p, hg_s)
```
